# Optimizing a Trainium2 kernel written in Bass

```python
import functools
import jax, jax.numpy as jnp
from jax import lax
import numpy as np


D_MODEL = 1024
BATCH = 8
SEQ = 2048
DEPTH = 1
DEC_BATCH = 32
DEC_SEQ = 1
PAST_LEN = 8192
PAGE_SIZE = 128

ATT_W = D_MODEL // 2
HEAD_DIM = 64
ATT_H = ATT_W // HEAD_DIM
IDX_H = 16
IDX_D = 64
IDX_W_SCALE = (IDX_H ** -0.5) * (IDX_D ** -0.5)
TOPK_MAX = 256
Q_BLOCK = 128
RW_W = D_MODEL - ATT_W
RW_N = 64
RW_H = RW_W // RW_N
W_LORA = 64
A_LORA = 64
G_LORA = 128
GN_EPS = 64e-5
MIX_W = ATT_W + RW_W
ATT_COLS = 3 * ATT_W + IDX_H * IDX_D + IDX_D + IDX_H
RW_COLS = 3 * RW_W + W_LORA + A_LORA + G_LORA
IN_COLS = ATT_COLS + RW_COLS
FFN_DIM = ((8 * D_MODEL + 3 * 256 - 1) // (3 * 256)) * 256
RMS_EPS = 1e-6

kernel_name = 'hymba_dsa_rwkv7_adaln_step'


def rms_norm(x, g=None, eps=RMS_EPS):
    xf = x.astype(jnp.float32)
    y = xf * lax.rsqrt(jnp.mean(xf * xf, axis=-1, keepdims=True) + eps)
    if g is not None:
        y = y * g.astype(jnp.float32)
    return y.astype(x.dtype)


def gather_rows(src, idx):
    return jax.vmap(lambda s, i: s[i])(src, idx)


def indexer_scores(qi, ki, wi):
    dots = jnp.einsum('bqhd,bld->bqhl', qi.astype(jnp.float32), ki.astype(jnp.float32))
    return jnp.einsum('bqhl,bqh->bql', jax.nn.relu(dots), wi.astype(jnp.float32))


def attend_selected(q, kg, vg, valid):
    s = jnp.einsum('bqhd,bqkhd->bqhk', q.astype(jnp.float32), kg.astype(jnp.float32)) * (HEAD_DIM ** -0.5)
    s = jnp.where(valid[:, :, None, :], s, -jnp.inf)
    p = jax.nn.softmax(s, axis=-1)
    o = jnp.einsum('bqhk,bqkhd->bqhd', p, vg.astype(jnp.float32))
    return o.astype(q.dtype)


def prompt_attention(q, k, v, qi, ki, wi):
    B, T = q.shape[:2]
    n_sel = min(TOPK_MAX, T // 4)
    nb = T // Q_BLOCK

    def blocks(z):
        return jnp.moveaxis(z.reshape((B, nb, Q_BLOCK) + z.shape[2:]), 1, 0)

    def one_block(args):
        qb, qib, wib, t0 = args
        tpos = t0 + jnp.arange(Q_BLOCK)
        score = indexer_scores(qib, ki, wib)
        allowed = jnp.arange(T)[None, :] <= tpos[:, None]
        score = jnp.where(allowed[None], score, -jnp.inf)
        _, idx = lax.top_k(score, n_sel)
        valid = idx <= tpos[None, :, None]
        return attend_selected(qb, gather_rows(k, idx), gather_rows(v, idx), valid)

    out = lax.map(one_block, (blocks(q), blocks(qi), blocks(wi), jnp.arange(nb, dtype=jnp.int32) * Q_BLOCK))
    return jnp.moveaxis(out, 0, 1).reshape(B, T, ATT_H, HEAD_DIM)


def sample_attention(q, k_new, v_new, qi, ki_new, wi, cache_k, cache_v, cache_ik, page_table):
    B, T = q.shape[:2]
    past = page_table.shape[1] * PAGE_SIZE
    L = past + T
    n_sel = min(TOPK_MAX, L // 4)
    past_ik = cache_ik[page_table].reshape(B, past, IDX_D)
    ki_all = jnp.concatenate([past_ik.astype(ki_new.dtype), ki_new], axis=1)
    tpos = past + jnp.arange(T)
    score = indexer_scores(qi, ki_all, wi)
    allowed = jnp.arange(L)[None, :] <= tpos[:, None]
    score = jnp.where(allowed[None], score, -jnp.inf)
    _, idx = lax.top_k(score, n_sel)
    valid = idx <= tpos[None, :, None]
    is_past = idx < past
    pidx = jnp.minimum(idx, past - 1)
    page = jnp.take_along_axis(page_table, (pidx // PAGE_SIZE).reshape(B, -1), axis=1).reshape(idx.shape)
    row = page * PAGE_SIZE + pidx % PAGE_SIZE
    nidx = jnp.clip(idx - past, 0, T - 1)

    def pick(pool, new):
        flat = pool.reshape((-1,) + pool.shape[2:])
        return jnp.where(is_past[..., None, None], flat[row].astype(new.dtype), gather_rows(new, nidx))

    return attend_selected(q, pick(cache_k, k_new), pick(cache_v, v_new), valid)


def wkv7_step(S, inp):
    r, w, k, v, kk, a = inp
    sa = jnp.einsum('bhij,bhj->bhi', S, -kk)
    S = S * w[:, :, None, :] + sa[..., None] * (kk * a)[:, :, None, :] + v[..., None] * k[:, :, None, :]
    return S, jnp.einsum('bhij,bhj->bhi', S, r)


def rwkv7_mix(pr, prev_row, s0, lp):
    B, T, _ = pr.shape
    prev = jnp.concatenate([prev_row[:, None, :].astype(pr.dtype), pr[:, :-1]], axis=1)
    xm = pr + (prev - pr) * lp['rw_mu']
    p0 = 3 * RW_W
    r = xm[..., :RW_W]
    k = xm[..., RW_W:2 * RW_W]
    v = xm[..., 2 * RW_W:p0]
    wd = xm[..., p0:p0 + W_LORA]
    ad = xm[..., p0 + W_LORA:p0 + W_LORA + A_LORA]
    gd = xm[..., p0 + W_LORA + A_LORA:]
    w_log = -jax.nn.softplus(-(lp['rw_w0'] + jnp.tanh(wd) @ lp['rw_w2'])) - 0.5
    a = jax.nn.sigmoid(lp['rw_a0'] + ad @ lp['rw_a2'])
    g = jax.nn.sigmoid(gd) @ lp['rw_g2']

    def heads(z):
        return z.astype(jnp.float32).reshape(B, T, RW_H, RW_N)

    kk = heads(k * lp['rw_k_k'])
    kk = kk * lax.rsqrt(jnp.maximum(jnp.sum(kk * kk, axis=-1, keepdims=True), 1e-24))
    k = k * (1 + (a - 1) * lp['rw_k_a'])
    rh, kh, vh, ah = heads(r), heads(k), heads(v), heads(a)
    decay = jnp.exp(-jnp.exp(heads(w_log)))
    xs = tuple(jnp.moveaxis(z, 1, 0) for z in (rh, decay, kh, vh, kk, ah))
    s_fin, ys = lax.scan(wkv7_step, s0.astype(jnp.float32), xs)
    y = jnp.moveaxis(ys, 0, 1)
    mu = jnp.mean(y, axis=-1, keepdims=True)
    var = jnp.mean((y - mu) ** 2, axis=-1, keepdims=True)
    yn = ((y - mu) * lax.rsqrt(var + GN_EPS)).reshape(B, T, RW_W) * lp['rw_ln_w'].astype(jnp.float32) + lp['rw_ln_b'].astype(jnp.float32)
    bonus = (jnp.sum(rh * kh * lp['rw_r_k'].astype(jnp.float32), axis=-1, keepdims=True) * vh).reshape(B, T, RW_W)
    out = ((yn + bonus) * g.astype(jnp.float32)).astype(pr.dtype)
    return out, s_fin, pr[:, -1]


def decoder_layer(x, c, attn_fn, prev_row, s0, lp):
    B, T, _ = x.shape
    mod = (jax.nn.silu(c) @ lp['w_ada'] + lp['b_ada'])[:, None, :]
    shift1, scale1, gate1, shift2, scale2, gate2 = jnp.split(mod, 6, axis=-1)
    h = rms_norm(x, lp['norm1_g']) * (1 + scale1) + shift1
    proj = h @ lp['w_in']
    pa, pr = proj[..., :ATT_COLS], proj[..., ATT_COLS:]
    o1, o2, o3 = ATT_W, 2 * ATT_W, 3 * ATT_W
    o4 = o3 + IDX_H * IDX_D
    o5 = o4 + IDX_D
    q = rms_norm(pa[..., :o1].reshape(B, T, ATT_H, HEAD_DIM), lp['q_norm_g'])
    k = rms_norm(pa[..., o1:o2].reshape(B, T, ATT_H, HEAD_DIM), lp['k_norm_g'])
    v = pa[..., o2:o3].reshape(B, T, ATT_H, HEAD_DIM)
    qi = pa[..., o3:o4].reshape(B, T, IDX_H, IDX_D)
    ki = rms_norm(pa[..., o4:o5])
    wi = pa[..., o5:ATT_COLS] * IDX_W_SCALE
    att = attn_fn(q, k, v, qi, ki, wi).reshape(B, T, ATT_W)
    rw, s_fin, last_row = rwkv7_mix(pr, prev_row, s0, lp)
    x = x + gate1 * (jnp.concatenate([att, rw], axis=-1) @ lp['w_out'])
    h2 = rms_norm(x, lp['norm2_g']) * (1 + scale2) + shift2
    ffn = (jax.nn.silu(h2 @ lp['w_ffn_gate']) * (h2 @ lp['w_ffn_up'])) @ lp['w_ffn_down']
    x = x + gate2 * ffn
    return x, (k, v, ki, s_fin, last_row)


def setup_inputs(seed: int = 0) -> dict:
    key = jax.random.key(seed)
    ks = iter(jax.random.split(key, 48))
    nrm = lambda shape, s=1.0: jax.random.normal(next(ks), shape, jnp.float32) * s
    n_pages = PAST_LEN // PAGE_SIZE
    n_used = DEC_BATCH * n_pages
    n_pool = n_used + max(1, n_used // 4)
    page_table = jax.random.permutation(next(ks), n_pool)[:n_used].reshape(DEC_BATCH, n_pages).astype(jnp.int32)
    return {
        'x_prompt': nrm((BATCH, SEQ, D_MODEL)),
        'x_sample': nrm((DEC_BATCH, DEC_SEQ, D_MODEL)),
        'cache_k': nrm((DEPTH, n_pool, PAGE_SIZE, ATT_H, HEAD_DIM)),
        'cache_v': nrm((DEPTH, n_pool, PAGE_SIZE, ATT_H, HEAD_DIM)),
        'cache_idx_k': nrm((DEPTH, n_pool, PAGE_SIZE, IDX_D)),
        'state_wkv': nrm((DEPTH, DEC_BATCH, RW_H, RW_N, RW_N), 0.3),
        'state_shift': nrm((DEPTH, DEC_BATCH, RW_COLS)),
        'page_table': page_table,
        'c_prompt': nrm((BATCH, D_MODEL)),
        'c_sample': nrm((DEC_BATCH, D_MODEL)),
        'norm1_g': 1.0 + nrm((DEPTH, D_MODEL), 0.05),
        'norm2_g': 1.0 + nrm((DEPTH, D_MODEL), 0.05),
        'w_ada': nrm((DEPTH, D_MODEL, 6 * D_MODEL), 0.3 * D_MODEL ** -0.5),
        'b_ada': nrm((DEPTH, 6 * D_MODEL), 0.02),
        'w_in': nrm((DEPTH, D_MODEL, IN_COLS), D_MODEL ** -0.5),
        'q_norm_g': 1.0 + nrm((DEPTH, HEAD_DIM), 0.05),
        'k_norm_g': 1.0 + nrm((DEPTH, HEAD_DIM), 0.05),
        'rw_mu': jax.random.uniform(next(ks), (DEPTH, RW_COLS), jnp.float32),
        'rw_w0': nrm((DEPTH, RW_W), 0.5),
        'rw_w2': nrm((DEPTH, W_LORA, RW_W), 0.5 * W_LORA ** -0.5),
        'rw_a0': nrm((DEPTH, RW_W), 0.1),
        'rw_a2': nrm((DEPTH, A_LORA, RW_W), 0.5 * A_LORA ** -0.5),
        'rw_g2': nrm((DEPTH, G_LORA, RW_W), G_LORA ** -0.5),
        'rw_k_k': 0.85 + nrm((DEPTH, RW_W), 0.05),
        'rw_k_a': 1.0 + nrm((DEPTH, RW_W), 0.05),
        'rw_r_k': nrm((DEPTH, RW_H, RW_N), 0.1),
        'rw_ln_w': 1.0 + nrm((DEPTH, RW_W), 0.05),
        'rw_ln_b': nrm((DEPTH, RW_W), 0.02),
        'w_out': nrm((DEPTH, MIX_W, D_MODEL), MIX_W ** -0.5),
        'w_ffn_gate': nrm((DEPTH, D_MODEL, FFN_DIM), D_MODEL ** -0.5),
        'w_ffn_up': nrm((DEPTH, D_MODEL, FFN_DIM), D_MODEL ** -0.5),
        'w_ffn_down': nrm((DEPTH, FFN_DIM, D_MODEL), FFN_DIM ** -0.5),
    }


def reference(x_prompt, x_sample, cache_k, cache_v, cache_idx_k, state_wkv, state_shift, page_table,
              c_prompt, c_sample, norm1_g, norm2_g, w_ada, b_ada, w_in, q_norm_g, k_norm_g,
              rw_mu, rw_w0, rw_w2, rw_a0, rw_a2, rw_g2, rw_k_k, rw_k_a, rw_r_k, rw_ln_w, rw_ln_b,
              w_out, w_ffn_gate, w_ffn_up, w_ffn_down):
    y_p, y_s = x_prompt, x_sample
    Bp = x_prompt.shape[0]
    kp, vp, ikp, wkvp, shp = [], [], [], [], []
    ksm, vsm, iks, wkvs, shs = [], [], [], [], []
    for l in range(DEPTH):
        lp = {
            'norm1_g': norm1_g[l], 'norm2_g': norm2_g[l], 'w_ada': w_ada[l], 'b_ada': b_ada[l],
            'w_in': w_in[l], 'q_norm_g': q_norm_g[l], 'k_norm_g': k_norm_g[l],
            'rw_mu': rw_mu[l], 'rw_w0': rw_w0[l], 'rw_w2': rw_w2[l], 'rw_a0': rw_a0[l], 'rw_a2': rw_a2[l],
            'rw_g2': rw_g2[l], 'rw_k_k': rw_k_k[l], 'rw_k_a': rw_k_a[l], 'rw_r_k': rw_r_k[l],
            'rw_ln_w': rw_ln_w[l], 'rw_ln_b': rw_ln_b[l], 'w_out': w_out[l],
            'w_ffn_gate': w_ffn_gate[l], 'w_ffn_up': w_ffn_up[l], 'w_ffn_down': w_ffn_down[l],
        }
        prev0 = jnp.zeros((Bp, RW_COLS), x_prompt.dtype)
        s00 = jnp.zeros((Bp, RW_H, RW_N, RW_N), jnp.float32)
        y_p, st_p = decoder_layer(y_p, c_prompt, prompt_attention, prev0, s00, lp)
        samp_attn = functools.partial(sample_attention, cache_k=cache_k[l], cache_v=cache_v[l],
                                      cache_ik=cache_idx_k[l], page_table=page_table)
        y_s, st_s = decoder_layer(y_s, c_sample, samp_attn, state_shift[l], state_wkv[l], lp)
        kp.append(st_p[0]); vp.append(st_p[1]); ikp.append(st_p[2]); wkvp.append(st_p[3]); shp.append(st_p[4])
        ksm.append(st_s[0]); vsm.append(st_s[1]); iks.append(st_s[2]); wkvs.append(st_s[3]); shs.append(st_s[4])
    return (y_p, y_s,
            jnp.stack(kp), jnp.stack(vp), jnp.stack(ikp), jnp.stack(wkvp), jnp.stack(shp),
            jnp.stack(ksm), jnp.stack(vsm), jnp.stack(iks), jnp.stack(wkvs), jnp.stack(shs))
```

```python
import os
import numpy as np
from contextlib import ExitStack
import concourse.bass as bass
import concourse.mybir as mybir
from concourse.bass_utils import run_bass_kernel_spmd

F32 = mybir.dt.float32
BF16 = mybir.dt.bfloat16
I32 = mybir.dt.int32
U32 = mybir.dt.uint32
AF = mybir.ActivationFunctionType
ALU = mybir.AluOpType
AX = mybir.AxisListType

D = 1024
T = 2048
NT = T // 128
NS = 4
NPAGE = 64
PAGE = 128
NPOOL = 2560
IN_COLS = 4432
ATT_COLS = 2640
RW_COLS = 1792
FFN = 2816
NFC = FFN // 128
IDX_W_SCALE = (16 ** -0.5) * (64 ** -0.5)
RMS_EPS = 1e-6
GN_EPS = 64e-5
BIG = 1.0e30

_DT_SIZE = {F32: 4, BF16: 2, I32: 4, U32: 4}


def _prod(s):
    r = 1
    for x in s:
        r *= x
    return r


class Arena:
    def __init__(self, ap):
        self.ap = ap
        self.off = 0
        self.n = ap.shape[1]
        self.peak = 0

    def mark(self):
        return self.off

    def release(self, m):
        self.off = m

    def alloc(self, shape, dtype=F32, parts=128):
        if isinstance(shape, int):
            shape = [shape]
        ne = _prod(shape)
        nw = (ne * _DT_SIZE[dtype] + 3) // 4
        nw = (nw + 1) // 2 * 2
        assert self.off + nw <= self.n, f"arena overflow {self.off}+{nw}>{self.n}"
        v = self.ap[0:parts, self.off:self.off + nw]
        self.off += nw
        self.peak = max(self.peak, self.off)
        if dtype != F32:
            v = v.bitcast(dtype)
        v = v[:, 0:ne]
        if len(shape) == 2:
            v = v.rearrange("p (a b) -> p a b", a=shape[0])
        elif len(shape) == 3:
            v = v.rearrange("p (a b c) -> p a b c", a=shape[0], b=shape[1])
        return v


class Sched:
    CH = 2000
    NSEM = {"pe": 22, "act": 14, "dve": 18, "pool": 10}
    NDMA = 10
    NPDMA = 20

    def __init__(self, nc, es):
        self.nc = nc
        self.E = {"pe": nc.tensor, "act": nc.scalar, "dve": nc.vector, "pool": nc.gpsimd, "sp": nc.sync}
        self.sem = {e: [es.enter_context(nc.semaphore(f"s_{e}{k}")) for k in range(n)] for e, n in self.NSEM.items()}
        self.dsem = [es.enter_context(nc.semaphore(f"s_dma{k}")) for k in range(self.NDMA)]
        self.psem = [es.enter_context(nc.semaphore(f"s_pdma{k}")) for k in range(self.NPDMA)]
        self.pmark = None
        self.pnext = 0
        self.duse = [0] * self.NDMA
        self.dnext = 0
        self.ops = {e: [] for e in self.E}
        self.cnt = {e: 0 for e in self.NSEM}
        self.seen = {e: {} for e in self.E}
        self.lastw = {}
        self.readers = {}
        self.all_dma = []

    def _deps(self, eng, r, w):
        deps = []
        for x in r:
            if x in self.lastw:
                deps.append(self.lastw[x])
        for x in w:
            if x in self.lastw:
                deps.append(self.lastw[x])
            deps.extend(self.readers.get(x, []))
        waits = []
        seen = self.seen[eng]
        for tok in deps:
            if tok[0] == "e":
                _, e2, idx = tok
                if e2 == eng and eng == "pe":
                    continue
                if seen.get(e2, 0) >= idx:
                    continue
                seen[e2] = idx
            else:
                _, slot, val = tok
                if seen.get(("d", slot), 0) >= val:
                    continue
                seen[("d", slot)] = val
        return deps

    def _waits_for(self, eng, deps):
        need_e = {}
        need_d = {}
        for tok in deps:
            if tok[0] == "e":
                _, e2, idx = tok
                if e2 == eng and eng == "pe":
                    continue
                need_e[e2] = max(need_e.get(e2, 0), idx)
            else:
                _, slot, val = tok
                need_d[slot] = max(need_d.get(slot, 0), val)
        out = []
        seen = self.seen[eng]
        for e2, idx in need_e.items():
            if seen.get(e2, 0) >= idx:
                continue
            seen[e2] = idx
            k, v = (idx - 1) // self.CH, (idx - 1) % self.CH + 1
            out.append((self.sem[e2][k], v))
        for slot, val in need_d.items():
            if seen.get(("d", slot), 0) >= val:
                continue
            seen[("d", slot)] = val
            out.append((self.dsem[slot], val))
        return out

    def _collect(self, r, w):
        deps = []
        for x in r:
            if x in self.lastw:
                deps.append(self.lastw[x])
        for x in w:
            if x in self.lastw:
                deps.append(self.lastw[x])
            deps.extend(self.readers.get(x, []))
        return deps

    def _commit(self, tok, r, w):
        for x in r:
            self.readers.setdefault(x, []).append(tok)
        for x in w:
            self.lastw[x] = tok
            self.readers[x] = []

    def op(self, eng, fn, r=(), w=()):
        deps = self._collect(r, w)
        waits = self._waits_for(eng, deps)
        self.cnt[eng] += 1
        idx = self.cnt[eng]
        k = (idx - 1) // self.CH
        assert k < len(self.sem[eng]), f"too many ops on {eng}"
        self.ops[eng].append((waits, fn, self.sem[eng][k], 1))
        tok = ("e", eng, idx)
        if eng != "pe":
            self.seen[eng][eng] = max(self.seen[eng].get(eng, 0), 0)
        self._commit(tok, r, w)
        return tok

    def dma(self, q, fn, r=(), w=()):
        deps = self._collect(r, w)
        slot = self.dnext
        self.dnext = (self.dnext + 1) % self.NDMA
        if self.duse[slot] > 0:
            deps.append(("d", slot, 16 * self.duse[slot]))
        waits = self._waits_for(q, deps)
        self.duse[slot] += 1
        val = 16 * self.duse[slot]
        self.ops[q].append((waits, fn, self.dsem[slot], 16))
        tok = ("d", slot, val)
        self._commit(tok, r, w)
        self.all_dma.append(tok)
        return tok

    def pool_dma_batch(self, fns, r=(), w=()):
        assert len(fns) <= self.NPDMA
        deps = self._collect(r, w)
        waits = self._waits_for("pool", deps)
        psem = self.psem
        n = len(fns)

        def g(e):
            for i, fn in enumerate(fns):
                fn(e).then_inc(psem[i], 16)
            for i in range(n):
                e.wait_ge(psem[i], 16)
        self.ops["pool"].append((waits, ("raw", g), None, 0))
        pm = self.pmark
        tok = self.op("pool", lambda e: e.memset(pm, 0.0), r=r, w=list(w) + ["pmark"])
        self.barrier()

        def clr(e):
            for i in range(n):
                e.sem_clear(psem[i])
        self.ops["pool"].append(([], ("raw", clr), None, 0))
        return tok

    def pool_dma_once(self, fns, r=(), w=()):
        deps = self._collect(r, w)
        waits = self._waits_for("pool", deps)
        sems = [self.psem[self.pnext + i] for i in range(len(fns))]
        self.pnext += len(fns)
        assert self.pnext <= self.NPDMA

        def g(e):
            for sm, fn in zip(sems, fns):
                fn(e).then_inc(sm, 16)
            for sm in sems:
                e.wait_ge(sm, 16)
        self.ops["pool"].append((waits, ("raw", g), None, 0))
        pm = self.pmark
        return self.op("pool", lambda e: e.memset(pm, 0.0), r=r, w=list(w) + ["pmark"])

    def barrier(self):
        toks = [("e", e, c) for e, c in self.cnt.items() if c > 0]
        toks += [("d", s, 16 * u) for s, u in enumerate(self.duse) if u > 0]
        for eng in self.E:
            waits = self._waits_for(eng, toks)
            if waits:
                self.ops[eng].append((waits, None, None, 0))

    def finish(self):
        toks = [("d", s, 16 * u) for s, u in enumerate(self.duse) if u > 0]
        toks += [("e", e, c) for e, c in self.cnt.items() if c > 0]
        waits = self._waits_for("sp", toks)
        self.ops["sp"].append((waits, None, None, 0))

    def emit(self, block):
        def mk(ename):
            def body(e):
                for waits, fn, sem, inc in self.ops[ename]:
                    for s, v in waits:
                        e.wait_ge(s, v)
                    if isinstance(fn, tuple):
                        fn[1](e)
                    elif fn is not None:
                        ins = fn(e)
                        ins.then_inc(sem, inc)
            return body
        block.tensor(mk("pe"))
        block.scalar(mk("act"))
        block.vector(mk("dve"))
        block.gpsimd(mk("pool"))
        block.sync(mk("sp"))


def bc_last(ap, n):
    return ap.unsqueeze(len(ap.shape)).to_broadcast(list(ap.shape) + [n])


def bc_mid(ap, n):
    return ap.unsqueeze(1).to_broadcast([ap.shape[0], n, ap.shape[1]])


STAGE = int(os.environ.get("MK_STAGE", "99"))
ATT_CUT = int(os.environ.get("MK_ATT_CUT", "99"))
P2_CUT = int(os.environ.get("MK_P2_CUT", "99"))
SAMPLE_ATT = os.environ.get("MK_SAMPLE_ATT", "1") == "1"
SCAN_CUT = int(os.environ.get("MK_SCAN_CUT", "99"))
NOCACHE = os.environ.get("MK_NOCACHE", "0") == "1"


def build_program():
    nc = bass.Bass("TRN2", target_bir_lowering=False)

    def din(name, shape, dt=F32):
        return nc.dram_tensor(name, list(shape), dt, kind="ExternalInput").ap()

    def dout(name, shape, dt=F32):
        return nc.dram_tensor(name, list(shape), dt, kind="ExternalOutput").ap()

    x_p = din("x_p", [T, D])
    x_s = din("x_s", [NS, D])
    c5 = din("c5", [5, D])
    st_wkv = din("st_wkv", [NS, 8, 64, 64])
    st_shift = din("st_shift", [NS, RW_COLS])
    ptab = din("ptab", [NS, NPAGE], I32)
    npool_rows = 128 if NOCACHE else NPOOL * PAGE
    cache_k = din("cache_k", [npool_rows, 512])
    cache_v = din("cache_v", [npool_rows, 512])
    cache_ik = din("cache_ik", [npool_rows, 64])
    norm1_g = din("norm1_g", [D])
    norm2_g = din("norm2_g", [D])
    w_ada = din("w_ada", [D, 6 * D])
    b_ada = din("b_ada", [6 * D])
    w_in = din("w_in", [D, IN_COLS])
    q_norm_g = din("q_norm_g", [64])
    k_norm_g = din("k_norm_g", [64])
    rw_mu = din("rw_mu", [RW_COLS])
    rw_w0 = din("rw_w0", [512])
    rw_w2 = din("rw_w2", [64, 512])
    rw_a0 = din("rw_a0", [512])
    rw_a2 = din("rw_a2", [64, 512])
    rw_g2 = din("rw_g2", [128, 512])
    rw_k_k = din("rw_k_k", [512])
    rw_k_a = din("rw_k_a", [512])
    rw_r_k = din("rw_r_k", [512])
    rw_ln_w = din("rw_ln_w", [512])
    rw_ln_b = din("rw_ln_b", [512])
    w_out = din("w_out", [D, D])
    w_gate = din("w_gate", [D, FFN])
    w_up = din("w_up", [D, FFN])
    w_down = din("w_down", [FFN, D])

    y_p = dout("y_p", [T, D])
    y_s = dout("y_s", [NS, D])
    k_po = dout("k_po", [T, 512])
    v_po = dout("v_po", [T, 512])
    ik_po = dout("ik_po", [T, 64])
    wkv_po = dout("wkv_po", [8, 64, 64])
    sh_po = dout("sh_po", [RW_COLS])
    k_so = dout("k_so", [NS, 512])
    v_so = dout("v_so", [NS, 512])
    ik_so = dout("ik_so", [NS, 64])
    wkv_so = dout("wkv_so", [NS, 8, 64, 64])
    sh_so = dout("sh_so", [NS, RW_COLS])

    es = ExitStack()
    with es:
        arena_t = es.enter_context(nc.sbuf_tensor("arena", [128, 51800], F32))
        AR = Arena(arena_t[:, :])
        ptrow_t = es.enter_context(nc.sbuf_tensor("ptrow", [1, NS * NPAGE], I32))
        PS = [es.enter_context(nc.psum_tensor(f"psb{i}", [128, 512], F32))[:, :] for i in range(8)]
        S = Sched(nc, es)
        op, dma = S.op, S.dma
        S.pmark = AR.alloc([2], F32)

        iot = AR.alloc([128], I32)
        ident = AR.alloc([128], F32)
        identb = AR.alloc([128], BF16)
        negmask = AR.alloc([128], F32)
        ones_f = AR.alloc([128], F32)
        op("pool", lambda e: e.iota(iot, [[1, 128]], base=0, channel_multiplier=-1), w=["iot"])
        op("dve", lambda e: e.tensor_single_scalar(ident, iot, 0, ALU.is_equal), r=["iot"], w=["ident"])
        op("dve", lambda e: e.tensor_single_scalar(identb, iot, 0, ALU.is_equal), r=["iot"], w=["identb"])
        op("dve", lambda e: e.tensor_scalar(negmask, iot, 0, -BIG, ALU.is_gt, ALU.mult), r=["iot"], w=["negmask"])
        op("pool", lambda e: e.memset(ones_f, 1.0), w=["ones_f"])

        vecA = AR.alloc([78], F32)
        vecB = AR.alloc([56], F32, parts=64)
        modT = AR.alloc([48, 5], F32)
        A1 = AR.alloc([8, 5], F32)
        A2 = AR.alloc([8, 5], F32)
        gq_rep = AR.alloc([64], F32)
        gk_rep = AR.alloc([64], F32)
        m0 = AR.mark()
        c5t = AR.alloc([D], F32, parts=5)
        sct = AR.alloc([D], F32, parts=5)
        scT = AR.alloc([8, 5], F32)
        stA = AR.alloc([128], F32, parts=78)
        stB = AR.alloc([64], F32, parts=56)
        wst = [AR.alloc([8, 512], F32) for _ in range(2)]

        dma("sp", lambda e: e.dma_start(out=c5t, in_=c5), w=["c5t"])
        dma("sp", lambda e: e.dma_start(out=stA[0:48, :], in_=b_ada.rearrange("(a b) -> a b", b=128)), w=["stA0"])
        dma("sp", lambda e: e.dma_start(out=stA[48:56, :], in_=norm1_g.rearrange("(a b) -> a b", b=128)), w=["stA1"])
        dma("sp", lambda e: e.dma_start(out=stA[56:64, :], in_=norm2_g.rearrange("(a b) -> a b", b=128)), w=["stA2"])
        dma("sp", lambda e: e.dma_start(out=stA[64:78, :], in_=rw_mu.rearrange("(a b) -> a b", b=128)), w=["stA3"])
        for i, v in enumerate([rw_w0, rw_a0, rw_k_k, rw_k_a, rw_ln_w, rw_ln_b, rw_r_k]):
            dma("sp", lambda e, v=v, i=i: e.dma_start(out=stB[8 * i:8 * i + 8, :], in_=v.rearrange("(a b) -> a b", b=64)), w=[f"stB{i}"])
        dma("sp", lambda e: e.dma_start(out=gq_rep, in_=q_norm_g.partition_broadcast(128)), w=["gq_rep"])
        dma("sp", lambda e: e.dma_start(out=gk_rep, in_=k_norm_g.partition_broadcast(128)), w=["gk_rep"])

        op("act", lambda e: e.activation(sct, c5t, AF.Silu), r=["c5t"], w=["sct"])
        for kc in range(8):
            op("pe", lambda e, kc=kc: e.transpose(PS[0][:, kc * 8:kc * 8 + 5], sct[:, kc * 128:(kc + 1) * 128], ident[0:5, 0:5]),
               r=["sct", "ident"], w=["ps0"])
        op("dve", lambda e: e.tensor_copy(scT, PS[0][:, 0:64].rearrange("p (a b) -> p a b", b=8)[:, :, 0:5]), r=["ps0"], w=["scT"])
        op("pe", lambda e: e.transpose(PS[1][:, 0:78], stA, ident[0:78, 0:78]), r=["stA0", "stA1", "stA2", "stA3", "ident"], w=["ps1"])
        op("dve", lambda e: e.tensor_copy(vecA, PS[1][:, 0:78]), r=["ps1"], w=["vecA"])
        op("pe", lambda e: e.transpose(PS[1][0:64, 128:184], stB, ident[0:56, 0:56]), r=[f"stB{i}" for i in range(7)] + ["ident"], w=["ps1"])
        op("dve", lambda e: e.tensor_copy(vecB, PS[1][0:64, 128:184]), r=["ps1"], w=["vecB"])
        w_ada_v = w_ada.rearrange("(kc p) n -> p kc n", p=128)
        for nb in range(12):
            b = nb % 2
            dma("sp", lambda e, nb=nb, b=b: e.dma_start(out=wst[b], in_=w_ada_v[:, :, nb * 512:(nb + 1) * 512]), w=[f"wst{b}"])
            for fc in range(4):
                col = (nb * 4 + fc) * 8
                for kc in range(8):
                    op("pe", lambda e, b=b, fc=fc, kc=kc, col=col: e.matmul(PS[2][:, col:col + 5], lhsT=wst[b][:, kc, fc * 128:(fc + 1) * 128],
                                                                            rhs=scT[:, kc, :], start=(kc == 0), stop=(kc == 7)),
                       r=[f"wst{b}", "scT"], w=["ps2"])
        op("dve", lambda e: e.tensor_tensor(modT, PS[2][:, 0:384].rearrange("p (a b) -> p a b", b=8)[:, :, 0:5], bc_last(vecA[:, 0:48], 5), ALU.add),
           r=["ps2", "vecA"], w=["modT"])
        op("dve", lambda e: e.tensor_scalar(A1, modT[:, 8:16, :], 1.0, None, ALU.add), r=["modT"], w=["A1"])
        op("dve", lambda e: e.tensor_tensor(A1, A1, bc_last(vecA[:, 48:56], 5), ALU.mult), r=["A1", "vecA"], w=["A1"])
        op("dve", lambda e: e.tensor_scalar(A2, modT[:, 32:40, :], 1.0, None, ALU.add), r=["modT"], w=["A2"])
        op("dve", lambda e: e.tensor_tensor(A2, A2, bc_last(vecA[:, 56:64], 5), ALU.mult), r=["A2", "vecA"], w=["A2"])
        S.barrier()
        AR.release(m0)

        if STAGE <= 0:
            dbg_mod = nc.dram_tensor("dbg_mod", [128, 240], F32, kind="ExternalOutput").ap()
            dma("sp", lambda e: e.dma_start(out=dbg_mod, in_=modT.rearrange("p a b -> p (a b)")), r=["modT"])
            S.finish()
            with nc.Block() as block:
                S.emit(block)
            return nc

        shift1 = modT[:, 0:8, :]
        gate1 = modT[:, 16:24, :]
        shift2 = modT[:, 24:32, :]
        gate2 = modT[:, 40:48, :]

        m_attT = AR.mark()
        attT = AR.alloc([4, T + NS], BF16)
        m_att = AR.mark()
        win_att = AR.alloc([8, ATT_COLS], BF16)
        KT = AR.alloc([4, T], BF16)
        Vaug = AR.alloc([NT, 8, 65], BF16)
        kiT = AR.alloc([T], BF16)
        wabs = AR.alloc([NT, 16], F32)
        wsgn = AR.alloc([NT, 16], F32)
        qiTs = AR.alloc([8, NS], BF16)
        qs32 = AR.alloc([512], F32, parts=NS)
        ks32 = AR.alloc([512], F32, parts=NS)
        vs32 = AR.alloc([512], F32, parts=NS)
        kis32 = AR.alloc([64], F32, parts=NS)
        ws_abs = AR.alloc([16], F32, parts=NS)
        ws_sgn = AR.alloc([16], F32, parts=NS)
        m1 = AR.mark()
        wst = [AR.alloc([8, 512], F32) for _ in range(2)]
        w_in_v = w_in.rearrange("(kc p) n -> p kc n", p=128)
        nblk = 0
        for (dst, c0, c1) in [(win_att, 0, ATT_COLS)]:
            c = c0
            while c < c1:
                n = min(512, c1 - c)
                b = nblk % 2
                dma("sp", lambda e, b=b, c=c, n=n: e.dma_start(out=wst[b][:, :, 0:n], in_=w_in_v[:, :, c:c + n]), w=[f"wst{b}"])
                op("pool", lambda e, b=b, c=c, n=n, dst=dst, c0=c0: e.tensor_copy(dst[:, :, c - c0:c - c0 + n], wst[b][:, :, 0:n]),
                   r=[f"wst{b}"], w=["win"])
                c += n
                nblk += 1
        op("pool", lambda e: e.memset(Vaug[:, :, :, 64:65], 1.0), w=["Vaug"])
        S.barrier()
        AR.release(m1)

        xt = [AR.alloc([D], F32) for _ in range(2)]
        xn = AR.alloc([D], F32)
        hT = AR.alloc([8, 512], BF16)
        scr = [AR.alloc([512], F32) for _ in range(3)]
        small = AR.alloc([64], F32)
        qnb = AR.alloc([512], BF16)
        knb = AR.alloc([512], BF16)
        kidup = AR.alloc([128], BF16)
        qT = AR.alloc([4, 512], BF16)
        qiT = AR.alloc([8, 512], BF16)
        hTs = AR.alloc([8, NS], BF16)
        v32 = AR.alloc([512], F32)
        ki32 = AR.alloc([64], F32)
        PSb = [p.bitcast(BF16) for p in PS]
        Ibuf = AR.alloc([T], F32)
        Rbuf = [AR.alloc([512], F32) for _ in range(2)]
        maskb = AR.alloc([T], BF16)
        maskT = AR.alloc([NT, 128], BF16)
        PTb = [AR.alloc([512], BF16) for _ in range(2)]
        PmT = [AR.alloc([4, 128], BF16) for _ in range(2)]
        attb = AR.alloc([512], BF16)
        bs = AR.alloc([16], F32)
        NIT = int(os.environ.get("MK_NIT", "24"))
        NTILES = int(os.environ.get("MK_NT", str(NT)))

        def rms_rstd(ssum, n, eps, dst, P):
            op("dve", lambda e: e.tensor_scalar(dst, ssum, 1.0 / n, eps, ALU.mult, ALU.add), r=["small"], w=["small"])
            op("act", lambda e: e.activation(dst, dst, AF.Sqrt), r=["small"], w=["small"])
            op("dve", lambda e: e.reciprocal(dst, dst), r=["small"], w=["small"])

        def front_tile(ti, P, x_src, cond, hT_dst, k_dst, v_dst, ik_dst, tcol):
            b = ti % 2
            xtb = xt[b]
            dma("sp", lambda e: e.dma_start(out=xtb[0:P, :], in_=x_src), w=[f"xt{b}"])
            ss = small[0:P, 0:1]
            rs = small[0:P, 1:2]
            op("act", lambda e: e.activation(xn[0:P, :], xtb[0:P, :], AF.Square, accum_out=ss), r=[f"xt{b}"], w=["xn", "small"])
            rms_rstd(ss, D, RMS_EPS, rs, P)
            op("dve", lambda e: e.tensor_scalar(xn[0:P, :], xtb[0:P, :], rs, None, ALU.mult), r=[f"xt{b}", "small"], w=["xn"])
            for half in range(2):
                pb = PS[half]
                for j in range(4):
                    kc = half * 4 + j
                    op("pe", lambda e, kc=kc, j=j, pb=pb: e.transpose(pb[:, j * 128:j * 128 + P], xn[0:P, kc * 128:(kc + 1) * 128], ident[0:P, 0:P]),
                       r=["xn", "ident"], w=[f"ps{half}"])
                if cond is None:
                    for j in range(4):
                        kc = half * 4 + j
                        op("act", lambda e, kc=kc, j=j, pb=pb: e.activation(hT_dst[:, kc, tcol:tcol + P], pb[:, j * 128:j * 128 + P], AF.Identity,
                                                                          bias=shift1[:, kc, 0:1], scale=A1[:, kc, 0:1]),
                           r=[f"ps{half}", "A1", "modT"], w=["hT"])
                else:
                    for j in range(4):
                        kc = half * 4 + j
                        op("dve", lambda e, kc=kc, j=j, pb=pb: e.tensor_tensor(scr[0][:, 0:P], pb[:, j * 128:j * 128 + P], A1[:, kc, 1:1 + P], ALU.mult),
                           r=[f"ps{half}", "A1"], w=["scr0"])
                        op("dve", lambda e, kc=kc: e.tensor_tensor(hT_dst[:, kc, tcol:tcol + P], scr[0][:, 0:P], shift1[:, kc, 1:1 + P], ALU.add),
                           r=["scr0", "modT"], w=["hT"])
            blocks = [(2, 0), (3, 512), (4, 1024), (5, 2560)]
            for (pbi, c0) in blocks:
                n = 512 if c0 < 2560 else 80
                for kc in range(8):
                    op("pe", lambda e, pbi=pbi, c0=c0, n=n, kc=kc: e.matmul(PS[pbi][0:P, 0:n], lhsT=hT_dst[:, kc, tcol:tcol + P],
                                                                          rhs=win_att[:, kc, c0:c0 + n], start=(kc == 0), stop=(kc == 7)),
                       r=["hT", "win"], w=[f"ps{pbi}"])
            for (pbi, grep, dstb, is_k) in [(2, gq_rep, qnb, False), (3, gk_rep, knb, True)]:
                pq = PS[pbi][0:P, :]
                ssq = small[0:P, 8:16]
                rq = small[0:P, 16:24]
                op("act", lambda e, pq=pq: e.activation(scr[0][0:P, :], pq, AF.Square), r=[f"ps{pbi}"], w=["scr0"])
                op("dve", lambda e, ssq=ssq: e.tensor_reduce(ssq, scr[0][0:P, :].rearrange("p (h d) -> p h d", d=64), AX.X, ALU.add), r=["scr0"], w=["small"])
                rms_rstd(ssq, 64, RMS_EPS, rq, P)
                op("dve", lambda e, pq=pq, rq=rq: e.tensor_tensor(scr[1][0:P, :].rearrange("p (h d) -> p h d", d=64), pq.rearrange("p (h d) -> p h d", d=64),
                                                               bc_last(rq, 64), ALU.mult), r=[f"ps{pbi}", "small"], w=["scr1"])
                if is_k:
                    op("pool", lambda e, grep=grep: e.tensor_tensor(scr[2][0:P, :].rearrange("p (h d) -> p h d", d=64), scr[1][0:P, :].rearrange("p (h d) -> p h d", d=64),
                                                                  bc_mid(grep[0:P, :], 8), ALU.mult), r=["scr1", "gk_rep"], w=["scr2"])
                    dma("sp", lambda e: e.dma_start(out=k_dst, in_=scr[2][0:P, :]), r=["scr2"])
                    op("pool", lambda e, dstb=dstb: e.tensor_copy(dstb[0:P, :], scr[2][0:P, :]), r=["scr2"], w=["knb"])
                else:
                    op("pool", lambda e, grep=grep, dstb=dstb: e.tensor_tensor(dstb[0:P, :].rearrange("p (h d) -> p h d", d=64), scr[1][0:P, :].rearrange("p (h d) -> p h d", d=64),
                                                                             bc_mid(grep[0:P, :], 8), ALU.mult), r=["scr1", "gq_rep"], w=["qnb"])
                    if cond is not None:
                        op("pool", lambda e: e.tensor_tensor(qs32.rearrange("p (h d) -> p h d", d=64), scr[1][0:P, :].rearrange("p (h d) -> p h d", d=64),
                                                             bc_mid(gq_rep[0:P, :], 8), ALU.mult), r=["scr1", "gq_rep"], w=["qs32"])
            is_s = cond is not None
            op("act", lambda e: e.activation(v32[0:P, :], PS[4][0:P, :], AF.Copy), r=["ps4"], w=["v32"])
            dma("sp", lambda e: e.dma_start(out=v_dst, in_=v32[0:P, :]), r=["v32"])
            if not is_s:
                op("pool", lambda e: e.tensor_copy(Vaug[:, ti, :, 0:64], v32.rearrange("p (h d) -> p h d", d=64)), r=["v32"], w=["Vaug"])
            else:
                op("pool", lambda e: e.tensor_copy(vs32, v32[0:P, :]), r=["v32"], w=["vs32"])
                op("pool", lambda e: e.tensor_copy(ks32, scr[2][0:P, :]), r=["scr2"], w=["ks32"])
            sk = small[0:P, 2:3]
            rk = small[0:P, 3:4]
            op("act", lambda e: e.activation(scr[0][0:P, 0:64], PS[5][0:P, 0:64], AF.Square, accum_out=sk), r=["ps5"], w=["scr0", "small"])
            rms_rstd(sk, 64, RMS_EPS, rk, P)
            op("dve", lambda e: e.tensor_scalar(ki32[0:P, :], PS[5][0:P, 0:64], rk, None, ALU.mult), r=["ps5", "small"], w=["ki32"])
            dma("sp", lambda e: e.dma_start(out=ik_dst, in_=ki32[0:P, :]), r=["ki32"])
            wa = ws_abs if is_s else wabs[:, ti, :]
            wsg = ws_sgn if is_s else wsgn[:, ti, :]
            op("act", lambda e: e.activation(wa, PS[5][0:P, 64:80], AF.Abs, scale=IDX_W_SCALE), r=["ps5"], w=["wabs"])
            op("act", lambda e: e.activation(wsg, PS[5][0:P, 64:80], AF.Sign), r=["ps5"], w=["wsgn"])
            if is_s:
                op("pool", lambda e: e.tensor_copy(kis32, ki32[0:P, :]), r=["ki32"], w=["kis32"])
                return
            tl = ti % 4
            op("pool", lambda e: e.tensor_copy(kidup[:, 0:64], ki32), r=["ki32"], w=["kidup"])
            op("pool", lambda e: e.tensor_copy(kidup[:, 64:128], ki32), r=["ki32"], w=["kidup"])
            for hp in range(4):
                op("pe", lambda e, hp=hp: e.transpose(PSb[6][:, hp * 128:(hp + 1) * 128], qnb[:, hp * 128:(hp + 1) * 128], identb), r=["qnb", "identb"], w=["ps6"])
            op("pe", lambda e: e.transpose(PSb[6][:, 512:640], kidup, identb), r=["kidup", "identb"], w=["ps6"])
            op("act", lambda e: e.activation(qT[:, :, tl * 128:(tl + 1) * 128], PSb[6][:, 0:512].rearrange("p (a b) -> p a b", a=4), AF.Copy), r=["ps6"], w=["qT"])
            op("act", lambda e: e.activation(kiT[:, ti * 128:(ti + 1) * 128], PSb[6][:, 512:640], AF.Copy), r=["ps6"], w=["kiT"])
            for hp in range(4):
                op("pe", lambda e, hp=hp: e.transpose(PSb[7][:, hp * 128:(hp + 1) * 128], knb[:, hp * 128:(hp + 1) * 128], identb), r=["knb", "identb"], w=["ps7"])
            op("act", lambda e: e.activation(KT[:, :, ti * 128:(ti + 1) * 128], PSb[7][:, 0:512].rearrange("p (a b) -> p a b", a=4), AF.Copy), r=["ps7"], w=["KT"])

        def qi_group(hT_src, ncols, dst):
            for c in range(8):
                pb = PS[c % 2]
                for kc in range(8):
                    op("pe", lambda e, c=c, kc=kc, pb=pb: e.matmul(pb[:, 0:ncols], lhsT=win_att[:, kc, 1536 + c * 128:1536 + (c + 1) * 128],
                                                                  rhs=hT_src[:, kc, 0:ncols], start=(kc == 0), stop=(kc == 7)), r=["hT", "win"], w=[f"ps{c % 2}"])
                op("act", lambda e, c=c, pb=pb: e.activation(dst[:, c, 0:ncols], pb[:, 0:ncols], AF.Copy), r=[f"ps{c % 2}"], w=["qiT"])


        def attention_tile(ti):
            tl = ti % 4
            L = (ti + 1) * 128
            nsp = (L + 511) // 512
            tq = slice(tl * 128, (tl + 1) * 128)
            cnt = [0, 0]
            for h in range(16):
                hp, half = h // 2, h % 2
                pr_ = slice(half * 64, (half + 1) * 64)
                for sp in range(nsp):
                    s0 = sp * 512
                    n = min(512, L - s0)
                    bk = 2 * half + cnt[half] % 2
                    cnt[half] += 1
                    rb = (h * nsp + sp) % 2
                    op("pe", lambda e, bk=bk, n=n, pr_=pr_, hp=hp, s0=s0: e.matmul(PS[bk][:, 0:n], lhsT=qiT[pr_, hp, tq], rhs=kiT[pr_, s0:s0 + n], start=True, stop=True),
                       r=["qiT", "kiT"], w=[f"ps{bk}"])
                    op("act", lambda e, bk=bk, n=n, rb=rb, h=h: e.activation(Rbuf[rb][:, 0:n], PS[bk][:, 0:n], AF.Relu, scale=wabs[:, ti, h:h + 1]),
                       r=[f"ps{bk}", "wabs"], w=[f"R{rb}"])
                    if h == 0:
                        op("dve", lambda e, n=n, rb=rb, s0=s0: e.tensor_scalar(Ibuf[:, s0:s0 + n], Rbuf[rb][:, 0:n], wsgn[:, ti, 0:1], None, ALU.mult),
                           r=[f"R{rb}", "wsgn"], w=["I"])
                    else:
                        op("dve", lambda e, n=n, rb=rb, s0=s0, h=h: e.scalar_tensor_tensor(Ibuf[:, s0:s0 + n], Rbuf[rb][:, 0:n], wsgn[:, ti, h:h + 1], Ibuf[:, s0:s0 + n], ALU.mult, ALU.add),
                           r=[f"R{rb}", "wsgn", "I"], w=["I"])
            if ATT_CUT <= 1:
                return
            lo, hi, step, mid, cntv, tmp, tau = (bs[:, i:i + 1] for i in range(7))
            if ti >= 2:
                op("dve", lambda e: e.tensor_reduce(hi, Ibuf[:, 0:L], AX.X, ALU.max), r=["I"], w=["bs"])
                op("dve", lambda e: e.tensor_reduce(lo, Ibuf[:, 0:L], AX.X, ALU.min), r=["I"], w=["bs"])
            op("dve", lambda e: e.tensor_tensor(Ibuf[:, ti * 128:(ti + 1) * 128], Ibuf[:, ti * 128:(ti + 1) * 128], negmask, ALU.add), r=["I", "negmask"], w=["I"])
            if ti >= 2:
                op("dve", lambda e: e.tensor_tensor(step, hi, lo, ALU.subtract), r=["bs"], w=["bs"])
                for it in range(NIT):
                    op("dve", lambda e: e.tensor_scalar(step, step, 0.5, None, ALU.mult), r=["bs"], w=["bs"])
                    op("dve", lambda e: e.tensor_tensor(mid, lo, step, ALU.add), r=["bs"], w=["bs"])
                    op("dve", lambda e: e.tensor_scalar(maskb[:, 0:L], Ibuf[:, 0:L], mid, 0.0, ALU.is_ge, ALU.add, accum_out=cntv), r=["I", "bs"], w=["maskb", "bs"])
                    op("dve", lambda e: e.scalar_tensor_tensor(tmp, cntv, 255.5, step, ALU.is_ge, ALU.mult), r=["bs"], w=["bs"])
                    op("dve", lambda e: e.tensor_tensor(lo, lo, tmp, ALU.add), r=["bs"], w=["bs"])
                thr = lo
            else:
                op("dve", lambda e: e.memset(tau, -1.0e29), w=["bs"])
                thr = tau
            op("dve", lambda e: e.tensor_scalar(maskb[:, 0:L], Ibuf[:, 0:L], thr, None, ALU.is_ge), r=["I", "bs"], w=["maskb"])
            if ATT_CUT <= 2:
                return
            for j0 in range(0, ti + 1, 8):
                nj = min(8, ti + 1 - j0)
                for j in range(nj):
                    sj = j0 + j
                    op("pe", lambda e, j=j, sj=sj: e.transpose(PSb[6][:, j * 128:(j + 1) * 128], maskb[:, sj * 128:(sj + 1) * 128], identb), r=["maskb", "identb"], w=["ps6"])
                op("act", lambda e, j0=j0, nj=nj: e.activation(maskT[:, j0:j0 + nj, :], PSb[6][:, 0:nj * 128].rearrange("p (a b) -> p a b", b=128), AF.Copy), r=["ps6"], w=["maskT"])
            if ATT_CUT <= 3:
                return
            cnt = [0, 0]
            pcount = 0
            for h in range(8):
                hp, half = h // 2, h % 2
                pr_ = slice(half * 64, (half + 1) * 64)
                pvb = 4 + h // 4
                pvc = (h % 4) * 65
                for sp in range(nsp):
                    j0 = sp * 4
                    nj = min(4, ti + 1 - j0)
                    bk = 2 * half + cnt[half] % 2
                    cnt[half] += 1
                    pb = pcount % 2
                    pcount += 1
                    for j in range(nj):
                        sj = j0 + j
                        op("pe", lambda e, bk=bk, j=j, sj=sj, pr_=pr_, hp=hp: e.matmul(PS[bk][:, j * 128:(j + 1) * 128], lhsT=KT[pr_, hp, sj * 128:(sj + 1) * 128],
                                                                                      rhs=qT[pr_, hp, tq], start=True, stop=True), r=["KT", "qT"], w=[f"ps{bk}"])
                    op("act", lambda e, bk=bk, nj=nj, pb=pb: e.activation(PTb[pb][:, 0:nj * 128], PS[bk][:, 0:nj * 128], AF.Exp, scale=0.125), r=[f"ps{bk}"], w=[f"PT{pb}"])
                    op("pool", lambda e, nj=nj, pb=pb, j0=j0: e.tensor_tensor(PmT[pb][:, 0:nj, :], PTb[pb][:, 0:nj * 128].rearrange("p (a b) -> p a b", b=128),
                                                                             maskT[:, j0:j0 + nj, :], ALU.mult), r=[f"PT{pb}", "maskT"], w=[f"PmT{pb}"])
                    for j in range(nj):
                        sj = j0 + j
                        op("pe", lambda e, pvb=pvb, pvc=pvc, pb=pb, j=j, sj=sj, h=h: e.matmul(PS[pvb][:, pvc:pvc + 65], lhsT=PmT[pb][:, j, :], rhs=Vaug[:, sj, h, :],
                                                                                           start=(sj == 0), stop=(sj == ti)), r=[f"PmT{pb}", "Vaug"], w=[f"ps{pvb}"])
            if ATT_CUT <= 4:
                return
            rden = bs[:, 8:16]
            for hb in range(2):
                pv = PS[4 + hb][:, 0:260].rearrange("p (h d) -> p h d", d=65)
                op("dve", lambda e, pv=pv, hb=hb: e.reciprocal(rden[:, hb * 4:(hb + 1) * 4], pv[:, :, 64]), r=[f"ps{4 + hb}"], w=["bs"])
                op("dve", lambda e, pv=pv, hb=hb: e.tensor_tensor(attb[:, hb * 256:(hb + 1) * 256].rearrange("p (h d) -> p h d", d=64), pv[:, :, 0:64],
                                                                 bc_last(rden[:, hb * 4:(hb + 1) * 4], 64), ALU.mult), r=[f"ps{4 + hb}", "bs"], w=["attb"])
            for hp in range(4):
                op("pe", lambda e, hp=hp: e.transpose(PSb[7][:, hp * 128:(hp + 1) * 128], attb[:, hp * 128:(hp + 1) * 128], identb), r=["attb", "identb"], w=["ps7"])
            op("act", lambda e: e.activation(attT[:, :, ti * 128:(ti + 1) * 128], PSb[7][:, 0:512].rearrange("p (a b) -> p a b", a=4), AF.Copy), r=["ps7"], w=["attT"])

        for g in range((NTILES + 3) // 4):
            for tl in range(4):
                ti = g * 4 + tl
                if ti >= NTILES:
                    break
                r0 = ti * 128
                front_tile(ti, 128, x_p[r0:r0 + 128, :], None, hT, k_po[r0:r0 + 128, :], v_po[r0:r0 + 128, :], ik_po[r0:r0 + 128, :], tl * 128)
            qi_group(hT, 512, qiT)
            if STAGE >= 2:
                for tl in range(4):
                    ti = g * 4 + tl
                    if ti < NTILES:
                        attention_tile(ti)
        front_tile(16, NS, x_s, 1, hTs, k_so, v_so, ik_so, 0)
        qi_group(hTs, NS, qiTs)
        if not SAMPLE_ATT:
            op("pool", lambda e: e.memset(attT[:, :, T:T + NS], 0.0), w=["attT"])
        else:
            S.barrier()
            AR.release(m1)
            NP1 = NPAGE + 1
            ptb = AR.alloc([NPAGE], I32)
            physf = AR.alloc([NS, NPAGE], F32)
            idx32 = AR.alloc([NS, NPAGE], I32)
            kib = AR.alloc([NP1, 64], F32)
            graw = AR.alloc([8320], F32)
            kdup = graw[:, 0:4160].bitcast(BF16).rearrange("p (a b) -> p a b", b=128)
            kiTa = graw[:, 4160:8320].bitcast(BF16).rearrange("p (a b) -> p a b", b=128)
            Gpg = graw[0:NPAGE, 0:8192]
            idxp = AR.alloc([NS], I32)
            ikscr = nc.dram_tensor("ikscr", [NS, NPAGE, PAGE * 64], F32, kind="Internal").ap()
            cache_ik_pg = cache_ik.rearrange("(n p) d -> n (p d)", p=128)
            knew = AR.alloc([NS, 64], F32)
            selB = AR.alloc([NS, 128], F32, parts=NS)
            wsig = AR.alloc([16], F32, parts=NS)
            wbc = AR.alloc([16], F32)
            rl = AR.alloc([NP1, 16], F32)
            score4 = AR.alloc([NS, NP1], F32)
            mask4 = AR.alloc([NS, NP1], F32)
            maskp = AR.alloc([NS, NPAGE], F32)
            negcols = AR.alloc([NS], F32)
            sb = AR.alloc([64], F32)
            iop_i = AR.alloc([2], I32)
            iop = AR.alloc([2], F32)
            islot_i = AR.alloc([256], I32)
            islot = AR.alloc([256], F32)
            ustrict = AR.alloc([128], F32)
            rank = AR.alloc([NS, NPAGE], F32)
            offs = AR.alloc([NS, NPAGE], F32)
            OH = [AR.alloc([256], F32) for _ in range(3)]
            selidx_f = AR.alloc([8], F32)
            selidx = AR.alloc([8], I32)
            Ksel = AR.alloc([2, 512], F32)
            Vsel = AR.alloc([2, 512], F32)
            prodb = AR.alloc([2, 512], F32)
            scb = AR.alloc([2, 8], F32)
            Pb = AR.alloc([2, 8], F32)
            validb = AR.alloc([2], F32)
            scn = AR.alloc([8], F32, parts=NS)
            Pn = AR.alloc([8], F32, parts=NS)
            Pnn = AR.alloc([8], F32, parts=NS)
            mnew = AR.alloc([4], F32, parts=NS)
            prodn = AR.alloc([512], F32, parts=NS)
            wvn = AR.alloc([512], F32, parts=NS)
            rdenb = AR.alloc([8], F32)
            cache_pages = cache_ik.rearrange("(n p) d -> n p d", p=128)
            ptrow = ptrow_t[:, :]
            dma("sp", lambda e: e.dma_start(out=ptrow, in_=ptab.rearrange("b c -> (b c)").unsqueeze(0)), w=["ptrow"])

            op("pool", lambda e: e.iota(iop_i, [[128, 2]], base=0, channel_multiplier=1), w=["iop_i"])
            op("dve", lambda e: e.tensor_copy(iop, iop_i), r=["iop_i"], w=["iop"])
            op("pool", lambda e: e.iota(islot_i, [[1, 256]], base=0, channel_multiplier=0), w=["islot_i"])
            op("dve", lambda e: e.tensor_copy(islot, islot_i), r=["islot_i"], w=["islot"])
            op("dve", lambda e: e.tensor_single_scalar(ustrict, iot, 0, ALU.is_gt), r=["iot"], w=["ustrict"])
            op("dve", lambda e: e.tensor_copy(selB, bc_last(ident[0:NS, 0:NS], 128)), r=["ident"], w=["selB"])
            op("dve", lambda e: e.tensor_scalar(negcols, ident[:, 0:NS], -1.0, BIG, ALU.add, ALU.mult), r=["ident"], w=["negcols"])
            op("dve", lambda e: e.tensor_tensor(wsig, ws_abs, ws_sgn, ALU.mult), r=["wabs", "wsgn"], w=["wsig"])
            op("pool", lambda e: e.memset(knew, 0.0), w=["knew"])
            op("dve", lambda e: e.tensor_tensor(knew[0:NS, :, :], bc_mid(kis32, NS), bc_last(ident[0:NS, 0:NS], 64), ALU.mult), r=["kis32", "ident", "knew"], w=["knew"])
            op("dve", lambda e: e.tensor_tensor(prodn, qs32, ks32, ALU.mult), r=["qs32", "ks32"], w=["prodn"])
            op("dve", lambda e: e.tensor_reduce(scn, prodn.rearrange("p (h d) -> p h d", d=64), AX.X, ALU.add), r=["prodn"], w=["scn"])
            op("act", lambda e: e.activation(Pn, scn, AF.Exp, scale=0.125), r=["scn"], w=["Pn"])

            for b_ in range(NS):
                dma("sp", lambda e, b_=b_: e.dma_start(out=ptb, in_=ptab[b_].partition_broadcast(128)), w=["ptb"])
                op("dve", lambda e, b_=b_: e.tensor_scalar(physf[:, b_, :], ptb, 128.0, iop[:, 0:1], ALU.mult, ALU.add), r=["ptb", "iop"], w=["physf"])
                op("dve", lambda e, b_=b_: e.tensor_copy(idx32[:, b_, :], physf[:, b_, :]), r=["physf"], w=["idx32"])
                dma("sp", lambda e, b_=b_: e.dma_start(out=idxp[0:NPAGE, b_:b_ + 1], in_=ptab[b_].rearrange("(c o) -> c o", o=1)), w=["idxp"])
                S.pool_dma_once([lambda e, b_=b_: e.indirect_dma_start(out=Gpg, out_offset=None, in_=cache_ik_pg,
                                                                       in_offset=bass.IndirectOffsetOnAxis(ap=idxp[0:NPAGE, b_:b_ + 1], axis=0))],
                                r=["idxp"], w=["kdup", "kiTa"])
                dma("sp", lambda e, b_=b_: e.dma_start(out=ikscr[b_], in_=Gpg), r=["kdup", "kiTa"], w=["ikscr"])
                dma("sp", lambda e, b_=b_: e.dma_start(out=kib[:, 0:NPAGE, :], in_=ikscr[b_].rearrange("c (s d) -> s c d", d=64)), r=["ikscr"], w=["kib"])
                op("pool", lambda e, b_=b_: e.tensor_copy(kib[:, NPAGE, :], knew[:, b_, :]), r=["knew"], w=["kib"])
                op("pool", lambda e: e.tensor_copy(kdup[:, :, 0:64], kib), r=["kib"], w=["kdup"])
                op("act", lambda e: e.activation(kdup[:, :, 64:128], kib, AF.Copy), r=["kib"], w=["kdup"])
                for c0 in range(0, NP1, 8):
                    ncc = min(8, NP1 - c0)
                    for j in range(ncc):
                        op("pe", lambda e, c0=c0, j=j: e.transpose(PSb[6][:, j * 128:(j + 1) * 128], kdup[:, c0 + j, :], identb), r=["kdup", "identb"], w=["ps6"])
                    op("act", lambda e, c0=c0, ncc=ncc: e.activation(kiTa[:, c0:c0 + ncc, :], PSb[6][:, 0:ncc * 128].rearrange("p (a b) -> p a b", b=128), AF.Copy),
                       r=["ps6"], w=["kiTa"])
                for c in range(NP1):
                    for half in range(2):
                        pr_ = slice(half * 64, (half + 1) * 64)
                        bk = 4 * half + (0 if c < NPAGE else 1)
                        col = (c % NPAGE) * 8
                        op("pe", lambda e, bk=bk, col=col, pr_=pr_, c=c, b_=b_: e.matmul(PS[bk][:, col:col + 8], lhsT=kiTa[pr_, c, :],
                                                                                       rhs=qiTs[pr_, :, b_], start=True, stop=True),
                           r=["kiTa", "qiT"], w=[f"ps{bk}"])
                op("pe", lambda e, b_=b_: e.matmul(PS[3][:, 0:16], lhsT=selB[:, b_, :], rhs=wsig, start=True, stop=True), r=["selB", "wsig"], w=["ps3"])
                op("dve", lambda e: e.tensor_copy(wbc.rearrange("p (a b) -> p a b", a=2), PS[3][:, 0:16].rearrange("p (b a) -> p a b", a=2)), r=["ps3"], w=["wbc"])
                for half in range(2):
                    op("dve", lambda e, half=half: e.tensor_scalar(rl[:, 0:NPAGE, half * 8:half * 8 + 8], PS[4 * half].rearrange("p (c h) -> p c h", h=8), 0.0, None, ALU.max),
                       r=[f"ps{4 * half}"], w=["rl"])
                    op("dve", lambda e, half=half: e.tensor_scalar(rl[:, NPAGE, half * 8:half * 8 + 8], PS[4 * half + 1][:, 0:8], 0.0, None, ALU.max),
                       r=[f"ps{4 * half + 1}"], w=["rl"])
                op("dve", lambda e: e.tensor_tensor(rl, rl, bc_mid(wbc, NP1), ALU.mult), r=["rl", "wbc"], w=["rl"])
                op("dve", lambda e, b_=b_: e.tensor_reduce(score4[:, b_, :], rl, AX.X, ALU.add), r=["rl"], w=["score4"])
            bnd = sb[:, 0:4]
            lo4, st4, mid4, cnt4, tmp4 = (sb[:, 4 + 4 * i:8 + 4 * i] for i in range(5))
            op("dve", lambda e: e.tensor_reduce(bnd, score4, AX.X, ALU.max, apply_absolute_value=True), r=["score4"], w=["sb"])
            op("pe", lambda e: e.matmul(PS[3][:, 16:20], lhsT=ones_f, rhs=bnd, start=True, stop=True), r=["ones_f", "sb"], w=["ps3"])
            op("dve", lambda e: e.tensor_scalar(lo4, PS[3][:, 16:20], -1.0, None, ALU.mult), r=["ps3"], w=["sb"])
            op("dve", lambda e: e.tensor_scalar(st4, PS[3][:, 16:20], 2.0, None, ALU.mult), r=["ps3"], w=["sb"])
            op("dve", lambda e: e.tensor_tensor(score4[:, :, NPAGE], score4[:, :, NPAGE], negcols, ALU.add), r=["score4", "negcols"], w=["score4"])
            for it in range(NIT + 8):
                op("dve", lambda e: e.tensor_scalar(st4, st4, 0.5, None, ALU.mult), r=["sb"], w=["sb"])
                op("dve", lambda e: e.tensor_tensor(mid4, lo4, st4, ALU.add), r=["sb"], w=["sb"])
                op("dve", lambda e: e.tensor_tensor(mask4, score4, bc_last(mid4, NP1), ALU.is_ge), r=["score4", "sb"], w=["mask4"])
                op("dve", lambda e: e.tensor_reduce(cnt4, mask4, AX.X, ALU.add), r=["mask4"], w=["sb"])
                op("pe", lambda e: e.matmul(PS[3][:, 16:20], lhsT=ones_f, rhs=cnt4, start=True, stop=True), r=["ones_f", "sb"], w=["ps3"])
                op("dve", lambda e: e.scalar_tensor_tensor(tmp4, PS[3][:, 16:20], 255.5, st4, ALU.is_ge, ALU.mult), r=["ps3", "sb"], w=["sb"])
                op("dve", lambda e: e.tensor_tensor(lo4, lo4, tmp4, ALU.add), r=["sb"], w=["sb"])
            op("dve", lambda e: e.tensor_tensor(mask4, score4, bc_last(lo4, NP1), ALU.is_ge), r=["score4", "sb"], w=["mask4"])
            op("dve", lambda e: e.tensor_copy(maskp, mask4[:, :, 0:NPAGE]), r=["mask4"], w=["maskp"])
            op("dve", lambda e: e.tensor_tensor(mnew, mask4[0:NS, :, NPAGE], ident[0:NS, 0:NS], ALU.mult), r=["mask4", "ident"], w=["mnew"])
            op("dve", lambda e: e.tensor_reduce(mnew[:, 0:1], mnew, AX.X, ALU.add), r=["mnew"], w=["mnew"])
            op("dve", lambda e: e.tensor_scalar(Pn, Pn, mnew[:, 0:1], None, ALU.mult), r=["Pn", "mnew"], w=["Pn"])
            maskp2 = maskp.rearrange("p a b -> p (a b)")
            op("pe", lambda e: e.matmul(PS[0][:, 0:256], lhsT=ustrict, rhs=maskp2, start=True, stop=True), r=["ustrict", "maskp"], w=["ps0"])
            op("pe", lambda e: e.matmul(PS[1][:, 0:256], lhsT=ones_f, rhs=maskp2, start=True, stop=True), r=["ones_f", "maskp"], w=["ps1"])
            op("dve", lambda e: e.tensor_copy(rank.rearrange("p a b -> p (a b)"), PS[1][:, 0:256]), r=["ps1"], w=["rank"])
            for b_ in range(NS):
                op("dve", lambda e, b_=b_: e.tensor_tensor_scan(offs[:, b_, :], ones_f[:, 0:NPAGE], rank[:, b_, :], 0.0, ALU.mult, ALU.add), r=["rank", "ones_f"], w=["offs"])
            op("dve", lambda e: e.tensor_tensor(rank, offs, rank, ALU.subtract), r=["offs", "rank"], w=["rank"])
            op("dve", lambda e: e.tensor_tensor(rank.rearrange("p a b -> p (a b)"), rank.rearrange("p a b -> p (a b)"), PS[0][:, 0:256], ALU.add), r=["rank", "ps0"], w=["rank"])
            ohc = 0
            for b_ in range(NS):
                for c in range(NPAGE):
                    ob = ohc % 3
                    ohc += 1
                    op("dve", lambda e, b_=b_, c=c, ob=ob: e.tensor_scalar(OH[ob], islot, rank[:, b_, c:c + 1], maskp[:, b_, c:c + 1], ALU.is_equal, ALU.mult),
                       r=["islot", "rank", "maskp"], w=[f"OH{ob}"])
                    for half in range(2):
                        op("pe", lambda e, b_=b_, c=c, ob=ob, half=half: e.matmul(PS[2 + half][:, b_:b_ + 1], lhsT=OH[ob][:, half * 128:(half + 1) * 128],
                                                                               rhs=physf[:, b_, c:c + 1], start=(c == 0), stop=(c == NPAGE - 1)),
                           r=[f"OH{ob}", "physf"], w=[f"ps{2 + half}"])
            for half in range(2):
                op("dve", lambda e, half=half: e.tensor_scalar(selidx_f.rearrange("p (b a) -> p b a", a=2)[:, :, half], PS[2 + half][:, 0:NS], 0.25, None, ALU.add),
                   r=[f"ps{2 + half}"], w=["selidx_f"])
            op("dve", lambda e: e.tensor_copy(selidx, selidx_f), r=["selidx_f"], w=["selidx"])
            for b_ in range(NS):
                fl = []
                for half in range(2):
                    fl.append(lambda e, b_=b_, half=half: e.indirect_dma_start(out=Ksel[:, half, :], out_offset=None, in_=cache_k,
                                                                              in_offset=bass.IndirectOffsetOnAxis(ap=selidx[:, b_ * 2 + half:b_ * 2 + half + 1], axis=0)))
                    fl.append(lambda e, b_=b_, half=half: e.indirect_dma_start(out=Vsel[:, half, :], out_offset=None, in_=cache_v,
                                                                              in_offset=bass.IndirectOffsetOnAxis(ap=selidx[:, b_ * 2 + half:b_ * 2 + half + 1], axis=0)))
                S.pool_dma_once(fl, r=["selidx"], w=["Ksel", "Vsel"])
                op("dve", lambda e, b_=b_: e.tensor_scalar(validb, iop, offs[:, b_, NPAGE - 1:NPAGE], None, ALU.is_lt), r=["iop", "offs"], w=["validb"])
                op("pe", lambda e, b_=b_: e.matmul(PS[4], lhsT=selB[:, b_, :], rhs=qs32, start=True, stop=True), r=["selB", "qs32"], w=["ps4"])
                for half in range(2):
                    op("dve", lambda e, half=half: e.tensor_tensor(prodb[:, half, :], Ksel[:, half, :], PS[4], ALU.mult), r=["Ksel", "ps4"], w=["prodb"])
                op("dve", lambda e: e.tensor_reduce(scb.rearrange("p a h -> p (a h)"), prodb.rearrange("p a (h d) -> p (a h) d", d=64), AX.X, ALU.add), r=["prodb"], w=["scb"])
                op("act", lambda e: e.activation(Pb, scb, AF.Exp, scale=0.125), r=["scb"], w=["Pb"])
                op("dve", lambda e: e.tensor_tensor(Pb, Pb, bc_last(validb, 8), ALU.mult), r=["Pb", "validb"], w=["Pb"])
                op("pe", lambda e: e.matmul(PS[5][:, 0:8], lhsT=ones_f, rhs=Pb[:, 0, :], start=True, stop=False), r=["ones_f", "Pb"], w=["ps5"])
                op("pe", lambda e: e.matmul(PS[5][:, 0:8], lhsT=ones_f, rhs=Pb[:, 1, :], start=False, stop=False), r=["ones_f", "Pb"], w=["ps5"])
                op("pe", lambda e, b_=b_: e.matmul(PS[5][:, 0:8], lhsT=selB[:, b_, :], rhs=Pn, start=False, stop=True), r=["selB", "Pn"], w=["ps5"])
                op("dve", lambda e: e.reciprocal(rdenb, PS[5][:, 0:8]), r=["ps5"], w=["rdenb"])
                op("dve", lambda e: e.tensor_tensor(Pb, Pb, bc_mid(rdenb, 2), ALU.mult), r=["Pb", "rdenb"], w=["Pb"])
                op("dve", lambda e: e.tensor_tensor(Pnn, Pn, rdenb[0:NS, :], ALU.mult), r=["Pn", "rdenb"], w=["Pnn"])
                for half in range(2):
                    op("dve", lambda e, half=half: e.tensor_tensor(prodb[:, half, :].rearrange("p (h d) -> p h d", d=64), Vsel[:, half, :].rearrange("p (h d) -> p h d", d=64),
                                                                 bc_last(Pb[:, half, :], 64), ALU.mult), r=["Vsel", "Pb", "prodb"], w=["prodb"])
                op("dve", lambda e: e.tensor_tensor(wvn.rearrange("p (h d) -> p h d", d=64), vs32.rearrange("p (h d) -> p h d", d=64), bc_last(Pnn, 64), ALU.mult),
                   r=["vs32", "Pnn"], w=["wvn"])
                for cch in range(4):
                    cs = slice(cch * 128, (cch + 1) * 128)
                    col = cch * NS + b_
                    op("pe", lambda e, cs=cs, col=col: e.matmul(PS[7][:, col:col + 1], lhsT=prodb[:, 0, cs], rhs=ones_f[:, 0:1], start=True, stop=False), r=["prodb", "ones_f"], w=["ps7"])
                    op("pe", lambda e, cs=cs, col=col: e.matmul(PS[7][:, col:col + 1], lhsT=prodb[:, 1, cs], rhs=ones_f[:, 0:1], start=False, stop=False), r=["prodb", "ones_f"], w=["ps7"])
                    op("pe", lambda e, cs=cs, col=col, b_=b_: e.matmul(PS[7][:, col:col + 1], lhsT=wvn[:, cs], rhs=ident[0:NS, b_:b_ + 1], start=False, stop=True), r=["wvn", "ident"], w=["ps7"])
            op("act", lambda e: e.activation(attT[:, :, T:T + NS], PS[7][:, 0:16].rearrange("p (c b) -> p c b", b=NS), AF.Copy), r=["ps7"], w=["attT"])
            if os.environ.get("MK_DBG_S", "0") == "1":
                d1 = nc.dram_tensor("dbg_score", [128, NS * NP1], F32, kind="ExternalOutput").ap()
                dma("sp", lambda e: e.dma_start(out=d1, in_=score4.rearrange("p a b -> p (a b)")), r=["score4"])
                d2 = nc.dram_tensor("dbg_mask", [128, NS * NP1], F32, kind="ExternalOutput").ap()
                dma("sp", lambda e: e.dma_start(out=d2, in_=mask4.rearrange("p a b -> p (a b)")), r=["mask4"])
                d3 = nc.dram_tensor("dbg_sel", [128, 8], F32, kind="ExternalOutput").ap()
                dma("sp", lambda e: e.dma_start(out=d3, in_=selidx_f), r=["selidx_f"])
                d4 = nc.dram_tensor("dbg_atts", [128, 4, NS], BF16, kind="ExternalOutput").ap()
                dma("sp", lambda e: e.dma_start(out=d4, in_=attT[:, :, T:T + NS]), r=["attT"])
                d6 = nc.dram_tensor("dbg_ksel", [128, 1024], F32, kind="ExternalOutput").ap()
                dma("sp", lambda e: e.dma_start(out=d6, in_=Ksel.rearrange("p a b -> p (a b)")), r=["Ksel"])
                d7 = nc.dram_tensor("dbg_pb", [128, 16], F32, kind="ExternalOutput").ap()
                dma("sp", lambda e: e.dma_start(out=d7, in_=Pb.rearrange("p a b -> p (a b)")), r=["Pb"])
                d8 = nc.dram_tensor("dbg_scb", [128, 16], F32, kind="ExternalOutput").ap()
                dma("sp", lambda e: e.dma_start(out=d8, in_=scb.rearrange("p a b -> p (a b)")), r=["scb"])
                d9 = nc.dram_tensor("dbg_selidx", [128, 8], I32, kind="ExternalOutput").ap()
                dma("sp", lambda e: e.dma_start(out=d9, in_=selidx), r=["selidx"])
                d5 = nc.dram_tensor("dbg_rank", [128, NS * NPAGE], F32, kind="ExternalOutput").ap()
                dma("sp", lambda e: e.dma_start(out=d5, in_=rank.rearrange("p a b -> p (a b)")), r=["rank"])


        if STAGE >= 4:
            S.barrier()
            AR.release(m_att)
            gsc = nc.dram_tensor("gsc", [2, 5, D], F32, kind="Internal").ap()
            win_rw = AR.alloc([8, RW_COLS], BF16)
            wo_att = AR.alloc([4, D], BF16)
            wo_rw = AR.alloc([8, D], BF16, parts=64)
            w2b = AR.alloc([512], BF16, parts=64)
            a2b = AR.alloc([512], BF16, parts=64)
            g2b = AR.alloc([2, 512], BF16, parts=64)
            muD = AR.alloc([28], F32, parts=64)
            Msel = AR.alloc([16], F32)
            Mh = AR.alloc([8], F32)
            maskLA = AR.alloc([16, 128], F32, parts=64)
            g1rep = AR.alloc([D], F32)
            g1s = AR.alloc([D], F32, parts=NS)
            shiftT = AR.alloc([28, NS], F32, parts=64)
            m2 = AR.mark()
            grow = AR.alloc([D], F32, parts=5)
            stg = [AR.alloc([4096], F32) for _ in range(2)]
            stcnt = [0]

            def load_cast(dst, src, parts, shape):
                b = stcnt[0] % 2
                stcnt[0] += 1
                ne = _prod(shape)
                v = stg[b][0:parts, 0:ne]
                if len(shape) == 2:
                    v = v.rearrange("p (a b) -> p a b", a=shape[0])
                dma("sp", lambda e: e.dma_start(out=v, in_=src), w=[f"stg{b}"])
                op("pool", lambda e: e.tensor_copy(dst, v), r=[f"stg{b}"], w=["wts2"])

            w_in_v2 = w_in.rearrange("(kc p) n -> p kc n", p=128)
            for c in range(0, RW_COLS, 512):
                n = min(512, RW_COLS - c)
                load_cast(win_rw[:, :, c:c + n], w_in_v2[:, :, ATT_COLS + c:ATT_COLS + c + n], 128, [8, n])
            load_cast(wo_att, w_out[0:512, :].rearrange("(c p) n -> p c n", p=128), 128, [4, D])
            wo_rw_v = w_out[512:1024, :].rearrange("(h p) n -> p h n", p=64)
            for hh in range(0, 8, 4):
                load_cast(wo_rw[:, hh:hh + 4, :], wo_rw_v[:, hh:hh + 4, :], 64, [4, D])
            load_cast(w2b, rw_w2, 64, [512])
            load_cast(a2b, rw_a2, 64, [512])
            load_cast(g2b, rw_g2.rearrange("(b p) n -> p b n", p=64), 64, [2, 512])
            stm = stg[0][0:28, 0:64]
            dma("sp", lambda e: e.dma_start(out=stm, in_=rw_mu.rearrange("(a b) -> a b", b=64)), w=["stg0"])
            op("pe", lambda e: e.transpose(PS[0][0:64, 0:28], stm, ident[0:28, 0:28]), r=["stg0", "ident"], w=["ps0"])
            op("dve", lambda e: e.tensor_copy(muD, PS[0][0:64, 0:28]), r=["ps0"], w=["muD"])
            op("dve", lambda e: e.tensor_reduce(Msel, ident.rearrange("p (h t) -> p t h", t=16), AX.X, ALU.add), r=["ident"], w=["Msel"])
            op("dve", lambda e: e.tensor_reduce(Mh, ident.rearrange("p (h t) -> p h t", t=16), AX.X, ALU.add), r=["ident"], w=["Mh"])
            op("pool", lambda e: e.memset(maskLA, 0.0), w=["maskLA"])
            mla = maskLA.rearrange("p a (h t) -> p a h t", t=16)
            for tp in range(16):
                op("pool", lambda e, tp=tp: e.memset(mla[:, tp, :, tp:tp + 1], 1.0), w=["maskLA"])
            for kc in range(8):
                op("pe", lambda e, kc=kc: e.transpose(PS[kc // 4][0:5, (kc % 4) * 128:(kc % 4 + 1) * 128], gate1[:, kc, :], ident), r=["modT", "ident"], w=[f"ps{kc // 4}"])
            op("dve", lambda e: e.tensor_copy(grow[:, 0:512], PS[0][0:5, :]), r=["ps0"], w=["grow"])
            op("dve", lambda e: e.tensor_copy(grow[:, 512:1024], PS[1][0:5, :]), r=["ps1"], w=["grow"])
            dma("sp", lambda e: e.dma_start(out=gsc[0], in_=grow), r=["grow"], w=["gsc0"])
            dma("sp", lambda e: e.dma_start(out=g1rep, in_=gsc[0, 0].partition_broadcast(128)), r=["gsc0"], w=["g1rep"])
            dma("sp", lambda e: e.dma_start(out=g1s, in_=gsc[0, 1:5, :]), r=["gsc0"], w=["g1s"])
            shs = stg[1][0:NS, 0:RW_COLS]
            dma("sp", lambda e: e.dma_start(out=shs, in_=st_shift), w=["stg1"])
            for blk in range(28):
                op("pe", lambda e, blk=blk: e.transpose(PS[2][0:64, blk * 4:blk * 4 + NS], shs[:, blk * 64:(blk + 1) * 64], ident[0:NS, 0:NS]), r=["stg1", "ident"], w=["ps2"])
            op("dve", lambda e: e.tensor_copy(shiftT, PS[2][0:64, 0:112].rearrange("p (a b) -> p a b", b=4)), r=["ps2"], w=["shiftT"])
            S.barrier()
            AR.release(m2)

            W = 128
            xt2 = AR.alloc([D], F32)
            xn2 = AR.alloc([D], F32)
            hT2 = AR.alloc([8, W], BF16)
            prd = AR.alloc([28, W + 1], F32, parts=64)
            xm = AR.alloc([28, W], F32, parts=64)
            dv = {k: AR.alloc([8, W], F32, parts=64) for k in ["wdec", "asig", "kkn", "na", "bb", "kmod", "gg", "bon", "t1", "t2"]}
            dv["Yd"] = dv["kkn"]
            rwd = AR.alloc([8, W], BF16, parts=64)
            tanh_wd = AR.alloc([W], BF16, parts=64)
            ad_bf = AR.alloc([W], BF16, parts=64)
            sg = AR.alloc([2, W], BF16, parts=64)
            LA_sel = AR.alloc([16, 128], F32, parts=64)
            Xall = AR.alloc([192], F32)
            Xb16 = Xall[:, 0:64]
            Xk_sel = AR.alloc([16, 64], F32)
            Vbd16 = AR.alloc([8, 64], F32)
            R128 = [AR.alloc([8, 64], F32) for _ in range(2)]
            ST = AR.alloc([8, 64], F32, parts=64)
            ST2 = AR.alloc([8, 64], F32, parts=64)
            Sio = AR.alloc([8, 64], F32, parts=64)
            sm2 = AR.alloc([8], F32)
            tstg = AR.alloc([3, 128], F32, parts=64)
            w0T, a0T, kkT, kaT, lnwT, lnbT, rkT = (vecB[:, 8 * i:8 * i + 8] for i in range(7))
            for k_ in dv:
                if k_ != "Yd":
                    op("pool", lambda e, k_=k_: e.memset(dv[k_], 0.0), w=[k_])
            op("pool", lambda e: e.memset(xm, 0.0), w=["xm"])
            op("pool", lambda e: e.memset(prd, 0.0), w=["prd"])
            op("pool", lambda e: e.memset(ST, 0.0), w=["ST"])
            ones64 = ones_f[0:64, 0:64]

            def v4(ap):
                return ap[0:64, :].rearrange("p (a b) -> p a b", b=128)

            def sum64(src, P):
                for hb in range(2):
                    if P == 128:
                        op("pe", lambda e, hb=hb: e.matmul(v4(PS[6 + hb])[:, :, 0:P], lhsT=ones64, rhs=src[:, hb * 4:(hb + 1) * 4, 0:P], start=True, stop=True),
                           r=["ones_f", "dvsrc"], w=[f"ps{6 + hb}"])
                    else:
                        for hl in range(4):
                            op("pe", lambda e, hb=hb, hl=hl: e.matmul(PS[6 + hb][0:64, hl * 128:hl * 128 + P], lhsT=ones64, rhs=src[:, hb * 4 + hl, 0:P], start=True, stop=True),
                               r=["ones_f", "dvsrc"], w=[f"ps{6 + hb}"])

            def rw_tile(ti, P, x_src, cond, tcol0, prev_view, is_sample):
                dma("sp", lambda e: e.dma_start(out=xt2[0:P, :], in_=x_src), w=["xt2"])
                ss = sm2[0:P, 0:1]
                rs = sm2[0:P, 1:2]
                op("act", lambda e: e.activation(xn2[0:P, :], xt2[0:P, :], AF.Square, accum_out=ss), r=["xt2"], w=["xn2", "sm2"])
                op("dve", lambda e: e.tensor_scalar(rs, ss, 1.0 / D, RMS_EPS, ALU.mult, ALU.add), r=["sm2"], w=["sm2"])
                op("act", lambda e: e.activation(rs, rs, AF.Sqrt), r=["sm2"], w=["sm2"])
                op("dve", lambda e: e.reciprocal(rs, rs), r=["sm2"], w=["sm2"])
                op("dve", lambda e: e.tensor_scalar(xn2[0:P, :], xt2[0:P, :], rs, None, ALU.mult), r=["xt2", "sm2"], w=["xn2"])
                for half in range(2):
                    pb = PS[half]
                    for j in range(4):
                        kc = half * 4 + j
                        op("pe", lambda e, kc=kc, j=j, pb=pb: e.transpose(pb[:, j * 128:j * 128 + P], xn2[0:P, kc * 128:(kc + 1) * 128], ident[0:P, 0:P]),
                           r=["xn2", "ident"], w=[f"ps{half}"])
                    for j in range(4):
                        kc = half * 4 + j
                        if not is_sample:
                            op("act", lambda e, kc=kc, j=j, pb=pb: e.activation(hT2[:, kc, 0:P], pb[:, j * 128:j * 128 + P], AF.Identity,
                                                                              bias=shift1[:, kc, 0:1], scale=A1[:, kc, 0:1]), r=[f"ps{half}", "A1", "modT"], w=["hT2"])
                        else:
                            op("dve", lambda e, kc=kc, j=j, pb=pb: e.tensor_tensor(scr2[:, 0:P], pb[:, j * 128:j * 128 + P], A1[:, kc, 1:1 + P], ALU.mult),
                               r=[f"ps{half}", "A1"], w=["scr2b"])
                            op("dve", lambda e, kc=kc: e.tensor_tensor(hT2[:, kc, 0:P], scr2[:, 0:P], shift1[:, kc, 1:1 + P], ALU.add), r=["scr2b", "modT"], w=["hT2"])
                for blk0 in range(0, 28, 4):
                    bk = 2 + (blk0 // 4) % 2
                    for j in range(4):
                        blk = blk0 + j
                        for kc in range(8):
                            op("pe", lambda e, bk=bk, j=j, blk=blk, kc=kc: e.matmul(PS[bk][0:64, j * 128:j * 128 + P], lhsT=win_rw[:, kc, blk * 64:(blk + 1) * 64],
                                                                                    rhs=hT2[:, kc, 0:P], start=(kc == 0), stop=(kc == 7)), r=["hT2", "wts2"], w=[f"ps{bk}"])
                    op("act", lambda e, bk=bk, blk0=blk0: e.activation(prd[:, blk0:blk0 + 4, 1:1 + P], v4(PS[bk])[:, :, 0:P], AF.Copy), r=[f"ps{bk}"], w=["prd"])
                if P2_CUT <= 2:
                    return
                cur = prd[:, :, 1:1 + P]
                prev = prev_view if prev_view is not None else prd[:, :, 0:P]
                xmv = xm[:, :, 0:P]
                op("dve", lambda e: e.tensor_tensor(xmv, prev, cur, ALU.subtract), r=["prd", "shiftT"], w=["xm"])
                op("dve", lambda e: e.tensor_tensor(xmv, xmv, bc_last(muD, P), ALU.mult), r=["xm", "muD"], w=["xm"])
                op("dve", lambda e: e.tensor_tensor(xmv, xmv, cur, ALU.add), r=["xm", "prd"], w=["xm"])
                if is_sample:
                    for b_ in range(P):
                        op("pe", lambda e, b_=b_: e.transpose(PS[0][0:28, b_ * 64:(b_ + 1) * 64], prd[:, :, 1 + b_], ident[0:64, 0:64]), r=["prd", "ident"], w=["ps0"])
                    op("dve", lambda e: e.tensor_copy(xn2[0:28, 0:P * 64], PS[0][0:28, 0:P * 64]), r=["ps0"], w=["xn2"])
                    for b_ in range(P):
                        dma("sp", lambda e, b_=b_: e.dma_start(out=sh_so[b_].rearrange("(a c) -> a c", c=64), in_=xn2[0:28, b_ * 64:(b_ + 1) * 64]), r=["xn2"])
                else:
                    if ti == NTILES - 1:
                        op("pe", lambda e: e.transpose(PS[0][0:28, 0:64], prd[:, :, P], ident[0:64, 0:64]), r=["prd", "ident"], w=["ps0"])
                        op("dve", lambda e: e.tensor_copy(xn2[0:28, 0:64], PS[0][0:28, 0:64]), r=["ps0"], w=["xn2"])
                        dma("sp", lambda e: e.dma_start(out=sh_po.rearrange("(a c) -> a c", c=64), in_=xn2[0:28, 0:64]), r=["xn2"])
                    op("pool", lambda e: e.tensor_copy(prd[:, :, 0:1], prd[:, :, P:P + 1]), r=["prd", "xm"], w=["prd"])
                if P2_CUT <= 3:
                    return
                r_ = xm[:, 0:8, :]
                k_ = xm[:, 8:16, :]
                v_ = xm[:, 16:24, :]
                D_ = {k2: dv[k2][:, :, 0:P] for k2 in dv}
                op("act", lambda e: e.activation(tanh_wd[:, 0:P], xm[:, 24, 0:P], AF.Tanh), r=["xm"], w=["tanh_wd"])
                op("act", lambda e: e.activation(ad_bf[:, 0:P], xm[:, 25, 0:P], AF.Copy), r=["xm"], w=["ad_bf"])
                op("act", lambda e: e.activation(sg[:, :, 0:P], xm[:, 26:28, 0:P], AF.Sigmoid), r=["xm"], w=["sg"])

                def lora(wb, rhs_ap, rname):
                    for h in range(8):
                        op("pe", lambda e, h=h: e.matmul(PS[4 + h // 4][0:64, (h % 4) * 128:(h % 4) * 128 + P], lhsT=wb[:, h * 64:(h + 1) * 64], rhs=rhs_ap,
                                                       start=True, stop=True), r=[rname, "wts2"], w=[f"ps{4 + h // 4}"])
                lora(w2b, tanh_wd[:, 0:P], "tanh_wd")
                for hb in range(2):
                    op("dve", lambda e, hb=hb: e.tensor_tensor(D_["t1"][:, hb * 4:(hb + 1) * 4, :], v4(PS[4 + hb])[:, :, 0:P], bc_last(w0T[:, hb * 4:(hb + 1) * 4], P), ALU.add),
                       r=[f"ps{4 + hb}", "vecB"], w=["t1"])
                op("act", lambda e: e.activation(D_["t1"], D_["t1"], AF.Sigmoid), r=["t1"], w=["t1"])
                op("act", lambda e: e.activation(D_["wdec"], D_["t1"], AF.Exp, scale=-0.6065306597126334), r=["t1"], w=["wdec"])
                lora(a2b, ad_bf[:, 0:P], "ad_bf")
                for hb in range(2):
                    op("dve", lambda e, hb=hb: e.tensor_tensor(D_["t1"][:, hb * 4:(hb + 1) * 4, :], v4(PS[4 + hb])[:, :, 0:P], bc_last(a0T[:, hb * 4:(hb + 1) * 4], P), ALU.add),
                       r=[f"ps{4 + hb}", "vecB"], w=["t1"])
                op("act", lambda e: e.activation(D_["asig"], D_["t1"], AF.Sigmoid), r=["t1"], w=["asig"])
                for h in range(8):
                    for blk in range(2):
                        op("pe", lambda e, h=h, blk=blk: e.matmul(PS[4 + h // 4][0:64, (h % 4) * 128:(h % 4) * 128 + P], lhsT=g2b[:, blk, h * 64:(h + 1) * 64], rhs=sg[:, blk, 0:P],
                                                                 start=(blk == 0), stop=(blk == 1)), r=["sg", "wts2"], w=[f"ps{4 + h // 4}"])
                for hb in range(2):
                    op("act", lambda e, hb=hb: e.activation(D_["gg"][:, hb * 4:(hb + 1) * 4, :], v4(PS[4 + hb])[:, :, 0:P], AF.Copy), r=[f"ps{4 + hb}"], w=["gg"])
                kP, rP, vP = k_[:, :, 0:P], r_[:, :, 0:P], v_[:, :, 0:P]
                op("dve", lambda e: e.tensor_tensor(D_["kkn"], kP, bc_last(kkT, P), ALU.mult), r=["xm", "vecB"], w=["kkn"])
                op("dve", lambda e: e.tensor_tensor(D_["t1"], D_["kkn"], D_["kkn"], ALU.mult), r=["kkn"], w=["t1", "dvsrc"])
                sum64(dv["t1"], P)
                for hb in range(2):
                    op("dve", lambda e, hb=hb: e.tensor_scalar(D_["t2"][:, hb * 4:(hb + 1) * 4, :], v4(PS[6 + hb])[:, :, 0:P], 1e-24, None, ALU.max), r=[f"ps{6 + hb}"], w=["t2"])
                op("act", lambda e: e.activation(D_["t2"], D_["t2"], AF.Sqrt), r=["t2"], w=["t2"])
                op("dve", lambda e: e.reciprocal(D_["t2"], D_["t2"]), r=["t2"], w=["t2"])
                op("dve", lambda e: e.tensor_tensor(D_["kkn"], D_["kkn"], D_["t2"], ALU.mult), r=["kkn", "t2"], w=["kkn"])
                op("dve", lambda e: e.tensor_scalar(D_["t1"], D_["asig"], -1.0, None, ALU.add), r=["asig"], w=["t1"])
                op("dve", lambda e: e.tensor_tensor(D_["t1"], D_["t1"], bc_last(kaT, P), ALU.mult), r=["t1", "vecB"], w=["t1"])
                op("dve", lambda e: e.tensor_scalar(D_["t1"], D_["t1"], 1.0, None, ALU.add), r=["t1"], w=["t1"])
                op("dve", lambda e: e.tensor_tensor(D_["kmod"], kP, D_["t1"], ALU.mult), r=["xm", "t1"], w=["kmod"])
                op("dve", lambda e: e.tensor_tensor(D_["bb"], D_["kkn"], D_["asig"], ALU.mult), r=["kkn", "asig"], w=["bb"])
                op("pool", lambda e: e.tensor_scalar(D_["na"], D_["kkn"], -1.0, None, ALU.mult), r=["kkn"], w=["na"])
                op("dve", lambda e: e.tensor_tensor(D_["t1"], rP, D_["kmod"], ALU.mult), r=["xm", "kmod"], w=["t1"])
                op("dve", lambda e: e.tensor_tensor(D_["t1"], D_["t1"], bc_last(rkT, P), ALU.mult), r=["t1", "vecB"], w=["t1", "dvsrc"])
                sum64(dv["t1"], P)
                for hb in range(2):
                    op("dve", lambda e, hb=hb: e.tensor_tensor(D_["bon"][:, hb * 4:(hb + 1) * 4, :], v4(PS[6 + hb])[:, :, 0:P], vP[:, hb * 4:(hb + 1) * 4, :], ALU.mult),
                       r=[f"ps{6 + hb}", "xm"], w=["bon"])
                if os.environ.get("MK_DBG_DV", "0") == "1" and ti == 0:
                    dbg_dv = nc.dram_tensor("dbg_dv", [64, 7, 8, 128], F32, kind="ExternalOutput").ap()
                    for i_, nm in enumerate(["wdec", "asig", "kkn", "kmod", "bb", "gg", "bon"]):
                        dma("sp", lambda e, i_=i_, nm=nm: e.dma_start(out=dbg_dv[:, i_], in_=dv[nm]), r=[nm])
                    dbg_xm = nc.dram_tensor("dbg_xm", [64, 28, 128], F32, kind="ExternalOutput").ap()
                    dma("sp", lambda e: e.dma_start(out=dbg_xm, in_=xm), r=["xm"])
                if P2_CUT <= 4:
                    return
                acnt = 0
                PSC = min(P, int(os.environ.get("MK_NSTEPS", "100000")))
                for t0 in range(0, PSC, 16):
                    nst = min(16, PSC - t0)
                    for (ci, srcv, rn) in [(0, dv["bb"], "bb"), (1, dv["kmod"], "kmod"), (2, v_, "xm")]:
                        op("pool", lambda e, ci=ci, srcv=srcv, t0=t0: e.tensor_copy(tstg[:, ci, :].rearrange("p (h t) -> p h t", t=16), srcv[:, :, t0:t0 + 16]), r=[rn], w=["tstg"])
                        op("pe", lambda e, ci=ci: e.transpose(PS[6][:, ci * 64:ci * 64 + 64], tstg[:, ci, :], ident[0:64, 0:64]), r=["tstg", "ident"], w=["ps6"])
                    op("act", lambda e: e.activation(Xall, PS[6][:, 0:192], AF.Copy), r=["ps6"], w=["Xb16"])
                    if os.environ.get("MK_T2", "0") != "1":
                        op("dve", lambda e: e.tensor_tensor(Xk_sel, bc_mid(Xall[:, 64:128], 16), bc_last(Msel, 64), ALU.mult), r=["Xb16", "Msel"], w=["Xk_sel"])
                        op("dve", lambda e: e.tensor_tensor(Vbd16, bc_mid(Xall[:, 128:192], 8), bc_last(Mh, 64), ALU.mult), r=["Xb16", "Mh"], w=["Vbd16"])
                    if os.environ.get("MK_T1", "0") != "1":
                        op("pool", lambda e, t0=t0: e.tensor_tensor(LA_sel.rearrange("p a (h t) -> p a h t", t=16), dv["na"][:, :, t0:t0 + 16].unsqueeze(1).to_broadcast([64, 16, 8, 16]),
                                                             maskLA.rearrange("p a (h t) -> p a h t", t=16), ALU.mult), r=["na", "maskLA"], w=["LA_sel"])
                    for tp in range(nst if SCAN_CUT > 1 else 0):
                        t = t0 + tp
                        if is_sample:
                            load_state(t)
                        pa = acnt % 2
                        pu = 2 + acnt % 2
                        rb = acnt % 2
                        acnt += 1
                        op("pe", lambda e, pa=pa, tp=tp: e.matmul(PS[pa], lhsT=LA_sel[:, tp, :], rhs=ST.rearrange("p h i -> p (h i)"), start=True, stop=True),
                           r=["LA_sel", "ST"], w=[f"ps{pa}"])
                        op("dve", lambda e, pa=pa, rb=rb: e.tensor_tensor(R128[rb], PS[pa].rearrange("p (h i) -> p h i", i=64), bc_last(Mh, 64), ALU.mult),
                           r=[f"ps{pa}", "Mh"], w=[f"R{rb}"])
                        if SCAN_CUT <= 2:
                            continue
                        op("pool", lambda e, t=t: e.tensor_tensor(ST2, ST, bc_last(dv["wdec"][:, :, t], 64), ALU.mult), r=["ST", "wdec"], w=["ST2"])
                        op("pe", lambda e, pu=pu, tp=tp: e.matmul(PS[pu][0:64, :], lhsT=Xk_sel[:, tp, :], rhs=Vbd16.rearrange("p h i -> p (h i)"), start=True, stop=False),
                           r=["Xk_sel", "Vbd16"], w=[f"ps{pu}"])
                        op("pe", lambda e, pu=pu, rb=rb: e.matmul(PS[pu][0:64, :], lhsT=Xb16, rhs=R128[rb].rearrange("p h i -> p (h i)"), start=False, stop=True),
                           r=["Xb16", f"R{rb}"], w=[f"ps{pu}"])
                        op("dve", lambda e, pu=pu: e.tensor_tensor(ST.rearrange("p h i -> p (h i)"), ST2.rearrange("p h i -> p (h i)"), PS[pu][0:64, :], ALU.add),
                           r=["ST2", f"ps{pu}"], w=["ST"])
                        for h in range(8 if SCAN_CUT > 3 else 0):
                            op("pe", lambda e, h=h, t=t: e.matmul(PS[4 + h // 4][0:64, (h % 4) * 128 + t:(h % 4) * 128 + t + 1], lhsT=ST[:, h, :], rhs=r_[:, h, t:t + 1],
                                                               start=True, stop=True), r=["ST", "xm"], w=[f"ps{4 + h // 4}"])
                        if is_sample:
                            store_state(wkv_so[t])
                if P2_CUT <= 5:
                    return
                for hb in range(2):
                    op("act", lambda e, hb=hb: e.activation(D_["Yd"][:, hb * 4:(hb + 1) * 4, :], v4(PS[4 + hb])[:, :, 0:P], AF.Copy), r=[f"ps{4 + hb}"], w=["kkn", "dvsrc"])
                sum64(dv["Yd"], P)
                for hb in range(2):
                    op("dve", lambda e, hb=hb: e.scalar_tensor_tensor(D_["t1"][:, hb * 4:(hb + 1) * 4, :], v4(PS[6 + hb])[:, :, 0:P], -1.0 / 64, D_["Yd"][:, hb * 4:(hb + 1) * 4, :],
                                                                      ALU.mult, ALU.add), r=[f"ps{6 + hb}", "kkn"], w=["t1"])
                op("dve", lambda e: e.tensor_tensor(D_["t2"], D_["t1"], D_["t1"], ALU.mult), r=["t1"], w=["t2", "dvsrc"])
                sum64(dv["t2"], P)
                for hb in range(2):
                    op("dve", lambda e, hb=hb: e.tensor_scalar(D_["t2"][:, hb * 4:(hb + 1) * 4, :], v4(PS[6 + hb])[:, :, 0:P], 1.0 / 64, GN_EPS, ALU.mult, ALU.add),
                       r=[f"ps{6 + hb}"], w=["t2"])
                op("act", lambda e: e.activation(D_["t2"], D_["t2"], AF.Sqrt), r=["t2"], w=["t2"])
                op("dve", lambda e: e.reciprocal(D_["t2"], D_["t2"]), r=["t2"], w=["t2"])
                op("dve", lambda e: e.tensor_tensor(D_["t1"], D_["t1"], D_["t2"], ALU.mult), r=["t1", "t2"], w=["t1"])
                op("dve", lambda e: e.tensor_tensor(D_["t1"], D_["t1"], bc_last(lnwT, P), ALU.mult), r=["t1", "vecB"], w=["t1"])
                op("dve", lambda e: e.tensor_tensor(D_["t1"], D_["t1"], bc_last(lnbT, P), ALU.add), r=["t1", "vecB"], w=["t1"])
                op("dve", lambda e: e.tensor_tensor(D_["t1"], D_["t1"], D_["bon"], ALU.add), r=["t1", "bon"], w=["t1"])
                op("dve", lambda e: e.tensor_tensor(rwd[:, :, 0:P], D_["t1"], D_["gg"], ALU.mult), r=["t1", "gg"], w=["rwd"])
                grep_ = g1s if is_sample else g1rep
                for half in range(2):
                    cs = slice(half * 512, (half + 1) * 512)
                    pb = 2 + half
                    for c in range(4):
                        op("pe", lambda e, c=c, cs=cs, pb=pb: e.matmul(PS[pb][0:P, :], lhsT=attT[:, c, tcol0:tcol0 + P], rhs=wo_att[:, c, cs], start=(c == 0), stop=False),
                           r=["attT", "wts2"], w=[f"ps{pb}"])
                    for h in range(8):
                        op("pe", lambda e, h=h, cs=cs, pb=pb: e.matmul(PS[pb][0:P, :], lhsT=rwd[:, h, 0:P], rhs=wo_rw[:, h, cs], start=False, stop=(h == 7)),
                           r=["rwd", "wts2"], w=[f"ps{pb}"])
                    op("dve", lambda e, cs=cs, pb=pb: e.tensor_tensor(xn2[0:P, cs], PS[pb][0:P, :], grep_[0:P, cs], ALU.mult), r=[f"ps{pb}", "g1rep", "g1s"], w=["xn2"])
                    op("dve", lambda e, cs=cs: e.tensor_tensor(xn2[0:P, cs], xn2[0:P, cs], xt2[0:P, cs], ALU.add), r=["xn2", "xt2"], w=["xn2"])

            scr2 = AR.alloc([W], F32)

            def load_state(b_):
                dma("sp", lambda e: e.dma_start(out=Sio, in_=st_wkv[b_].rearrange("h i j -> i h j")), w=["Sio"])
                for h in range(8):
                    op("pe", lambda e, h=h: e.transpose(PS[7][0:64, h * 64:(h + 1) * 64], Sio[:, h, :], ident[0:64, 0:64]), r=["Sio", "ident"], w=["ps7"])
                op("dve", lambda e: e.tensor_copy(ST.rearrange("p h i -> p (h i)"), PS[7][0:64, :]), r=["ps7"], w=["ST"])

            def store_state(dst):
                for h in range(8):
                    op("pe", lambda e, h=h: e.transpose(PS[7][0:64, h * 64:(h + 1) * 64], ST[:, h, :], ident[0:64, 0:64]), r=["ST", "ident"], w=["ps7"])
                op("dve", lambda e: e.tensor_copy(Sio.rearrange("p h i -> p (h i)"), PS[7][0:64, :]), r=["ps7"], w=["Sio"])
                dma("sp", lambda e: e.dma_start(out=dst.rearrange("h i j -> i h j"), in_=Sio), r=["Sio"])

            for ti in range(NTILES if P2_CUT > 1 else 0):
                r0 = ti * 128
                rw_tile(ti, 128, x_p[r0:r0 + 128, :], None, r0, None, False)
                dma("sp", lambda e, r0=r0: e.dma_start(out=y_p[r0:r0 + 128, :], in_=xn2), r=["xn2"], w=["y_p"])
            if P2_CUT > 5:
                store_state(wkv_po)
            if P2_CUT > 1 and os.environ.get("MK_NOSAMP", "0") != "1":
                rw_tile(16, NS, x_s, 1, T, shiftT, True)
                dma("sp", lambda e: e.dma_start(out=y_s, in_=xn2[0:NS, :]), r=["xn2"], w=["y_s"])


        if STAGE >= 5:
            S.barrier()
            AR.release(m_attT)
            wg = AR.alloc([8, FFN], BF16)
            wu = AR.alloc([8, FFN], BF16)
            wd_ = AR.alloc([NFC, D], BF16)
            g2rep = AR.alloc([D], F32)
            g2s = AR.alloc([D], F32, parts=NS)
            m3 = AR.mark()
            grow2 = AR.alloc([D], F32, parts=5)
            stg3 = [AR.alloc([4096], F32) for _ in range(2)]
            st3 = [0]

            def load_cast3(dst, src, shape):
                b = st3[0] % 2
                st3[0] += 1
                ne = _prod(shape)
                v = stg3[b][:, 0:ne].rearrange("p (a b) -> p a b", a=shape[0])
                dma("sp", lambda e: e.dma_start(out=v, in_=src), w=[f"stg3{b}"])
                op("pool", lambda e: e.tensor_copy(dst, v), r=[f"stg3{b}"], w=["wts3"])

            wg_v = w_gate.rearrange("(kc p) n -> p kc n", p=128)
            wu_v = w_up.rearrange("(kc p) n -> p kc n", p=128)
            wd_v = w_down.rearrange("(fc p) n -> p fc n", p=128)
            for c in range(0, FFN, 512):
                n = min(512, FFN - c)
                load_cast3(wg[:, :, c:c + n], wg_v[:, :, c:c + n], [8, n])
                load_cast3(wu[:, :, c:c + n], wu_v[:, :, c:c + n], [8, n])
            for fc0 in range(0, NFC, 4):
                nf = min(4, NFC - fc0)
                load_cast3(wd_[:, fc0:fc0 + nf, :], wd_v[:, fc0:fc0 + nf, :], [nf, D])
            for kc in range(8):
                op("pe", lambda e, kc=kc: e.transpose(PS[kc // 4][0:5, (kc % 4) * 128:(kc % 4 + 1) * 128], gate2[:, kc, :], ident), r=["modT", "ident"], w=[f"ps{kc // 4}"])
            op("dve", lambda e: e.tensor_copy(grow2[:, 0:512], PS[0][0:5, :]), r=["ps0"], w=["grow2"])
            op("dve", lambda e: e.tensor_copy(grow2[:, 512:1024], PS[1][0:5, :]), r=["ps1"], w=["grow2"])
            dma("sp", lambda e: e.dma_start(out=gsc[1], in_=grow2), r=["grow2"], w=["gsc1"])
            dma("sp", lambda e: e.dma_start(out=g2rep, in_=gsc[1, 0].partition_broadcast(128)), r=["gsc1"], w=["g2rep"])
            dma("sp", lambda e: e.dma_start(out=g2s, in_=gsc[1, 1:5, :]), r=["gsc1"], w=["g2s"])
            S.barrier()
            AR.release(m3)
            x1t = [AR.alloc([D], F32) for _ in range(2)]
            xn3 = AR.alloc([D], F32)
            h2T = AR.alloc([8, 256], BF16)
            hidT = AR.alloc([NFC, 256], BF16)
            sil = [AR.alloc([256], F32) for _ in range(2)]
            yt = [AR.alloc([D], F32) for _ in range(2)]
            sm3 = AR.alloc([8], F32)
            scr3 = AR.alloc([NS], F32)

            def ffn_group(tiles, is_sample):
                ncol = 0
                for i_, (P, src, dst) in enumerate(tiles):
                    xb = x1t[i_]
                    dma("sp", lambda e, xb=xb, P=P, src=src: e.dma_start(out=xb[0:P, :], in_=src), r=["y_p", "y_s"], w=[f"x1t{i_}"])
                    ss = sm3[0:P, 0:1]
                    rs = sm3[0:P, 1:2]
                    op("act", lambda e, xb=xb, P=P, ss=ss: e.activation(xn3[0:P, :], xb[0:P, :], AF.Square, accum_out=ss), r=[f"x1t{i_}"], w=["xn3", "sm3"])
                    op("dve", lambda e, ss=ss, rs=rs: e.tensor_scalar(rs, ss, 1.0 / D, RMS_EPS, ALU.mult, ALU.add), r=["sm3"], w=["sm3"])
                    op("act", lambda e, rs=rs: e.activation(rs, rs, AF.Sqrt), r=["sm3"], w=["sm3"])
                    op("dve", lambda e, rs=rs: e.reciprocal(rs, rs), r=["sm3"], w=["sm3"])
                    op("dve", lambda e, xb=xb, P=P, rs=rs: e.tensor_scalar(xn3[0:P, :], xb[0:P, :], rs, None, ALU.mult), r=[f"x1t{i_}", "sm3"], w=["xn3"])
                    for half in range(2):
                        pb = PS[half]
                        for j in range(4):
                            kc = half * 4 + j
                            op("pe", lambda e, kc=kc, j=j, pb=pb, P=P: e.transpose(pb[:, j * 128:j * 128 + P], xn3[0:P, kc * 128:(kc + 1) * 128], ident[0:P, 0:P]),
                               r=["xn3", "ident"], w=[f"ps{half}"])
                        for j in range(4):
                            kc = half * 4 + j
                            if not is_sample:
                                op("act", lambda e, kc=kc, j=j, pb=pb, P=P, ncol=ncol: e.activation(h2T[:, kc, ncol:ncol + P], pb[:, j * 128:j * 128 + P], AF.Identity,
                                                                                                  bias=shift2[:, kc, 0:1], scale=A2[:, kc, 0:1]), r=[f"ps{half}", "A2", "modT"], w=["h2T"])
                            else:
                                op("dve", lambda e, kc=kc, j=j, pb=pb, P=P: e.tensor_tensor(scr3[:, 0:P], pb[:, j * 128:j * 128 + P], A2[:, kc, 1:1 + P], ALU.mult),
                                   r=[f"ps{half}", "A2"], w=["scr3"])
                                op("dve", lambda e, kc=kc, P=P, ncol=ncol: e.tensor_tensor(h2T[:, kc, ncol:ncol + P], scr3[:, 0:P], shift2[:, kc, 1:1 + P], ALU.add),
                                   r=["scr3", "modT"], w=["h2T"])
                    ncol += P
                N = ncol
                for fc in range(NFC):
                    pg = PS[2 + fc % 2]
                    pu_ = PS[4 + fc % 2]
                    sb = sil[fc % 2]
                    for kc in range(8):
                        op("pe", lambda e, fc=fc, kc=kc, pg=pg: e.matmul(pg[:, 0:N], lhsT=wg[:, kc, fc * 128:(fc + 1) * 128], rhs=h2T[:, kc, 0:N], start=(kc == 0), stop=(kc == 7)),
                           r=["h2T", "wts3"], w=[f"ps{2 + fc % 2}"])
                    for kc in range(8):
                        op("pe", lambda e, fc=fc, kc=kc, pu_=pu_: e.matmul(pu_[:, 0:N], lhsT=wu[:, kc, fc * 128:(fc + 1) * 128], rhs=h2T[:, kc, 0:N], start=(kc == 0), stop=(kc == 7)),
                           r=["h2T", "wts3"], w=[f"ps{4 + fc % 2}"])
                    op("act", lambda e, pg=pg, sb=sb: e.activation(sb[:, 0:N], pg[:, 0:N], AF.Silu), r=[f"ps{2 + fc % 2}"], w=[f"sil{fc % 2}"])
                    op("dve", lambda e, fc=fc, pu_=pu_, sb=sb: e.tensor_tensor(hidT[:, fc, 0:N], sb[:, 0:N], pu_[:, 0:N], ALU.mult), r=[f"sil{fc % 2}", f"ps{4 + fc % 2}"], w=["hidT"])
                ncol = 0
                for i_, (P, src, dst) in enumerate(tiles):
                    xb = x1t[i_]
                    yb = yt[i_ % 2]
                    grep_ = g2s if is_sample else g2rep
                    for half in range(2):
                        cs = slice(half * 512, (half + 1) * 512)
                        pb = 6 + half
                        for fc in range(NFC):
                            op("pe", lambda e, fc=fc, cs=cs, pb=pb, P=P, ncol=ncol: e.matmul(PS[pb][0:P, :], lhsT=hidT[:, fc, ncol:ncol + P], rhs=wd_[:, fc, cs],
                                                                                         start=(fc == 0), stop=(fc == NFC - 1)), r=["hidT", "wts3"], w=[f"ps{pb}"])
                        op("dve", lambda e, cs=cs, pb=pb, P=P, yb=yb, grep_=grep_: e.tensor_tensor(yb[0:P, cs], PS[pb][0:P, :], grep_[0:P, cs], ALU.mult),
                           r=[f"ps{pb}", "g2rep", "g2s"], w=[f"yt{i_ % 2}"])
                        op("dve", lambda e, cs=cs, P=P, yb=yb, xb=xb: e.tensor_tensor(yb[0:P, cs], yb[0:P, cs], xb[0:P, cs], ALU.add), r=[f"yt{i_ % 2}", f"x1t{i_}"], w=[f"yt{i_ % 2}"])
                    dma("sp", lambda e, yb=yb, P=P, dst=dst: e.dma_start(out=dst, in_=yb[0:P, :]), r=[f"yt{i_ % 2}"], w=["y_out"])
                    ncol += P

            for g in range((NTILES + 1) // 2):
                tl_ = []
                for tl in range(2):
                    ti = g * 2 + tl
                    if ti < NTILES:
                        tl_.append((128, y_p[ti * 128:(ti + 1) * 128, :], y_p[ti * 128:(ti + 1) * 128, :]))
                ffn_group(tl_, False)
            ffn_group([(NS, y_s, y_s)], True)

        if os.environ.get("MK_DBG_ATT", "0") == "1":
            dbg_att = nc.dram_tensor("dbg_att", [128, 4, NTILES * 128], BF16, kind="ExternalOutput").ap()
            dma("sp", lambda e: e.dma_start(out=dbg_att, in_=attT[:, :, 0:NTILES * 128]), r=["attT"])
        S.finish()
        with nc.Block() as block:
            S.emit(block)
        print("arena peak words", AR.peak, "of", AR.n)
    return nc


_NC_CACHE = {}


def kernel(x_prompt, x_sample, cache_k, cache_v, cache_idx_k, state_wkv, state_shift, page_table,
           c_prompt, c_sample, norm1_g, norm2_g, w_ada, b_ada, w_in, q_norm_g, k_norm_g,
           rw_mu, rw_w0, rw_w2, rw_a0, rw_a2, rw_g2, rw_k_k, rw_k_a, rw_r_k, rw_ln_w, rw_ln_b,
           w_out, w_ffn_gate, w_ffn_up, w_ffn_down):
    f = lambda a: np.ascontiguousarray(np.asarray(a, dtype=np.float32))
    if "nc" not in _NC_CACHE:
        _NC_CACHE["nc"] = build_program()
    nc = _NC_CACHE["nc"]
    if NOCACHE:
        ck = np.zeros((128, 512), np.float32)
        cv = ck
        cik = np.zeros((128, 64), np.float32)
    else:
        ck = f(cache_k).reshape(NPOOL * PAGE, 512)
        cv = f(cache_v).reshape(NPOOL * PAGE, 512)
        cik = f(cache_idx_k).reshape(NPOOL * PAGE, 64)
    shared = {
        "cache_k": ck, "cache_v": cv, "cache_ik": cik,
        "norm1_g": f(norm1_g).reshape(D), "norm2_g": f(norm2_g).reshape(D),
        "w_ada": f(w_ada).reshape(D, 6 * D), "b_ada": f(b_ada).reshape(6 * D),
        "w_in": f(w_in).reshape(D, IN_COLS), "q_norm_g": f(q_norm_g).reshape(64), "k_norm_g": f(k_norm_g).reshape(64),
        "rw_mu": f(rw_mu).reshape(RW_COLS), "rw_w0": f(rw_w0).reshape(512), "rw_w2": f(rw_w2).reshape(64, 512),
        "rw_a0": f(rw_a0).reshape(512), "rw_a2": f(rw_a2).reshape(64, 512), "rw_g2": f(rw_g2).reshape(128, 512),
        "rw_k_k": f(rw_k_k).reshape(512), "rw_k_a": f(rw_k_a).reshape(512), "rw_r_k": f(rw_r_k).reshape(512),
        "rw_ln_w": f(rw_ln_w).reshape(512), "rw_ln_b": f(rw_ln_b).reshape(512),
        "w_out": f(w_out).reshape(D, D), "w_gate": f(w_ffn_gate).reshape(D, FFN), "w_up": f(w_ffn_up).reshape(D, FFN),
        "w_down": f(w_ffn_down).reshape(FFN, D),
    }
    xp = f(x_prompt)
    xs = f(x_sample).reshape(32, D)
    cp = f(c_prompt)
    cs = f(c_sample)
    sw = f(state_wkv).reshape(32, 8, 64, 64)
    ss = f(state_shift).reshape(32, RW_COLS)
    pt = np.ascontiguousarray(np.asarray(page_table, dtype=np.int32))
    in_maps = []
    for i in range(8):
        m = dict(shared)
        m["x_p"] = xp[i]
        m["x_s"] = np.ascontiguousarray(xs[4 * i:4 * i + 4])
        m["c5"] = np.ascontiguousarray(np.concatenate([cp[i:i + 1], cs[4 * i:4 * i + 4]], axis=0))
        m["st_wkv"] = np.ascontiguousarray(sw[4 * i:4 * i + 4])
        m["st_shift"] = np.ascontiguousarray(ss[4 * i:4 * i + 4])
        m["ptab"] = np.ascontiguousarray(pt[4 * i:4 * i + 4])
        in_maps.append(m)
    res = run_bass_kernel_spmd(nc, in_maps, core_ids=list(range(8)))
    R = res.results
    _NC_CACHE["last"] = R
    g = lambda name: np.stack([np.asarray(R[i][name]) for i in range(8)], axis=0)
    y_p = g("y_p")
    y_s = g("y_s").reshape(32, 1, D)
    k_p = g("k_po").reshape(1, 8, T, 8, 64)
    v_p = g("v_po").reshape(1, 8, T, 8, 64)
    ik_p = g("ik_po").reshape(1, 8, T, 64)
    wkv_p = g("wkv_po").reshape(1, 8, 8, 64, 64)
    sh_p = g("sh_po").reshape(1, 8, RW_COLS)
    k_s = g("k_so").reshape(1, 32, 1, 8, 64)
    v_s = g("v_so").reshape(1, 32, 1, 8, 64)
    ik_s = g("ik_so").reshape(1, 32, 1, 64)
    wkv_s = g("wkv_so").reshape(1, 32, 8, 64, 64)
    sh_s = g("sh_so").reshape(1, 32, RW_COLS)
    return (y_p, y_s, k_p, v_p, ik_p, wkv_p, sh_p, k_s, v_s, ik_s, wkv_s, sh_s)
```

```python
import os
import numpy as np
from contextlib import ExitStack
import concourse.bass as bass
import concourse.mybir as mybir
from concourse.bass_utils import run_bass_kernel_spmd

F32 = mybir.dt.float32
BF16 = mybir.dt.bfloat16
I32 = mybir.dt.int32
U32 = mybir.dt.uint32
F32R = mybir.dt.float32r
AF = mybir.ActivationFunctionType
ALU = mybir.AluOpType
AX = mybir.AxisListType

D = 1024
T = 2048
NT = T // 128
NS = 4
NPAGE = 64
PAGE = 128
NPOOL = 2560
IN_COLS = 4432
ATT_COLS = 2640
RW_COLS = 1792
FFN = 2816
NFC = FFN // 128
IDX_W_SCALE = (16 ** -0.5) * (64 ** -0.5)
RMS_EPS = 1e-6
GN_EPS = 64e-5
BIG = 1.0e30

_DT_SIZE = {F32: 4, BF16: 2, I32: 4, U32: 4}


def _prod(s):
    r = 1
    for x in s:
        r *= x
    return r


class Arena:
    def __init__(self, ap):
        self.ap = ap
        self.off = 0
        self.n = ap.shape[1]
        self.peak = 0

    def mark(self):
        return self.off

    def release(self, m):
        self.off = m

    def alloc(self, shape, dtype=F32, parts=128):
        if isinstance(shape, int):
            shape = [shape]
        ne = _prod(shape)
        nw = (ne * _DT_SIZE[dtype] + 3) // 4
        nw = (nw + 1) // 2 * 2
        assert self.off + nw <= self.n, f"arena overflow {self.off}+{nw}>{self.n}"
        v = self.ap[0:parts, self.off:self.off + nw]
        self.off += nw
        self.peak = max(self.peak, self.off)
        if dtype != F32:
            v = v.bitcast(dtype)
        v = v[:, 0:ne]
        if len(shape) == 2:
            v = v.rearrange("p (a b) -> p a b", a=shape[0])
        elif len(shape) == 3:
            v = v.rearrange("p (a b c) -> p a b c", a=shape[0], b=shape[1])
        return v


class Sched:
    CH = 2000
    NSEM = {"pe": 22, "act": 14, "dve": 18, "pool": 10}
    NDMA = 10
    NPDMA = 20

    def __init__(self, nc, es):
        self.nc = nc
        self.E = {"pe": nc.tensor, "act": nc.scalar, "dve": nc.vector, "pool": nc.gpsimd, "sp": nc.sync}
        self.sem = {e: [es.enter_context(nc.semaphore(f"s_{e}{k}")) for k in range(n)] for e, n in self.NSEM.items()}
        self.dsem = [es.enter_context(nc.semaphore(f"s_dma{k}")) for k in range(self.NDMA)]
        self.psem = [es.enter_context(nc.semaphore(f"s_pdma{k}")) for k in range(self.NPDMA)]
        self.pmark = None
        self.pnext = 0
        self.duse = [0] * self.NDMA
        self.dnext = 0
        self.ops = {e: [] for e in self.E}
        self.cnt = {e: 0 for e in self.NSEM}
        self.seen = {e: {} for e in self.E}
        self.lastw = {}
        self.readers = {}
        self.all_dma = []

    def _deps(self, eng, r, w):
        deps = []
        for x in r:
            if x in self.lastw:
                deps.append(self.lastw[x])
        for x in w:
            if x in self.lastw:
                deps.append(self.lastw[x])
            deps.extend(self.readers.get(x, []))
        waits = []
        seen = self.seen[eng]
        for tok in deps:
            if tok[0] == "e":
                _, e2, idx = tok
                if e2 == eng and eng == "pe":
                    continue
                if seen.get(e2, 0) >= idx:
                    continue
                seen[e2] = idx
            else:
                _, slot, val = tok
                if seen.get(("d", slot), 0) >= val:
                    continue
                seen[("d", slot)] = val
        return deps

    def _waits_for(self, eng, deps):
        need_e = {}
        need_d = {}
        for tok in deps:
            if tok[0] == "e":
                _, e2, idx = tok
                if e2 == eng and eng == "pe":
                    continue
                need_e[e2] = max(need_e.get(e2, 0), idx)
            else:
                _, slot, val = tok
                need_d[slot] = max(need_d.get(slot, 0), val)
        out = []
        seen = self.seen[eng]
        for e2, idx in need_e.items():
            if seen.get(e2, 0) >= idx:
                continue
            seen[e2] = idx
            k, v = (idx - 1) // self.CH, (idx - 1) % self.CH + 1
            out.append((self.sem[e2][k], v))
        for slot, val in need_d.items():
            if seen.get(("d", slot), 0) >= val:
                continue
            seen[("d", slot)] = val
            out.append((self.dsem[slot], val))
        return out

    def _collect(self, r, w):
        deps = []
        for x in r:
            if x in self.lastw:
                deps.append(self.lastw[x])
        for x in w:
            if x in self.lastw:
                deps.append(self.lastw[x])
            deps.extend(self.readers.get(x, []))
        return deps

    def _commit(self, tok, r, w):
        for x in r:
            self.readers.setdefault(x, []).append(tok)
        for x in w:
            self.lastw[x] = tok
            self.readers[x] = []

    def op(self, eng, fn, r=(), w=()):
        deps = self._collect(r, w)
        waits = self._waits_for(eng, deps)
        self.cnt[eng] += 1
        idx = self.cnt[eng]
        k = (idx - 1) // self.CH
        assert k < len(self.sem[eng]), f"too many ops on {eng}"
        self.ops[eng].append((waits, fn, self.sem[eng][k], 1))
        tok = ("e", eng, idx)
        if eng != "pe":
            self.seen[eng][eng] = max(self.seen[eng].get(eng, 0), 0)
        self._commit(tok, r, w)
        return tok

    def dma(self, q, fn, r=(), w=()):
        deps = self._collect(r, w)
        slot = self.dnext
        self.dnext = (self.dnext + 1) % self.NDMA
        if self.duse[slot] > 0:
            deps.append(("d", slot, 16 * self.duse[slot]))
        waits = self._waits_for(q, deps)
        self.duse[slot] += 1
        val = 16 * self.duse[slot]
        self.ops[q].append((waits, fn, self.dsem[slot], 16))
        tok = ("d", slot, val)
        self._commit(tok, r, w)
        self.all_dma.append(tok)
        return tok

    def pool_dma_batch(self, fns, r=(), w=()):
        assert len(fns) <= self.NPDMA
        deps = self._collect(r, w)
        waits = self._waits_for("pool", deps)
        psem = self.psem
        n = len(fns)

        def g(e):
            for i, fn in enumerate(fns):
                fn(e).then_inc(psem[i], 16)
            for i in range(n):
                e.wait_ge(psem[i], 16)
        self.ops["pool"].append((waits, ("raw", g), None, 0))
        pm = self.pmark
        tok = self.op("pool", lambda e: e.memset(pm, 0.0), r=r, w=list(w) + ["pmark"])
        self.barrier()

        def clr(e):
            for i in range(n):
                e.sem_clear(psem[i])
        self.ops["pool"].append(([], ("raw", clr), None, 0))
        return tok

    def pool_dma_once(self, fns, r=(), w=()):
        deps = self._collect(r, w)
        waits = self._waits_for("pool", deps)
        sems = [self.psem[self.pnext + i] for i in range(len(fns))]
        self.pnext += len(fns)
        assert self.pnext <= self.NPDMA

        def g(e):
            for sm, fn in zip(sems, fns):
                fn(e).then_inc(sm, 16)
            for sm in sems:
                e.wait_ge(sm, 16)
        self.ops["pool"].append((waits, ("raw", g), None, 0))
        pm = self.pmark
        return self.op("pool", lambda e: e.memset(pm, 0.0), r=r, w=list(w) + ["pmark"])

    def barrier(self):
        toks = [("e", e, c) for e, c in self.cnt.items() if c > 0]
        toks += [("d", s, 16 * u) for s, u in enumerate(self.duse) if u > 0]
        for eng in self.E:
            waits = self._waits_for(eng, toks)
            if waits:
                self.ops[eng].append((waits, None, None, 0))

    def finish(self):
        toks = [("d", s, 16 * u) for s, u in enumerate(self.duse) if u > 0]
        toks += [("e", e, c) for e, c in self.cnt.items() if c > 0]
        waits = self._waits_for("sp", toks)
        self.ops["sp"].append((waits, None, None, 0))

    def emit(self, block):
        def mk(ename):
            def body(e):
                for waits, fn, sem, inc in self.ops[ename]:
                    for s, v in waits:
                        e.wait_ge(s, v)
                    if isinstance(fn, tuple):
                        fn[1](e)
                    elif fn is not None:
                        ins = fn(e)
                        ins.then_inc(sem, inc)
            return body
        block.tensor(mk("pe"))
        block.scalar(mk("act"))
        block.vector(mk("dve"))
        block.gpsimd(mk("pool"))
        block.sync(mk("sp"))


def bc_last(ap, n):
    return ap.unsqueeze(len(ap.shape)).to_broadcast(list(ap.shape) + [n])


def bc_mid(ap, n):
    return ap.unsqueeze(1).to_broadcast([ap.shape[0], n, ap.shape[1]])


STAGE = int(os.environ.get("MK_STAGE", "99"))
ATT_CUT = int(os.environ.get("MK_ATT_CUT", "99"))
P2_CUT = int(os.environ.get("MK_P2_CUT", "99"))
SAMPLE_ATT = os.environ.get("MK_SAMPLE_ATT", "1") == "1"
SCAN_CUT = int(os.environ.get("MK_SCAN_CUT", "99"))
NOCACHE = os.environ.get("MK_NOCACHE", "0") == "1"


def build_program():
    nc = bass.Bass("TRN2", target_bir_lowering=False)

    def din(name, shape, dt=F32):
        return nc.dram_tensor(name, list(shape), dt, kind="ExternalInput").ap()

    def dout(name, shape, dt=F32):
        return nc.dram_tensor(name, list(shape), dt, kind="ExternalOutput").ap()

    x_p = din("x_p", [T, D])
    x_s = din("x_s", [NS, D])
    c5 = din("c5", [5, D])
    st_wkv = din("st_wkv", [NS, 8, 64, 64])
    st_shift = din("st_shift", [NS, RW_COLS])
    ptab = din("ptab", [NS, NPAGE], I32)
    npool_rows = 128 if NOCACHE else NPOOL * PAGE
    cache_k = din("cache_k", [npool_rows, 512])
    cache_v = din("cache_v", [npool_rows, 512])
    cache_ik = din("cache_ik", [npool_rows, 64])
    norm1_g = din("norm1_g", [D])
    norm2_g = din("norm2_g", [D])
    w_ada = din("w_ada", [D, 6 * D])
    b_ada = din("b_ada", [6 * D])
    w_in = din("w_in", [D, IN_COLS])
    q_norm_g = din("q_norm_g", [64])
    k_norm_g = din("k_norm_g", [64])
    rw_mu = din("rw_mu", [RW_COLS])
    rw_w0 = din("rw_w0", [512])
    rw_w2 = din("rw_w2", [64, 512])
    rw_a0 = din("rw_a0", [512])
    rw_a2 = din("rw_a2", [64, 512])
    rw_g2 = din("rw_g2", [128, 512])
    rw_k_k = din("rw_k_k", [512])
    rw_k_a = din("rw_k_a", [512])
    rw_r_k = din("rw_r_k", [512])
    rw_ln_w = din("rw_ln_w", [512])
    rw_ln_b = din("rw_ln_b", [512])
    w_out = din("w_out", [D, D])
    w_gate = din("w_gate", [D, FFN])
    w_up = din("w_up", [D, FFN])
    w_down = din("w_down", [FFN, D])

    y_p = dout("y_p", [T, D])
    y_s = dout("y_s", [NS, D])
    k_po = dout("k_po", [T, 512])
    v_po = dout("v_po", [T, 512])
    ik_po = dout("ik_po", [T, 64])
    wkv_po = dout("wkv_po", [8, 64, 64])
    sh_po = dout("sh_po", [RW_COLS])
    k_so = dout("k_so", [NS, 512])
    v_so = dout("v_so", [NS, 512])
    ik_so = dout("ik_so", [NS, 64])
    wkv_so = dout("wkv_so", [NS, 8, 64, 64])
    sh_so = dout("sh_so", [NS, RW_COLS])

    es = ExitStack()
    with es:
        arena_t = es.enter_context(nc.sbuf_tensor("arena", [128, 47640], F32))
        AR = Arena(arena_t[:, :])
        ptrow_t = es.enter_context(nc.sbuf_tensor("ptrow", [1, NS * NPAGE], I32))
        PS = [es.enter_context(nc.psum_tensor(f"psb{i}", [128, 512], F32))[:, :] for i in range(8)]
        S = Sched(nc, es)
        op, dma = S.op, S.dma
        S.pmark = AR.alloc([2], F32)

        Xall_r = es.enter_context(nc.sbuf_tensor("xall_r", [128, 192], F32R))[:, :]
        Xk_sel_r = es.enter_context(nc.sbuf_tensor("xksel_r", [128, 1024], F32R))[:, :].rearrange("p (a b) -> p a b", a=16)
        Vbd16_r = es.enter_context(nc.sbuf_tensor("vbd16_r", [128, 512], F32R))[:, :].rearrange("p (a b) -> p a b", a=8)
        R128_r = [es.enter_context(nc.sbuf_tensor(f"r128_r{i}", [128, 512], F32R))[:, :].rearrange("p (a b) -> p a b", a=8) for i in range(2)]
        Xall = Xall_r.bitcast(F32)
        STr = es.enter_context(nc.sbuf_tensor("st_r", [64, 512], F32R))[:, :]
        LA_sel_r = es.enter_context(nc.sbuf_tensor("lasel_r", [64, 2048], F32R))[:, :].rearrange("p (a b) -> p a b", a=16)
        iot = AR.alloc([128], I32)
        ident = AR.alloc([128], F32)
        identb = AR.alloc([128], BF16)
        negmask = AR.alloc([128], F32)
        ones_f = AR.alloc([128], F32)
        op("pool", lambda e: e.iota(iot, [[1, 128]], base=0, channel_multiplier=-1), w=["iot"])
        op("dve", lambda e: e.tensor_single_scalar(ident, iot, 0, ALU.is_equal), r=["iot"], w=["ident"])
        op("dve", lambda e: e.tensor_single_scalar(identb, iot, 0, ALU.is_equal), r=["iot"], w=["identb"])
        op("dve", lambda e: e.tensor_scalar(negmask, iot, 0, -BIG, ALU.is_gt, ALU.mult), r=["iot"], w=["negmask"])
        op("pool", lambda e: e.memset(ones_f, 1.0), w=["ones_f"])

        vecA = AR.alloc([78], F32)
        vecB = AR.alloc([56], F32, parts=64)
        modT = AR.alloc([48, 5], F32)
        A1 = AR.alloc([8, 5], F32)
        A2 = AR.alloc([8, 5], F32)
        gq_rep = AR.alloc([64], F32)
        gk_rep = AR.alloc([64], F32)
        m0 = AR.mark()
        c5t = AR.alloc([D], F32, parts=5)
        sct = AR.alloc([D], F32, parts=5)
        scT = AR.alloc([8, 5], F32)
        stA = AR.alloc([128], F32, parts=78)
        stB = AR.alloc([64], F32, parts=56)
        wst = [AR.alloc([8, 512], F32) for _ in range(2)]

        dma("sp", lambda e: e.dma_start(out=c5t, in_=c5), w=["c5t"])
        dma("sp", lambda e: e.dma_start(out=stA[0:48, :], in_=b_ada.rearrange("(a b) -> a b", b=128)), w=["stA0"])
        dma("sp", lambda e: e.dma_start(out=stA[48:56, :], in_=norm1_g.rearrange("(a b) -> a b", b=128)), w=["stA1"])
        dma("sp", lambda e: e.dma_start(out=stA[56:64, :], in_=norm2_g.rearrange("(a b) -> a b", b=128)), w=["stA2"])
        dma("sp", lambda e: e.dma_start(out=stA[64:78, :], in_=rw_mu.rearrange("(a b) -> a b", b=128)), w=["stA3"])
        for i, v in enumerate([rw_w0, rw_a0, rw_k_k, rw_k_a, rw_ln_w, rw_ln_b, rw_r_k]):
            dma("sp", lambda e, v=v, i=i: e.dma_start(out=stB[8 * i:8 * i + 8, :], in_=v.rearrange("(a b) -> a b", b=64)), w=[f"stB{i}"])
        dma("sp", lambda e: e.dma_start(out=gq_rep, in_=q_norm_g.partition_broadcast(128)), w=["gq_rep"])
        dma("sp", lambda e: e.dma_start(out=gk_rep, in_=k_norm_g.partition_broadcast(128)), w=["gk_rep"])

        op("act", lambda e: e.activation(sct, c5t, AF.Silu), r=["c5t"], w=["sct"])
        for kc in range(8):
            op("pe", lambda e, kc=kc: e.transpose(PS[0][:, kc * 8:kc * 8 + 5], sct[:, kc * 128:(kc + 1) * 128], ident[0:5, 0:5]),
               r=["sct", "ident"], w=["ps0"])
        op("dve", lambda e: e.tensor_copy(scT, PS[0][:, 0:64].rearrange("p (a b) -> p a b", b=8)[:, :, 0:5]), r=["ps0"], w=["scT"])
        op("pe", lambda e: e.transpose(PS[1][:, 0:78], stA, ident[0:78, 0:78]), r=["stA0", "stA1", "stA2", "stA3", "ident"], w=["ps1"])
        op("dve", lambda e: e.tensor_copy(vecA, PS[1][:, 0:78]), r=["ps1"], w=["vecA"])
        op("pe", lambda e: e.transpose(PS[1][0:64, 128:184], stB, ident[0:56, 0:56]), r=[f"stB{i}" for i in range(7)] + ["ident"], w=["ps1"])
        op("dve", lambda e: e.tensor_copy(vecB, PS[1][0:64, 128:184]), r=["ps1"], w=["vecB"])
        w_ada_v = w_ada.rearrange("(kc p) n -> p kc n", p=128)
        for nb in range(12):
            b = nb % 2
            dma("sp", lambda e, nb=nb, b=b: e.dma_start(out=wst[b], in_=w_ada_v[:, :, nb * 512:(nb + 1) * 512]), w=[f"wst{b}"])
            for fc in range(4):
                col = (nb * 4 + fc) * 8
                for kc in range(8):
                    op("pe", lambda e, b=b, fc=fc, kc=kc, col=col: e.matmul(PS[2][:, col:col + 5], lhsT=wst[b][:, kc, fc * 128:(fc + 1) * 128],
                                                                            rhs=scT[:, kc, :], start=(kc == 0), stop=(kc == 7)),
                       r=[f"wst{b}", "scT"], w=["ps2"])
        op("dve", lambda e: e.tensor_tensor(modT, PS[2][:, 0:384].rearrange("p (a b) -> p a b", b=8)[:, :, 0:5], bc_last(vecA[:, 0:48], 5), ALU.add),
           r=["ps2", "vecA"], w=["modT"])
        op("dve", lambda e: e.tensor_scalar(A1, modT[:, 8:16, :], 1.0, None, ALU.add), r=["modT"], w=["A1"])
        op("dve", lambda e: e.tensor_tensor(A1, A1, bc_last(vecA[:, 48:56], 5), ALU.mult), r=["A1", "vecA"], w=["A1"])
        op("dve", lambda e: e.tensor_scalar(A2, modT[:, 32:40, :], 1.0, None, ALU.add), r=["modT"], w=["A2"])
        op("dve", lambda e: e.tensor_tensor(A2, A2, bc_last(vecA[:, 56:64], 5), ALU.mult), r=["A2", "vecA"], w=["A2"])
        S.barrier()
        AR.release(m0)

        if STAGE <= 0:
            dbg_mod = nc.dram_tensor("dbg_mod", [128, 240], F32, kind="ExternalOutput").ap()
            dma("sp", lambda e: e.dma_start(out=dbg_mod, in_=modT.rearrange("p a b -> p (a b)")), r=["modT"])
            S.finish()
            with nc.Block() as block:
                S.emit(block)
            return nc

        shift1 = modT[:, 0:8, :]
        gate1 = modT[:, 16:24, :]
        shift2 = modT[:, 24:32, :]
        gate2 = modT[:, 40:48, :]

        m_attT = AR.mark()
        attT = AR.alloc([4, T + NS], BF16)
        m_att = AR.mark()
        win_att = AR.alloc([8, ATT_COLS], BF16)
        KT = AR.alloc([4, T], BF16)
        Vaug = AR.alloc([NT, 8, 65], BF16)
        kiT = AR.alloc([T], BF16)
        wabs = AR.alloc([NT, 16], F32)
        wsgn = AR.alloc([NT, 16], F32)
        qiTs = AR.alloc([8, NS], BF16)
        qs32 = AR.alloc([512], F32, parts=NS)
        ks32 = AR.alloc([512], F32, parts=NS)
        vs32 = AR.alloc([512], F32, parts=NS)
        kis32 = AR.alloc([64], F32, parts=NS)
        ws_abs = AR.alloc([16], F32, parts=NS)
        ws_sgn = AR.alloc([16], F32, parts=NS)
        m1 = AR.mark()
        wst = [AR.alloc([8, 512], F32) for _ in range(2)]
        w_in_v = w_in.rearrange("(kc p) n -> p kc n", p=128)
        nblk = 0
        for (dst, c0, c1) in [(win_att, 0, ATT_COLS)]:
            c = c0
            while c < c1:
                n = min(512, c1 - c)
                b = nblk % 2
                dma("sp", lambda e, b=b, c=c, n=n: e.dma_start(out=wst[b][:, :, 0:n], in_=w_in_v[:, :, c:c + n]), w=[f"wst{b}"])
                op("pool", lambda e, b=b, c=c, n=n, dst=dst, c0=c0: e.tensor_copy(dst[:, :, c - c0:c - c0 + n], wst[b][:, :, 0:n]),
                   r=[f"wst{b}"], w=["win"])
                c += n
                nblk += 1
        op("pool", lambda e: e.memset(Vaug[:, :, :, 64:65], 1.0), w=["Vaug"])
        S.barrier()
        AR.release(m1)

        xt = [AR.alloc([D], F32)] * 2
        xn = AR.alloc([D], F32)
        hT = AR.alloc([8, 512], BF16)
        scr = [AR.alloc([512], F32) for _ in range(3)]
        small = AR.alloc([64], F32)
        qnb = AR.alloc([512], BF16)
        knb = AR.alloc([512], BF16)
        kidup = AR.alloc([128], BF16)
        qT = AR.alloc([4, 512], BF16)
        qiT = AR.alloc([8, 512], BF16)
        hTs = AR.alloc([8, NS], BF16)
        v32 = AR.alloc([512], F32)
        ki32 = AR.alloc([64], F32)
        PSb = [p.bitcast(BF16) for p in PS]
        Ibuf = AR.alloc([T], F32)
        Rbuf = [AR.alloc([512], F32) for _ in range(2)]
        maskb = AR.alloc([T], BF16)
        maskT = AR.alloc([NT, 128], BF16)
        PTb = [AR.alloc([512], BF16) for _ in range(2)]
        PmT = [AR.alloc([4, 128], BF16) for _ in range(2)]
        attb = AR.alloc([512], BF16)
        bs = AR.alloc([16], F32)
        NIT = int(os.environ.get("MK_NIT", "24"))
        NTILES = int(os.environ.get("MK_NT", str(NT)))

        def rms_rstd(ssum, n, eps, dst, P):
            op("dve", lambda e: e.tensor_scalar(dst, ssum, 1.0 / n, eps, ALU.mult, ALU.add), r=["small"], w=["small"])
            op("act", lambda e: e.activation(dst, dst, AF.Sqrt), r=["small"], w=["small"])
            op("dve", lambda e: e.reciprocal(dst, dst), r=["small"], w=["small"])

        def front_tile(ti, P, x_src, cond, hT_dst, k_dst, v_dst, ik_dst, tcol):
            b = 0
            xtb = xt[b]
            dma("sp", lambda e: e.dma_start(out=xtb[0:P, :], in_=x_src), w=[f"xt{b}"])
            ss = small[0:P, 0:1]
            rs = small[0:P, 1:2]
            op("act", lambda e: e.activation(xn[0:P, :], xtb[0:P, :], AF.Square, accum_out=ss), r=[f"xt{b}"], w=["xn", "small"])
            rms_rstd(ss, D, RMS_EPS, rs, P)
            op("dve", lambda e: e.tensor_scalar(xn[0:P, :], xtb[0:P, :], rs, None, ALU.mult), r=[f"xt{b}", "small"], w=["xn"])
            for half in range(2):
                pb = PS[half]
                for j in range(4):
                    kc = half * 4 + j
                    op("pe", lambda e, kc=kc, j=j, pb=pb: e.transpose(pb[:, j * 128:j * 128 + P], xn[0:P, kc * 128:(kc + 1) * 128], ident[0:P, 0:P]),
                       r=["xn", "ident"], w=[f"ps{half}"])
                if cond is None:
                    for j in range(4):
                        kc = half * 4 + j
                        op("act", lambda e, kc=kc, j=j, pb=pb: e.activation(hT_dst[:, kc, tcol:tcol + P], pb[:, j * 128:j * 128 + P], AF.Identity,
                                                                          bias=shift1[:, kc, 0:1], scale=A1[:, kc, 0:1]),
                           r=[f"ps{half}", "A1", "modT"], w=["hT"])
                else:
                    for j in range(4):
                        kc = half * 4 + j
                        op("dve", lambda e, kc=kc, j=j, pb=pb: e.tensor_tensor(scr[0][:, 0:P], pb[:, j * 128:j * 128 + P], A1[:, kc, 1:1 + P], ALU.mult),
                           r=[f"ps{half}", "A1"], w=["scr0"])
                        op("dve", lambda e, kc=kc: e.tensor_tensor(hT_dst[:, kc, tcol:tcol + P], scr[0][:, 0:P], shift1[:, kc, 1:1 + P], ALU.add),
                           r=["scr0", "modT"], w=["hT"])
            blocks = [(2, 0), (3, 512), (4, 1024), (5, 2560)]
            for (pbi, c0) in blocks:
                n = 512 if c0 < 2560 else 80
                for kc in range(8):
                    op("pe", lambda e, pbi=pbi, c0=c0, n=n, kc=kc: e.matmul(PS[pbi][0:P, 0:n], lhsT=hT_dst[:, kc, tcol:tcol + P],
                                                                          rhs=win_att[:, kc, c0:c0 + n], start=(kc == 0), stop=(kc == 7)),
                       r=["hT", "win"], w=[f"ps{pbi}"])
            for (pbi, grep, dstb, is_k) in [(2, gq_rep, qnb, False), (3, gk_rep, knb, True)]:
                pq = PS[pbi][0:P, :]
                ssq = small[0:P, 8:16]
                rq = small[0:P, 16:24]
                op("act", lambda e, pq=pq: e.activation(scr[0][0:P, :], pq, AF.Square), r=[f"ps{pbi}"], w=["scr0"])
                op("dve", lambda e, ssq=ssq: e.tensor_reduce(ssq, scr[0][0:P, :].rearrange("p (h d) -> p h d", d=64), AX.X, ALU.add), r=["scr0"], w=["small"])
                rms_rstd(ssq, 64, RMS_EPS, rq, P)
                op("dve", lambda e, pq=pq, rq=rq: e.tensor_tensor(scr[1][0:P, :].rearrange("p (h d) -> p h d", d=64), pq.rearrange("p (h d) -> p h d", d=64),
                                                               bc_last(rq, 64), ALU.mult), r=[f"ps{pbi}", "small"], w=["scr1"])
                if is_k:
                    op("pool", lambda e, grep=grep: e.tensor_tensor(scr[2][0:P, :].rearrange("p (h d) -> p h d", d=64), scr[1][0:P, :].rearrange("p (h d) -> p h d", d=64),
                                                                  bc_mid(grep[0:P, :], 8), ALU.mult), r=["scr1", "gk_rep"], w=["scr2"])
                    dma("sp", lambda e: e.dma_start(out=k_dst, in_=scr[2][0:P, :]), r=["scr2"])
                    op("pool", lambda e, dstb=dstb: e.tensor_copy(dstb[0:P, :], scr[2][0:P, :]), r=["scr2"], w=["knb"])
                else:
                    op("pool", lambda e, grep=grep, dstb=dstb: e.tensor_tensor(dstb[0:P, :].rearrange("p (h d) -> p h d", d=64), scr[1][0:P, :].rearrange("p (h d) -> p h d", d=64),
                                                                             bc_mid(grep[0:P, :], 8), ALU.mult), r=["scr1", "gq_rep"], w=["qnb"])
                    if cond is not None:
                        op("pool", lambda e: e.tensor_tensor(qs32.rearrange("p (h d) -> p h d", d=64), scr[1][0:P, :].rearrange("p (h d) -> p h d", d=64),
                                                             bc_mid(gq_rep[0:P, :], 8), ALU.mult), r=["scr1", "gq_rep"], w=["qs32"])
            is_s = cond is not None
            op("act", lambda e: e.activation(v32[0:P, :], PS[4][0:P, :], AF.Copy), r=["ps4"], w=["v32"])
            dma("sp", lambda e: e.dma_start(out=v_dst, in_=v32[0:P, :]), r=["v32"])
            if not is_s:
                op("pool", lambda e: e.tensor_copy(Vaug[:, ti, :, 0:64], v32.rearrange("p (h d) -> p h d", d=64)), r=["v32"], w=["Vaug"])
            else:
                op("pool", lambda e: e.tensor_copy(vs32, v32[0:P, :]), r=["v32"], w=["vs32"])
                op("pool", lambda e: e.tensor_copy(ks32, scr[2][0:P, :]), r=["scr2"], w=["ks32"])
            sk = small[0:P, 2:3]
            rk = small[0:P, 3:4]
            op("act", lambda e: e.activation(scr[0][0:P, 0:64], PS[5][0:P, 0:64], AF.Square, accum_out=sk), r=["ps5"], w=["scr0", "small"])
            rms_rstd(sk, 64, RMS_EPS, rk, P)
            op("dve", lambda e: e.tensor_scalar(ki32[0:P, :], PS[5][0:P, 0:64], rk, None, ALU.mult), r=["ps5", "small"], w=["ki32"])
            dma("sp", lambda e: e.dma_start(out=ik_dst, in_=ki32[0:P, :]), r=["ki32"])
            wa = ws_abs if is_s else wabs[:, ti, :]
            wsg = ws_sgn if is_s else wsgn[:, ti, :]
            op("act", lambda e: e.activation(wa, PS[5][0:P, 64:80], AF.Abs, scale=IDX_W_SCALE), r=["ps5"], w=["wabs"])
            op("act", lambda e: e.activation(wsg, PS[5][0:P, 64:80], AF.Sign), r=["ps5"], w=["wsgn"])
            if is_s:
                op("pool", lambda e: e.tensor_copy(kis32, ki32[0:P, :]), r=["ki32"], w=["kis32"])
                return
            tl = ti % 4
            op("pool", lambda e: e.tensor_copy(kidup[:, 0:64], ki32), r=["ki32"], w=["kidup"])
            op("pool", lambda e: e.tensor_copy(kidup[:, 64:128], ki32), r=["ki32"], w=["kidup"])
            for hp in range(4):
                op("pe", lambda e, hp=hp: e.transpose(PSb[6][:, hp * 128:(hp + 1) * 128], qnb[:, hp * 128:(hp + 1) * 128], identb), r=["qnb", "identb"], w=["ps6"])
            op("pe", lambda e: e.transpose(PSb[6][:, 512:640], kidup, identb), r=["kidup", "identb"], w=["ps6"])
            op("act", lambda e: e.activation(qT[:, :, tl * 128:(tl + 1) * 128], PSb[6][:, 0:512].rearrange("p (a b) -> p a b", a=4), AF.Copy), r=["ps6"], w=["qT"])
            op("act", lambda e: e.activation(kiT[:, ti * 128:(ti + 1) * 128], PSb[6][:, 512:640], AF.Copy), r=["ps6"], w=["kiT"])
            for hp in range(4):
                op("pe", lambda e, hp=hp: e.transpose(PSb[7][:, hp * 128:(hp + 1) * 128], knb[:, hp * 128:(hp + 1) * 128], identb), r=["knb", "identb"], w=["ps7"])
            op("act", lambda e: e.activation(KT[:, :, ti * 128:(ti + 1) * 128], PSb[7][:, 0:512].rearrange("p (a b) -> p a b", a=4), AF.Copy), r=["ps7"], w=["KT"])

        def qi_group(hT_src, ncols, dst):
            for c in range(8):
                pb = PS[c % 2]
                for kc in range(8):
                    op("pe", lambda e, c=c, kc=kc, pb=pb: e.matmul(pb[:, 0:ncols], lhsT=win_att[:, kc, 1536 + c * 128:1536 + (c + 1) * 128],
                                                                  rhs=hT_src[:, kc, 0:ncols], start=(kc == 0), stop=(kc == 7)), r=["hT", "win"], w=[f"ps{c % 2}"])
                op("act", lambda e, c=c, pb=pb: e.activation(dst[:, c, 0:ncols], pb[:, 0:ncols], AF.Copy), r=[f"ps{c % 2}"], w=["qiT"])


        def attention_tile(ti):
            tl = ti % 4
            L = (ti + 1) * 128
            nsp = (L + 511) // 512
            tq = slice(tl * 128, (tl + 1) * 128)
            cnt = [0, 0]
            for h in range(16):
                hp, half = h // 2, h % 2
                pr_ = slice(half * 64, (half + 1) * 64)
                for sp in range(nsp):
                    s0 = sp * 512
                    n = min(512, L - s0)
                    bk = 2 * half + cnt[half] % 2
                    cnt[half] += 1
                    rb = (h * nsp + sp) % 2
                    op("pe", lambda e, bk=bk, n=n, pr_=pr_, hp=hp, s0=s0: e.matmul(PS[bk][:, 0:n], lhsT=qiT[pr_, hp, tq], rhs=kiT[pr_, s0:s0 + n], start=True, stop=True),
                       r=["qiT", "kiT"], w=[f"ps{bk}"])
                    op("act", lambda e, bk=bk, n=n, rb=rb, h=h: e.activation(Rbuf[rb][:, 0:n], PS[bk][:, 0:n], AF.Relu, scale=wabs[:, ti, h:h + 1]),
                       r=[f"ps{bk}", "wabs"], w=[f"R{rb}"])
                    if h == 0:
                        op("dve", lambda e, n=n, rb=rb, s0=s0: e.tensor_scalar(Ibuf[:, s0:s0 + n], Rbuf[rb][:, 0:n], wsgn[:, ti, 0:1], None, ALU.mult),
                           r=[f"R{rb}", "wsgn"], w=["I"])
                    else:
                        op("dve", lambda e, n=n, rb=rb, s0=s0, h=h: e.scalar_tensor_tensor(Ibuf[:, s0:s0 + n], Rbuf[rb][:, 0:n], wsgn[:, ti, h:h + 1], Ibuf[:, s0:s0 + n], ALU.mult, ALU.add),
                           r=[f"R{rb}", "wsgn", "I"], w=["I"])
            if ATT_CUT <= 1:
                return
            lo, hi, step, mid, cntv, tmp, tau = (bs[:, i:i + 1] for i in range(7))
            if ti >= 2:
                op("dve", lambda e: e.tensor_reduce(hi, Ibuf[:, 0:L], AX.X, ALU.max), r=["I"], w=["bs"])
                op("dve", lambda e: e.tensor_reduce(lo, Ibuf[:, 0:L], AX.X, ALU.min), r=["I"], w=["bs"])
            op("dve", lambda e: e.tensor_tensor(Ibuf[:, ti * 128:(ti + 1) * 128], Ibuf[:, ti * 128:(ti + 1) * 128], negmask, ALU.add), r=["I", "negmask"], w=["I"])
            if ti >= 2:
                op("dve", lambda e: e.tensor_tensor(step, hi, lo, ALU.subtract), r=["bs"], w=["bs"])
                for it in range(NIT):
                    op("dve", lambda e: e.tensor_scalar(step, step, 0.5, None, ALU.mult), r=["bs"], w=["bs"])
                    op("dve", lambda e: e.tensor_tensor(mid, lo, step, ALU.add), r=["bs"], w=["bs"])
                    op("dve", lambda e: e.tensor_scalar(maskb[:, 0:L], Ibuf[:, 0:L], mid, 0.0, ALU.is_ge, ALU.add, accum_out=cntv), r=["I", "bs"], w=["maskb", "bs"])
                    op("dve", lambda e: e.scalar_tensor_tensor(tmp, cntv, 255.5, step, ALU.is_ge, ALU.mult), r=["bs"], w=["bs"])
                    op("dve", lambda e: e.tensor_tensor(lo, lo, tmp, ALU.add), r=["bs"], w=["bs"])
                thr = lo
            else:
                op("dve", lambda e: e.memset(tau, -1.0e29), w=["bs"])
                thr = tau
            op("dve", lambda e: e.tensor_scalar(maskb[:, 0:L], Ibuf[:, 0:L], thr, None, ALU.is_ge), r=["I", "bs"], w=["maskb"])
            if ATT_CUT <= 2:
                return
            for j0 in range(0, ti + 1, 8):
                nj = min(8, ti + 1 - j0)
                for j in range(nj):
                    sj = j0 + j
                    op("pe", lambda e, j=j, sj=sj: e.transpose(PSb[6][:, j * 128:(j + 1) * 128], maskb[:, sj * 128:(sj + 1) * 128], identb), r=["maskb", "identb"], w=["ps6"])
                op("act", lambda e, j0=j0, nj=nj: e.activation(maskT[:, j0:j0 + nj, :], PSb[6][:, 0:nj * 128].rearrange("p (a b) -> p a b", b=128), AF.Copy), r=["ps6"], w=["maskT"])
            if ATT_CUT <= 3:
                return
            cnt = [0, 0]
            pcount = 0
            for h in range(8):
                hp, half = h // 2, h % 2
                pr_ = slice(half * 64, (half + 1) * 64)
                pvb = 4 + h // 4
                pvc = (h % 4) * 65
                for sp in range(nsp):
                    j0 = sp * 4
                    nj = min(4, ti + 1 - j0)
                    bk = 2 * half + cnt[half] % 2
                    cnt[half] += 1
                    pb = pcount % 2
                    pcount += 1
                    for j in range(nj):
                        sj = j0 + j
                        op("pe", lambda e, bk=bk, j=j, sj=sj, pr_=pr_, hp=hp: e.matmul(PS[bk][:, j * 128:(j + 1) * 128], lhsT=KT[pr_, hp, sj * 128:(sj + 1) * 128],
                                                                                      rhs=qT[pr_, hp, tq], start=True, stop=True), r=["KT", "qT"], w=[f"ps{bk}"])
                    op("act", lambda e, bk=bk, nj=nj, pb=pb: e.activation(PTb[pb][:, 0:nj * 128], PS[bk][:, 0:nj * 128], AF.Exp, scale=0.125), r=[f"ps{bk}"], w=[f"PT{pb}"])
                    op("pool", lambda e, nj=nj, pb=pb, j0=j0: e.tensor_tensor(PmT[pb][:, 0:nj, :], PTb[pb][:, 0:nj * 128].rearrange("p (a b) -> p a b", b=128),
                                                                             maskT[:, j0:j0 + nj, :], ALU.mult), r=[f"PT{pb}", "maskT"], w=[f"PmT{pb}"])
                    for j in range(nj):
                        sj = j0 + j
                        op("pe", lambda e, pvb=pvb, pvc=pvc, pb=pb, j=j, sj=sj, h=h: e.matmul(PS[pvb][:, pvc:pvc + 65], lhsT=PmT[pb][:, j, :], rhs=Vaug[:, sj, h, :],
                                                                                           start=(sj == 0), stop=(sj == ti)), r=[f"PmT{pb}", "Vaug"], w=[f"ps{pvb}"])
            if ATT_CUT <= 4:
                return
            rden = bs[:, 8:16]
            for hb in range(2):
                pv = PS[4 + hb][:, 0:260].rearrange("p (h d) -> p h d", d=65)
                op("dve", lambda e, pv=pv, hb=hb: e.reciprocal(rden[:, hb * 4:(hb + 1) * 4], pv[:, :, 64]), r=[f"ps{4 + hb}"], w=["bs"])
                op("dve", lambda e, pv=pv, hb=hb: e.tensor_tensor(attb[:, hb * 256:(hb + 1) * 256].rearrange("p (h d) -> p h d", d=64), pv[:, :, 0:64],
                                                                 bc_last(rden[:, hb * 4:(hb + 1) * 4], 64), ALU.mult), r=[f"ps{4 + hb}", "bs"], w=["attb"])
            for hp in range(4):
                op("pe", lambda e, hp=hp: e.transpose(PSb[7][:, hp * 128:(hp + 1) * 128], attb[:, hp * 128:(hp + 1) * 128], identb), r=["attb", "identb"], w=["ps7"])
            op("act", lambda e: e.activation(attT[:, :, ti * 128:(ti + 1) * 128], PSb[7][:, 0:512].rearrange("p (a b) -> p a b", a=4), AF.Copy), r=["ps7"], w=["attT"])

        for g in range((NTILES + 3) // 4):
            for tl in range(4):
                ti = g * 4 + tl
                if ti >= NTILES:
                    break
                r0 = ti * 128
                front_tile(ti, 128, x_p[r0:r0 + 128, :], None, hT, k_po[r0:r0 + 128, :], v_po[r0:r0 + 128, :], ik_po[r0:r0 + 128, :], tl * 128)
            qi_group(hT, 512, qiT)
            if STAGE >= 2:
                for tl in range(4):
                    ti = g * 4 + tl
                    if ti < NTILES:
                        attention_tile(ti)
        front_tile(16, NS, x_s, 1, hTs, k_so, v_so, ik_so, 0)
        qi_group(hTs, NS, qiTs)
        if not SAMPLE_ATT:
            op("pool", lambda e: e.memset(attT[:, :, T:T + NS], 0.0), w=["attT"])
        else:
            S.barrier()
            AR.release(m1)
            NP1 = NPAGE + 1
            ptb = AR.alloc([NPAGE], I32)
            physf = AR.alloc([NS, NPAGE], F32)
            idx32 = AR.alloc([NS, NPAGE], I32)
            kib = AR.alloc([NP1, 64], F32)
            graw = AR.alloc([8320], F32)
            kdup = graw[:, 0:4160].bitcast(BF16).rearrange("p (a b) -> p a b", b=128)
            kiTa = graw[:, 4160:8320].bitcast(BF16).rearrange("p (a b) -> p a b", b=128)
            Gpg = graw[0:NPAGE, 0:8192]
            idxp = AR.alloc([NS], I32)
            ikscr = nc.dram_tensor("ikscr", [NS, NPAGE, PAGE * 64], F32, kind="Internal").ap()
            cache_ik_pg = cache_ik.rearrange("(n p) d -> n (p d)", p=128)
            knew = AR.alloc([NS, 64], F32)
            selB = AR.alloc([NS, 128], F32, parts=NS)
            wsig = AR.alloc([16], F32, parts=NS)
            wbc = AR.alloc([16], F32)
            rl = AR.alloc([NP1, 16], F32)
            score4 = AR.alloc([NS, NP1], F32)
            mask4 = AR.alloc([NS, NP1], F32)
            maskp = AR.alloc([NS, NPAGE], F32)
            negcols = AR.alloc([NS], F32)
            sb = AR.alloc([64], F32)
            iop_i = AR.alloc([2], I32)
            iop = AR.alloc([2], F32)
            islot_i = AR.alloc([256], I32)
            islot = AR.alloc([256], F32)
            ustrict = AR.alloc([128], F32)
            rank = AR.alloc([NS, NPAGE], F32)
            offs = AR.alloc([NS, NPAGE], F32)
            OH = [AR.alloc([256], F32) for _ in range(2)]
            selidx_f = AR.alloc([8], F32)
            selidx = AR.alloc([8], I32)
            Ksel = AR.alloc([2, 512], F32)
            Vsel = AR.alloc([2, 512], F32)
            scb = AR.alloc([2, 8], F32)
            Pb = AR.alloc([2, 8], F32)
            validb = AR.alloc([2], F32)
            scn = AR.alloc([8], F32, parts=NS)
            Pn = AR.alloc([8], F32, parts=NS)
            Pnn = AR.alloc([8], F32, parts=NS)
            mnew = AR.alloc([4], F32, parts=NS)
            prodn = AR.alloc([512], F32, parts=NS)
            wvn = prodn
            rdenb = AR.alloc([8], F32)
            cache_pages = cache_ik.rearrange("(n p) d -> n p d", p=128)
            ptrow = ptrow_t[:, :]
            dma("sp", lambda e: e.dma_start(out=ptrow, in_=ptab.rearrange("b c -> (b c)").unsqueeze(0)), w=["ptrow"])

            op("pool", lambda e: e.iota(iop_i, [[128, 2]], base=0, channel_multiplier=1), w=["iop_i"])
            op("dve", lambda e: e.tensor_copy(iop, iop_i), r=["iop_i"], w=["iop"])
            op("pool", lambda e: e.iota(islot_i, [[1, 256]], base=0, channel_multiplier=0), w=["islot_i"])
            op("dve", lambda e: e.tensor_copy(islot, islot_i), r=["islot_i"], w=["islot"])
            op("dve", lambda e: e.tensor_single_scalar(ustrict, iot, 0, ALU.is_gt), r=["iot"], w=["ustrict"])
            op("dve", lambda e: e.tensor_copy(selB, bc_last(ident[0:NS, 0:NS], 128)), r=["ident"], w=["selB"])
            op("dve", lambda e: e.tensor_scalar(negcols, ident[:, 0:NS], -1.0, BIG, ALU.add, ALU.mult), r=["ident"], w=["negcols"])
            op("dve", lambda e: e.tensor_tensor(wsig, ws_abs, ws_sgn, ALU.mult), r=["wabs", "wsgn"], w=["wsig"])
            op("pool", lambda e: e.memset(knew, 0.0), w=["knew"])
            op("dve", lambda e: e.tensor_tensor(knew[0:NS, :, :], bc_mid(kis32, NS), bc_last(ident[0:NS, 0:NS], 64), ALU.mult), r=["kis32", "ident", "knew"], w=["knew"])
            op("dve", lambda e: e.tensor_tensor(prodn, qs32, ks32, ALU.mult), r=["qs32", "ks32"], w=["prodn"])
            op("dve", lambda e: e.tensor_reduce(scn, prodn.rearrange("p (h d) -> p h d", d=64), AX.X, ALU.add), r=["prodn"], w=["scn"])
            op("act", lambda e: e.activation(Pn, scn, AF.Exp, scale=0.125), r=["scn"], w=["Pn"])

            for b_ in range(NS):
                dma("sp", lambda e, b_=b_: e.dma_start(out=ptb, in_=ptab[b_].partition_broadcast(128)), w=["ptb"])
                op("dve", lambda e, b_=b_: e.tensor_scalar(physf[:, b_, :], ptb, 128.0, iop[:, 0:1], ALU.mult, ALU.add), r=["ptb", "iop"], w=["physf"])
                op("dve", lambda e, b_=b_: e.tensor_copy(idx32[:, b_, :], physf[:, b_, :]), r=["physf"], w=["idx32"])
                dma("sp", lambda e, b_=b_: e.dma_start(out=idxp[0:NPAGE, b_:b_ + 1], in_=ptab[b_].rearrange("(c o) -> c o", o=1)), w=["idxp"])
                S.pool_dma_once([lambda e, b_=b_: e.indirect_dma_start(out=Gpg, out_offset=None, in_=cache_ik_pg,
                                                                       in_offset=bass.IndirectOffsetOnAxis(ap=idxp[0:NPAGE, b_:b_ + 1], axis=0))],
                                r=["idxp"], w=["kdup", "kiTa"])
                dma("sp", lambda e, b_=b_: e.dma_start(out=ikscr[b_], in_=Gpg), r=["kdup", "kiTa"], w=["ikscr"])
                dma("sp", lambda e, b_=b_: e.dma_start(out=kib[:, 0:NPAGE, :], in_=ikscr[b_].rearrange("c (s d) -> s c d", d=64)), r=["ikscr"], w=["kib"])
                op("pool", lambda e, b_=b_: e.tensor_copy(kib[:, NPAGE, :], knew[:, b_, :]), r=["knew"], w=["kib"])
                op("pool", lambda e: e.tensor_copy(kdup[:, :, 0:64], kib), r=["kib"], w=["kdup"])
                op("act", lambda e: e.activation(kdup[:, :, 64:128], kib, AF.Copy), r=["kib"], w=["kdup"])
                for c0 in range(0, NP1, 8):
                    ncc = min(8, NP1 - c0)
                    for j in range(ncc):
                        op("pe", lambda e, c0=c0, j=j: e.transpose(PSb[6][:, j * 128:(j + 1) * 128], kdup[:, c0 + j, :], identb), r=["kdup", "identb"], w=["ps6"])
                    op("act", lambda e, c0=c0, ncc=ncc: e.activation(kiTa[:, c0:c0 + ncc, :], PSb[6][:, 0:ncc * 128].rearrange("p (a b) -> p a b", b=128), AF.Copy),
                       r=["ps6"], w=["kiTa"])
                for c in range(NP1):
                    for half in range(2):
                        pr_ = slice(half * 64, (half + 1) * 64)
                        bk = 4 * half + (0 if c < NPAGE else 1)
                        col = (c % NPAGE) * 8
                        op("pe", lambda e, bk=bk, col=col, pr_=pr_, c=c, b_=b_: e.matmul(PS[bk][:, col:col + 8], lhsT=kiTa[pr_, c, :],
                                                                                       rhs=qiTs[pr_, :, b_], start=True, stop=True),
                           r=["kiTa", "qiT"], w=[f"ps{bk}"])
                op("pe", lambda e, b_=b_: e.matmul(PS[3][:, 0:16], lhsT=selB[:, b_, :], rhs=wsig, start=True, stop=True), r=["selB", "wsig"], w=["ps3"])
                op("dve", lambda e: e.tensor_copy(wbc.rearrange("p (a b) -> p a b", a=2), PS[3][:, 0:16].rearrange("p (b a) -> p a b", a=2)), r=["ps3"], w=["wbc"])
                for half in range(2):
                    op("dve", lambda e, half=half: e.tensor_scalar(rl[:, 0:NPAGE, half * 8:half * 8 + 8], PS[4 * half].rearrange("p (c h) -> p c h", h=8), 0.0, None, ALU.max),
                       r=[f"ps{4 * half}"], w=["rl"])
                    op("dve", lambda e, half=half: e.tensor_scalar(rl[:, NPAGE, half * 8:half * 8 + 8], PS[4 * half + 1][:, 0:8], 0.0, None, ALU.max),
                       r=[f"ps{4 * half + 1}"], w=["rl"])
                op("dve", lambda e: e.tensor_tensor(rl, rl, bc_mid(wbc, NP1), ALU.mult), r=["rl", "wbc"], w=["rl"])
                op("dve", lambda e, b_=b_: e.tensor_reduce(score4[:, b_, :], rl, AX.X, ALU.add), r=["rl"], w=["score4"])
            bnd = sb[:, 0:4]
            lo4, st4, mid4, cnt4, tmp4 = (sb[:, 4 + 4 * i:8 + 4 * i] for i in range(5))
            op("dve", lambda e: e.tensor_reduce(bnd, score4, AX.X, ALU.max, apply_absolute_value=True), r=["score4"], w=["sb"])
            op("pe", lambda e: e.matmul(PS[3][:, 16:20], lhsT=ones_f, rhs=bnd, start=True, stop=True), r=["ones_f", "sb"], w=["ps3"])
            op("dve", lambda e: e.tensor_scalar(lo4, PS[3][:, 16:20], -1.0, None, ALU.mult), r=["ps3"], w=["sb"])
            op("dve", lambda e: e.tensor_scalar(st4, PS[3][:, 16:20], 2.0, None, ALU.mult), r=["ps3"], w=["sb"])
            op("dve", lambda e: e.tensor_tensor(score4[:, :, NPAGE], score4[:, :, NPAGE], negcols, ALU.add), r=["score4", "negcols"], w=["score4"])
            for it in range(NIT + 8):
                op("dve", lambda e: e.tensor_scalar(st4, st4, 0.5, None, ALU.mult), r=["sb"], w=["sb"])
                op("dve", lambda e: e.tensor_tensor(mid4, lo4, st4, ALU.add), r=["sb"], w=["sb"])
                op("dve", lambda e: e.tensor_tensor(mask4, score4, bc_last(mid4, NP1), ALU.is_ge), r=["score4", "sb"], w=["mask4"])
                op("dve", lambda e: e.tensor_reduce(cnt4, mask4, AX.X, ALU.add), r=["mask4"], w=["sb"])
                op("pe", lambda e: e.matmul(PS[3][:, 16:20], lhsT=ones_f, rhs=cnt4, start=True, stop=True), r=["ones_f", "sb"], w=["ps3"])
                op("dve", lambda e: e.scalar_tensor_tensor(tmp4, PS[3][:, 16:20], 255.5, st4, ALU.is_ge, ALU.mult), r=["ps3", "sb"], w=["sb"])
                op("dve", lambda e: e.tensor_tensor(lo4, lo4, tmp4, ALU.add), r=["sb"], w=["sb"])
            op("dve", lambda e: e.tensor_tensor(mask4, score4, bc_last(lo4, NP1), ALU.is_ge), r=["score4", "sb"], w=["mask4"])
            op("dve", lambda e: e.tensor_copy(maskp, mask4[:, :, 0:NPAGE]), r=["mask4"], w=["maskp"])
            op("dve", lambda e: e.tensor_tensor(mnew, mask4[0:NS, :, NPAGE], ident[0:NS, 0:NS], ALU.mult), r=["mask4", "ident"], w=["mnew"])
            op("dve", lambda e: e.tensor_reduce(mnew[:, 0:1], mnew, AX.X, ALU.add), r=["mnew"], w=["mnew"])
            op("dve", lambda e: e.tensor_scalar(Pn, Pn, mnew[:, 0:1], None, ALU.mult), r=["Pn", "mnew"], w=["Pn"])
            maskp2 = maskp.rearrange("p a b -> p (a b)")
            op("pe", lambda e: e.matmul(PS[0][:, 0:256], lhsT=ustrict, rhs=maskp2, start=True, stop=True), r=["ustrict", "maskp"], w=["ps0"])
            op("pe", lambda e: e.matmul(PS[1][:, 0:256], lhsT=ones_f, rhs=maskp2, start=True, stop=True), r=["ones_f", "maskp"], w=["ps1"])
            op("dve", lambda e: e.tensor_copy(rank.rearrange("p a b -> p (a b)"), PS[1][:, 0:256]), r=["ps1"], w=["rank"])
            for b_ in range(NS):
                op("dve", lambda e, b_=b_: e.tensor_tensor_scan(offs[:, b_, :], ones_f[:, 0:NPAGE], rank[:, b_, :], 0.0, ALU.mult, ALU.add), r=["rank", "ones_f"], w=["offs"])
            op("dve", lambda e: e.tensor_tensor(rank, offs, rank, ALU.subtract), r=["offs", "rank"], w=["rank"])
            op("dve", lambda e: e.tensor_tensor(rank.rearrange("p a b -> p (a b)"), rank.rearrange("p a b -> p (a b)"), PS[0][:, 0:256], ALU.add), r=["rank", "ps0"], w=["rank"])
            ohc = 0
            for b_ in range(NS):
                for c in range(NPAGE):
                    ob = ohc % 2
                    ohc += 1
                    op("dve", lambda e, b_=b_, c=c, ob=ob: e.tensor_scalar(OH[ob], islot, rank[:, b_, c:c + 1], maskp[:, b_, c:c + 1], ALU.is_equal, ALU.mult),
                       r=["islot", "rank", "maskp"], w=[f"OH{ob}"])
                    for half in range(2):
                        op("pe", lambda e, b_=b_, c=c, ob=ob, half=half: e.matmul(PS[2 + half][:, b_:b_ + 1], lhsT=OH[ob][:, half * 128:(half + 1) * 128],
                                                                               rhs=physf[:, b_, c:c + 1], start=(c == 0), stop=(c == NPAGE - 1)),
                           r=[f"OH{ob}", "physf"], w=[f"ps{2 + half}"])
            for half in range(2):
                op("dve", lambda e, half=half: e.tensor_scalar(selidx_f.rearrange("p (b a) -> p b a", a=2)[:, :, half], PS[2 + half][:, 0:NS], 0.25, None, ALU.add),
                   r=[f"ps{2 + half}"], w=["selidx_f"])
            op("dve", lambda e: e.tensor_copy(selidx, selidx_f), r=["selidx_f"], w=["selidx"])
            for b_ in range(NS):
                fl = []
                for half in range(2):
                    fl.append(lambda e, b_=b_, half=half: e.indirect_dma_start(out=Ksel[:, half, :], out_offset=None, in_=cache_k,
                                                                              in_offset=bass.IndirectOffsetOnAxis(ap=selidx[:, b_ * 2 + half:b_ * 2 + half + 1], axis=0)))
                    fl.append(lambda e, b_=b_, half=half: e.indirect_dma_start(out=Vsel[:, half, :], out_offset=None, in_=cache_v,
                                                                              in_offset=bass.IndirectOffsetOnAxis(ap=selidx[:, b_ * 2 + half:b_ * 2 + half + 1], axis=0)))
                S.pool_dma_once(fl, r=["selidx"], w=["Ksel", "Vsel"])
                op("dve", lambda e, b_=b_: e.tensor_scalar(validb, iop, offs[:, b_, NPAGE - 1:NPAGE], None, ALU.is_lt), r=["iop", "offs"], w=["validb"])
                op("pe", lambda e, b_=b_: e.matmul(PS[4], lhsT=selB[:, b_, :], rhs=qs32, start=True, stop=True), r=["selB", "qs32"], w=["ps4"])
                for half in range(2):
                    op("dve", lambda e, half=half: e.tensor_tensor(Ksel[:, half, :], Ksel[:, half, :], PS[4], ALU.mult), r=["Ksel", "ps4"], w=["Ksel"])
                op("dve", lambda e: e.tensor_reduce(scb.rearrange("p a h -> p (a h)"), Ksel.rearrange("p a (h d) -> p (a h) d", d=64), AX.X, ALU.add), r=["Ksel"], w=["scb"])
                op("act", lambda e: e.activation(Pb, scb, AF.Exp, scale=0.125), r=["scb"], w=["Pb"])
                op("dve", lambda e: e.tensor_tensor(Pb, Pb, bc_last(validb, 8), ALU.mult), r=["Pb", "validb"], w=["Pb"])
                op("pe", lambda e: e.matmul(PS[5][:, 0:8], lhsT=ones_f, rhs=Pb[:, 0, :], start=True, stop=False), r=["ones_f", "Pb"], w=["ps5"])
                op("pe", lambda e: e.matmul(PS[5][:, 0:8], lhsT=ones_f, rhs=Pb[:, 1, :], start=False, stop=False), r=["ones_f", "Pb"], w=["ps5"])
                op("pe", lambda e, b_=b_: e.matmul(PS[5][:, 0:8], lhsT=selB[:, b_, :], rhs=Pn, start=False, stop=True), r=["selB", "Pn"], w=["ps5"])
                op("dve", lambda e: e.reciprocal(rdenb, PS[5][:, 0:8]), r=["ps5"], w=["rdenb"])
                op("dve", lambda e: e.tensor_tensor(Pb, Pb, bc_mid(rdenb, 2), ALU.mult), r=["Pb", "rdenb"], w=["Pb"])
                op("dve", lambda e: e.tensor_tensor(Pnn, Pn, rdenb[0:NS, :], ALU.mult), r=["Pn", "rdenb"], w=["Pnn"])
                for half in range(2):
                    op("dve", lambda e, half=half: e.tensor_tensor(Vsel[:, half, :].rearrange("p (h d) -> p h d", d=64), Vsel[:, half, :].rearrange("p (h d) -> p h d", d=64),
                                                                 bc_last(Pb[:, half, :], 64), ALU.mult), r=["Vsel", "Pb"], w=["Vsel"])
                op("dve", lambda e: e.tensor_tensor(wvn.rearrange("p (h d) -> p h d", d=64), vs32.rearrange("p (h d) -> p h d", d=64), bc_last(Pnn, 64), ALU.mult),
                   r=["vs32", "Pnn"], w=["prodn"])
                for cch in range(4):
                    cs = slice(cch * 128, (cch + 1) * 128)
                    col = cch * NS + b_
                    op("pe", lambda e, cs=cs, col=col: e.matmul(PS[7][:, col:col + 1], lhsT=Vsel[:, 0, cs], rhs=ones_f[:, 0:1], start=True, stop=False), r=["Vsel", "ones_f"], w=["ps7"])
                    op("pe", lambda e, cs=cs, col=col: e.matmul(PS[7][:, col:col + 1], lhsT=Vsel[:, 1, cs], rhs=ones_f[:, 0:1], start=False, stop=False), r=["Vsel", "ones_f"], w=["ps7"])
                    op("pe", lambda e, cs=cs, col=col, b_=b_: e.matmul(PS[7][:, col:col + 1], lhsT=wvn[:, cs], rhs=ident[0:NS, b_:b_ + 1], start=False, stop=True), r=["prodn", "ident"], w=["ps7"])
            op("act", lambda e: e.activation(attT[:, :, T:T + NS], PS[7][:, 0:16].rearrange("p (c b) -> p c b", b=NS), AF.Copy), r=["ps7"], w=["attT"])
            if os.environ.get("MK_DBG_S", "0") == "1":
                d1 = nc.dram_tensor("dbg_score", [128, NS * NP1], F32, kind="ExternalOutput").ap()
                dma("sp", lambda e: e.dma_start(out=d1, in_=score4.rearrange("p a b -> p (a b)")), r=["score4"])
                d2 = nc.dram_tensor("dbg_mask", [128, NS * NP1], F32, kind="ExternalOutput").ap()
                dma("sp", lambda e: e.dma_start(out=d2, in_=mask4.rearrange("p a b -> p (a b)")), r=["mask4"])
                d3 = nc.dram_tensor("dbg_sel", [128, 8], F32, kind="ExternalOutput").ap()
                dma("sp", lambda e: e.dma_start(out=d3, in_=selidx_f), r=["selidx_f"])
                d4 = nc.dram_tensor("dbg_atts", [128, 4, NS], BF16, kind="ExternalOutput").ap()
                dma("sp", lambda e: e.dma_start(out=d4, in_=attT[:, :, T:T + NS]), r=["attT"])
                d6 = nc.dram_tensor("dbg_ksel", [128, 1024], F32, kind="ExternalOutput").ap()
                dma("sp", lambda e: e.dma_start(out=d6, in_=Ksel.rearrange("p a b -> p (a b)")), r=["Ksel"])
                d7 = nc.dram_tensor("dbg_pb", [128, 16], F32, kind="ExternalOutput").ap()
                dma("sp", lambda e: e.dma_start(out=d7, in_=Pb.rearrange("p a b -> p (a b)")), r=["Pb"])
                d8 = nc.dram_tensor("dbg_scb", [128, 16], F32, kind="ExternalOutput").ap()
                dma("sp", lambda e: e.dma_start(out=d8, in_=scb.rearrange("p a b -> p (a b)")), r=["scb"])
                d9 = nc.dram_tensor("dbg_selidx", [128, 8], I32, kind="ExternalOutput").ap()
                dma("sp", lambda e: e.dma_start(out=d9, in_=selidx), r=["selidx"])
                d5 = nc.dram_tensor("dbg_rank", [128, NS * NPAGE], F32, kind="ExternalOutput").ap()
                dma("sp", lambda e: e.dma_start(out=d5, in_=rank.rearrange("p a b -> p (a b)")), r=["rank"])


        if STAGE >= 4:
            S.barrier()
            AR.release(m_att)
            gsc = nc.dram_tensor("gsc", [2, 5, D], F32, kind="Internal").ap()
            win_rw = AR.alloc([8, RW_COLS], BF16)
            wo_att = AR.alloc([4, D], BF16)
            wo_rw = AR.alloc([8, D], BF16, parts=64)
            w2b = AR.alloc([512], BF16, parts=64)
            a2b = AR.alloc([512], BF16, parts=64)
            g2b = AR.alloc([2, 512], BF16, parts=64)
            muD = AR.alloc([28], F32, parts=64)
            Msel = AR.alloc([16], F32)
            Mh = AR.alloc([8], F32)
            maskLA = AR.alloc([16, 128], F32, parts=64)
            g1rep = AR.alloc([D], F32)
            g1s = AR.alloc([D], F32, parts=NS)
            shiftT = AR.alloc([28, NS], F32, parts=64)
            m2 = AR.mark()
            grow = AR.alloc([D], F32, parts=5)
            stg = [AR.alloc([4096], F32) for _ in range(2)]
            stcnt = [0]

            def load_cast(dst, src, parts, shape):
                b = stcnt[0] % 2
                stcnt[0] += 1
                ne = _prod(shape)
                v = stg[b][0:parts, 0:ne]
                if len(shape) == 2:
                    v = v.rearrange("p (a b) -> p a b", a=shape[0])
                dma("sp", lambda e: e.dma_start(out=v, in_=src), w=[f"stg{b}"])
                op("pool", lambda e: e.tensor_copy(dst, v), r=[f"stg{b}"], w=["wts2"])

            w_in_v2 = w_in.rearrange("(kc p) n -> p kc n", p=128)
            for c in range(0, RW_COLS, 512):
                n = min(512, RW_COLS - c)
                load_cast(win_rw[:, :, c:c + n], w_in_v2[:, :, ATT_COLS + c:ATT_COLS + c + n], 128, [8, n])
            load_cast(wo_att, w_out[0:512, :].rearrange("(c p) n -> p c n", p=128), 128, [4, D])
            wo_rw_v = w_out[512:1024, :].rearrange("(h p) n -> p h n", p=64)
            for hh in range(0, 8, 4):
                load_cast(wo_rw[:, hh:hh + 4, :], wo_rw_v[:, hh:hh + 4, :], 64, [4, D])
            load_cast(w2b, rw_w2, 64, [512])
            load_cast(a2b, rw_a2, 64, [512])
            load_cast(g2b, rw_g2.rearrange("(b p) n -> p b n", p=64), 64, [2, 512])
            stm = stg[0][0:28, 0:64]
            dma("sp", lambda e: e.dma_start(out=stm, in_=rw_mu.rearrange("(a b) -> a b", b=64)), w=["stg0"])
            op("pe", lambda e: e.transpose(PS[0][0:64, 0:28], stm, ident[0:28, 0:28]), r=["stg0", "ident"], w=["ps0"])
            op("dve", lambda e: e.tensor_copy(muD, PS[0][0:64, 0:28]), r=["ps0"], w=["muD"])
            op("dve", lambda e: e.tensor_reduce(Msel, ident.rearrange("p (h t) -> p t h", t=16), AX.X, ALU.add), r=["ident"], w=["Msel"])
            op("dve", lambda e: e.tensor_reduce(Mh, ident.rearrange("p (h t) -> p h t", t=16), AX.X, ALU.add), r=["ident"], w=["Mh"])
            op("pool", lambda e: e.memset(maskLA, 0.0), w=["maskLA"])
            mla = maskLA.rearrange("p a (h t) -> p a h t", t=16)
            for tp in range(16):
                op("pool", lambda e, tp=tp: e.memset(mla[:, tp, :, tp:tp + 1], 1.0), w=["maskLA"])
            for kc in range(8):
                op("pe", lambda e, kc=kc: e.transpose(PS[kc // 4][0:5, (kc % 4) * 128:(kc % 4 + 1) * 128], gate1[:, kc, :], ident), r=["modT", "ident"], w=[f"ps{kc // 4}"])
            op("dve", lambda e: e.tensor_copy(grow[:, 0:512], PS[0][0:5, :]), r=["ps0"], w=["grow"])
            op("dve", lambda e: e.tensor_copy(grow[:, 512:1024], PS[1][0:5, :]), r=["ps1"], w=["grow"])
            dma("sp", lambda e: e.dma_start(out=gsc[0], in_=grow), r=["grow"], w=["gsc0"])
            dma("sp", lambda e: e.dma_start(out=g1rep, in_=gsc[0, 0].partition_broadcast(128)), r=["gsc0"], w=["g1rep"])
            dma("sp", lambda e: e.dma_start(out=g1s, in_=gsc[0, 1:5, :]), r=["gsc0"], w=["g1s"])
            shs = stg[1][0:NS, 0:RW_COLS]
            dma("sp", lambda e: e.dma_start(out=shs, in_=st_shift), w=["stg1"])
            for blk in range(28):
                op("pe", lambda e, blk=blk: e.transpose(PS[2][0:64, blk * 4:blk * 4 + NS], shs[:, blk * 64:(blk + 1) * 64], ident[0:NS, 0:NS]), r=["stg1", "ident"], w=["ps2"])
            op("dve", lambda e: e.tensor_copy(shiftT, PS[2][0:64, 0:112].rearrange("p (a b) -> p a b", b=4)), r=["ps2"], w=["shiftT"])
            S.barrier()
            AR.release(m2)

            W = 128
            xt2 = AR.alloc([D], F32)
            xn2 = AR.alloc([D], F32)
            hT2 = AR.alloc([8, W], BF16)
            prd = AR.alloc([28, W + 1], F32, parts=64)
            xm = AR.alloc([28, W], F32, parts=64)
            dv = {k: AR.alloc([8, W], F32, parts=64) for k in ["wdec", "asig", "kkn", "na", "bb", "kmod", "gg", "bon", "t1", "t2"]}
            dv["Yd"] = dv["kkn"]
            rwd = AR.alloc([8, W], BF16, parts=64)
            tanh_wd = AR.alloc([W], BF16, parts=64)
            ad_bf = AR.alloc([W], BF16, parts=64)
            sg = AR.alloc([2, W], BF16, parts=64)
            LA_sel = LA_sel_r
            Xb16 = Xall[:, 0:64]
            Xb16_r = Xall_r[:, 0:64]
            STb = AR.alloc([8, 64], BF16, parts=64)
            rbf = AR.alloc([8, W], BF16, parts=64)
            ST = AR.alloc([8, 64], F32, parts=64)
            ST2 = AR.alloc([8, 64], F32, parts=64)
            Sio = AR.alloc([8, 64], F32, parts=64)
            sm2 = AR.alloc([8], F32)
            tstg = AR.alloc([3, 128], F32, parts=64)
            w0T, a0T, kkT, kaT, lnwT, lnbT, rkT = (vecB[:, 8 * i:8 * i + 8] for i in range(7))
            for k_ in dv:
                if k_ != "Yd":
                    op("pool", lambda e, k_=k_: e.memset(dv[k_], 0.0), w=[k_])
            op("pool", lambda e: e.memset(xm, 0.0), w=["xm"])
            op("pool", lambda e: e.memset(prd, 0.0), w=["prd"])
            op("pool", lambda e: e.memset(ST, 0.0), w=["ST"])
            op("act", lambda e: e.activation(STr, ST.rearrange("p h i -> p (h i)"), AF.Copy), r=["ST"], w=["STr"])
            ones64 = ones_f[0:64, 0:64]

            def v4(ap):
                return ap[0:64, :].rearrange("p (a b) -> p a b", b=128)

            def sum64(src, P):
                for hb in range(2):
                    if P == 128:
                        op("pe", lambda e, hb=hb: e.matmul(v4(PS[6 + hb])[:, :, 0:P], lhsT=ones64, rhs=src[:, hb * 4:(hb + 1) * 4, 0:P], start=True, stop=True),
                           r=["ones_f", "dvsrc"], w=[f"ps{6 + hb}"])
                    else:
                        for hl in range(4):
                            op("pe", lambda e, hb=hb, hl=hl: e.matmul(PS[6 + hb][0:64, hl * 128:hl * 128 + P], lhsT=ones64, rhs=src[:, hb * 4 + hl, 0:P], start=True, stop=True),
                               r=["ones_f", "dvsrc"], w=[f"ps{6 + hb}"])

            def rw_tile(ti, P, x_src, cond, tcol0, prev_view, is_sample):
                dma("sp", lambda e: e.dma_start(out=xt2[0:P, :], in_=x_src), w=["xt2"])
                ss = sm2[0:P, 0:1]
                rs = sm2[0:P, 1:2]
                op("act", lambda e: e.activation(xn2[0:P, :], xt2[0:P, :], AF.Square, accum_out=ss), r=["xt2"], w=["xn2", "sm2"])
                op("dve", lambda e: e.tensor_scalar(rs, ss, 1.0 / D, RMS_EPS, ALU.mult, ALU.add), r=["sm2"], w=["sm2"])
                op("act", lambda e: e.activation(rs, rs, AF.Sqrt), r=["sm2"], w=["sm2"])
                op("dve", lambda e: e.reciprocal(rs, rs), r=["sm2"], w=["sm2"])
                op("dve", lambda e: e.tensor_scalar(xn2[0:P, :], xt2[0:P, :], rs, None, ALU.mult), r=["xt2", "sm2"], w=["xn2"])
                for half in range(2):
                    pb = PS[half]
                    for j in range(4):
                        kc = half * 4 + j
                        op("pe", lambda e, kc=kc, j=j, pb=pb: e.transpose(pb[:, j * 128:j * 128 + P], xn2[0:P, kc * 128:(kc + 1) * 128], ident[0:P, 0:P]),
                           r=["xn2", "ident"], w=[f"ps{half}"])
                    for j in range(4):
                        kc = half * 4 + j
                        if not is_sample:
                            op("act", lambda e, kc=kc, j=j, pb=pb: e.activation(hT2[:, kc, 0:P], pb[:, j * 128:j * 128 + P], AF.Identity,
                                                                              bias=shift1[:, kc, 0:1], scale=A1[:, kc, 0:1]), r=[f"ps{half}", "A1", "modT"], w=["hT2"])
                        else:
                            op("dve", lambda e, kc=kc, j=j, pb=pb: e.tensor_tensor(scr2[:, 0:P], pb[:, j * 128:j * 128 + P], A1[:, kc, 1:1 + P], ALU.mult),
                               r=[f"ps{half}", "A1"], w=["scr2b"])
                            op("dve", lambda e, kc=kc: e.tensor_tensor(hT2[:, kc, 0:P], scr2[:, 0:P], shift1[:, kc, 1:1 + P], ALU.add), r=["scr2b", "modT"], w=["hT2"])
                for blk0 in range(0, 28, 4):
                    bk = 2 + (blk0 // 4) % 2
                    for j in range(4):
                        blk = blk0 + j
                        for kc in range(8):
                            op("pe", lambda e, bk=bk, j=j, blk=blk, kc=kc: e.matmul(PS[bk][0:64, j * 128:j * 128 + P], lhsT=win_rw[:, kc, blk * 64:(blk + 1) * 64],
                                                                                    rhs=hT2[:, kc, 0:P], start=(kc == 0), stop=(kc == 7)), r=["hT2", "wts2"], w=[f"ps{bk}"])
                    op("act", lambda e, bk=bk, blk0=blk0: e.activation(prd[:, blk0:blk0 + 4, 1:1 + P], v4(PS[bk])[:, :, 0:P], AF.Copy), r=[f"ps{bk}"], w=["prd"])
                if P2_CUT <= 2:
                    return
                cur = prd[:, :, 1:1 + P]
                prev = prev_view if prev_view is not None else prd[:, :, 0:P]
                xmv = xm[:, :, 0:P]
                op("dve", lambda e: e.tensor_tensor(xmv, prev, cur, ALU.subtract), r=["prd", "shiftT"], w=["xm"])
                op("dve", lambda e: e.tensor_tensor(xmv, xmv, bc_last(muD, P), ALU.mult), r=["xm", "muD"], w=["xm"])
                op("dve", lambda e: e.tensor_tensor(xmv, xmv, cur, ALU.add), r=["xm", "prd"], w=["xm"])
                if is_sample:
                    for b_ in range(P):
                        op("pe", lambda e, b_=b_: e.transpose(PS[0][0:28, b_ * 64:(b_ + 1) * 64], prd[:, :, 1 + b_], ident[0:64, 0:64]), r=["prd", "ident"], w=["ps0"])
                    op("dve", lambda e: e.tensor_copy(xn2[0:28, 0:P * 64], PS[0][0:28, 0:P * 64]), r=["ps0"], w=["xn2"])
                    for b_ in range(P):
                        dma("sp", lambda e, b_=b_: e.dma_start(out=sh_so[b_].rearrange("(a c) -> a c", c=64), in_=xn2[0:28, b_ * 64:(b_ + 1) * 64]), r=["xn2"])
                else:
                    if ti == NTILES - 1:
                        op("pe", lambda e: e.transpose(PS[0][0:28, 0:64], prd[:, :, P], ident[0:64, 0:64]), r=["prd", "ident"], w=["ps0"])
                        op("dve", lambda e: e.tensor_copy(xn2[0:28, 0:64], PS[0][0:28, 0:64]), r=["ps0"], w=["xn2"])
                        dma("sp", lambda e: e.dma_start(out=sh_po.rearrange("(a c) -> a c", c=64), in_=xn2[0:28, 0:64]), r=["xn2"])
                    op("pool", lambda e: e.tensor_copy(prd[:, :, 0:1], prd[:, :, P:P + 1]), r=["prd", "xm"], w=["prd"])
                if P2_CUT <= 3:
                    return
                r_ = xm[:, 0:8, :]
                k_ = xm[:, 8:16, :]
                v_ = xm[:, 16:24, :]
                D_ = {k2: dv[k2][:, :, 0:P] for k2 in dv}
                op("act", lambda e: e.activation(tanh_wd[:, 0:P], xm[:, 24, 0:P], AF.Tanh), r=["xm"], w=["tanh_wd"])
                op("act", lambda e: e.activation(ad_bf[:, 0:P], xm[:, 25, 0:P], AF.Copy), r=["xm"], w=["ad_bf"])
                op("act", lambda e: e.activation(sg[:, :, 0:P], xm[:, 26:28, 0:P], AF.Sigmoid), r=["xm"], w=["sg"])

                def lora(wb, rhs_ap, rname):
                    for h in range(8):
                        op("pe", lambda e, h=h: e.matmul(PS[4 + h // 4][0:64, (h % 4) * 128:(h % 4) * 128 + P], lhsT=wb[:, h * 64:(h + 1) * 64], rhs=rhs_ap,
                                                       start=True, stop=True), r=[rname, "wts2"], w=[f"ps{4 + h // 4}"])
                lora(w2b, tanh_wd[:, 0:P], "tanh_wd")
                for hb in range(2):
                    op("dve", lambda e, hb=hb: e.tensor_tensor(D_["t1"][:, hb * 4:(hb + 1) * 4, :], v4(PS[4 + hb])[:, :, 0:P], bc_last(w0T[:, hb * 4:(hb + 1) * 4], P), ALU.add),
                       r=[f"ps{4 + hb}", "vecB"], w=["t1"])
                op("act", lambda e: e.activation(D_["t1"], D_["t1"], AF.Sigmoid), r=["t1"], w=["t1"])
                op("act", lambda e: e.activation(D_["wdec"], D_["t1"], AF.Exp, scale=-0.6065306597126334), r=["t1"], w=["wdec"])
                lora(a2b, ad_bf[:, 0:P], "ad_bf")
                for hb in range(2):
                    op("dve", lambda e, hb=hb: e.tensor_tensor(D_["t1"][:, hb * 4:(hb + 1) * 4, :], v4(PS[4 + hb])[:, :, 0:P], bc_last(a0T[:, hb * 4:(hb + 1) * 4], P), ALU.add),
                       r=[f"ps{4 + hb}", "vecB"], w=["t1"])
                op("act", lambda e: e.activation(D_["asig"], D_["t1"], AF.Sigmoid), r=["t1"], w=["asig"])
                for h in range(8):
                    for blk in range(2):
                        op("pe", lambda e, h=h, blk=blk: e.matmul(PS[4 + h // 4][0:64, (h % 4) * 128:(h % 4) * 128 + P], lhsT=g2b[:, blk, h * 64:(h + 1) * 64], rhs=sg[:, blk, 0:P],
                                                                 start=(blk == 0), stop=(blk == 1)), r=["sg", "wts2"], w=[f"ps{4 + h // 4}"])
                for hb in range(2):
                    op("act", lambda e, hb=hb: e.activation(D_["gg"][:, hb * 4:(hb + 1) * 4, :], v4(PS[4 + hb])[:, :, 0:P], AF.Copy), r=[f"ps{4 + hb}"], w=["gg"])
                kP, rP, vP = k_[:, :, 0:P], r_[:, :, 0:P], v_[:, :, 0:P]
                op("dve", lambda e: e.tensor_tensor(D_["kkn"], kP, bc_last(kkT, P), ALU.mult), r=["xm", "vecB"], w=["kkn"])
                op("dve", lambda e: e.tensor_tensor(D_["t1"], D_["kkn"], D_["kkn"], ALU.mult), r=["kkn"], w=["t1", "dvsrc"])
                sum64(dv["t1"], P)
                for hb in range(2):
                    op("dve", lambda e, hb=hb: e.tensor_scalar(D_["t2"][:, hb * 4:(hb + 1) * 4, :], v4(PS[6 + hb])[:, :, 0:P], 1e-24, None, ALU.max), r=[f"ps{6 + hb}"], w=["t2"])
                op("act", lambda e: e.activation(D_["t2"], D_["t2"], AF.Sqrt), r=["t2"], w=["t2"])
                op("dve", lambda e: e.reciprocal(D_["t2"], D_["t2"]), r=["t2"], w=["t2"])
                op("dve", lambda e: e.tensor_tensor(D_["kkn"], D_["kkn"], D_["t2"], ALU.mult), r=["kkn", "t2"], w=["kkn"])
                op("dve", lambda e: e.tensor_scalar(D_["t1"], D_["asig"], -1.0, None, ALU.add), r=["asig"], w=["t1"])
                op("dve", lambda e: e.tensor_tensor(D_["t1"], D_["t1"], bc_last(kaT, P), ALU.mult), r=["t1", "vecB"], w=["t1"])
                op("dve", lambda e: e.tensor_scalar(D_["t1"], D_["t1"], 1.0, None, ALU.add), r=["t1"], w=["t1"])
                op("dve", lambda e: e.tensor_tensor(D_["kmod"], kP, D_["t1"], ALU.mult), r=["xm", "t1"], w=["kmod"])
                op("dve", lambda e: e.tensor_tensor(D_["bb"], D_["kkn"], D_["asig"], ALU.mult), r=["kkn", "asig"], w=["bb"])
                op("pool", lambda e: e.tensor_scalar(D_["na"], D_["kkn"], -1.0, None, ALU.mult), r=["kkn"], w=["na"])
                op("dve", lambda e: e.tensor_tensor(D_["t1"], rP, D_["kmod"], ALU.mult), r=["xm", "kmod"], w=["t1"])
                op("dve", lambda e: e.tensor_tensor(D_["t1"], D_["t1"], bc_last(rkT, P), ALU.mult), r=["t1", "vecB"], w=["t1", "dvsrc"])
                sum64(dv["t1"], P)
                for hb in range(2):
                    op("dve", lambda e, hb=hb: e.tensor_tensor(D_["bon"][:, hb * 4:(hb + 1) * 4, :], v4(PS[6 + hb])[:, :, 0:P], vP[:, hb * 4:(hb + 1) * 4, :], ALU.mult),
                       r=[f"ps{6 + hb}", "xm"], w=["bon"])
                if os.environ.get("MK_DBG_DV", "0") == "1" and ti == 0:
                    dbg_dv = nc.dram_tensor("dbg_dv", [64, 7, 8, 128], F32, kind="ExternalOutput").ap()
                    for i_, nm in enumerate(["wdec", "asig", "kkn", "kmod", "bb", "gg", "bon"]):
                        dma("sp", lambda e, i_=i_, nm=nm: e.dma_start(out=dbg_dv[:, i_], in_=dv[nm]), r=[nm])
                    dbg_xm = nc.dram_tensor("dbg_xm", [64, 28, 128], F32, kind="ExternalOutput").ap()
                    dma("sp", lambda e: e.dma_start(out=dbg_xm, in_=xm), r=["xm"])
                if P2_CUT <= 4:
                    return
                acnt = 0
                pend_y = []
                op("act", lambda e: e.activation(rbf[:, :, 0:P], r_[:, :, 0:P], AF.Copy), r=["xm"], w=["rbf"])
                PSC = min(P, int(os.environ.get("MK_NSTEPS", "100000")))
                for t0 in range(0, PSC, 16):
                    nst = min(16, PSC - t0)
                    for (ci, srcv, rn) in [(0, dv["bb"], "bb"), (1, dv["kmod"], "kmod"), (2, v_, "xm")]:
                        op("pool", lambda e, ci=ci, srcv=srcv, t0=t0: e.tensor_copy(tstg[:, ci, :].rearrange("p (h t) -> p h t", t=16), srcv[:, :, t0:t0 + 16]), r=[rn], w=["tstg"])
                        op("pe", lambda e, ci=ci: e.transpose(PS[6][:, ci * 64:ci * 64 + 64], tstg[:, ci, :], ident[0:64, 0:64]), r=["tstg", "ident"], w=["ps6"])
                    op("act", lambda e: e.activation(Xall_r, PS[6][:, 0:192], AF.Copy), r=["ps6"], w=["Xb16"])
                    if os.environ.get("MK_T2", "0") != "1":
                        op("dve", lambda e: e.tensor_tensor(Xk_sel_r, bc_mid(Xall[:, 64:128], 16), bc_last(Msel, 64), ALU.mult), r=["Xb16", "Msel"], w=["Xk_sel"])
                        op("dve", lambda e: e.tensor_tensor(Vbd16_r, bc_mid(Xall[:, 128:192], 8), bc_last(Mh, 64), ALU.mult), r=["Xb16", "Mh"], w=["Vbd16"])
                    if os.environ.get("MK_T1", "0") != "1":
                        op("dve", lambda e, t0=t0: e.tensor_tensor(LA_sel.rearrange("p a (h t) -> p a h t", t=16), dv["na"][:, :, t0:t0 + 16].unsqueeze(1).to_broadcast([64, 16, 8, 16]),
                                                             maskLA.rearrange("p a (h t) -> p a h t", t=16), ALU.mult), r=["na", "maskLA"], w=["LA_sel"])
                    for tp in range(nst if SCAN_CUT > 1 else 0):
                        t = t0 + tp
                        if is_sample:
                            load_state(t)
                        pa = acnt % 2
                        pu = 2 + acnt % 2
                        rb = acnt % 2
                        acnt += 1
                        op("pe", lambda e, pa=pa, tp=tp: e.matmul(PS[pa], lhsT=LA_sel_r[:, tp, :], rhs=STr, start=True, stop=True),
                           r=["LA_sel", "STr"], w=[f"ps{pa}"])
                        op("dve", lambda e, pa=pa, rb=rb: e.tensor_tensor(R128_r[rb], PS[pa].rearrange("p (h i) -> p h i", i=64), bc_last(Mh, 64), ALU.mult),
                           r=[f"ps{pa}", "Mh"], w=[f"R{rb}"])
                        if SCAN_CUT <= 2:
                            continue
                        op("pool", lambda e, t=t: e.tensor_tensor(ST2, ST, bc_last(dv["wdec"][:, :, t], 64), ALU.mult), r=["ST", "wdec"], w=["ST2"])
                        op("pe", lambda e, pu=pu, tp=tp: e.matmul(PS[pu][0:64, :], lhsT=Xk_sel_r[:, tp, :], rhs=Vbd16_r.rearrange("p h i -> p (h i)"), start=True, stop=False),
                           r=["Xk_sel", "Vbd16"], w=[f"ps{pu}"])
                        op("pe", lambda e, pu=pu, rb=rb: e.matmul(PS[pu][0:64, :], lhsT=Xb16_r, rhs=R128_r[rb].rearrange("p h i -> p (h i)"), start=False, stop=True),
                           r=["Xb16", f"R{rb}"], w=[f"ps{pu}"])
                        while pend_y:
                            pend_y.pop(0)()
                        op("dve", lambda e, pu=pu: e.tensor_tensor(ST.rearrange("p h i -> p (h i)"), ST2.rearrange("p h i -> p (h i)"), PS[pu][0:64, :], ALU.add),
                           r=["ST2", f"ps{pu}"], w=["ST"])
                        op("act", lambda e: e.activation(STr, ST.rearrange("p h i -> p (h i)"), AF.Copy), r=["ST"], w=["STr"])
                        op("act", lambda e: e.activation(STb, ST, AF.Copy), r=["ST"], w=["STb"])

                        def emit_y(t=t):
                            for h in range(8):
                                op("pe", lambda e, h=h, t=t: e.matmul(PS[4 + h // 4][0:64, (h % 4) * 128 + t:(h % 4) * 128 + t + 1], lhsT=STb[:, h, :], rhs=rbf[:, h, t:t + 1],
                                                                   start=True, stop=True), r=["STb", "rbf"], w=[f"ps{4 + h // 4}"])
                        if is_sample:
                            emit_y()
                            store_state(wkv_so[t])
                        else:
                            pend_y.append(emit_y)
                while pend_y:
                    pend_y.pop(0)()
                if P2_CUT <= 5:
                    return
                for hb in range(2):
                    op("act", lambda e, hb=hb: e.activation(D_["Yd"][:, hb * 4:(hb + 1) * 4, :], v4(PS[4 + hb])[:, :, 0:P], AF.Copy), r=[f"ps{4 + hb}"], w=["kkn", "dvsrc"])
                sum64(dv["Yd"], P)
                for hb in range(2):
                    op("dve", lambda e, hb=hb: e.scalar_tensor_tensor(D_["t1"][:, hb * 4:(hb + 1) * 4, :], v4(PS[6 + hb])[:, :, 0:P], -1.0 / 64, D_["Yd"][:, hb * 4:(hb + 1) * 4, :],
                                                                      ALU.mult, ALU.add), r=[f"ps{6 + hb}", "kkn"], w=["t1"])
                op("dve", lambda e: e.tensor_tensor(D_["t2"], D_["t1"], D_["t1"], ALU.mult), r=["t1"], w=["t2", "dvsrc"])
                sum64(dv["t2"], P)
                for hb in range(2):
                    op("dve", lambda e, hb=hb: e.tensor_scalar(D_["t2"][:, hb * 4:(hb + 1) * 4, :], v4(PS[6 + hb])[:, :, 0:P], 1.0 / 64, GN_EPS, ALU.mult, ALU.add),
                       r=[f"ps{6 + hb}"], w=["t2"])
                op("act", lambda e: e.activation(D_["t2"], D_["t2"], AF.Sqrt), r=["t2"], w=["t2"])
                op("dve", lambda e: e.reciprocal(D_["t2"], D_["t2"]), r=["t2"], w=["t2"])
                op("dve", lambda e: e.tensor_tensor(D_["t1"], D_["t1"], D_["t2"], ALU.mult), r=["t1", "t2"], w=["t1"])
                op("dve", lambda e: e.tensor_tensor(D_["t1"], D_["t1"], bc_last(lnwT, P), ALU.mult), r=["t1", "vecB"], w=["t1"])
                op("dve", lambda e: e.tensor_tensor(D_["t1"], D_["t1"], bc_last(lnbT, P), ALU.add), r=["t1", "vecB"], w=["t1"])
                op("dve", lambda e: e.tensor_tensor(D_["t1"], D_["t1"], D_["bon"], ALU.add), r=["t1", "bon"], w=["t1"])
                op("dve", lambda e: e.tensor_tensor(rwd[:, :, 0:P], D_["t1"], D_["gg"], ALU.mult), r=["t1", "gg"], w=["rwd"])
                grep_ = g1s if is_sample else g1rep
                for half in range(2):
                    cs = slice(half * 512, (half + 1) * 512)
                    pb = 2 + half
                    for c in range(4):
                        op("pe", lambda e, c=c, cs=cs, pb=pb: e.matmul(PS[pb][0:P, :], lhsT=attT[:, c, tcol0:tcol0 + P], rhs=wo_att[:, c, cs], start=(c == 0), stop=False),
                           r=["attT", "wts2"], w=[f"ps{pb}"])
                    for h in range(8):
                        op("pe", lambda e, h=h, cs=cs, pb=pb: e.matmul(PS[pb][0:P, :], lhsT=rwd[:, h, 0:P], rhs=wo_rw[:, h, cs], start=False, stop=(h == 7)),
                           r=["rwd", "wts2"], w=[f"ps{pb}"])
                    op("dve", lambda e, cs=cs, pb=pb: e.tensor_tensor(xn2[0:P, cs], PS[pb][0:P, :], grep_[0:P, cs], ALU.mult), r=[f"ps{pb}", "g1rep", "g1s"], w=["xn2"])
                    op("dve", lambda e, cs=cs: e.tensor_tensor(xn2[0:P, cs], xn2[0:P, cs], xt2[0:P, cs], ALU.add), r=["xn2", "xt2"], w=["xn2"])

            scr2 = AR.alloc([W], F32)

            def load_state(b_):
                dma("sp", lambda e: e.dma_start(out=Sio, in_=st_wkv[b_].rearrange("h i j -> i h j")), w=["Sio"])
                for h in range(8):
                    op("pe", lambda e, h=h: e.transpose(PS[7][0:64, h * 64:(h + 1) * 64], Sio[:, h, :], ident[0:64, 0:64]), r=["Sio", "ident"], w=["ps7"])
                op("dve", lambda e: e.tensor_copy(ST.rearrange("p h i -> p (h i)"), PS[7][0:64, :]), r=["ps7"], w=["ST"])
                op("act", lambda e: e.activation(STr, ST.rearrange("p h i -> p (h i)"), AF.Copy), r=["ST"], w=["STr"])

            def store_state(dst):
                for h in range(8):
                    op("pe", lambda e, h=h: e.transpose(PS[7][0:64, h * 64:(h + 1) * 64], ST[:, h, :], ident[0:64, 0:64]), r=["ST", "ident"], w=["ps7"])
                op("dve", lambda e: e.tensor_copy(Sio.rearrange("p h i -> p (h i)"), PS[7][0:64, :]), r=["ps7"], w=["Sio"])
                dma("sp", lambda e: e.dma_start(out=dst.rearrange("h i j -> i h j"), in_=Sio), r=["Sio"])

            for ti in range(NTILES if P2_CUT > 1 else 0):
                r0 = ti * 128
                rw_tile(ti, 128, x_p[r0:r0 + 128, :], None, r0, None, False)
                dma("sp", lambda e, r0=r0: e.dma_start(out=y_p[r0:r0 + 128, :], in_=xn2), r=["xn2"], w=["y_p"])
            if P2_CUT > 5:
                store_state(wkv_po)
            if P2_CUT > 1 and os.environ.get("MK_NOSAMP", "0") != "1":
                rw_tile(16, NS, x_s, 1, T, shiftT, True)
                dma("sp", lambda e: e.dma_start(out=y_s, in_=xn2[0:NS, :]), r=["xn2"], w=["y_s"])


        if STAGE >= 5:
            S.barrier()
            AR.release(m_attT)
            wg = AR.alloc([8, FFN], BF16)
            wu = AR.alloc([8, FFN], BF16)
            wd_ = AR.alloc([NFC, D], BF16)
            g2rep = AR.alloc([D], F32)
            g2s = AR.alloc([D], F32, parts=NS)
            m3 = AR.mark()
            grow2 = AR.alloc([D], F32, parts=5)
            stg3 = [AR.alloc([4096], F32) for _ in range(2)]
            st3 = [0]

            def load_cast3(dst, src, shape):
                b = st3[0] % 2
                st3[0] += 1
                ne = _prod(shape)
                v = stg3[b][:, 0:ne].rearrange("p (a b) -> p a b", a=shape[0])
                dma("sp", lambda e: e.dma_start(out=v, in_=src), w=[f"stg3{b}"])
                op("pool", lambda e: e.tensor_copy(dst, v), r=[f"stg3{b}"], w=["wts3"])

            wg_v = w_gate.rearrange("(kc p) n -> p kc n", p=128)
            wu_v = w_up.rearrange("(kc p) n -> p kc n", p=128)
            wd_v = w_down.rearrange("(fc p) n -> p fc n", p=128)
            for c in range(0, FFN, 512):
                n = min(512, FFN - c)
                load_cast3(wg[:, :, c:c + n], wg_v[:, :, c:c + n], [8, n])
                load_cast3(wu[:, :, c:c + n], wu_v[:, :, c:c + n], [8, n])
            for fc0 in range(0, NFC, 4):
                nf = min(4, NFC - fc0)
                load_cast3(wd_[:, fc0:fc0 + nf, :], wd_v[:, fc0:fc0 + nf, :], [nf, D])
            for kc in range(8):
                op("pe", lambda e, kc=kc: e.transpose(PS[kc // 4][0:5, (kc % 4) * 128:(kc % 4 + 1) * 128], gate2[:, kc, :], ident), r=["modT", "ident"], w=[f"ps{kc // 4}"])
            op("dve", lambda e: e.tensor_copy(grow2[:, 0:512], PS[0][0:5, :]), r=["ps0"], w=["grow2"])
            op("dve", lambda e: e.tensor_copy(grow2[:, 512:1024], PS[1][0:5, :]), r=["ps1"], w=["grow2"])
            dma("sp", lambda e: e.dma_start(out=gsc[1], in_=grow2), r=["grow2"], w=["gsc1"])
            dma("sp", lambda e: e.dma_start(out=g2rep, in_=gsc[1, 0].partition_broadcast(128)), r=["gsc1"], w=["g2rep"])
            dma("sp", lambda e: e.dma_start(out=g2s, in_=gsc[1, 1:5, :]), r=["gsc1"], w=["g2s"])
            S.barrier()
            AR.release(m3)
            x1t = [AR.alloc([D], F32) for _ in range(1)]
            xn3 = AR.alloc([D], F32)
            h2T = AR.alloc([8, 128], BF16)
            hidT = AR.alloc([NFC, 128], BF16)
            sil = [AR.alloc([128], F32) for _ in range(2)]
            yt = [AR.alloc([D], F32)] * 2
            sm3 = AR.alloc([8], F32)
            scr3 = AR.alloc([NS], F32)

            def ffn_group(tiles, is_sample):
                ncol = 0
                for i_, (P, src, dst) in enumerate(tiles):
                    xb = x1t[i_]
                    dma("sp", lambda e, xb=xb, P=P, src=src: e.dma_start(out=xb[0:P, :], in_=src), r=["y_p", "y_s"], w=[f"x1t{i_}"])
                    ss = sm3[0:P, 0:1]
                    rs = sm3[0:P, 1:2]
                    op("act", lambda e, xb=xb, P=P, ss=ss: e.activation(xn3[0:P, :], xb[0:P, :], AF.Square, accum_out=ss), r=[f"x1t{i_}"], w=["xn3", "sm3"])
                    op("dve", lambda e, ss=ss, rs=rs: e.tensor_scalar(rs, ss, 1.0 / D, RMS_EPS, ALU.mult, ALU.add), r=["sm3"], w=["sm3"])
                    op("act", lambda e, rs=rs: e.activation(rs, rs, AF.Sqrt), r=["sm3"], w=["sm3"])
                    op("dve", lambda e, rs=rs: e.reciprocal(rs, rs), r=["sm3"], w=["sm3"])
                    op("dve", lambda e, xb=xb, P=P, rs=rs: e.tensor_scalar(xn3[0:P, :], xb[0:P, :], rs, None, ALU.mult), r=[f"x1t{i_}", "sm3"], w=["xn3"])
                    for half in range(2):
                        pb = PS[half]
                        for j in range(4):
                            kc = half * 4 + j
                            op("pe", lambda e, kc=kc, j=j, pb=pb, P=P: e.transpose(pb[:, j * 128:j * 128 + P], xn3[0:P, kc * 128:(kc + 1) * 128], ident[0:P, 0:P]),
                               r=["xn3", "ident"], w=[f"ps{half}"])
                        for j in range(4):
                            kc = half * 4 + j
                            if not is_sample:
                                op("act", lambda e, kc=kc, j=j, pb=pb, P=P, ncol=ncol: e.activation(h2T[:, kc, ncol:ncol + P], pb[:, j * 128:j * 128 + P], AF.Identity,
                                                                                                  bias=shift2[:, kc, 0:1], scale=A2[:, kc, 0:1]), r=[f"ps{half}", "A2", "modT"], w=["h2T"])
                            else:
                                op("dve", lambda e, kc=kc, j=j, pb=pb, P=P: e.tensor_tensor(scr3[:, 0:P], pb[:, j * 128:j * 128 + P], A2[:, kc, 1:1 + P], ALU.mult),
                                   r=[f"ps{half}", "A2"], w=["scr3"])
                                op("dve", lambda e, kc=kc, P=P, ncol=ncol: e.tensor_tensor(h2T[:, kc, ncol:ncol + P], scr3[:, 0:P], shift2[:, kc, 1:1 + P], ALU.add),
                                   r=["scr3", "modT"], w=["h2T"])
                    ncol += P
                N = ncol
                for fc in range(NFC):
                    pg = PS[2 + fc % 2]
                    pu_ = PS[4 + fc % 2]
                    sb = sil[fc % 2]
                    for kc in range(8):
                        op("pe", lambda e, fc=fc, kc=kc, pg=pg: e.matmul(pg[:, 0:N], lhsT=wg[:, kc, fc * 128:(fc + 1) * 128], rhs=h2T[:, kc, 0:N], start=(kc == 0), stop=(kc == 7)),
                           r=["h2T", "wts3"], w=[f"ps{2 + fc % 2}"])
                    for kc in range(8):
                        op("pe", lambda e, fc=fc, kc=kc, pu_=pu_: e.matmul(pu_[:, 0:N], lhsT=wu[:, kc, fc * 128:(fc + 1) * 128], rhs=h2T[:, kc, 0:N], start=(kc == 0), stop=(kc == 7)),
                           r=["h2T", "wts3"], w=[f"ps{4 + fc % 2}"])
                    op("act", lambda e, pg=pg, sb=sb: e.activation(sb[:, 0:N], pg[:, 0:N], AF.Silu), r=[f"ps{2 + fc % 2}"], w=[f"sil{fc % 2}"])
                    op("dve", lambda e, fc=fc, pu_=pu_, sb=sb: e.tensor_tensor(hidT[:, fc, 0:N], sb[:, 0:N], pu_[:, 0:N], ALU.mult), r=[f"sil{fc % 2}", f"ps{4 + fc % 2}"], w=["hidT"])
                ncol = 0
                for i_, (P, src, dst) in enumerate(tiles):
                    xb = x1t[i_]
                    yb = yt[i_ % 2]
                    grep_ = g2s if is_sample else g2rep
                    for half in range(2):
                        cs = slice(half * 512, (half + 1) * 512)
                        pb = 6 + half
                        for fc in range(NFC):
                            op("pe", lambda e, fc=fc, cs=cs, pb=pb, P=P, ncol=ncol: e.matmul(PS[pb][0:P, :], lhsT=hidT[:, fc, ncol:ncol + P], rhs=wd_[:, fc, cs],
                                                                                         start=(fc == 0), stop=(fc == NFC - 1)), r=["hidT", "wts3"], w=[f"ps{pb}"])
                        op("dve", lambda e, cs=cs, pb=pb, P=P, yb=yb, grep_=grep_: e.tensor_tensor(yb[0:P, cs], PS[pb][0:P, :], grep_[0:P, cs], ALU.mult),
                           r=[f"ps{pb}", "g2rep", "g2s"], w=["yt0"])
                        op("dve", lambda e, cs=cs, P=P, yb=yb, xb=xb: e.tensor_tensor(yb[0:P, cs], yb[0:P, cs], xb[0:P, cs], ALU.add), r=["yt0", f"x1t{i_}"], w=["yt0"])
                    dma("sp", lambda e, yb=yb, P=P, dst=dst: e.dma_start(out=dst, in_=yb[0:P, :]), r=["yt0"], w=["y_out"])
                    ncol += P

            for g in range(NTILES):
                tl_ = []
                for tl in range(1):
                    ti = g + tl
                    if ti < NTILES:
                        tl_.append((128, y_p[ti * 128:(ti + 1) * 128, :], y_p[ti * 128:(ti + 1) * 128, :]))
                ffn_group(tl_, False)
            ffn_group([(NS, y_s, y_s)], True)

        if os.environ.get("MK_DBG_ATT", "0") == "1":
            dbg_att = nc.dram_tensor("dbg_att", [128, 4, NTILES * 128], BF16, kind="ExternalOutput").ap()
            dma("sp", lambda e: e.dma_start(out=dbg_att, in_=attT[:, :, 0:NTILES * 128]), r=["attT"])
        S.finish()
        with nc.Block() as block:
            S.emit(block)
        print("arena peak words", AR.peak, "of", AR.n)
    return nc


_NC_CACHE = {}


def kernel(x_prompt, x_sample, cache_k, cache_v, cache_idx_k, state_wkv, state_shift, page_table,
           c_prompt, c_sample, norm1_g, norm2_g, w_ada, b_ada, w_in, q_norm_g, k_norm_g,
           rw_mu, rw_w0, rw_w2, rw_a0, rw_a2, rw_g2, rw_k_k, rw_k_a, rw_r_k, rw_ln_w, rw_ln_b,
           w_out, w_ffn_gate, w_ffn_up, w_ffn_down):
    f = lambda a: np.ascontiguousarray(np.asarray(a, dtype=np.float32))
    if "nc" not in _NC_CACHE:
        _NC_CACHE["nc"] = build_program()
    nc = _NC_CACHE["nc"]
    if NOCACHE:
        ck = np.zeros((128, 512), np.float32)
        cv = ck
        cik = np.zeros((128, 64), np.float32)
    else:
        ck = f(cache_k).reshape(NPOOL * PAGE, 512)
        cv = f(cache_v).reshape(NPOOL * PAGE, 512)
        cik = f(cache_idx_k).reshape(NPOOL * PAGE, 64)
    shared = {
        "cache_k": ck, "cache_v": cv, "cache_ik": cik,
        "norm1_g": f(norm1_g).reshape(D), "norm2_g": f(norm2_g).reshape(D),
        "w_ada": f(w_ada).reshape(D, 6 * D), "b_ada": f(b_ada).reshape(6 * D),
        "w_in": f(w_in).reshape(D, IN_COLS), "q_norm_g": f(q_norm_g).reshape(64), "k_norm_g": f(k_norm_g).reshape(64),
        "rw_mu": f(rw_mu).reshape(RW_COLS), "rw_w0": f(rw_w0).reshape(512), "rw_w2": f(rw_w2).reshape(64, 512),
        "rw_a0": f(rw_a0).reshape(512), "rw_a2": f(rw_a2).reshape(64, 512), "rw_g2": f(rw_g2).reshape(128, 512),
        "rw_k_k": f(rw_k_k).reshape(512), "rw_k_a": f(rw_k_a).reshape(512), "rw_r_k": f(rw_r_k).reshape(512),
        "rw_ln_w": f(rw_ln_w).reshape(512), "rw_ln_b": f(rw_ln_b).reshape(512),
        "w_out": f(w_out).reshape(D, D), "w_gate": f(w_ffn_gate).reshape(D, FFN), "w_up": f(w_ffn_up).reshape(D, FFN),
        "w_down": f(w_ffn_down).reshape(FFN, D),
    }
    xp = f(x_prompt)
    xs = f(x_sample).reshape(32, D)
    cp = f(c_prompt)
    cs = f(c_sample)
    sw = f(state_wkv).reshape(32, 8, 64, 64)
    ss = f(state_shift).reshape(32, RW_COLS)
    pt = np.ascontiguousarray(np.asarray(page_table, dtype=np.int32))
    in_maps = []
    for i in range(8):
        m = dict(shared)
        m["x_p"] = xp[i]
        m["x_s"] = np.ascontiguousarray(xs[4 * i:4 * i + 4])
        m["c5"] = np.ascontiguousarray(np.concatenate([cp[i:i + 1], cs[4 * i:4 * i + 4]], axis=0))
        m["st_wkv"] = np.ascontiguousarray(sw[4 * i:4 * i + 4])
        m["st_shift"] = np.ascontiguousarray(ss[4 * i:4 * i + 4])
        m["ptab"] = np.ascontiguousarray(pt[4 * i:4 * i + 4])
        in_maps.append(m)
    res = run_bass_kernel_spmd(nc, in_maps, core_ids=list(range(8)))
    R = res.results
    _NC_CACHE["last"] = R
    g = lambda name: np.stack([np.asarray(R[i][name]) for i in range(8)], axis=0)
    y_p = g("y_p")
    y_s = g("y_s").reshape(32, 1, D)
    k_p = g("k_po").reshape(1, 8, T, 8, 64)
    v_p = g("v_po").reshape(1, 8, T, 8, 64)
    ik_p = g("ik_po").reshape(1, 8, T, 64)
    wkv_p = g("wkv_po").reshape(1, 8, 8, 64, 64)
    sh_p = g("sh_po").reshape(1, 8, RW_COLS)
    k_s = g("k_so").reshape(1, 32, 1, 8, 64)
    v_s = g("v_so").reshape(1, 32, 1, 8, 64)
    ik_s = g("ik_so").reshape(1, 32, 1, 64)
    wkv_s = g("wkv_so").reshape(1, 32, 8, 64, 64)
    sh_s = g("sh_so").reshape(1, 32, RW_COLS)
    return (y_p, y_s, k_p, v_p, ik_p, wkv_p, sh_p, k_s, v_s, ik_s, wkv_s, sh_s)
```

```python
import os
import numpy as np
from contextlib import ExitStack
import concourse.bass as bass
import concourse.mybir as mybir
from concourse.bass_utils import run_bass_kernel_spmd

F32 = mybir.dt.float32
BF16 = mybir.dt.bfloat16
I32 = mybir.dt.int32
U32 = mybir.dt.uint32
F32R = mybir.dt.float32r
AF = mybir.ActivationFunctionType
ALU = mybir.AluOpType
AX = mybir.AxisListType

D = 1024
T = 2048
NT = T // 128
NS = 4
NPAGE = 64
PAGE = 128
NPOOL = 2560
IN_COLS = 4432
ATT_COLS = 2640
RW_COLS = 1792
FFN = 2816
NFC = FFN // 128
IDX_W_SCALE = (16 ** -0.5) * (64 ** -0.5)
RMS_EPS = 1e-6
GN_EPS = 64e-5
BIG = 1.0e30

_DT_SIZE = {F32: 4, BF16: 2, I32: 4, U32: 4}


def _prod(s):
    r = 1
    for x in s:
        r *= x
    return r


class Arena:
    def __init__(self, ap):
        self.ap = ap
        self.off = 0
        self.n = ap.shape[1]
        self.peak = 0

    def mark(self):
        return self.off

    def release(self, m):
        self.off = m

    def alloc(self, shape, dtype=F32, parts=128):
        if isinstance(shape, int):
            shape = [shape]
        ne = _prod(shape)
        nw = (ne * _DT_SIZE[dtype] + 3) // 4
        nw = (nw + 1) // 2 * 2
        assert self.off + nw <= self.n, f"arena overflow {self.off}+{nw}>{self.n}"
        v = self.ap[0:parts, self.off:self.off + nw]
        self.off += nw
        self.peak = max(self.peak, self.off)
        if dtype != F32:
            v = v.bitcast(dtype)
        v = v[:, 0:ne]
        if len(shape) == 2:
            v = v.rearrange("p (a b) -> p a b", a=shape[0])
        elif len(shape) == 3:
            v = v.rearrange("p (a b c) -> p a b c", a=shape[0], b=shape[1])
        return v


class Sched:
    CH = 2000
    NSEM = {"pe": 22, "act": 14, "dve": 18, "pool": 10}
    NDMA = 10
    NPDMA = 20

    def __init__(self, nc, es):
        self.nc = nc
        self.E = {"pe": nc.tensor, "act": nc.scalar, "dve": nc.vector, "pool": nc.gpsimd, "sp": nc.sync}
        self.sem = {e: [es.enter_context(nc.semaphore(f"s_{e}{k}")) for k in range(n)] for e, n in self.NSEM.items()}
        self.dsem = [es.enter_context(nc.semaphore(f"s_dma{k}")) for k in range(self.NDMA)]
        self.psem = [es.enter_context(nc.semaphore(f"s_pdma{k}")) for k in range(self.NPDMA)]
        self.pmark = None
        self.pnext = 0
        self.duse = [0] * self.NDMA
        self.dnext = 0
        self.ops = {e: [] for e in self.E}
        self.cnt = {e: 0 for e in self.NSEM}
        self.seen = {e: {} for e in self.E}
        self.lastw = {}
        self.readers = {}
        self.all_dma = []

    def _deps(self, eng, r, w):
        deps = []
        for x in r:
            if x in self.lastw:
                deps.append(self.lastw[x])
        for x in w:
            if x in self.lastw:
                deps.append(self.lastw[x])
            deps.extend(self.readers.get(x, []))
        waits = []
        seen = self.seen[eng]
        for tok in deps:
            if tok[0] == "e":
                _, e2, idx = tok
                if e2 == eng and eng == "pe":
                    continue
                if seen.get(e2, 0) >= idx:
                    continue
                seen[e2] = idx
            else:
                _, slot, val = tok
                if seen.get(("d", slot), 0) >= val:
                    continue
                seen[("d", slot)] = val
        return deps

    def _waits_for(self, eng, deps):
        need_e = {}
        need_d = {}
        for tok in deps:
            if tok[0] == "e":
                _, e2, idx = tok
                if e2 == eng and eng == "pe":
                    continue
                need_e[e2] = max(need_e.get(e2, 0), idx)
            else:
                _, slot, val = tok
                need_d[slot] = max(need_d.get(slot, 0), val)
        out = []
        seen = self.seen[eng]
        for e2, idx in need_e.items():
            if seen.get(e2, 0) >= idx:
                continue
            seen[e2] = idx
            k, v = (idx - 1) // self.CH, (idx - 1) % self.CH + 1
            out.append((self.sem[e2][k], v))
        for slot, val in need_d.items():
            if seen.get(("d", slot), 0) >= val:
                continue
            seen[("d", slot)] = val
            out.append((self.dsem[slot], val))
        return out

    def _collect(self, r, w):
        deps = []
        for x in r:
            if x in self.lastw:
                deps.append(self.lastw[x])
        for x in w:
            if x in self.lastw:
                deps.append(self.lastw[x])
            deps.extend(self.readers.get(x, []))
        return deps

    def _commit(self, tok, r, w):
        for x in r:
            self.readers.setdefault(x, []).append(tok)
        for x in w:
            self.lastw[x] = tok
            self.readers[x] = []

    def op(self, eng, fn, r=(), w=()):
        deps = self._collect(r, w)
        waits = self._waits_for(eng, deps)
        self.cnt[eng] += 1
        idx = self.cnt[eng]
        k = (idx - 1) // self.CH
        assert k < len(self.sem[eng]), f"too many ops on {eng}"
        self.ops[eng].append((waits, fn, self.sem[eng][k], 1))
        tok = ("e", eng, idx)
        if eng != "pe":
            self.seen[eng][eng] = max(self.seen[eng].get(eng, 0), 0)
        self._commit(tok, r, w)
        return tok

    def dma(self, q, fn, r=(), w=()):
        deps = self._collect(r, w)
        slot = self.dnext
        self.dnext = (self.dnext + 1) % self.NDMA
        if self.duse[slot] > 0:
            deps.append(("d", slot, 16 * self.duse[slot]))
        waits = self._waits_for(q, deps)
        self.duse[slot] += 1
        val = 16 * self.duse[slot]
        self.ops[q].append((waits, fn, self.dsem[slot], 16))
        tok = ("d", slot, val)
        self._commit(tok, r, w)
        self.all_dma.append(tok)
        return tok

    def pool_dma_batch(self, fns, r=(), w=()):
        assert len(fns) <= self.NPDMA
        deps = self._collect(r, w)
        waits = self._waits_for("pool", deps)
        psem = self.psem
        n = len(fns)

        def g(e):
            for i, fn in enumerate(fns):
                fn(e).then_inc(psem[i], 16)
            for i in range(n):
                e.wait_ge(psem[i], 16)
        self.ops["pool"].append((waits, ("raw", g), None, 0))
        pm = self.pmark
        tok = self.op("pool", lambda e: e.memset(pm, 0.0), r=r, w=list(w) + ["pmark"])
        self.barrier()

        def clr(e):
            for i in range(n):
                e.sem_clear(psem[i])
        self.ops["pool"].append(([], ("raw", clr), None, 0))
        return tok

    def pool_dma_once(self, fns, r=(), w=()):
        deps = self._collect(r, w)
        waits = self._waits_for("pool", deps)
        sems = [self.psem[self.pnext + i] for i in range(len(fns))]
        self.pnext += len(fns)
        assert self.pnext <= self.NPDMA

        def g(e):
            for sm, fn in zip(sems, fns):
                fn(e).then_inc(sm, 16)
            for sm in sems:
                e.wait_ge(sm, 16)
        self.ops["pool"].append((waits, ("raw", g), None, 0))
        pm = self.pmark
        return self.op("pool", lambda e: e.memset(pm, 0.0), r=r, w=list(w) + ["pmark"])

    def barrier(self):
        toks = [("e", e, c) for e, c in self.cnt.items() if c > 0]
        toks += [("d", s, 16 * u) for s, u in enumerate(self.duse) if u > 0]
        for eng in self.E:
            waits = self._waits_for(eng, toks)
            if waits:
                self.ops[eng].append((waits, None, None, 0))

    def finish(self):
        toks = [("d", s, 16 * u) for s, u in enumerate(self.duse) if u > 0]
        toks += [("e", e, c) for e, c in self.cnt.items() if c > 0]
        waits = self._waits_for("sp", toks)
        self.ops["sp"].append((waits, None, None, 0))

    def emit(self, block):
        def mk(ename):
            def body(e):
                for waits, fn, sem, inc in self.ops[ename]:
                    for s, v in waits:
                        e.wait_ge(s, v)
                    if isinstance(fn, tuple):
                        fn[1](e)
                    elif fn is not None:
                        ins = fn(e)
                        ins.then_inc(sem, inc)
            return body
        block.tensor(mk("pe"))
        block.scalar(mk("act"))
        block.vector(mk("dve"))
        block.gpsimd(mk("pool"))
        block.sync(mk("sp"))


def bc_last(ap, n):
    return ap.unsqueeze(len(ap.shape)).to_broadcast(list(ap.shape) + [n])


def bc_mid(ap, n):
    return ap.unsqueeze(1).to_broadcast([ap.shape[0], n, ap.shape[1]])


STAGE = int(os.environ.get("MK_STAGE", "99"))
ATT_CUT = int(os.environ.get("MK_ATT_CUT", "99"))
P2_CUT = int(os.environ.get("MK_P2_CUT", "99"))
SAMPLE_ATT = os.environ.get("MK_SAMPLE_ATT", "1") == "1"
SCAN_CUT = int(os.environ.get("MK_SCAN_CUT", "99"))
NOCACHE = os.environ.get("MK_NOCACHE", "0") == "1"


def build_program():
    nc = bass.Bass("TRN2", target_bir_lowering=False)

    def din(name, shape, dt=F32):
        return nc.dram_tensor(name, list(shape), dt, kind="ExternalInput").ap()

    def dout(name, shape, dt=F32):
        return nc.dram_tensor(name, list(shape), dt, kind="ExternalOutput").ap()

    x_p = din("x_p", [T, D])
    x_s = din("x_s", [NS, D])
    c5 = din("c5", [5, D])
    st_wkv = din("st_wkv", [NS, 8, 64, 64])
    st_shift = din("st_shift", [NS, RW_COLS])
    ptab = din("ptab", [NS, NPAGE], I32)
    npool_rows = 128 if NOCACHE else NPOOL * PAGE
    cache_k = din("cache_k", [npool_rows, 512])
    cache_v = din("cache_v", [npool_rows, 512])
    cache_ik = din("cache_ik", [npool_rows, 64])
    norm1_g = din("norm1_g", [D])
    norm2_g = din("norm2_g", [D])
    w_ada = din("w_ada", [D, 6 * D])
    b_ada = din("b_ada", [6 * D])
    w_in = din("w_in", [D, IN_COLS])
    q_norm_g = din("q_norm_g", [64])
    k_norm_g = din("k_norm_g", [64])
    rw_mu = din("rw_mu", [RW_COLS])
    rw_w0 = din("rw_w0", [512])
    rw_w2 = din("rw_w2", [64, 512])
    rw_a0 = din("rw_a0", [512])
    rw_a2 = din("rw_a2", [64, 512])
    rw_g2 = din("rw_g2", [128, 512])
    rw_k_k = din("rw_k_k", [512])
    rw_k_a = din("rw_k_a", [512])
    rw_r_k = din("rw_r_k", [512])
    rw_ln_w = din("rw_ln_w", [512])
    rw_ln_b = din("rw_ln_b", [512])
    w_out = din("w_out", [D, D])
    w_gate = din("w_gate", [D, FFN])
    w_up = din("w_up", [D, FFN])
    w_down = din("w_down", [FFN, D])

    y_p = dout("y_p", [T, D])
    y_s = dout("y_s", [NS, D])
    k_po = dout("k_po", [T, 512])
    v_po = dout("v_po", [T, 512])
    ik_po = dout("ik_po", [T, 64])
    wkv_po = dout("wkv_po", [8, 64, 64])
    sh_po = dout("sh_po", [RW_COLS])
    k_so = dout("k_so", [NS, 512])
    v_so = dout("v_so", [NS, 512])
    ik_so = dout("ik_so", [NS, 64])
    wkv_so = dout("wkv_so", [NS, 8, 64, 64])
    sh_so = dout("sh_so", [NS, RW_COLS])

    es = ExitStack()
    with es:
        arena_t = es.enter_context(nc.sbuf_tensor("arena", [128, 47640], F32))
        AR = Arena(arena_t[:, :])
        ptrow_t = es.enter_context(nc.sbuf_tensor("ptrow", [1, NS * NPAGE], I32))
        PS = [es.enter_context(nc.psum_tensor(f"psb{i}", [128, 512], F32))[:, :] for i in range(8)]
        S = Sched(nc, es)
        op, dma = S.op, S.dma
        S.pmark = AR.alloc([2], F32)

        Xall_r = es.enter_context(nc.sbuf_tensor("xall_r", [128, 192], F32R))[:, :]
        Xk_sel_r = es.enter_context(nc.sbuf_tensor("xksel_r", [128, 1024], F32R))[:, :].rearrange("p (a b) -> p a b", a=16)
        Vbd16_r = es.enter_context(nc.sbuf_tensor("vbd16_r", [128, 512], F32R))[:, :].rearrange("p (a b) -> p a b", a=8)
        R128_r = [es.enter_context(nc.sbuf_tensor(f"r128_r{i}", [128, 512], F32R))[:, :].rearrange("p (a b) -> p a b", a=8) for i in range(2)]
        Xall = Xall_r.bitcast(F32)
        STr = es.enter_context(nc.sbuf_tensor("st_r", [64, 512], F32R))[:, :]
        LA_sel_r = es.enter_context(nc.sbuf_tensor("lasel_r", [64, 2048], F32R))[:, :].rearrange("p (a b) -> p a b", a=16)
        iot = AR.alloc([128], I32)
        ident = AR.alloc([128], F32)
        identb = AR.alloc([128], BF16)
        negmask = AR.alloc([128], F32)
        ones_f = AR.alloc([128], F32)
        op("pool", lambda e: e.iota(iot, [[1, 128]], base=0, channel_multiplier=-1), w=["iot"])
        op("dve", lambda e: e.tensor_single_scalar(ident, iot, 0, ALU.is_equal), r=["iot"], w=["ident"])
        op("dve", lambda e: e.tensor_single_scalar(identb, iot, 0, ALU.is_equal), r=["iot"], w=["identb"])
        op("dve", lambda e: e.tensor_scalar(negmask, iot, 0, -BIG, ALU.is_gt, ALU.mult), r=["iot"], w=["negmask"])
        op("pool", lambda e: e.memset(ones_f, 1.0), w=["ones_f"])

        vecA = AR.alloc([78], F32)
        vecB = AR.alloc([56], F32, parts=64)
        modT = AR.alloc([48, 5], F32)
        A1 = AR.alloc([8, 5], F32)
        A2 = AR.alloc([8, 5], F32)
        gq_rep = AR.alloc([64], F32)
        gk_rep = AR.alloc([64], F32)
        m0 = AR.mark()
        c5t = AR.alloc([D], F32, parts=5)
        sct = AR.alloc([D], F32, parts=5)
        scT = AR.alloc([8, 5], F32)
        stA = AR.alloc([128], F32, parts=78)
        stB = AR.alloc([64], F32, parts=56)
        wst = [AR.alloc([8, 512], F32) for _ in range(2)]

        dma("sp", lambda e: e.dma_start(out=c5t, in_=c5), w=["c5t"])
        dma("sp", lambda e: e.dma_start(out=stA[0:48, :], in_=b_ada.rearrange("(a b) -> a b", b=128)), w=["stA0"])
        dma("sp", lambda e: e.dma_start(out=stA[48:56, :], in_=norm1_g.rearrange("(a b) -> a b", b=128)), w=["stA1"])
        dma("sp", lambda e: e.dma_start(out=stA[56:64, :], in_=norm2_g.rearrange("(a b) -> a b", b=128)), w=["stA2"])
        dma("sp", lambda e: e.dma_start(out=stA[64:78, :], in_=rw_mu.rearrange("(a b) -> a b", b=128)), w=["stA3"])
        for i, v in enumerate([rw_w0, rw_a0, rw_k_k, rw_k_a, rw_ln_w, rw_ln_b, rw_r_k]):
            dma("sp", lambda e, v=v, i=i: e.dma_start(out=stB[8 * i:8 * i + 8, :], in_=v.rearrange("(a b) -> a b", b=64)), w=[f"stB{i}"])
        dma("sp", lambda e: e.dma_start(out=gq_rep, in_=q_norm_g.partition_broadcast(128)), w=["gq_rep"])
        dma("sp", lambda e: e.dma_start(out=gk_rep, in_=k_norm_g.partition_broadcast(128)), w=["gk_rep"])

        op("act", lambda e: e.activation(sct, c5t, AF.Silu), r=["c5t"], w=["sct"])
        for kc in range(8):
            op("pe", lambda e, kc=kc: e.transpose(PS[0][:, kc * 8:kc * 8 + 5], sct[:, kc * 128:(kc + 1) * 128], ident[0:5, 0:5]),
               r=["sct", "ident"], w=["ps0"])
        op("dve", lambda e: e.tensor_copy(scT, PS[0][:, 0:64].rearrange("p (a b) -> p a b", b=8)[:, :, 0:5]), r=["ps0"], w=["scT"])
        op("pe", lambda e: e.transpose(PS[1][:, 0:78], stA, ident[0:78, 0:78]), r=["stA0", "stA1", "stA2", "stA3", "ident"], w=["ps1"])
        op("dve", lambda e: e.tensor_copy(vecA, PS[1][:, 0:78]), r=["ps1"], w=["vecA"])
        op("pe", lambda e: e.transpose(PS[1][0:64, 128:184], stB, ident[0:56, 0:56]), r=[f"stB{i}" for i in range(7)] + ["ident"], w=["ps1"])
        op("dve", lambda e: e.tensor_copy(vecB, PS[1][0:64, 128:184]), r=["ps1"], w=["vecB"])
        w_ada_v = w_ada.rearrange("(kc p) n -> p kc n", p=128)
        for nb in range(12):
            b = nb % 2
            dma("sp", lambda e, nb=nb, b=b: e.dma_start(out=wst[b], in_=w_ada_v[:, :, nb * 512:(nb + 1) * 512]), w=[f"wst{b}"])
            for fc in range(4):
                col = (nb * 4 + fc) * 8
                for kc in range(8):
                    op("pe", lambda e, b=b, fc=fc, kc=kc, col=col: e.matmul(PS[2][:, col:col + 5], lhsT=wst[b][:, kc, fc * 128:(fc + 1) * 128],
                                                                            rhs=scT[:, kc, :], start=(kc == 0), stop=(kc == 7)),
                       r=[f"wst{b}", "scT"], w=["ps2"])
        op("dve", lambda e: e.tensor_tensor(modT, PS[2][:, 0:384].rearrange("p (a b) -> p a b", b=8)[:, :, 0:5], bc_last(vecA[:, 0:48], 5), ALU.add),
           r=["ps2", "vecA"], w=["modT"])
        op("dve", lambda e: e.tensor_scalar(A1, modT[:, 8:16, :], 1.0, None, ALU.add), r=["modT"], w=["A1"])
        op("dve", lambda e: e.tensor_tensor(A1, A1, bc_last(vecA[:, 48:56], 5), ALU.mult), r=["A1", "vecA"], w=["A1"])
        op("dve", lambda e: e.tensor_scalar(A2, modT[:, 32:40, :], 1.0, None, ALU.add), r=["modT"], w=["A2"])
        op("dve", lambda e: e.tensor_tensor(A2, A2, bc_last(vecA[:, 56:64], 5), ALU.mult), r=["A2", "vecA"], w=["A2"])
        S.barrier()
        AR.release(m0)

        if STAGE <= 0:
            dbg_mod = nc.dram_tensor("dbg_mod", [128, 240], F32, kind="ExternalOutput").ap()
            dma("sp", lambda e: e.dma_start(out=dbg_mod, in_=modT.rearrange("p a b -> p (a b)")), r=["modT"])
            S.finish()
            with nc.Block() as block:
                S.emit(block)
            return nc

        shift1 = modT[:, 0:8, :]
        gate1 = modT[:, 16:24, :]
        shift2 = modT[:, 24:32, :]
        gate2 = modT[:, 40:48, :]

        m_attT = AR.mark()
        attT = AR.alloc([4, T + NS], BF16)
        m_att = AR.mark()
        win_att = AR.alloc([8, ATT_COLS], BF16)
        KT = AR.alloc([4, T], BF16)
        Vaug = AR.alloc([NT, 8, 65], BF16)
        kiT = AR.alloc([T], BF16)
        wabs = AR.alloc([NT, 16], F32)
        wsgn = AR.alloc([NT, 16], F32)
        qiTs = AR.alloc([8, NS], BF16)
        qs32 = AR.alloc([512], F32, parts=NS)
        ks32 = AR.alloc([512], F32, parts=NS)
        vs32 = AR.alloc([512], F32, parts=NS)
        kis32 = AR.alloc([64], F32, parts=NS)
        ws_abs = AR.alloc([16], F32, parts=NS)
        ws_sgn = AR.alloc([16], F32, parts=NS)
        m1 = AR.mark()
        wst = [AR.alloc([8, 512], F32) for _ in range(2)]
        w_in_v = w_in.rearrange("(kc p) n -> p kc n", p=128)
        nblk = 0
        for (dst, c0, c1) in [(win_att, 0, ATT_COLS)]:
            c = c0
            while c < c1:
                n = min(512, c1 - c)
                b = nblk % 2
                dma("sp", lambda e, b=b, c=c, n=n: e.dma_start(out=wst[b][:, :, 0:n], in_=w_in_v[:, :, c:c + n]), w=[f"wst{b}"])
                op("pool", lambda e, b=b, c=c, n=n, dst=dst, c0=c0: e.tensor_copy(dst[:, :, c - c0:c - c0 + n], wst[b][:, :, 0:n]),
                   r=[f"wst{b}"], w=["win"])
                c += n
                nblk += 1
        op("pool", lambda e: e.memset(Vaug[:, :, :, 64:65], 1.0), w=["Vaug"])
        S.barrier()
        AR.release(m1)

        xt = [AR.alloc([D], F32)] * 2
        xn = AR.alloc([D], F32)
        hT = AR.alloc([8, 512], BF16)
        scr = [AR.alloc([512], F32) for _ in range(3)]
        small = AR.alloc([64], F32)
        qnb = AR.alloc([512], BF16)
        knb = AR.alloc([512], BF16)
        kidup = AR.alloc([128], BF16)
        qT = AR.alloc([4, 512], BF16)
        qiT = AR.alloc([8, 512], BF16)
        hTs = AR.alloc([8, NS], BF16)
        v32 = AR.alloc([512], F32)
        ki32 = AR.alloc([64], F32)
        PSb = [p.bitcast(BF16) for p in PS]
        Ibuf = AR.alloc([T], F32)
        Rbuf = [AR.alloc([512], F32) for _ in range(2)]
        maskb = AR.alloc([T], BF16)
        maskT = AR.alloc([NT, 128], BF16)
        PTb = [AR.alloc([512], BF16) for _ in range(2)]
        PmT = [AR.alloc([4, 128], BF16) for _ in range(2)]
        attb = AR.alloc([512], BF16)
        bs = AR.alloc([16], F32)
        NIT = int(os.environ.get("MK_NIT", "18"))
        NTILES = int(os.environ.get("MK_NT", str(NT)))

        def rms_rstd(ssum, n, eps, dst, P):
            op("dve", lambda e: e.tensor_scalar(dst, ssum, 1.0 / n, eps, ALU.mult, ALU.add), r=["small"], w=["small"])
            op("act", lambda e: e.activation(dst, dst, AF.Sqrt), r=["small"], w=["small"])
            op("dve", lambda e: e.reciprocal(dst, dst), r=["small"], w=["small"])

        def front_tile(ti, P, x_src, cond, hT_dst, k_dst, v_dst, ik_dst, tcol):
            b = 0
            xtb = xt[b]
            dma("sp", lambda e: e.dma_start(out=xtb[0:P, :], in_=x_src), w=[f"xt{b}"])
            ss = small[0:P, 0:1]
            rs = small[0:P, 1:2]
            op("act", lambda e: e.activation(xn[0:P, :], xtb[0:P, :], AF.Square, accum_out=ss), r=[f"xt{b}"], w=["xn", "small"])
            rms_rstd(ss, D, RMS_EPS, rs, P)
            op("dve", lambda e: e.tensor_scalar(xn[0:P, :], xtb[0:P, :], rs, None, ALU.mult), r=[f"xt{b}", "small"], w=["xn"])
            for half in range(2):
                pb = PS[half]
                for j in range(4):
                    kc = half * 4 + j
                    op("pe", lambda e, kc=kc, j=j, pb=pb: e.transpose(pb[:, j * 128:j * 128 + P], xn[0:P, kc * 128:(kc + 1) * 128], ident[0:P, 0:P]),
                       r=["xn", "ident"], w=[f"ps{half}"])
                if cond is None:
                    for j in range(4):
                        kc = half * 4 + j
                        op("act", lambda e, kc=kc, j=j, pb=pb: e.activation(hT_dst[:, kc, tcol:tcol + P], pb[:, j * 128:j * 128 + P], AF.Identity,
                                                                          bias=shift1[:, kc, 0:1], scale=A1[:, kc, 0:1]),
                           r=[f"ps{half}", "A1", "modT"], w=["hT"])
                else:
                    for j in range(4):
                        kc = half * 4 + j
                        op("dve", lambda e, kc=kc, j=j, pb=pb: e.tensor_tensor(scr[0][:, 0:P], pb[:, j * 128:j * 128 + P], A1[:, kc, 1:1 + P], ALU.mult),
                           r=[f"ps{half}", "A1"], w=["scr0"])
                        op("dve", lambda e, kc=kc: e.tensor_tensor(hT_dst[:, kc, tcol:tcol + P], scr[0][:, 0:P], shift1[:, kc, 1:1 + P], ALU.add),
                           r=["scr0", "modT"], w=["hT"])
            blocks = [(2, 0), (3, 512), (4, 1024), (5, 2560)]
            for (pbi, c0) in blocks:
                n = 512 if c0 < 2560 else 80
                for kc in range(8):
                    op("pe", lambda e, pbi=pbi, c0=c0, n=n, kc=kc: e.matmul(PS[pbi][0:P, 0:n], lhsT=hT_dst[:, kc, tcol:tcol + P],
                                                                          rhs=win_att[:, kc, c0:c0 + n], start=(kc == 0), stop=(kc == 7)),
                       r=["hT", "win"], w=[f"ps{pbi}"])
            for (pbi, grep, dstb, is_k) in [(2, gq_rep, qnb, False), (3, gk_rep, knb, True)]:
                pq = PS[pbi][0:P, :]
                ssq = small[0:P, 8:16]
                rq = small[0:P, 16:24]
                op("act", lambda e, pq=pq: e.activation(scr[0][0:P, :], pq, AF.Square), r=[f"ps{pbi}"], w=["scr0"])
                op("dve", lambda e, ssq=ssq: e.tensor_reduce(ssq, scr[0][0:P, :].rearrange("p (h d) -> p h d", d=64), AX.X, ALU.add), r=["scr0"], w=["small"])
                rms_rstd(ssq, 64, RMS_EPS, rq, P)
                op("dve", lambda e, pq=pq, rq=rq: e.tensor_tensor(scr[1][0:P, :].rearrange("p (h d) -> p h d", d=64), pq.rearrange("p (h d) -> p h d", d=64),
                                                               bc_last(rq, 64), ALU.mult), r=[f"ps{pbi}", "small"], w=["scr1"])
                if is_k:
                    op("pool", lambda e, grep=grep: e.tensor_tensor(scr[2][0:P, :].rearrange("p (h d) -> p h d", d=64), scr[1][0:P, :].rearrange("p (h d) -> p h d", d=64),
                                                                  bc_mid(grep[0:P, :], 8), ALU.mult), r=["scr1", "gk_rep"], w=["scr2"])
                    dma("sp", lambda e: e.dma_start(out=k_dst, in_=scr[2][0:P, :]), r=["scr2"])
                    op("pool", lambda e, dstb=dstb: e.tensor_copy(dstb[0:P, :], scr[2][0:P, :]), r=["scr2"], w=["knb"])
                else:
                    op("pool", lambda e, grep=grep, dstb=dstb: e.tensor_tensor(dstb[0:P, :].rearrange("p (h d) -> p h d", d=64), scr[1][0:P, :].rearrange("p (h d) -> p h d", d=64),
                                                                             bc_mid(grep[0:P, :], 8), ALU.mult), r=["scr1", "gq_rep"], w=["qnb"])
                    if cond is not None:
                        op("pool", lambda e: e.tensor_tensor(qs32.rearrange("p (h d) -> p h d", d=64), scr[1][0:P, :].rearrange("p (h d) -> p h d", d=64),
                                                             bc_mid(gq_rep[0:P, :], 8), ALU.mult), r=["scr1", "gq_rep"], w=["qs32"])
            is_s = cond is not None
            op("act", lambda e: e.activation(v32[0:P, :], PS[4][0:P, :], AF.Copy), r=["ps4"], w=["v32"])
            dma("sp", lambda e: e.dma_start(out=v_dst, in_=v32[0:P, :]), r=["v32"])
            if not is_s:
                op("pool", lambda e: e.tensor_copy(Vaug[:, ti, :, 0:64], v32.rearrange("p (h d) -> p h d", d=64)), r=["v32"], w=["Vaug"])
            else:
                op("pool", lambda e: e.tensor_copy(vs32, v32[0:P, :]), r=["v32"], w=["vs32"])
                op("pool", lambda e: e.tensor_copy(ks32, scr[2][0:P, :]), r=["scr2"], w=["ks32"])
            sk = small[0:P, 2:3]
            rk = small[0:P, 3:4]
            op("act", lambda e: e.activation(scr[0][0:P, 0:64], PS[5][0:P, 0:64], AF.Square, accum_out=sk), r=["ps5"], w=["scr0", "small"])
            rms_rstd(sk, 64, RMS_EPS, rk, P)
            op("dve", lambda e: e.tensor_scalar(ki32[0:P, :], PS[5][0:P, 0:64], rk, None, ALU.mult), r=["ps5", "small"], w=["ki32"])
            dma("sp", lambda e: e.dma_start(out=ik_dst, in_=ki32[0:P, :]), r=["ki32"])
            wa = ws_abs if is_s else wabs[:, ti, :]
            wsg = ws_sgn if is_s else wsgn[:, ti, :]
            op("act", lambda e: e.activation(wa, PS[5][0:P, 64:80], AF.Abs, scale=IDX_W_SCALE), r=["ps5"], w=["wabs"])
            op("act", lambda e: e.activation(wsg, PS[5][0:P, 64:80], AF.Sign), r=["ps5"], w=["wsgn"])
            if is_s:
                op("pool", lambda e: e.tensor_copy(kis32, ki32[0:P, :]), r=["ki32"], w=["kis32"])
                return
            tl = ti % 4
            op("pool", lambda e: e.tensor_copy(kidup[:, 0:64], ki32), r=["ki32"], w=["kidup"])
            op("pool", lambda e: e.tensor_copy(kidup[:, 64:128], ki32), r=["ki32"], w=["kidup"])
            for hp in range(4):
                op("pe", lambda e, hp=hp: e.transpose(PSb[6][:, hp * 128:(hp + 1) * 128], qnb[:, hp * 128:(hp + 1) * 128], identb), r=["qnb", "identb"], w=["ps6"])
            op("pe", lambda e: e.transpose(PSb[6][:, 512:640], kidup, identb), r=["kidup", "identb"], w=["ps6"])
            op("act", lambda e: e.activation(qT[:, :, tl * 128:(tl + 1) * 128], PSb[6][:, 0:512].rearrange("p (a b) -> p a b", a=4), AF.Copy), r=["ps6"], w=["qT"])
            op("act", lambda e: e.activation(kiT[:, ti * 128:(ti + 1) * 128], PSb[6][:, 512:640], AF.Copy), r=["ps6"], w=["kiT"])
            for hp in range(4):
                op("pe", lambda e, hp=hp: e.transpose(PSb[7][:, hp * 128:(hp + 1) * 128], knb[:, hp * 128:(hp + 1) * 128], identb), r=["knb", "identb"], w=["ps7"])
            op("act", lambda e: e.activation(KT[:, :, ti * 128:(ti + 1) * 128], PSb[7][:, 0:512].rearrange("p (a b) -> p a b", a=4), AF.Copy), r=["ps7"], w=["KT"])

        def qi_group(hT_src, ncols, dst):
            for c in range(8):
                pb = PS[c % 2]
                for kc in range(8):
                    op("pe", lambda e, c=c, kc=kc, pb=pb: e.matmul(pb[:, 0:ncols], lhsT=win_att[:, kc, 1536 + c * 128:1536 + (c + 1) * 128],
                                                                  rhs=hT_src[:, kc, 0:ncols], start=(kc == 0), stop=(kc == 7)), r=["hT", "win"], w=[f"ps{c % 2}"])
                op("act", lambda e, c=c, pb=pb: e.activation(dst[:, c, 0:ncols], pb[:, 0:ncols], AF.Copy), r=[f"ps{c % 2}"], w=["qiT"])


        def attention_tile(ti):
            tl = ti % 4
            L = (ti + 1) * 128
            nsp = (L + 511) // 512
            tq = slice(tl * 128, (tl + 1) * 128)
            cnt = [0, 0]
            for h in range(16):
                hp, half = h // 2, h % 2
                pr_ = slice(half * 64, (half + 1) * 64)
                for sp in range(nsp):
                    s0 = sp * 512
                    n = min(512, L - s0)
                    bk = 2 * half + cnt[half] % 2
                    cnt[half] += 1
                    rb = (h * nsp + sp) % 2
                    op("pe", lambda e, bk=bk, n=n, pr_=pr_, hp=hp, s0=s0: e.matmul(PS[bk][:, 0:n], lhsT=qiT[pr_, hp, tq], rhs=kiT[pr_, s0:s0 + n], start=True, stop=True),
                       r=["qiT", "kiT"], w=[f"ps{bk}"])
                    op("act", lambda e, bk=bk, n=n, rb=rb, h=h: e.activation(Rbuf[rb][:, 0:n], PS[bk][:, 0:n], AF.Relu, scale=wabs[:, ti, h:h + 1]),
                       r=[f"ps{bk}", "wabs"], w=[f"R{rb}"])
                    if h == 0:
                        op("dve", lambda e, n=n, rb=rb, s0=s0: e.tensor_scalar(Ibuf[:, s0:s0 + n], Rbuf[rb][:, 0:n], wsgn[:, ti, 0:1], None, ALU.mult),
                           r=[f"R{rb}", "wsgn"], w=["I"])
                    else:
                        op("dve", lambda e, n=n, rb=rb, s0=s0, h=h: e.scalar_tensor_tensor(Ibuf[:, s0:s0 + n], Rbuf[rb][:, 0:n], wsgn[:, ti, h:h + 1], Ibuf[:, s0:s0 + n], ALU.mult, ALU.add),
                           r=[f"R{rb}", "wsgn", "I"], w=["I"])
            if ATT_CUT <= 1:
                return
            lo, hi, step, mid, cntv, tmp, tau = (bs[:, i:i + 1] for i in range(7))
            if ti >= 2:
                op("dve", lambda e: e.tensor_reduce(hi, Ibuf[:, 0:L], AX.X, ALU.max), r=["I"], w=["bs"])
                op("dve", lambda e: e.tensor_reduce(lo, Ibuf[:, 0:L], AX.X, ALU.min), r=["I"], w=["bs"])
            op("dve", lambda e: e.tensor_tensor(Ibuf[:, ti * 128:(ti + 1) * 128], Ibuf[:, ti * 128:(ti + 1) * 128], negmask, ALU.add), r=["I", "negmask"], w=["I"])
            if ti >= 2:
                op("dve", lambda e: e.tensor_tensor(step, hi, lo, ALU.subtract), r=["bs"], w=["bs"])
                for it in range(NIT):
                    op("dve", lambda e: e.tensor_scalar(step, step, 0.5, None, ALU.mult), r=["bs"], w=["bs"])
                    op("dve", lambda e: e.tensor_tensor(mid, lo, step, ALU.add), r=["bs"], w=["bs"])
                    op("dve", lambda e: e.tensor_scalar(maskb[:, 0:L], Ibuf[:, 0:L], mid, 0.0, ALU.is_ge, ALU.add, accum_out=cntv), r=["I", "bs"], w=["maskb", "bs"])
                    op("dve", lambda e: e.scalar_tensor_tensor(tmp, cntv, 255.5, step, ALU.is_ge, ALU.mult), r=["bs"], w=["bs"])
                    op("dve", lambda e: e.tensor_tensor(lo, lo, tmp, ALU.add), r=["bs"], w=["bs"])
                thr = lo
            else:
                op("dve", lambda e: e.memset(tau, -1.0e29), w=["bs"])
                thr = tau
            op("dve", lambda e: e.tensor_scalar(maskb[:, 0:L], Ibuf[:, 0:L], thr, None, ALU.is_ge), r=["I", "bs"], w=["maskb"])
            if ATT_CUT <= 2:
                return
            for j0 in range(0, ti + 1, 8):
                nj = min(8, ti + 1 - j0)
                for j in range(nj):
                    sj = j0 + j
                    op("pe", lambda e, j=j, sj=sj: e.transpose(PSb[6][:, j * 128:(j + 1) * 128], maskb[:, sj * 128:(sj + 1) * 128], identb), r=["maskb", "identb"], w=["ps6"])
                op("act", lambda e, j0=j0, nj=nj: e.activation(maskT[:, j0:j0 + nj, :], PSb[6][:, 0:nj * 128].rearrange("p (a b) -> p a b", b=128), AF.Copy), r=["ps6"], w=["maskT"])
            if ATT_CUT <= 3:
                return
            cnt = [0, 0]
            pcount = 0
            for h in range(8):
                hp, half = h // 2, h % 2
                pr_ = slice(half * 64, (half + 1) * 64)
                pvb = 4 + h // 4
                pvc = (h % 4) * 65
                for sp in range(nsp):
                    j0 = sp * 4
                    nj = min(4, ti + 1 - j0)
                    bk = 2 * half + cnt[half] % 2
                    cnt[half] += 1
                    pb = pcount % 2
                    pcount += 1
                    for j in range(nj):
                        sj = j0 + j
                        op("pe", lambda e, bk=bk, j=j, sj=sj, pr_=pr_, hp=hp: e.matmul(PS[bk][:, j * 128:(j + 1) * 128], lhsT=KT[pr_, hp, sj * 128:(sj + 1) * 128],
                                                                                      rhs=qT[pr_, hp, tq], start=True, stop=True), r=["KT", "qT"], w=[f"ps{bk}"])
                    op("act", lambda e, bk=bk, nj=nj, pb=pb: e.activation(PTb[pb][:, 0:nj * 128], PS[bk][:, 0:nj * 128], AF.Exp, scale=0.125), r=[f"ps{bk}"], w=[f"PT{pb}"])
                    op("pool", lambda e, nj=nj, pb=pb, j0=j0: e.tensor_tensor(PmT[pb][:, 0:nj, :], PTb[pb][:, 0:nj * 128].rearrange("p (a b) -> p a b", b=128),
                                                                             maskT[:, j0:j0 + nj, :], ALU.mult), r=[f"PT{pb}", "maskT"], w=[f"PmT{pb}"])
                    for j in range(nj):
                        sj = j0 + j
                        op("pe", lambda e, pvb=pvb, pvc=pvc, pb=pb, j=j, sj=sj, h=h: e.matmul(PS[pvb][:, pvc:pvc + 65], lhsT=PmT[pb][:, j, :], rhs=Vaug[:, sj, h, :],
                                                                                           start=(sj == 0), stop=(sj == ti)), r=[f"PmT{pb}", "Vaug"], w=[f"ps{pvb}"])
            if ATT_CUT <= 4:
                return
            rden = bs[:, 8:16]
            for hb in range(2):
                pv = PS[4 + hb][:, 0:260].rearrange("p (h d) -> p h d", d=65)
                op("dve", lambda e, pv=pv, hb=hb: e.reciprocal(rden[:, hb * 4:(hb + 1) * 4], pv[:, :, 64]), r=[f"ps{4 + hb}"], w=["bs"])
                op("dve", lambda e, pv=pv, hb=hb: e.tensor_tensor(attb[:, hb * 256:(hb + 1) * 256].rearrange("p (h d) -> p h d", d=64), pv[:, :, 0:64],
                                                                 bc_last(rden[:, hb * 4:(hb + 1) * 4], 64), ALU.mult), r=[f"ps{4 + hb}", "bs"], w=["attb"])
            for hp in range(4):
                op("pe", lambda e, hp=hp: e.transpose(PSb[7][:, hp * 128:(hp + 1) * 128], attb[:, hp * 128:(hp + 1) * 128], identb), r=["attb", "identb"], w=["ps7"])
            op("act", lambda e: e.activation(attT[:, :, ti * 128:(ti + 1) * 128], PSb[7][:, 0:512].rearrange("p (a b) -> p a b", a=4), AF.Copy), r=["ps7"], w=["attT"])

        for g in range((NTILES + 3) // 4):
            for tl in range(4):
                ti = g * 4 + tl
                if ti >= NTILES:
                    break
                r0 = ti * 128
                front_tile(ti, 128, x_p[r0:r0 + 128, :], None, hT, k_po[r0:r0 + 128, :], v_po[r0:r0 + 128, :], ik_po[r0:r0 + 128, :], tl * 128)
            qi_group(hT, 512, qiT)
            if STAGE >= 2:
                for tl in range(4):
                    ti = g * 4 + tl
                    if ti < NTILES:
                        attention_tile(ti)
        front_tile(16, NS, x_s, 1, hTs, k_so, v_so, ik_so, 0)
        qi_group(hTs, NS, qiTs)
        if not SAMPLE_ATT:
            op("pool", lambda e: e.memset(attT[:, :, T:T + NS], 0.0), w=["attT"])
        else:
            S.barrier()
            AR.release(m1)
            NP1 = NPAGE + 1
            ptb = AR.alloc([NPAGE], I32)
            physf = AR.alloc([NS, NPAGE], F32)
            idx32 = AR.alloc([NS, NPAGE], I32)
            kib = AR.alloc([NP1, 64], F32)
            graw = AR.alloc([8320], F32)
            kdup = graw[:, 0:4160].bitcast(BF16).rearrange("p (a b) -> p a b", b=128)
            kiTa = graw[:, 4160:8320].bitcast(BF16).rearrange("p (a b) -> p a b", b=128)
            Gpg = graw[0:NPAGE, 0:8192]
            idxp = AR.alloc([NS], I32)
            ikscr = nc.dram_tensor("ikscr", [NS, NPAGE, PAGE * 64], F32, kind="Internal").ap()
            cache_ik_pg = cache_ik.rearrange("(n p) d -> n (p d)", p=128)
            knew = AR.alloc([NS, 64], F32)
            selB = AR.alloc([NS, 128], F32, parts=NS)
            wsig = AR.alloc([16], F32, parts=NS)
            wbc = AR.alloc([16], F32)
            rl = AR.alloc([NP1, 16], F32)
            score4 = AR.alloc([NS, NP1], F32)
            mask4 = AR.alloc([NS, NP1], F32)
            maskp = AR.alloc([NS, NPAGE], F32)
            negcols = AR.alloc([NS], F32)
            sb = AR.alloc([64], F32)
            iop_i = AR.alloc([2], I32)
            iop = AR.alloc([2], F32)
            islot_i = AR.alloc([256], I32)
            islot = AR.alloc([256], F32)
            ustrict = AR.alloc([128], F32)
            rank = AR.alloc([NS, NPAGE], F32)
            offs = AR.alloc([NS, NPAGE], F32)
            OH = [AR.alloc([256], F32) for _ in range(2)]
            selidx_f = AR.alloc([8], F32)
            selidx = AR.alloc([8], I32)
            Ksel = AR.alloc([2, 512], F32)
            Vsel = AR.alloc([2, 512], F32)
            scb = AR.alloc([2, 8], F32)
            Pb = AR.alloc([2, 8], F32)
            validb = AR.alloc([2], F32)
            scn = AR.alloc([8], F32, parts=NS)
            Pn = AR.alloc([8], F32, parts=NS)
            Pnn = AR.alloc([8], F32, parts=NS)
            mnew = AR.alloc([4], F32, parts=NS)
            prodn = AR.alloc([512], F32, parts=NS)
            wvn = prodn
            rdenb = AR.alloc([8], F32)
            cache_pages = cache_ik.rearrange("(n p) d -> n p d", p=128)
            ptrow = ptrow_t[:, :]
            dma("sp", lambda e: e.dma_start(out=ptrow, in_=ptab.rearrange("b c -> (b c)").unsqueeze(0)), w=["ptrow"])

            op("pool", lambda e: e.iota(iop_i, [[128, 2]], base=0, channel_multiplier=1), w=["iop_i"])
            op("dve", lambda e: e.tensor_copy(iop, iop_i), r=["iop_i"], w=["iop"])
            op("pool", lambda e: e.iota(islot_i, [[1, 256]], base=0, channel_multiplier=0), w=["islot_i"])
            op("dve", lambda e: e.tensor_copy(islot, islot_i), r=["islot_i"], w=["islot"])
            op("dve", lambda e: e.tensor_single_scalar(ustrict, iot, 0, ALU.is_gt), r=["iot"], w=["ustrict"])
            op("dve", lambda e: e.tensor_copy(selB, bc_last(ident[0:NS, 0:NS], 128)), r=["ident"], w=["selB"])
            op("dve", lambda e: e.tensor_scalar(negcols, ident[:, 0:NS], -1.0, BIG, ALU.add, ALU.mult), r=["ident"], w=["negcols"])
            op("dve", lambda e: e.tensor_tensor(wsig, ws_abs, ws_sgn, ALU.mult), r=["wabs", "wsgn"], w=["wsig"])
            op("pool", lambda e: e.memset(knew, 0.0), w=["knew"])
            op("dve", lambda e: e.tensor_tensor(knew[0:NS, :, :], bc_mid(kis32, NS), bc_last(ident[0:NS, 0:NS], 64), ALU.mult), r=["kis32", "ident", "knew"], w=["knew"])
            op("dve", lambda e: e.tensor_tensor(prodn, qs32, ks32, ALU.mult), r=["qs32", "ks32"], w=["prodn"])
            op("dve", lambda e: e.tensor_reduce(scn, prodn.rearrange("p (h d) -> p h d", d=64), AX.X, ALU.add), r=["prodn"], w=["scn"])
            op("act", lambda e: e.activation(Pn, scn, AF.Exp, scale=0.125), r=["scn"], w=["Pn"])

            for b_ in range(NS):
                dma("sp", lambda e, b_=b_: e.dma_start(out=ptb, in_=ptab[b_].partition_broadcast(128)), w=["ptb"])
                op("dve", lambda e, b_=b_: e.tensor_scalar(physf[:, b_, :], ptb, 128.0, iop[:, 0:1], ALU.mult, ALU.add), r=["ptb", "iop"], w=["physf"])
                op("dve", lambda e, b_=b_: e.tensor_copy(idx32[:, b_, :], physf[:, b_, :]), r=["physf"], w=["idx32"])
                dma("sp", lambda e, b_=b_: e.dma_start(out=idxp[0:NPAGE, b_:b_ + 1], in_=ptab[b_].rearrange("(c o) -> c o", o=1)), w=["idxp"])
                S.pool_dma_once([lambda e, b_=b_: e.indirect_dma_start(out=Gpg, out_offset=None, in_=cache_ik_pg,
                                                                       in_offset=bass.IndirectOffsetOnAxis(ap=idxp[0:NPAGE, b_:b_ + 1], axis=0))],
                                r=["idxp"], w=["kdup", "kiTa"])
                dma("sp", lambda e, b_=b_: e.dma_start(out=ikscr[b_], in_=Gpg), r=["kdup", "kiTa"], w=["ikscr"])
                dma("sp", lambda e, b_=b_: e.dma_start(out=kib[:, 0:NPAGE, :], in_=ikscr[b_].rearrange("c (s d) -> s c d", d=64)), r=["ikscr"], w=["kib"])
                op("pool", lambda e, b_=b_: e.tensor_copy(kib[:, NPAGE, :], knew[:, b_, :]), r=["knew"], w=["kib"])
                op("pool", lambda e: e.tensor_copy(kdup[:, :, 0:64], kib), r=["kib"], w=["kdup"])
                op("act", lambda e: e.activation(kdup[:, :, 64:128], kib, AF.Copy), r=["kib"], w=["kdup"])
                for c0 in range(0, NP1, 8):
                    ncc = min(8, NP1 - c0)
                    for j in range(ncc):
                        op("pe", lambda e, c0=c0, j=j: e.transpose(PSb[6][:, j * 128:(j + 1) * 128], kdup[:, c0 + j, :], identb), r=["kdup", "identb"], w=["ps6"])
                    op("act", lambda e, c0=c0, ncc=ncc: e.activation(kiTa[:, c0:c0 + ncc, :], PSb[6][:, 0:ncc * 128].rearrange("p (a b) -> p a b", b=128), AF.Copy),
                       r=["ps6"], w=["kiTa"])
                for c in range(NP1):
                    for half in range(2):
                        pr_ = slice(half * 64, (half + 1) * 64)
                        bk = 4 * half + (0 if c < NPAGE else 1)
                        col = (c % NPAGE) * 8
                        op("pe", lambda e, bk=bk, col=col, pr_=pr_, c=c, b_=b_: e.matmul(PS[bk][:, col:col + 8], lhsT=kiTa[pr_, c, :],
                                                                                       rhs=qiTs[pr_, :, b_], start=True, stop=True),
                           r=["kiTa", "qiT"], w=[f"ps{bk}"])
                op("pe", lambda e, b_=b_: e.matmul(PS[3][:, 0:16], lhsT=selB[:, b_, :], rhs=wsig, start=True, stop=True), r=["selB", "wsig"], w=["ps3"])
                op("dve", lambda e: e.tensor_copy(wbc.rearrange("p (a b) -> p a b", a=2), PS[3][:, 0:16].rearrange("p (b a) -> p a b", a=2)), r=["ps3"], w=["wbc"])
                for half in range(2):
                    op("dve", lambda e, half=half: e.tensor_scalar(rl[:, 0:NPAGE, half * 8:half * 8 + 8], PS[4 * half].rearrange("p (c h) -> p c h", h=8), 0.0, None, ALU.max),
                       r=[f"ps{4 * half}"], w=["rl"])
                    op("dve", lambda e, half=half: e.tensor_scalar(rl[:, NPAGE, half * 8:half * 8 + 8], PS[4 * half + 1][:, 0:8], 0.0, None, ALU.max),
                       r=[f"ps{4 * half + 1}"], w=["rl"])
                op("dve", lambda e: e.tensor_tensor(rl, rl, bc_mid(wbc, NP1), ALU.mult), r=["rl", "wbc"], w=["rl"])
                op("dve", lambda e, b_=b_: e.tensor_reduce(score4[:, b_, :], rl, AX.X, ALU.add), r=["rl"], w=["score4"])
            bnd = sb[:, 0:4]
            lo4, st4, mid4, cnt4, tmp4 = (sb[:, 4 + 4 * i:8 + 4 * i] for i in range(5))
            op("dve", lambda e: e.tensor_reduce(bnd, score4, AX.X, ALU.max, apply_absolute_value=True), r=["score4"], w=["sb"])
            op("pe", lambda e: e.matmul(PS[3][:, 16:20], lhsT=ones_f, rhs=bnd, start=True, stop=True), r=["ones_f", "sb"], w=["ps3"])
            op("dve", lambda e: e.tensor_scalar(lo4, PS[3][:, 16:20], -1.0, None, ALU.mult), r=["ps3"], w=["sb"])
            op("dve", lambda e: e.tensor_scalar(st4, PS[3][:, 16:20], 2.0, None, ALU.mult), r=["ps3"], w=["sb"])
            op("dve", lambda e: e.tensor_tensor(score4[:, :, NPAGE], score4[:, :, NPAGE], negcols, ALU.add), r=["score4", "negcols"], w=["score4"])
            for it in range(NIT + 8):
                op("dve", lambda e: e.tensor_scalar(st4, st4, 0.5, None, ALU.mult), r=["sb"], w=["sb"])
                op("dve", lambda e: e.tensor_tensor(mid4, lo4, st4, ALU.add), r=["sb"], w=["sb"])
                op("dve", lambda e: e.tensor_tensor(mask4, score4, bc_last(mid4, NP1), ALU.is_ge), r=["score4", "sb"], w=["mask4"])
                op("dve", lambda e: e.tensor_reduce(cnt4, mask4, AX.X, ALU.add), r=["mask4"], w=["sb"])
                op("pe", lambda e: e.matmul(PS[3][:, 16:20], lhsT=ones_f, rhs=cnt4, start=True, stop=True), r=["ones_f", "sb"], w=["ps3"])
                op("dve", lambda e: e.scalar_tensor_tensor(tmp4, PS[3][:, 16:20], 255.5, st4, ALU.is_ge, ALU.mult), r=["ps3", "sb"], w=["sb"])
                op("dve", lambda e: e.tensor_tensor(lo4, lo4, tmp4, ALU.add), r=["sb"], w=["sb"])
            op("dve", lambda e: e.tensor_tensor(mask4, score4, bc_last(lo4, NP1), ALU.is_ge), r=["score4", "sb"], w=["mask4"])
            op("dve", lambda e: e.tensor_copy(maskp, mask4[:, :, 0:NPAGE]), r=["mask4"], w=["maskp"])
            op("dve", lambda e: e.tensor_tensor(mnew, mask4[0:NS, :, NPAGE], ident[0:NS, 0:NS], ALU.mult), r=["mask4", "ident"], w=["mnew"])
            op("dve", lambda e: e.tensor_reduce(mnew[:, 0:1], mnew, AX.X, ALU.add), r=["mnew"], w=["mnew"])
            op("dve", lambda e: e.tensor_scalar(Pn, Pn, mnew[:, 0:1], None, ALU.mult), r=["Pn", "mnew"], w=["Pn"])
            maskp2 = maskp.rearrange("p a b -> p (a b)")
            op("pe", lambda e: e.matmul(PS[0][:, 0:256], lhsT=ustrict, rhs=maskp2, start=True, stop=True), r=["ustrict", "maskp"], w=["ps0"])
            op("pe", lambda e: e.matmul(PS[1][:, 0:256], lhsT=ones_f, rhs=maskp2, start=True, stop=True), r=["ones_f", "maskp"], w=["ps1"])
            op("dve", lambda e: e.tensor_copy(rank.rearrange("p a b -> p (a b)"), PS[1][:, 0:256]), r=["ps1"], w=["rank"])
            for b_ in range(NS):
                op("dve", lambda e, b_=b_: e.tensor_tensor_scan(offs[:, b_, :], ones_f[:, 0:NPAGE], rank[:, b_, :], 0.0, ALU.mult, ALU.add), r=["rank", "ones_f"], w=["offs"])
            op("dve", lambda e: e.tensor_tensor(rank, offs, rank, ALU.subtract), r=["offs", "rank"], w=["rank"])
            op("dve", lambda e: e.tensor_tensor(rank.rearrange("p a b -> p (a b)"), rank.rearrange("p a b -> p (a b)"), PS[0][:, 0:256], ALU.add), r=["rank", "ps0"], w=["rank"])
            ohc = 0
            for b_ in range(NS):
                for c in range(NPAGE):
                    ob = ohc % 2
                    ohc += 1
                    op("dve", lambda e, b_=b_, c=c, ob=ob: e.tensor_scalar(OH[ob], islot, rank[:, b_, c:c + 1], maskp[:, b_, c:c + 1], ALU.is_equal, ALU.mult),
                       r=["islot", "rank", "maskp"], w=[f"OH{ob}"])
                    for half in range(2):
                        op("pe", lambda e, b_=b_, c=c, ob=ob, half=half: e.matmul(PS[2 + half][:, b_:b_ + 1], lhsT=OH[ob][:, half * 128:(half + 1) * 128],
                                                                               rhs=physf[:, b_, c:c + 1], start=(c == 0), stop=(c == NPAGE - 1)),
                           r=[f"OH{ob}", "physf"], w=[f"ps{2 + half}"])
            for half in range(2):
                op("dve", lambda e, half=half: e.tensor_scalar(selidx_f.rearrange("p (b a) -> p b a", a=2)[:, :, half], PS[2 + half][:, 0:NS], 0.25, None, ALU.add),
                   r=[f"ps{2 + half}"], w=["selidx_f"])
            op("dve", lambda e: e.tensor_copy(selidx, selidx_f), r=["selidx_f"], w=["selidx"])
            for b_ in range(NS):
                fl = []
                for half in range(2):
                    fl.append(lambda e, b_=b_, half=half: e.indirect_dma_start(out=Ksel[:, half, :], out_offset=None, in_=cache_k,
                                                                              in_offset=bass.IndirectOffsetOnAxis(ap=selidx[:, b_ * 2 + half:b_ * 2 + half + 1], axis=0)))
                    fl.append(lambda e, b_=b_, half=half: e.indirect_dma_start(out=Vsel[:, half, :], out_offset=None, in_=cache_v,
                                                                              in_offset=bass.IndirectOffsetOnAxis(ap=selidx[:, b_ * 2 + half:b_ * 2 + half + 1], axis=0)))
                S.pool_dma_once(fl, r=["selidx"], w=["Ksel", "Vsel"])
                op("dve", lambda e, b_=b_: e.tensor_scalar(validb, iop, offs[:, b_, NPAGE - 1:NPAGE], None, ALU.is_lt), r=["iop", "offs"], w=["validb"])
                op("pe", lambda e, b_=b_: e.matmul(PS[4], lhsT=selB[:, b_, :], rhs=qs32, start=True, stop=True), r=["selB", "qs32"], w=["ps4"])
                for half in range(2):
                    op("dve", lambda e, half=half: e.tensor_tensor(Ksel[:, half, :], Ksel[:, half, :], PS[4], ALU.mult), r=["Ksel", "ps4"], w=["Ksel"])
                op("dve", lambda e: e.tensor_reduce(scb.rearrange("p a h -> p (a h)"), Ksel.rearrange("p a (h d) -> p (a h) d", d=64), AX.X, ALU.add), r=["Ksel"], w=["scb"])
                op("act", lambda e: e.activation(Pb, scb, AF.Exp, scale=0.125), r=["scb"], w=["Pb"])
                op("dve", lambda e: e.tensor_tensor(Pb, Pb, bc_last(validb, 8), ALU.mult), r=["Pb", "validb"], w=["Pb"])
                op("pe", lambda e: e.matmul(PS[5][:, 0:8], lhsT=ones_f, rhs=Pb[:, 0, :], start=True, stop=False), r=["ones_f", "Pb"], w=["ps5"])
                op("pe", lambda e: e.matmul(PS[5][:, 0:8], lhsT=ones_f, rhs=Pb[:, 1, :], start=False, stop=False), r=["ones_f", "Pb"], w=["ps5"])
                op("pe", lambda e, b_=b_: e.matmul(PS[5][:, 0:8], lhsT=selB[:, b_, :], rhs=Pn, start=False, stop=True), r=["selB", "Pn"], w=["ps5"])
                op("dve", lambda e: e.reciprocal(rdenb, PS[5][:, 0:8]), r=["ps5"], w=["rdenb"])
                op("dve", lambda e: e.tensor_tensor(Pb, Pb, bc_mid(rdenb, 2), ALU.mult), r=["Pb", "rdenb"], w=["Pb"])
                op("dve", lambda e: e.tensor_tensor(Pnn, Pn, rdenb[0:NS, :], ALU.mult), r=["Pn", "rdenb"], w=["Pnn"])
                for half in range(2):
                    op("dve", lambda e, half=half: e.tensor_tensor(Vsel[:, half, :].rearrange("p (h d) -> p h d", d=64), Vsel[:, half, :].rearrange("p (h d) -> p h d", d=64),
                                                                 bc_last(Pb[:, half, :], 64), ALU.mult), r=["Vsel", "Pb"], w=["Vsel"])
                op("dve", lambda e: e.tensor_tensor(wvn.rearrange("p (h d) -> p h d", d=64), vs32.rearrange("p (h d) -> p h d", d=64), bc_last(Pnn, 64), ALU.mult),
                   r=["vs32", "Pnn"], w=["prodn"])
                for cch in range(4):
                    cs = slice(cch * 128, (cch + 1) * 128)
                    col = cch * NS + b_
                    op("pe", lambda e, cs=cs, col=col: e.matmul(PS[7][:, col:col + 1], lhsT=Vsel[:, 0, cs], rhs=ones_f[:, 0:1], start=True, stop=False), r=["Vsel", "ones_f"], w=["ps7"])
                    op("pe", lambda e, cs=cs, col=col: e.matmul(PS[7][:, col:col + 1], lhsT=Vsel[:, 1, cs], rhs=ones_f[:, 0:1], start=False, stop=False), r=["Vsel", "ones_f"], w=["ps7"])
                    op("pe", lambda e, cs=cs, col=col, b_=b_: e.matmul(PS[7][:, col:col + 1], lhsT=wvn[:, cs], rhs=ident[0:NS, b_:b_ + 1], start=False, stop=True), r=["prodn", "ident"], w=["ps7"])
            op("act", lambda e: e.activation(attT[:, :, T:T + NS], PS[7][:, 0:16].rearrange("p (c b) -> p c b", b=NS), AF.Copy), r=["ps7"], w=["attT"])
            if os.environ.get("MK_DBG_S", "0") == "1":
                d1 = nc.dram_tensor("dbg_score", [128, NS * NP1], F32, kind="ExternalOutput").ap()
                dma("sp", lambda e: e.dma_start(out=d1, in_=score4.rearrange("p a b -> p (a b)")), r=["score4"])
                d2 = nc.dram_tensor("dbg_mask", [128, NS * NP1], F32, kind="ExternalOutput").ap()
                dma("sp", lambda e: e.dma_start(out=d2, in_=mask4.rearrange("p a b -> p (a b)")), r=["mask4"])
                d3 = nc.dram_tensor("dbg_sel", [128, 8], F32, kind="ExternalOutput").ap()
                dma("sp", lambda e: e.dma_start(out=d3, in_=selidx_f), r=["selidx_f"])
                d4 = nc.dram_tensor("dbg_atts", [128, 4, NS], BF16, kind="ExternalOutput").ap()
                dma("sp", lambda e: e.dma_start(out=d4, in_=attT[:, :, T:T + NS]), r=["attT"])
                d6 = nc.dram_tensor("dbg_ksel", [128, 1024], F32, kind="ExternalOutput").ap()
                dma("sp", lambda e: e.dma_start(out=d6, in_=Ksel.rearrange("p a b -> p (a b)")), r=["Ksel"])
                d7 = nc.dram_tensor("dbg_pb", [128, 16], F32, kind="ExternalOutput").ap()
                dma("sp", lambda e: e.dma_start(out=d7, in_=Pb.rearrange("p a b -> p (a b)")), r=["Pb"])
                d8 = nc.dram_tensor("dbg_scb", [128, 16], F32, kind="ExternalOutput").ap()
                dma("sp", lambda e: e.dma_start(out=d8, in_=scb.rearrange("p a b -> p (a b)")), r=["scb"])
                d9 = nc.dram_tensor("dbg_selidx", [128, 8], I32, kind="ExternalOutput").ap()
                dma("sp", lambda e: e.dma_start(out=d9, in_=selidx), r=["selidx"])
                d5 = nc.dram_tensor("dbg_rank", [128, NS * NPAGE], F32, kind="ExternalOutput").ap()
                dma("sp", lambda e: e.dma_start(out=d5, in_=rank.rearrange("p a b -> p (a b)")), r=["rank"])


        if STAGE >= 4:
            S.barrier()
            AR.release(m_att)
            gsc = nc.dram_tensor("gsc", [2, 5, D], F32, kind="Internal").ap()
            win_rw = AR.alloc([8, RW_COLS], BF16)
            wo_att = AR.alloc([4, D], BF16)
            wo_rw = AR.alloc([8, D], BF16, parts=64)
            w2b = AR.alloc([512], BF16, parts=64)
            a2b = AR.alloc([512], BF16, parts=64)
            g2b = AR.alloc([2, 512], BF16, parts=64)
            muD = AR.alloc([28], F32, parts=64)
            Msel = AR.alloc([16], F32)
            Mh = AR.alloc([8], F32)
            maskLA = AR.alloc([16, 128], F32, parts=64)
            g1rep = AR.alloc([D], F32)
            g1s = AR.alloc([D], F32, parts=NS)
            shiftT = AR.alloc([28, NS], F32, parts=64)
            m2 = AR.mark()
            grow = AR.alloc([D], F32, parts=5)
            stg = [AR.alloc([4096], F32) for _ in range(2)]
            stcnt = [0]

            def load_cast(dst, src, parts, shape):
                b = stcnt[0] % 2
                stcnt[0] += 1
                ne = _prod(shape)
                v = stg[b][0:parts, 0:ne]
                if len(shape) == 2:
                    v = v.rearrange("p (a b) -> p a b", a=shape[0])
                dma("sp", lambda e: e.dma_start(out=v, in_=src), w=[f"stg{b}"])
                op("pool", lambda e: e.tensor_copy(dst, v), r=[f"stg{b}"], w=["wts2"])

            w_in_v2 = w_in.rearrange("(kc p) n -> p kc n", p=128)
            for c in range(0, RW_COLS, 512):
                n = min(512, RW_COLS - c)
                load_cast(win_rw[:, :, c:c + n], w_in_v2[:, :, ATT_COLS + c:ATT_COLS + c + n], 128, [8, n])
            load_cast(wo_att, w_out[0:512, :].rearrange("(c p) n -> p c n", p=128), 128, [4, D])
            wo_rw_v = w_out[512:1024, :].rearrange("(h p) n -> p h n", p=64)
            for hh in range(0, 8, 4):
                load_cast(wo_rw[:, hh:hh + 4, :], wo_rw_v[:, hh:hh + 4, :], 64, [4, D])
            load_cast(w2b, rw_w2, 64, [512])
            load_cast(a2b, rw_a2, 64, [512])
            load_cast(g2b, rw_g2.rearrange("(b p) n -> p b n", p=64), 64, [2, 512])
            stm = stg[0][0:28, 0:64]
            dma("sp", lambda e: e.dma_start(out=stm, in_=rw_mu.rearrange("(a b) -> a b", b=64)), w=["stg0"])
            op("pe", lambda e: e.transpose(PS[0][0:64, 0:28], stm, ident[0:28, 0:28]), r=["stg0", "ident"], w=["ps0"])
            op("dve", lambda e: e.tensor_copy(muD, PS[0][0:64, 0:28]), r=["ps0"], w=["muD"])
            op("dve", lambda e: e.tensor_reduce(Msel, ident.rearrange("p (h t) -> p t h", t=16), AX.X, ALU.add), r=["ident"], w=["Msel"])
            op("dve", lambda e: e.tensor_reduce(Mh, ident.rearrange("p (h t) -> p h t", t=16), AX.X, ALU.add), r=["ident"], w=["Mh"])
            op("pool", lambda e: e.memset(maskLA, 0.0), w=["maskLA"])
            mla = maskLA.rearrange("p a (h t) -> p a h t", t=16)
            for tp in range(16):
                op("pool", lambda e, tp=tp: e.memset(mla[:, tp, :, tp:tp + 1], 1.0), w=["maskLA"])
            for kc in range(8):
                op("pe", lambda e, kc=kc: e.transpose(PS[kc // 4][0:5, (kc % 4) * 128:(kc % 4 + 1) * 128], gate1[:, kc, :], ident), r=["modT", "ident"], w=[f"ps{kc // 4}"])
            op("dve", lambda e: e.tensor_copy(grow[:, 0:512], PS[0][0:5, :]), r=["ps0"], w=["grow"])
            op("dve", lambda e: e.tensor_copy(grow[:, 512:1024], PS[1][0:5, :]), r=["ps1"], w=["grow"])
            dma("sp", lambda e: e.dma_start(out=gsc[0], in_=grow), r=["grow"], w=["gsc0"])
            dma("sp", lambda e: e.dma_start(out=g1rep, in_=gsc[0, 0].partition_broadcast(128)), r=["gsc0"], w=["g1rep"])
            dma("sp", lambda e: e.dma_start(out=g1s, in_=gsc[0, 1:5, :]), r=["gsc0"], w=["g1s"])
            shs = stg[1][0:NS, 0:RW_COLS]
            dma("sp", lambda e: e.dma_start(out=shs, in_=st_shift), w=["stg1"])
            for blk in range(28):
                op("pe", lambda e, blk=blk: e.transpose(PS[2][0:64, blk * 4:blk * 4 + NS], shs[:, blk * 64:(blk + 1) * 64], ident[0:NS, 0:NS]), r=["stg1", "ident"], w=["ps2"])
            op("dve", lambda e: e.tensor_copy(shiftT, PS[2][0:64, 0:112].rearrange("p (a b) -> p a b", b=4)), r=["ps2"], w=["shiftT"])
            S.barrier()
            AR.release(m2)

            W = 128
            xt2 = AR.alloc([D], F32)
            xn2 = AR.alloc([D], F32)
            hT2 = AR.alloc([8, W], BF16)
            prd = AR.alloc([28, W + 1], F32, parts=64)
            xm = AR.alloc([28, W], F32, parts=64)
            dv = {k: AR.alloc([8, W], F32, parts=64) for k in ["wdec", "asig", "kkn", "na", "bb", "kmod", "gg", "bon", "t1", "t2"]}
            dv["Yd"] = dv["kkn"]
            rwd = AR.alloc([8, W], BF16, parts=64)
            tanh_wd = AR.alloc([W], BF16, parts=64)
            ad_bf = AR.alloc([W], BF16, parts=64)
            sg = AR.alloc([2, W], BF16, parts=64)
            LA_sel = LA_sel_r
            Xb16 = Xall[:, 0:64]
            Xb16_r = Xall_r[:, 0:64]
            STb = AR.alloc([8, 64], BF16, parts=64)
            rbf = AR.alloc([8, W], BF16, parts=64)
            ST = AR.alloc([8, 64], F32, parts=64)
            ST2 = AR.alloc([8, 64], F32, parts=64)
            Sio = AR.alloc([8, 64], F32, parts=64)
            sm2 = AR.alloc([8], F32)
            tstg = AR.alloc([3, 128], F32, parts=64)
            w0T, a0T, kkT, kaT, lnwT, lnbT, rkT = (vecB[:, 8 * i:8 * i + 8] for i in range(7))
            for k_ in dv:
                if k_ != "Yd":
                    op("pool", lambda e, k_=k_: e.memset(dv[k_], 0.0), w=[k_])
            op("pool", lambda e: e.memset(xm, 0.0), w=["xm"])
            op("pool", lambda e: e.memset(prd, 0.0), w=["prd"])
            op("pool", lambda e: e.memset(ST, 0.0), w=["ST"])
            op("act", lambda e: e.activation(STr, ST.rearrange("p h i -> p (h i)"), AF.Copy), r=["ST"], w=["STr"])
            ones64 = ones_f[0:64, 0:64]

            def v4(ap):
                return ap[0:64, :].rearrange("p (a b) -> p a b", b=128)

            def sum64(src, P):
                for hb in range(2):
                    if P == 128:
                        op("pe", lambda e, hb=hb: e.matmul(v4(PS[6 + hb])[:, :, 0:P], lhsT=ones64, rhs=src[:, hb * 4:(hb + 1) * 4, 0:P], start=True, stop=True),
                           r=["ones_f", "dvsrc"], w=[f"ps{6 + hb}"])
                    else:
                        for hl in range(4):
                            op("pe", lambda e, hb=hb, hl=hl: e.matmul(PS[6 + hb][0:64, hl * 128:hl * 128 + P], lhsT=ones64, rhs=src[:, hb * 4 + hl, 0:P], start=True, stop=True),
                               r=["ones_f", "dvsrc"], w=[f"ps{6 + hb}"])

            def rw_tile(ti, P, x_src, cond, tcol0, prev_view, is_sample):
                dma("sp", lambda e: e.dma_start(out=xt2[0:P, :], in_=x_src), w=["xt2"])
                ss = sm2[0:P, 0:1]
                rs = sm2[0:P, 1:2]
                op("act", lambda e: e.activation(xn2[0:P, :], xt2[0:P, :], AF.Square, accum_out=ss), r=["xt2"], w=["xn2", "sm2"])
                op("dve", lambda e: e.tensor_scalar(rs, ss, 1.0 / D, RMS_EPS, ALU.mult, ALU.add), r=["sm2"], w=["sm2"])
                op("act", lambda e: e.activation(rs, rs, AF.Sqrt), r=["sm2"], w=["sm2"])
                op("dve", lambda e: e.reciprocal(rs, rs), r=["sm2"], w=["sm2"])
                op("dve", lambda e: e.tensor_scalar(xn2[0:P, :], xt2[0:P, :], rs, None, ALU.mult), r=["xt2", "sm2"], w=["xn2"])
                for half in range(2):
                    pb = PS[half]
                    for j in range(4):
                        kc = half * 4 + j
                        op("pe", lambda e, kc=kc, j=j, pb=pb: e.transpose(pb[:, j * 128:j * 128 + P], xn2[0:P, kc * 128:(kc + 1) * 128], ident[0:P, 0:P]),
                           r=["xn2", "ident"], w=[f"ps{half}"])
                    for j in range(4):
                        kc = half * 4 + j
                        if not is_sample:
                            op("act", lambda e, kc=kc, j=j, pb=pb: e.activation(hT2[:, kc, 0:P], pb[:, j * 128:j * 128 + P], AF.Identity,
                                                                              bias=shift1[:, kc, 0:1], scale=A1[:, kc, 0:1]), r=[f"ps{half}", "A1", "modT"], w=["hT2"])
                        else:
                            op("dve", lambda e, kc=kc, j=j, pb=pb: e.tensor_tensor(scr2[:, 0:P], pb[:, j * 128:j * 128 + P], A1[:, kc, 1:1 + P], ALU.mult),
                               r=[f"ps{half}", "A1"], w=["scr2b"])
                            op("dve", lambda e, kc=kc: e.tensor_tensor(hT2[:, kc, 0:P], scr2[:, 0:P], shift1[:, kc, 1:1 + P], ALU.add), r=["scr2b", "modT"], w=["hT2"])
                for blk0 in range(0, 28, 4):
                    bk = 2 + (blk0 // 4) % 2
                    for j in range(4):
                        blk = blk0 + j
                        for kc in range(8):
                            op("pe", lambda e, bk=bk, j=j, blk=blk, kc=kc: e.matmul(PS[bk][0:64, j * 128:j * 128 + P], lhsT=win_rw[:, kc, blk * 64:(blk + 1) * 64],
                                                                                    rhs=hT2[:, kc, 0:P], start=(kc == 0), stop=(kc == 7)), r=["hT2", "wts2"], w=[f"ps{bk}"])
                    op("act", lambda e, bk=bk, blk0=blk0: e.activation(prd[:, blk0:blk0 + 4, 1:1 + P], v4(PS[bk])[:, :, 0:P], AF.Copy), r=[f"ps{bk}"], w=["prd"])
                if P2_CUT <= 2:
                    return
                cur = prd[:, :, 1:1 + P]
                prev = prev_view if prev_view is not None else prd[:, :, 0:P]
                xmv = xm[:, :, 0:P]
                op("dve", lambda e: e.tensor_tensor(xmv, prev, cur, ALU.subtract), r=["prd", "shiftT"], w=["xm"])
                op("dve", lambda e: e.tensor_tensor(xmv, xmv, bc_last(muD, P), ALU.mult), r=["xm", "muD"], w=["xm"])
                op("dve", lambda e: e.tensor_tensor(xmv, xmv, cur, ALU.add), r=["xm", "prd"], w=["xm"])
                if is_sample:
                    for b_ in range(P):
                        op("pe", lambda e, b_=b_: e.transpose(PS[0][0:28, b_ * 64:(b_ + 1) * 64], prd[:, :, 1 + b_], ident[0:64, 0:64]), r=["prd", "ident"], w=["ps0"])
                    op("dve", lambda e: e.tensor_copy(xn2[0:28, 0:P * 64], PS[0][0:28, 0:P * 64]), r=["ps0"], w=["xn2"])
                    for b_ in range(P):
                        dma("sp", lambda e, b_=b_: e.dma_start(out=sh_so[b_].rearrange("(a c) -> a c", c=64), in_=xn2[0:28, b_ * 64:(b_ + 1) * 64]), r=["xn2"])
                else:
                    if ti == NTILES - 1:
                        op("pe", lambda e: e.transpose(PS[0][0:28, 0:64], prd[:, :, P], ident[0:64, 0:64]), r=["prd", "ident"], w=["ps0"])
                        op("dve", lambda e: e.tensor_copy(xn2[0:28, 0:64], PS[0][0:28, 0:64]), r=["ps0"], w=["xn2"])
                        dma("sp", lambda e: e.dma_start(out=sh_po.rearrange("(a c) -> a c", c=64), in_=xn2[0:28, 0:64]), r=["xn2"])
                    op("pool", lambda e: e.tensor_copy(prd[:, :, 0:1], prd[:, :, P:P + 1]), r=["prd", "xm"], w=["prd"])
                if P2_CUT <= 3:
                    return
                r_ = xm[:, 0:8, :]
                k_ = xm[:, 8:16, :]
                v_ = xm[:, 16:24, :]
                D_ = {k2: dv[k2][:, :, 0:P] for k2 in dv}
                op("act", lambda e: e.activation(tanh_wd[:, 0:P], xm[:, 24, 0:P], AF.Tanh), r=["xm"], w=["tanh_wd"])
                op("act", lambda e: e.activation(ad_bf[:, 0:P], xm[:, 25, 0:P], AF.Copy), r=["xm"], w=["ad_bf"])
                op("act", lambda e: e.activation(sg[:, :, 0:P], xm[:, 26:28, 0:P], AF.Sigmoid), r=["xm"], w=["sg"])

                def lora(wb, rhs_ap, rname):
                    for h in range(8):
                        op("pe", lambda e, h=h: e.matmul(PS[4 + h // 4][0:64, (h % 4) * 128:(h % 4) * 128 + P], lhsT=wb[:, h * 64:(h + 1) * 64], rhs=rhs_ap,
                                                       start=True, stop=True), r=[rname, "wts2"], w=[f"ps{4 + h // 4}"])
                lora(w2b, tanh_wd[:, 0:P], "tanh_wd")
                for hb in range(2):
                    op("dve", lambda e, hb=hb: e.tensor_tensor(D_["t1"][:, hb * 4:(hb + 1) * 4, :], v4(PS[4 + hb])[:, :, 0:P], bc_last(w0T[:, hb * 4:(hb + 1) * 4], P), ALU.add),
                       r=[f"ps{4 + hb}", "vecB"], w=["t1"])
                op("act", lambda e: e.activation(D_["t1"], D_["t1"], AF.Sigmoid), r=["t1"], w=["t1"])
                op("act", lambda e: e.activation(D_["wdec"], D_["t1"], AF.Exp, scale=-0.6065306597126334), r=["t1"], w=["wdec"])
                lora(a2b, ad_bf[:, 0:P], "ad_bf")
                for hb in range(2):
                    op("dve", lambda e, hb=hb: e.tensor_tensor(D_["t1"][:, hb * 4:(hb + 1) * 4, :], v4(PS[4 + hb])[:, :, 0:P], bc_last(a0T[:, hb * 4:(hb + 1) * 4], P), ALU.add),
                       r=[f"ps{4 + hb}", "vecB"], w=["t1"])
                op("act", lambda e: e.activation(D_["asig"], D_["t1"], AF.Sigmoid), r=["t1"], w=["asig"])
                for h in range(8):
                    for blk in range(2):
                        op("pe", lambda e, h=h, blk=blk: e.matmul(PS[4 + h // 4][0:64, (h % 4) * 128:(h % 4) * 128 + P], lhsT=g2b[:, blk, h * 64:(h + 1) * 64], rhs=sg[:, blk, 0:P],
                                                                 start=(blk == 0), stop=(blk == 1)), r=["sg", "wts2"], w=[f"ps{4 + h // 4}"])
                for hb in range(2):
                    op("act", lambda e, hb=hb: e.activation(D_["gg"][:, hb * 4:(hb + 1) * 4, :], v4(PS[4 + hb])[:, :, 0:P], AF.Copy), r=[f"ps{4 + hb}"], w=["gg"])
                kP, rP, vP = k_[:, :, 0:P], r_[:, :, 0:P], v_[:, :, 0:P]
                op("dve", lambda e: e.tensor_tensor(D_["kkn"], kP, bc_last(kkT, P), ALU.mult), r=["xm", "vecB"], w=["kkn"])
                op("dve", lambda e: e.tensor_tensor(D_["t1"], D_["kkn"], D_["kkn"], ALU.mult), r=["kkn"], w=["t1", "dvsrc"])
                sum64(dv["t1"], P)
                for hb in range(2):
                    op("dve", lambda e, hb=hb: e.tensor_scalar(D_["t2"][:, hb * 4:(hb + 1) * 4, :], v4(PS[6 + hb])[:, :, 0:P], 1e-24, None, ALU.max), r=[f"ps{6 + hb}"], w=["t2"])
                op("act", lambda e: e.activation(D_["t2"], D_["t2"], AF.Sqrt), r=["t2"], w=["t2"])
                op("dve", lambda e: e.reciprocal(D_["t2"], D_["t2"]), r=["t2"], w=["t2"])
                op("dve", lambda e: e.tensor_tensor(D_["kkn"], D_["kkn"], D_["t2"], ALU.mult), r=["kkn", "t2"], w=["kkn"])
                op("dve", lambda e: e.tensor_scalar(D_["t1"], D_["asig"], -1.0, None, ALU.add), r=["asig"], w=["t1"])
                op("dve", lambda e: e.tensor_tensor(D_["t1"], D_["t1"], bc_last(kaT, P), ALU.mult), r=["t1", "vecB"], w=["t1"])
                op("dve", lambda e: e.tensor_scalar(D_["t1"], D_["t1"], 1.0, None, ALU.add), r=["t1"], w=["t1"])
                op("dve", lambda e: e.tensor_tensor(D_["kmod"], kP, D_["t1"], ALU.mult), r=["xm", "t1"], w=["kmod"])
                op("dve", lambda e: e.tensor_tensor(D_["bb"], D_["kkn"], D_["asig"], ALU.mult), r=["kkn", "asig"], w=["bb"])
                op("pool", lambda e: e.tensor_scalar(D_["na"], D_["kkn"], -1.0, None, ALU.mult), r=["kkn"], w=["na"])
                op("dve", lambda e: e.tensor_tensor(D_["t1"], rP, D_["kmod"], ALU.mult), r=["xm", "kmod"], w=["t1"])
                op("dve", lambda e: e.tensor_tensor(D_["t1"], D_["t1"], bc_last(rkT, P), ALU.mult), r=["t1", "vecB"], w=["t1", "dvsrc"])
                sum64(dv["t1"], P)
                for hb in range(2):
                    op("dve", lambda e, hb=hb: e.tensor_tensor(D_["bon"][:, hb * 4:(hb + 1) * 4, :], v4(PS[6 + hb])[:, :, 0:P], vP[:, hb * 4:(hb + 1) * 4, :], ALU.mult),
                       r=[f"ps{6 + hb}", "xm"], w=["bon"])
                if os.environ.get("MK_DBG_DV", "0") == "1" and ti == 0:
                    dbg_dv = nc.dram_tensor("dbg_dv", [64, 7, 8, 128], F32, kind="ExternalOutput").ap()
                    for i_, nm in enumerate(["wdec", "asig", "kkn", "kmod", "bb", "gg", "bon"]):
                        dma("sp", lambda e, i_=i_, nm=nm: e.dma_start(out=dbg_dv[:, i_], in_=dv[nm]), r=[nm])
                    dbg_xm = nc.dram_tensor("dbg_xm", [64, 28, 128], F32, kind="ExternalOutput").ap()
                    dma("sp", lambda e: e.dma_start(out=dbg_xm, in_=xm), r=["xm"])
                if P2_CUT <= 4:
                    return
                acnt = 0
                pend_y = []
                op("act", lambda e: e.activation(rbf[:, :, 0:P], r_[:, :, 0:P], AF.Copy), r=["xm"], w=["rbf"])
                PSC = min(P, int(os.environ.get("MK_NSTEPS", "100000")))
                for t0 in range(0, PSC, 16):
                    nst = min(16, PSC - t0)
                    for (ci, srcv, rn) in [(0, dv["bb"], "bb"), (1, dv["kmod"], "kmod"), (2, v_, "xm")]:
                        op("pool", lambda e, ci=ci, srcv=srcv, t0=t0: e.tensor_copy(tstg[:, ci, :].rearrange("p (h t) -> p h t", t=16), srcv[:, :, t0:t0 + 16]), r=[rn], w=["tstg"])
                        op("pe", lambda e, ci=ci: e.transpose(PS[6][:, ci * 64:ci * 64 + 64], tstg[:, ci, :], ident[0:64, 0:64]), r=["tstg", "ident"], w=["ps6"])
                    op("act", lambda e: e.activation(Xall_r, PS[6][:, 0:192], AF.Copy), r=["ps6"], w=["Xb16"])
                    if os.environ.get("MK_T2", "0") != "1":
                        op("dve", lambda e: e.tensor_tensor(Xk_sel_r, bc_mid(Xall[:, 64:128], 16), bc_last(Msel, 64), ALU.mult), r=["Xb16", "Msel"], w=["Xk_sel"])
                        op("dve", lambda e: e.tensor_tensor(Vbd16_r, bc_mid(Xall[:, 128:192], 8), bc_last(Mh, 64), ALU.mult), r=["Xb16", "Mh"], w=["Vbd16"])
                    if os.environ.get("MK_T1", "0") != "1":
                        op("dve", lambda e, t0=t0: e.tensor_tensor(LA_sel.rearrange("p a (h t) -> p a h t", t=16), dv["na"][:, :, t0:t0 + 16].unsqueeze(1).to_broadcast([64, 16, 8, 16]),
                                                             maskLA.rearrange("p a (h t) -> p a h t", t=16), ALU.mult), r=["na", "maskLA"], w=["LA_sel"])
                    for tp in range(nst if SCAN_CUT > 1 else 0):
                        t = t0 + tp
                        if is_sample:
                            load_state(t)
                        pa = acnt % 2
                        pu = 2 + acnt % 2
                        rb = acnt % 2
                        acnt += 1
                        op("pe", lambda e, pa=pa, tp=tp: e.matmul(PS[pa], lhsT=LA_sel_r[:, tp, :], rhs=STr, start=True, stop=True),
                           r=["LA_sel", "STr"], w=[f"ps{pa}"])
                        op("dve", lambda e, pa=pa, rb=rb: e.tensor_tensor(R128_r[rb], PS[pa].rearrange("p (h i) -> p h i", i=64), bc_last(Mh, 64), ALU.mult),
                           r=[f"ps{pa}", "Mh"], w=[f"R{rb}"])
                        if SCAN_CUT <= 2:
                            continue
                        op("pool", lambda e, t=t: e.tensor_tensor(ST2, ST, bc_last(dv["wdec"][:, :, t], 64), ALU.mult), r=["ST", "wdec"], w=["ST2"])
                        op("pe", lambda e, pu=pu, tp=tp: e.matmul(PS[pu][0:64, :], lhsT=Xk_sel_r[:, tp, :], rhs=Vbd16_r.rearrange("p h i -> p (h i)"), start=True, stop=False),
                           r=["Xk_sel", "Vbd16"], w=[f"ps{pu}"])
                        op("pe", lambda e, pu=pu, rb=rb: e.matmul(PS[pu][0:64, :], lhsT=Xb16_r, rhs=R128_r[rb].rearrange("p h i -> p (h i)"), start=False, stop=True),
                           r=["Xb16", f"R{rb}"], w=[f"ps{pu}"])
                        while pend_y:
                            pend_y.pop(0)()
                        op("dve", lambda e, pu=pu: e.tensor_tensor(ST.rearrange("p h i -> p (h i)"), ST2.rearrange("p h i -> p (h i)"), PS[pu][0:64, :], ALU.add),
                           r=["ST2", f"ps{pu}"], w=["ST"])
                        op("act", lambda e: e.activation(STr, ST.rearrange("p h i -> p (h i)"), AF.Copy), r=["ST"], w=["STr"])
                        op("act", lambda e: e.activation(STb, ST, AF.Copy), r=["ST"], w=["STb"])

                        def emit_y(t=t):
                            for h in range(8):
                                op("pe", lambda e, h=h, t=t: e.matmul(PS[4 + h // 4][0:64, (h % 4) * 128 + t:(h % 4) * 128 + t + 1], lhsT=STb[:, h, :], rhs=rbf[:, h, t:t + 1],
                                                                   start=True, stop=True), r=["STb", "rbf"], w=[f"ps{4 + h // 4}"])
                        if is_sample:
                            emit_y()
                            store_state(wkv_so[t])
                        else:
                            pend_y.append(emit_y)
                while pend_y:
                    pend_y.pop(0)()
                if P2_CUT <= 5:
                    return
                for hb in range(2):
                    op("act", lambda e, hb=hb: e.activation(D_["Yd"][:, hb * 4:(hb + 1) * 4, :], v4(PS[4 + hb])[:, :, 0:P], AF.Copy), r=[f"ps{4 + hb}"], w=["kkn", "dvsrc"])
                sum64(dv["Yd"], P)
                for hb in range(2):
                    op("dve", lambda e, hb=hb: e.scalar_tensor_tensor(D_["t1"][:, hb * 4:(hb + 1) * 4, :], v4(PS[6 + hb])[:, :, 0:P], -1.0 / 64, D_["Yd"][:, hb * 4:(hb + 1) * 4, :],
                                                                      ALU.mult, ALU.add), r=[f"ps{6 + hb}", "kkn"], w=["t1"])
                op("dve", lambda e: e.tensor_tensor(D_["t2"], D_["t1"], D_["t1"], ALU.mult), r=["t1"], w=["t2", "dvsrc"])
                sum64(dv["t2"], P)
                for hb in range(2):
                    op("dve", lambda e, hb=hb: e.tensor_scalar(D_["t2"][:, hb * 4:(hb + 1) * 4, :], v4(PS[6 + hb])[:, :, 0:P], 1.0 / 64, GN_EPS, ALU.mult, ALU.add),
                       r=[f"ps{6 + hb}"], w=["t2"])
                op("act", lambda e: e.activation(D_["t2"], D_["t2"], AF.Sqrt), r=["t2"], w=["t2"])
                op("dve", lambda e: e.reciprocal(D_["t2"], D_["t2"]), r=["t2"], w=["t2"])
                op("dve", lambda e: e.tensor_tensor(D_["t1"], D_["t1"], D_["t2"], ALU.mult), r=["t1", "t2"], w=["t1"])
                op("dve", lambda e: e.tensor_tensor(D_["t1"], D_["t1"], bc_last(lnwT, P), ALU.mult), r=["t1", "vecB"], w=["t1"])
                op("dve", lambda e: e.tensor_tensor(D_["t1"], D_["t1"], bc_last(lnbT, P), ALU.add), r=["t1", "vecB"], w=["t1"])
                op("dve", lambda e: e.tensor_tensor(D_["t1"], D_["t1"], D_["bon"], ALU.add), r=["t1", "bon"], w=["t1"])
                op("dve", lambda e: e.tensor_tensor(rwd[:, :, 0:P], D_["t1"], D_["gg"], ALU.mult), r=["t1", "gg"], w=["rwd"])
                grep_ = g1s if is_sample else g1rep
                for half in range(2):
                    cs = slice(half * 512, (half + 1) * 512)
                    pb = 2 + half
                    for c in range(4):
                        op("pe", lambda e, c=c, cs=cs, pb=pb: e.matmul(PS[pb][0:P, :], lhsT=attT[:, c, tcol0:tcol0 + P], rhs=wo_att[:, c, cs], start=(c == 0), stop=False),
                           r=["attT", "wts2"], w=[f"ps{pb}"])
                    for h in range(8):
                        op("pe", lambda e, h=h, cs=cs, pb=pb: e.matmul(PS[pb][0:P, :], lhsT=rwd[:, h, 0:P], rhs=wo_rw[:, h, cs], start=False, stop=(h == 7)),
                           r=["rwd", "wts2"], w=[f"ps{pb}"])
                    op("dve", lambda e, cs=cs, pb=pb: e.tensor_tensor(xn2[0:P, cs], PS[pb][0:P, :], grep_[0:P, cs], ALU.mult), r=[f"ps{pb}", "g1rep", "g1s"], w=["xn2"])
                    op("dve", lambda e, cs=cs: e.tensor_tensor(xn2[0:P, cs], xn2[0:P, cs], xt2[0:P, cs], ALU.add), r=["xn2", "xt2"], w=["xn2"])

            scr2 = AR.alloc([W], F32)

            def load_state(b_):
                dma("sp", lambda e: e.dma_start(out=Sio, in_=st_wkv[b_].rearrange("h i j -> i h j")), w=["Sio"])
                for h in range(8):
                    op("pe", lambda e, h=h: e.transpose(PS[7][0:64, h * 64:(h + 1) * 64], Sio[:, h, :], ident[0:64, 0:64]), r=["Sio", "ident"], w=["ps7"])
                op("dve", lambda e: e.tensor_copy(ST.rearrange("p h i -> p (h i)"), PS[7][0:64, :]), r=["ps7"], w=["ST"])
                op("act", lambda e: e.activation(STr, ST.rearrange("p h i -> p (h i)"), AF.Copy), r=["ST"], w=["STr"])

            def store_state(dst):
                for h in range(8):
                    op("pe", lambda e, h=h: e.transpose(PS[7][0:64, h * 64:(h + 1) * 64], ST[:, h, :], ident[0:64, 0:64]), r=["ST", "ident"], w=["ps7"])
                op("dve", lambda e: e.tensor_copy(Sio.rearrange("p h i -> p (h i)"), PS[7][0:64, :]), r=["ps7"], w=["Sio"])
                dma("sp", lambda e: e.dma_start(out=dst.rearrange("h i j -> i h j"), in_=Sio), r=["Sio"])

            for ti in range(NTILES if P2_CUT > 1 else 0):
                r0 = ti * 128
                rw_tile(ti, 128, x_p[r0:r0 + 128, :], None, r0, None, False)
                dma("sp", lambda e, r0=r0: e.dma_start(out=y_p[r0:r0 + 128, :], in_=xn2), r=["xn2"], w=["y_p"])
            if P2_CUT > 5:
                store_state(wkv_po)
            if P2_CUT > 1 and os.environ.get("MK_NOSAMP", "0") != "1":
                rw_tile(16, NS, x_s, 1, T, shiftT, True)
                dma("sp", lambda e: e.dma_start(out=y_s, in_=xn2[0:NS, :]), r=["xn2"], w=["y_s"])


        if STAGE >= 5:
            S.barrier()
            AR.release(m_attT)
            wg = AR.alloc([8, FFN], BF16)
            wu = AR.alloc([8, FFN], BF16)
            wd_ = AR.alloc([NFC, D], BF16)
            g2rep = AR.alloc([D], F32)
            g2s = AR.alloc([D], F32, parts=NS)
            m3 = AR.mark()
            grow2 = AR.alloc([D], F32, parts=5)
            stg3 = [AR.alloc([4096], F32) for _ in range(2)]
            st3 = [0]

            def load_cast3(dst, src, shape):
                b = st3[0] % 2
                st3[0] += 1
                ne = _prod(shape)
                v = stg3[b][:, 0:ne].rearrange("p (a b) -> p a b", a=shape[0])
                dma("sp", lambda e: e.dma_start(out=v, in_=src), w=[f"stg3{b}"])
                op("pool", lambda e: e.tensor_copy(dst, v), r=[f"stg3{b}"], w=["wts3"])

            wg_v = w_gate.rearrange("(kc p) n -> p kc n", p=128)
            wu_v = w_up.rearrange("(kc p) n -> p kc n", p=128)
            wd_v = w_down.rearrange("(fc p) n -> p fc n", p=128)
            for c in range(0, FFN, 512):
                n = min(512, FFN - c)
                load_cast3(wg[:, :, c:c + n], wg_v[:, :, c:c + n], [8, n])
                load_cast3(wu[:, :, c:c + n], wu_v[:, :, c:c + n], [8, n])
            for fc0 in range(0, NFC, 4):
                nf = min(4, NFC - fc0)
                load_cast3(wd_[:, fc0:fc0 + nf, :], wd_v[:, fc0:fc0 + nf, :], [nf, D])
            for kc in range(8):
                op("pe", lambda e, kc=kc: e.transpose(PS[kc // 4][0:5, (kc % 4) * 128:(kc % 4 + 1) * 128], gate2[:, kc, :], ident), r=["modT", "ident"], w=[f"ps{kc // 4}"])
            op("dve", lambda e: e.tensor_copy(grow2[:, 0:512], PS[0][0:5, :]), r=["ps0"], w=["grow2"])
            op("dve", lambda e: e.tensor_copy(grow2[:, 512:1024], PS[1][0:5, :]), r=["ps1"], w=["grow2"])
            dma("sp", lambda e: e.dma_start(out=gsc[1], in_=grow2), r=["grow2"], w=["gsc1"])
            dma("sp", lambda e: e.dma_start(out=g2rep, in_=gsc[1, 0].partition_broadcast(128)), r=["gsc1"], w=["g2rep"])
            dma("sp", lambda e: e.dma_start(out=g2s, in_=gsc[1, 1:5, :]), r=["gsc1"], w=["g2s"])
            S.barrier()
            AR.release(m3)
            x1t = [AR.alloc([D], F32) for _ in range(1)]
            xn3 = AR.alloc([D], F32)
            h2T = AR.alloc([8, 128], BF16)
            hidT = AR.alloc([NFC, 128], BF16)
            sil = [AR.alloc([128], F32) for _ in range(2)]
            yt = [AR.alloc([D], F32)] * 2
            sm3 = AR.alloc([8], F32)
            scr3 = AR.alloc([NS], F32)

            def ffn_group(tiles, is_sample):
                ncol = 0
                for i_, (P, src, dst) in enumerate(tiles):
                    xb = x1t[i_]
                    dma("sp", lambda e, xb=xb, P=P, src=src: e.dma_start(out=xb[0:P, :], in_=src), r=["y_p", "y_s"], w=[f"x1t{i_}"])
                    ss = sm3[0:P, 0:1]
                    rs = sm3[0:P, 1:2]
                    op("act", lambda e, xb=xb, P=P, ss=ss: e.activation(xn3[0:P, :], xb[0:P, :], AF.Square, accum_out=ss), r=[f"x1t{i_}"], w=["xn3", "sm3"])
                    op("dve", lambda e, ss=ss, rs=rs: e.tensor_scalar(rs, ss, 1.0 / D, RMS_EPS, ALU.mult, ALU.add), r=["sm3"], w=["sm3"])
                    op("act", lambda e, rs=rs: e.activation(rs, rs, AF.Sqrt), r=["sm3"], w=["sm3"])
                    op("dve", lambda e, rs=rs: e.reciprocal(rs, rs), r=["sm3"], w=["sm3"])
                    op("dve", lambda e, xb=xb, P=P, rs=rs: e.tensor_scalar(xn3[0:P, :], xb[0:P, :], rs, None, ALU.mult), r=[f"x1t{i_}", "sm3"], w=["xn3"])
                    for half in range(2):
                        pb = PS[half]
                        for j in range(4):
                            kc = half * 4 + j
                            op("pe", lambda e, kc=kc, j=j, pb=pb, P=P: e.transpose(pb[:, j * 128:j * 128 + P], xn3[0:P, kc * 128:(kc + 1) * 128], ident[0:P, 0:P]),
                               r=["xn3", "ident"], w=[f"ps{half}"])
                        for j in range(4):
                            kc = half * 4 + j
                            if not is_sample:
                                op("act", lambda e, kc=kc, j=j, pb=pb, P=P, ncol=ncol: e.activation(h2T[:, kc, ncol:ncol + P], pb[:, j * 128:j * 128 + P], AF.Identity,
                                                                                                  bias=shift2[:, kc, 0:1], scale=A2[:, kc, 0:1]), r=[f"ps{half}", "A2", "modT"], w=["h2T"])
                            else:
                                op("dve", lambda e, kc=kc, j=j, pb=pb, P=P: e.tensor_tensor(scr3[:, 0:P], pb[:, j * 128:j * 128 + P], A2[:, kc, 1:1 + P], ALU.mult),
                                   r=[f"ps{half}", "A2"], w=["scr3"])
                                op("dve", lambda e, kc=kc, P=P, ncol=ncol: e.tensor_tensor(h2T[:, kc, ncol:ncol + P], scr3[:, 0:P], shift2[:, kc, 1:1 + P], ALU.add),
                                   r=["scr3", "modT"], w=["h2T"])
                    ncol += P
                N = ncol
                for fc in range(NFC):
                    pg = PS[2 + fc % 2]
                    pu_ = PS[4 + fc % 2]
                    sb = sil[fc % 2]
                    for kc in range(8):
                        op("pe", lambda e, fc=fc, kc=kc, pg=pg: e.matmul(pg[:, 0:N], lhsT=wg[:, kc, fc * 128:(fc + 1) * 128], rhs=h2T[:, kc, 0:N], start=(kc == 0), stop=(kc == 7)),
                           r=["h2T", "wts3"], w=[f"ps{2 + fc % 2}"])
                    for kc in range(8):
                        op("pe", lambda e, fc=fc, kc=kc, pu_=pu_: e.matmul(pu_[:, 0:N], lhsT=wu[:, kc, fc * 128:(fc + 1) * 128], rhs=h2T[:, kc, 0:N], start=(kc == 0), stop=(kc == 7)),
                           r=["h2T", "wts3"], w=[f"ps{4 + fc % 2}"])
                    op("act", lambda e, pg=pg, sb=sb: e.activation(sb[:, 0:N], pg[:, 0:N], AF.Silu), r=[f"ps{2 + fc % 2}"], w=[f"sil{fc % 2}"])
                    op("dve", lambda e, fc=fc, pu_=pu_, sb=sb: e.tensor_tensor(hidT[:, fc, 0:N], sb[:, 0:N], pu_[:, 0:N], ALU.mult), r=[f"sil{fc % 2}", f"ps{4 + fc % 2}"], w=["hidT"])
                ncol = 0
                for i_, (P, src, dst) in enumerate(tiles):
                    xb = x1t[i_]
                    yb = yt[i_ % 2]
                    grep_ = g2s if is_sample else g2rep
                    for half in range(2):
                        cs = slice(half * 512, (half + 1) * 512)
                        pb = 6 + half
                        for fc in range(NFC):
                            op("pe", lambda e, fc=fc, cs=cs, pb=pb, P=P, ncol=ncol: e.matmul(PS[pb][0:P, :], lhsT=hidT[:, fc, ncol:ncol + P], rhs=wd_[:, fc, cs],
                                                                                         start=(fc == 0), stop=(fc == NFC - 1)), r=["hidT", "wts3"], w=[f"ps{pb}"])
                        op("dve", lambda e, cs=cs, pb=pb, P=P, yb=yb, grep_=grep_: e.tensor_tensor(yb[0:P, cs], PS[pb][0:P, :], grep_[0:P, cs], ALU.mult),
                           r=[f"ps{pb}", "g2rep", "g2s"], w=["yt0"])
                        op("dve", lambda e, cs=cs, P=P, yb=yb, xb=xb: e.tensor_tensor(yb[0:P, cs], yb[0:P, cs], xb[0:P, cs], ALU.add), r=["yt0", f"x1t{i_}"], w=["yt0"])
                    dma("sp", lambda e, yb=yb, P=P, dst=dst: e.dma_start(out=dst, in_=yb[0:P, :]), r=["yt0"], w=["y_out"])
                    ncol += P

            for g in range(NTILES):
                tl_ = []
                for tl in range(1):
                    ti = g + tl
                    if ti < NTILES:
                        tl_.append((128, y_p[ti * 128:(ti + 1) * 128, :], y_p[ti * 128:(ti + 1) * 128, :]))
                ffn_group(tl_, False)
            ffn_group([(NS, y_s, y_s)], True)

        if os.environ.get("MK_DBG_ATT", "0") == "1":
            dbg_att = nc.dram_tensor("dbg_att", [128, 4, NTILES * 128], BF16, kind="ExternalOutput").ap()
            dma("sp", lambda e: e.dma_start(out=dbg_att, in_=attT[:, :, 0:NTILES * 128]), r=["attT"])
        S.finish()
        with nc.Block() as block:
            S.emit(block)
        print("arena peak words", AR.peak, "of", AR.n)
    return nc


_NC_CACHE = {}


def kernel(x_prompt, x_sample, cache_k, cache_v, cache_idx_k, state_wkv, state_shift, page_table,
           c_prompt, c_sample, norm1_g, norm2_g, w_ada, b_ada, w_in, q_norm_g, k_norm_g,
           rw_mu, rw_w0, rw_w2, rw_a0, rw_a2, rw_g2, rw_k_k, rw_k_a, rw_r_k, rw_ln_w, rw_ln_b,
           w_out, w_ffn_gate, w_ffn_up, w_ffn_down):
    f = lambda a: np.ascontiguousarray(np.asarray(a, dtype=np.float32))
    if "nc" not in _NC_CACHE:
        _NC_CACHE["nc"] = build_program()
    nc = _NC_CACHE["nc"]
    if NOCACHE:
        ck = np.zeros((128, 512), np.float32)
        cv = ck
        cik = np.zeros((128, 64), np.float32)
    else:
        ck = f(cache_k).reshape(NPOOL * PAGE, 512)
        cv = f(cache_v).reshape(NPOOL * PAGE, 512)
        cik = f(cache_idx_k).reshape(NPOOL * PAGE, 64)
    shared = {
        "cache_k": ck, "cache_v": cv, "cache_ik": cik,
        "norm1_g": f(norm1_g).reshape(D), "norm2_g": f(norm2_g).reshape(D),
        "w_ada": f(w_ada).reshape(D, 6 * D), "b_ada": f(b_ada).reshape(6 * D),
        "w_in": f(w_in).reshape(D, IN_COLS), "q_norm_g": f(q_norm_g).reshape(64), "k_norm_g": f(k_norm_g).reshape(64),
        "rw_mu": f(rw_mu).reshape(RW_COLS), "rw_w0": f(rw_w0).reshape(512), "rw_w2": f(rw_w2).reshape(64, 512),
        "rw_a0": f(rw_a0).reshape(512), "rw_a2": f(rw_a2).reshape(64, 512), "rw_g2": f(rw_g2).reshape(128, 512),
        "rw_k_k": f(rw_k_k).reshape(512), "rw_k_a": f(rw_k_a).reshape(512), "rw_r_k": f(rw_r_k).reshape(512),
        "rw_ln_w": f(rw_ln_w).reshape(512), "rw_ln_b": f(rw_ln_b).reshape(512),
        "w_out": f(w_out).reshape(D, D), "w_gate": f(w_ffn_gate).reshape(D, FFN), "w_up": f(w_ffn_up).reshape(D, FFN),
        "w_down": f(w_ffn_down).reshape(FFN, D),
    }
    xp = f(x_prompt)
    xs = f(x_sample).reshape(32, D)
    cp = f(c_prompt)
    cs = f(c_sample)
    sw = f(state_wkv).reshape(32, 8, 64, 64)
    ss = f(state_shift).reshape(32, RW_COLS)
    pt = np.ascontiguousarray(np.asarray(page_table, dtype=np.int32))
    in_maps = []
    for i in range(8):
        m = dict(shared)
        m["x_p"] = xp[i]
        m["x_s"] = np.ascontiguousarray(xs[4 * i:4 * i + 4])
        m["c5"] = np.ascontiguousarray(np.concatenate([cp[i:i + 1], cs[4 * i:4 * i + 4]], axis=0))
        m["st_wkv"] = np.ascontiguousarray(sw[4 * i:4 * i + 4])
        m["st_shift"] = np.ascontiguousarray(ss[4 * i:4 * i + 4])
        m["ptab"] = np.ascontiguousarray(pt[4 * i:4 * i + 4])
        in_maps.append(m)
    res = run_bass_kernel_spmd(nc, in_maps, core_ids=list(range(8)))
    R = res.results
    _NC_CACHE["last"] = R
    g = lambda name: np.stack([np.asarray(R[i][name]) for i in range(8)], axis=0)
    y_p = g("y_p")
    y_s = g("y_s").reshape(32, 1, D)
    k_p = g("k_po").reshape(1, 8, T, 8, 64)
    v_p = g("v_po").reshape(1, 8, T, 8, 64)
    ik_p = g("ik_po").reshape(1, 8, T, 64)
    wkv_p = g("wkv_po").reshape(1, 8, 8, 64, 64)
    sh_p = g("sh_po").reshape(1, 8, RW_COLS)
    k_s = g("k_so").reshape(1, 32, 1, 8, 64)
    v_s = g("v_so").reshape(1, 32, 1, 8, 64)
    ik_s = g("ik_so").reshape(1, 32, 1, 64)
    wkv_s = g("wkv_so").reshape(1, 32, 8, 64, 64)
    sh_s = g("sh_so").reshape(1, 32, RW_COLS)
    return (y_p, y_s, k_p, v_p, ik_p, wkv_p, sh_p, k_s, v_s, ik_s, wkv_s, sh_s)
```

```python
import os
import numpy as np
from contextlib import ExitStack
import concourse.bass as bass
import concourse.mybir as mybir
from concourse.bass_utils import run_bass_kernel_spmd

F32 = mybir.dt.float32
BF16 = mybir.dt.bfloat16
I32 = mybir.dt.int32
U32 = mybir.dt.uint32
F32R = mybir.dt.float32r
AF = mybir.ActivationFunctionType
ALU = mybir.AluOpType
AX = mybir.AxisListType

D = 1024
T = 2048
NT = T // 128
NS = 4
NPAGE = 64
PAGE = 128
NPOOL = 2560
IN_COLS = 4432
ATT_COLS = 2640
RW_COLS = 1792
FFN = 2816
NFC = FFN // 128
IDX_W_SCALE = (16 ** -0.5) * (64 ** -0.5)
RMS_EPS = 1e-6
GN_EPS = 64e-5
BIG = 1.0e30

_DT_SIZE = {F32: 4, BF16: 2, I32: 4, U32: 4}


def _prod(s):
    r = 1
    for x in s:
        r *= x
    return r


class Arena:
    def __init__(self, ap):
        self.ap = ap
        self.off = 0
        self.n = ap.shape[1]
        self.peak = 0

    def mark(self):
        return self.off

    def release(self, m):
        self.off = m

    def alloc(self, shape, dtype=F32, parts=128):
        if isinstance(shape, int):
            shape = [shape]
        ne = _prod(shape)
        nw = (ne * _DT_SIZE[dtype] + 3) // 4
        nw = (nw + 1) // 2 * 2
        assert self.off + nw <= self.n, f"arena overflow {self.off}+{nw}>{self.n}"
        v = self.ap[0:parts, self.off:self.off + nw]
        self.off += nw
        self.peak = max(self.peak, self.off)
        if dtype != F32:
            v = v.bitcast(dtype)
        v = v[:, 0:ne]
        if len(shape) == 2:
            v = v.rearrange("p (a b) -> p a b", a=shape[0])
        elif len(shape) == 3:
            v = v.rearrange("p (a b c) -> p a b c", a=shape[0], b=shape[1])
        return v


class Sched:
    CH = 2000
    NSEM = {"pe": 22, "act": 14, "dve": 18, "pool": 10}
    NDMA = 10
    NPDMA = 20

    def __init__(self, nc, es):
        self.nc = nc
        self.E = {"pe": nc.tensor, "act": nc.scalar, "dve": nc.vector, "pool": nc.gpsimd, "sp": nc.sync}
        self.sem = {e: [es.enter_context(nc.semaphore(f"s_{e}{k}")) for k in range(n)] for e, n in self.NSEM.items()}
        self.dsem = [es.enter_context(nc.semaphore(f"s_dma{k}")) for k in range(self.NDMA)]
        self.psem = [es.enter_context(nc.semaphore(f"s_pdma{k}")) for k in range(self.NPDMA)]
        self.pmark = None
        self.pnext = 0
        self.duse = [0] * self.NDMA
        self.dnext = 0
        self.ops = {e: [] for e in self.E}
        self.cnt = {e: 0 for e in self.NSEM}
        self.seen = {e: {} for e in self.E}
        self.know = {e: {} for e in self.E}
        self.vc = {e: [] for e in self.NSEM}
        self.dvc = {}
        self.lastw = {}
        self.readers = {}
        self.all_dma = []

    def _deps(self, eng, r, w):
        deps = []
        for x in r:
            if x in self.lastw:
                deps.append(self.lastw[x])
        for x in w:
            if x in self.lastw:
                deps.append(self.lastw[x])
            deps.extend(self.readers.get(x, []))
        waits = []
        seen = self.seen[eng]
        for tok in deps:
            if tok[0] == "e":
                _, e2, idx = tok
                if e2 == eng and eng == "pe":
                    continue
                if seen.get(e2, 0) >= idx:
                    continue
                seen[e2] = idx
            else:
                _, slot, val = tok
                if seen.get(("d", slot), 0) >= val:
                    continue
                seen[("d", slot)] = val
        return deps

    def _waits_for(self, eng, deps):
        need_e = {}
        need_d = {}
        for tok in deps:
            if tok[0] == "e":
                _, e2, idx = tok
                if e2 == eng and eng == "pe":
                    continue
                need_e[e2] = max(need_e.get(e2, 0), idx)
            else:
                _, slot, val = tok
                need_d[slot] = max(need_d.get(slot, 0), val)
        out = []
        seen = self.seen[eng]
        know = self.know[eng]
        for e2, idx in sorted(need_e.items(), key=lambda kv: -kv[1]):
            if know.get(e2, 0) >= idx:
                continue
            k, v = (idx - 1) // self.CH, (idx - 1) % self.CH + 1
            out.append((self.sem[e2][k], v))
            for e3, i3 in self.vc[e2][idx - 1].items():
                if know.get(e3, 0) < i3:
                    know[e3] = i3
        for slot, val in need_d.items():
            if seen.get(("d", slot), 0) >= val:
                continue
            seen[("d", slot)] = val
            out.append((self.dsem[slot], val))
            for e3, i3 in self.dvc.get((slot, val), {}).items():
                if know.get(e3, 0) < i3:
                    know[e3] = i3
        return out

    def _collect(self, r, w):
        deps = []
        for x in r:
            if x in self.lastw:
                deps.append(self.lastw[x])
        for x in w:
            if x in self.lastw:
                deps.append(self.lastw[x])
            deps.extend(self.readers.get(x, []))
        return deps

    def _commit(self, tok, r, w):
        for x in r:
            self.readers.setdefault(x, []).append(tok)
        for x in w:
            self.lastw[x] = tok
            self.readers[x] = []

    def op(self, eng, fn, r=(), w=()):
        deps = self._collect(r, w)
        waits = self._waits_for(eng, deps)
        self.cnt[eng] += 1
        idx = self.cnt[eng]
        k = (idx - 1) // self.CH
        assert k < len(self.sem[eng]), f"too many ops on {eng}"
        self.ops[eng].append((waits, fn, self.sem[eng][k], 1))
        tok = ("e", eng, idx)
        clk = dict(self.know[eng])
        clk[eng] = idx
        self.vc[eng].append(clk)
        self._commit(tok, r, w)
        return tok

    def dma(self, q, fn, r=(), w=()):
        deps = self._collect(r, w)
        slot = self.dnext
        self.dnext = (self.dnext + 1) % self.NDMA
        if self.duse[slot] > 0:
            deps.append(("d", slot, 16 * self.duse[slot]))
        waits = self._waits_for(q, deps)
        self.duse[slot] += 1
        val = 16 * self.duse[slot]
        self.ops[q].append((waits, fn, self.dsem[slot], 16))
        tok = ("d", slot, val)
        self.dvc[(slot, val)] = dict(self.know[q])
        self._commit(tok, r, w)
        self.all_dma.append(tok)
        return tok

    def pool_dma_batch(self, fns, r=(), w=()):
        assert len(fns) <= self.NPDMA
        deps = self._collect(r, w)
        waits = self._waits_for("pool", deps)
        psem = self.psem
        n = len(fns)

        def g(e):
            for i, fn in enumerate(fns):
                fn(e).then_inc(psem[i], 16)
            for i in range(n):
                e.wait_ge(psem[i], 16)
        self.ops["pool"].append((waits, ("raw", g), None, 0))
        pm = self.pmark
        tok = self.op("pool", lambda e: e.memset(pm, 0.0), r=r, w=list(w) + ["pmark"])
        self.barrier()

        def clr(e):
            for i in range(n):
                e.sem_clear(psem[i])
        self.ops["pool"].append(([], ("raw", clr), None, 0))
        return tok

    def pool_dma_once(self, fns, r=(), w=()):
        deps = self._collect(r, w)
        waits = self._waits_for("pool", deps)
        sems = [self.psem[self.pnext + i] for i in range(len(fns))]
        self.pnext += len(fns)
        assert self.pnext <= self.NPDMA

        def g(e):
            for sm, fn in zip(sems, fns):
                fn(e).then_inc(sm, 16)
            for sm in sems:
                e.wait_ge(sm, 16)
        self.ops["pool"].append((waits, ("raw", g), None, 0))
        pm = self.pmark
        return self.op("pool", lambda e: e.memset(pm, 0.0), r=r, w=list(w) + ["pmark"])

    def barrier(self):
        toks = [("e", e, c) for e, c in self.cnt.items() if c > 0]
        toks += [("d", s, 16 * u) for s, u in enumerate(self.duse) if u > 0]
        for eng in self.E:
            waits = self._waits_for(eng, toks)
            if waits:
                self.ops[eng].append((waits, None, None, 0))

    def finish(self):
        toks = [("d", s, 16 * u) for s, u in enumerate(self.duse) if u > 0]
        toks += [("e", e, c) for e, c in self.cnt.items() if c > 0]
        waits = self._waits_for("sp", toks)
        self.ops["sp"].append((waits, None, None, 0))

    def emit(self, block):
        def mk(ename):
            def body(e):
                for waits, fn, sem, inc in self.ops[ename]:
                    for s, v in waits:
                        e.wait_ge(s, v)
                    if isinstance(fn, tuple):
                        fn[1](e)
                    elif fn is not None:
                        ins = fn(e)
                        ins.then_inc(sem, inc)
            return body
        block.tensor(mk("pe"))
        block.scalar(mk("act"))
        block.vector(mk("dve"))
        block.gpsimd(mk("pool"))
        block.sync(mk("sp"))


def bc_last(ap, n):
    return ap.unsqueeze(len(ap.shape)).to_broadcast(list(ap.shape) + [n])


def bc_mid(ap, n):
    return ap.unsqueeze(1).to_broadcast([ap.shape[0], n, ap.shape[1]])


STAGE = int(os.environ.get("MK_STAGE", "99"))
ATT_CUT = int(os.environ.get("MK_ATT_CUT", "99"))
P2_CUT = int(os.environ.get("MK_P2_CUT", "99"))
SAMPLE_ATT = os.environ.get("MK_SAMPLE_ATT", "1") == "1"
SCAN_CUT = int(os.environ.get("MK_SCAN_CUT", "99"))
NOCACHE = os.environ.get("MK_NOCACHE", "0") == "1"


def build_program():
    nc = bass.Bass("TRN2", target_bir_lowering=False)

    def din(name, shape, dt=F32):
        return nc.dram_tensor(name, list(shape), dt, kind="ExternalInput").ap()

    def dout(name, shape, dt=F32):
        return nc.dram_tensor(name, list(shape), dt, kind="ExternalOutput").ap()

    x_p = din("x_p", [T, D])
    x_s = din("x_s", [NS, D])
    c5 = din("c5", [5, D])
    st_wkv = din("st_wkv", [NS, 8, 64, 64])
    st_shift = din("st_shift", [NS, RW_COLS])
    ptab = din("ptab", [NS, NPAGE], I32)
    npool_rows = 128 if NOCACHE else NPOOL * PAGE
    cache_k = din("cache_k", [npool_rows, 512])
    cache_v = din("cache_v", [npool_rows, 512])
    cache_ik = din("cache_ik", [npool_rows, 64])
    norm1_g = din("norm1_g", [D])
    norm2_g = din("norm2_g", [D])
    w_ada = din("w_ada", [D, 6 * D])
    b_ada = din("b_ada", [6 * D])
    w_in = din("w_in", [D, IN_COLS])
    q_norm_g = din("q_norm_g", [64])
    k_norm_g = din("k_norm_g", [64])
    rw_mu = din("rw_mu", [RW_COLS])
    rw_w0 = din("rw_w0", [512])
    rw_w2 = din("rw_w2", [64, 512])
    rw_a0 = din("rw_a0", [512])
    rw_a2 = din("rw_a2", [64, 512])
    rw_g2 = din("rw_g2", [128, 512])
    rw_k_k = din("rw_k_k", [512])
    rw_k_a = din("rw_k_a", [512])
    rw_r_k = din("rw_r_k", [512])
    rw_ln_w = din("rw_ln_w", [512])
    rw_ln_b = din("rw_ln_b", [512])
    w_out = din("w_out", [D, D])
    w_gate = din("w_gate", [D, FFN])
    w_up = din("w_up", [D, FFN])
    w_down = din("w_down", [FFN, D])

    y_p = dout("y_p", [T, D])
    y_s = dout("y_s", [NS, D])
    k_po = dout("k_po", [T, 512])
    v_po = dout("v_po", [T, 512])
    ik_po = dout("ik_po", [T, 64])
    wkv_po = dout("wkv_po", [8, 64, 64])
    sh_po = dout("sh_po", [RW_COLS])
    k_so = dout("k_so", [NS, 512])
    v_so = dout("v_so", [NS, 512])
    ik_so = dout("ik_so", [NS, 64])
    wkv_so = dout("wkv_so", [NS, 8, 64, 64])
    sh_so = dout("sh_so", [NS, RW_COLS])

    es = ExitStack()
    with es:
        arena_t = es.enter_context(nc.sbuf_tensor("arena", [128, 47640], F32))
        AR = Arena(arena_t[:, :])
        ptrow_t = es.enter_context(nc.sbuf_tensor("ptrow", [1, NS * NPAGE], I32))
        PS = [es.enter_context(nc.psum_tensor(f"psb{i}", [128, 512], F32))[:, :] for i in range(8)]
        S = Sched(nc, es)
        op, dma = S.op, S.dma
        S.pmark = AR.alloc([2], F32)

        Xall_r = es.enter_context(nc.sbuf_tensor("xall_r", [128, 192], F32R))[:, :]
        Xk_sel_r = es.enter_context(nc.sbuf_tensor("xksel_r", [128, 1024], F32R))[:, :].rearrange("p (a b) -> p a b", a=16)
        Vbd16_r = es.enter_context(nc.sbuf_tensor("vbd16_r", [128, 512], F32R))[:, :].rearrange("p (a b) -> p a b", a=8)
        R128_r = [es.enter_context(nc.sbuf_tensor(f"r128_r{i}", [128, 512], F32R))[:, :].rearrange("p (a b) -> p a b", a=8) for i in range(2)]
        Xall = Xall_r.bitcast(F32)
        STr = es.enter_context(nc.sbuf_tensor("st_r", [64, 512], F32R))[:, :]
        LA_sel_r = es.enter_context(nc.sbuf_tensor("lasel_r", [64, 2048], F32R))[:, :].rearrange("p (a b) -> p a b", a=16)
        iot = AR.alloc([128], I32)
        ident = AR.alloc([128], F32)
        identb = AR.alloc([128], BF16)
        negmask = AR.alloc([128], F32)
        ones_f = AR.alloc([128], F32)
        op("pool", lambda e: e.iota(iot, [[1, 128]], base=0, channel_multiplier=-1), w=["iot"])
        op("dve", lambda e: e.tensor_single_scalar(ident, iot, 0, ALU.is_equal), r=["iot"], w=["ident"])
        op("dve", lambda e: e.tensor_single_scalar(identb, iot, 0, ALU.is_equal), r=["iot"], w=["identb"])
        op("dve", lambda e: e.tensor_scalar(negmask, iot, 0, -BIG, ALU.is_gt, ALU.mult), r=["iot"], w=["negmask"])
        op("pool", lambda e: e.memset(ones_f, 1.0), w=["ones_f"])

        vecA = AR.alloc([78], F32)
        vecB = AR.alloc([56], F32, parts=64)
        modT = AR.alloc([48, 5], F32)
        A1 = AR.alloc([8, 5], F32)
        A2 = AR.alloc([8, 5], F32)
        gq_rep = AR.alloc([64], F32)
        gk_rep = AR.alloc([64], F32)
        m0 = AR.mark()
        c5t = AR.alloc([D], F32, parts=5)
        sct = AR.alloc([D], F32, parts=5)
        scT = AR.alloc([8, 5], F32)
        stA = AR.alloc([128], F32, parts=78)
        stB = AR.alloc([64], F32, parts=56)
        wst = [AR.alloc([8, 512], F32) for _ in range(2)]

        dma("sp", lambda e: e.dma_start(out=c5t, in_=c5), w=["c5t"])
        dma("sp", lambda e: e.dma_start(out=stA[0:48, :], in_=b_ada.rearrange("(a b) -> a b", b=128)), w=["stA0"])
        dma("sp", lambda e: e.dma_start(out=stA[48:56, :], in_=norm1_g.rearrange("(a b) -> a b", b=128)), w=["stA1"])
        dma("sp", lambda e: e.dma_start(out=stA[56:64, :], in_=norm2_g.rearrange("(a b) -> a b", b=128)), w=["stA2"])
        dma("sp", lambda e: e.dma_start(out=stA[64:78, :], in_=rw_mu.rearrange("(a b) -> a b", b=128)), w=["stA3"])
        for i, v in enumerate([rw_w0, rw_a0, rw_k_k, rw_k_a, rw_ln_w, rw_ln_b, rw_r_k]):
            dma("sp", lambda e, v=v, i=i: e.dma_start(out=stB[8 * i:8 * i + 8, :], in_=v.rearrange("(a b) -> a b", b=64)), w=[f"stB{i}"])
        dma("sp", lambda e: e.dma_start(out=gq_rep, in_=q_norm_g.partition_broadcast(128)), w=["gq_rep"])
        dma("sp", lambda e: e.dma_start(out=gk_rep, in_=k_norm_g.partition_broadcast(128)), w=["gk_rep"])

        op("act", lambda e: e.activation(sct, c5t, AF.Silu), r=["c5t"], w=["sct"])
        for kc in range(8):
            op("pe", lambda e, kc=kc: e.transpose(PS[0][:, kc * 8:kc * 8 + 5], sct[:, kc * 128:(kc + 1) * 128], ident[0:5, 0:5]),
               r=["sct", "ident"], w=["ps0"])
        op("dve", lambda e: e.tensor_copy(scT, PS[0][:, 0:64].rearrange("p (a b) -> p a b", b=8)[:, :, 0:5]), r=["ps0"], w=["scT"])
        op("pe", lambda e: e.transpose(PS[1][:, 0:78], stA, ident[0:78, 0:78]), r=["stA0", "stA1", "stA2", "stA3", "ident"], w=["ps1"])
        op("dve", lambda e: e.tensor_copy(vecA, PS[1][:, 0:78]), r=["ps1"], w=["vecA"])
        op("pe", lambda e: e.transpose(PS[1][0:64, 128:184], stB, ident[0:56, 0:56]), r=[f"stB{i}" for i in range(7)] + ["ident"], w=["ps1"])
        op("dve", lambda e: e.tensor_copy(vecB, PS[1][0:64, 128:184]), r=["ps1"], w=["vecB"])
        w_ada_v = w_ada.rearrange("(kc p) n -> p kc n", p=128)
        for nb in range(12):
            b = nb % 2
            dma("sp", lambda e, nb=nb, b=b: e.dma_start(out=wst[b], in_=w_ada_v[:, :, nb * 512:(nb + 1) * 512]), w=[f"wst{b}"])
            for fc in range(4):
                col = (nb * 4 + fc) * 8
                for kc in range(8):
                    op("pe", lambda e, b=b, fc=fc, kc=kc, col=col: e.matmul(PS[2][:, col:col + 5], lhsT=wst[b][:, kc, fc * 128:(fc + 1) * 128],
                                                                            rhs=scT[:, kc, :], start=(kc == 0), stop=(kc == 7)),
                       r=[f"wst{b}", "scT"], w=["ps2"])
        op("dve", lambda e: e.tensor_tensor(modT, PS[2][:, 0:384].rearrange("p (a b) -> p a b", b=8)[:, :, 0:5], bc_last(vecA[:, 0:48], 5), ALU.add),
           r=["ps2", "vecA"], w=["modT"])
        op("dve", lambda e: e.tensor_scalar(A1, modT[:, 8:16, :], 1.0, None, ALU.add), r=["modT"], w=["A1"])
        op("dve", lambda e: e.tensor_tensor(A1, A1, bc_last(vecA[:, 48:56], 5), ALU.mult), r=["A1", "vecA"], w=["A1"])
        op("dve", lambda e: e.tensor_scalar(A2, modT[:, 32:40, :], 1.0, None, ALU.add), r=["modT"], w=["A2"])
        op("dve", lambda e: e.tensor_tensor(A2, A2, bc_last(vecA[:, 56:64], 5), ALU.mult), r=["A2", "vecA"], w=["A2"])
        S.barrier()
        AR.release(m0)

        if STAGE <= 0:
            dbg_mod = nc.dram_tensor("dbg_mod", [128, 240], F32, kind="ExternalOutput").ap()
            dma("sp", lambda e: e.dma_start(out=dbg_mod, in_=modT.rearrange("p a b -> p (a b)")), r=["modT"])
            S.finish()
            with nc.Block() as block:
                S.emit(block)
            return nc

        shift1 = modT[:, 0:8, :]
        gate1 = modT[:, 16:24, :]
        shift2 = modT[:, 24:32, :]
        gate2 = modT[:, 40:48, :]

        m_attT = AR.mark()
        attT = AR.alloc([4, T + NS], BF16)
        m_att = AR.mark()
        win_att = AR.alloc([8, ATT_COLS], BF16)
        KT = AR.alloc([4, T], BF16)
        Vaug = AR.alloc([NT, 8, 65], BF16)
        kiT = AR.alloc([T], BF16)
        wabs = AR.alloc([NT, 16], F32)
        wsgn = AR.alloc([NT, 16], F32)
        qiTs = AR.alloc([8, NS], BF16)
        qs32 = AR.alloc([512], F32, parts=NS)
        ks32 = AR.alloc([512], F32, parts=NS)
        vs32 = AR.alloc([512], F32, parts=NS)
        kis32 = AR.alloc([64], F32, parts=NS)
        ws_abs = AR.alloc([16], F32, parts=NS)
        ws_sgn = AR.alloc([16], F32, parts=NS)
        m1 = AR.mark()
        wst = [AR.alloc([8, 512], F32) for _ in range(2)]
        w_in_v = w_in.rearrange("(kc p) n -> p kc n", p=128)
        nblk = 0
        for (dst, c0, c1) in [(win_att, 0, ATT_COLS)]:
            c = c0
            while c < c1:
                n = min(512, c1 - c)
                b = nblk % 2
                dma("sp", lambda e, b=b, c=c, n=n: e.dma_start(out=wst[b][:, :, 0:n], in_=w_in_v[:, :, c:c + n]), w=[f"wst{b}"])
                op("pool", lambda e, b=b, c=c, n=n, dst=dst, c0=c0: e.tensor_copy(dst[:, :, c - c0:c - c0 + n], wst[b][:, :, 0:n]),
                   r=[f"wst{b}"], w=["win"])
                c += n
                nblk += 1
        op("pool", lambda e: e.memset(Vaug[:, :, :, 64:65], 1.0), w=["Vaug"])
        S.barrier()
        AR.release(m1)

        xt = [AR.alloc([D], F32)] * 2
        xn = AR.alloc([D], F32)
        hT = AR.alloc([8, 512], BF16)
        scr = [AR.alloc([512], F32) for _ in range(3)]
        small = AR.alloc([64], F32)
        qnb = AR.alloc([512], BF16)
        knb = AR.alloc([512], BF16)
        kidup = AR.alloc([128], BF16)
        qT = AR.alloc([4, 512], BF16)
        qiT = AR.alloc([8, 512], BF16)
        hTs = AR.alloc([8, NS], BF16)
        v32 = AR.alloc([512], F32)
        ki32 = AR.alloc([64], F32)
        PSb = [p.bitcast(BF16) for p in PS]
        Ibuf = AR.alloc([T], F32)
        Rbuf = [AR.alloc([512], F32) for _ in range(2)]
        maskb = AR.alloc([T], BF16)
        maskT = AR.alloc([NT, 128], BF16)
        PTb = [AR.alloc([512], BF16) for _ in range(2)]
        PmT = [AR.alloc([4, 128], BF16) for _ in range(2)]
        attb = AR.alloc([512], BF16)
        bs = AR.alloc([16], F32)
        NIT = int(os.environ.get("MK_NIT", "18"))
        NTILES = int(os.environ.get("MK_NT", str(NT)))

        def rms_rstd(ssum, n, eps, dst, P):
            op("dve", lambda e: e.tensor_scalar(dst, ssum, 1.0 / n, eps, ALU.mult, ALU.add), r=["small"], w=["small"])
            op("act", lambda e: e.activation(dst, dst, AF.Sqrt), r=["small"], w=["small"])
            op("dve", lambda e: e.reciprocal(dst, dst), r=["small"], w=["small"])

        def front_tile(ti, P, x_src, cond, hT_dst, k_dst, v_dst, ik_dst, tcol):
            b = 0
            xtb = xt[b]
            dma("sp", lambda e: e.dma_start(out=xtb[0:P, :], in_=x_src), w=[f"xt{b}"])
            ss = small[0:P, 0:1]
            rs = small[0:P, 1:2]
            op("act", lambda e: e.activation(xn[0:P, :], xtb[0:P, :], AF.Square, accum_out=ss), r=[f"xt{b}"], w=["xn", "small"])
            rms_rstd(ss, D, RMS_EPS, rs, P)
            op("dve", lambda e: e.tensor_scalar(xn[0:P, :], xtb[0:P, :], rs, None, ALU.mult), r=[f"xt{b}", "small"], w=["xn"])
            for half in range(2):
                pb = PS[half]
                for j in range(4):
                    kc = half * 4 + j
                    op("pe", lambda e, kc=kc, j=j, pb=pb: e.transpose(pb[:, j * 128:j * 128 + P], xn[0:P, kc * 128:(kc + 1) * 128], ident[0:P, 0:P]),
                       r=["xn", "ident"], w=[f"ps{half}"])
                if cond is None:
                    for j in range(4):
                        kc = half * 4 + j
                        op("act", lambda e, kc=kc, j=j, pb=pb: e.activation(hT_dst[:, kc, tcol:tcol + P], pb[:, j * 128:j * 128 + P], AF.Identity,
                                                                          bias=shift1[:, kc, 0:1], scale=A1[:, kc, 0:1]),
                           r=[f"ps{half}", "A1", "modT"], w=["hT"])
                else:
                    for j in range(4):
                        kc = half * 4 + j
                        op("dve", lambda e, kc=kc, j=j, pb=pb: e.tensor_tensor(scr[0][:, 0:P], pb[:, j * 128:j * 128 + P], A1[:, kc, 1:1 + P], ALU.mult),
                           r=[f"ps{half}", "A1"], w=["scr0"])
                        op("dve", lambda e, kc=kc: e.tensor_tensor(hT_dst[:, kc, tcol:tcol + P], scr[0][:, 0:P], shift1[:, kc, 1:1 + P], ALU.add),
                           r=["scr0", "modT"], w=["hT"])
            blocks = [(2, 0), (3, 512), (4, 1024), (5, 2560)]
            for (pbi, c0) in blocks:
                n = 512 if c0 < 2560 else 80
                for kc in range(8):
                    op("pe", lambda e, pbi=pbi, c0=c0, n=n, kc=kc: e.matmul(PS[pbi][0:P, 0:n], lhsT=hT_dst[:, kc, tcol:tcol + P],
                                                                          rhs=win_att[:, kc, c0:c0 + n], start=(kc == 0), stop=(kc == 7)),
                       r=["hT", "win"], w=[f"ps{pbi}"])
            for (pbi, grep, dstb, is_k) in [(2, gq_rep, qnb, False), (3, gk_rep, knb, True)]:
                pq = PS[pbi][0:P, :]
                ssq = small[0:P, 8:16]
                rq = small[0:P, 16:24]
                op("act", lambda e, pq=pq: e.activation(scr[0][0:P, :], pq, AF.Square), r=[f"ps{pbi}"], w=["scr0"])
                op("dve", lambda e, ssq=ssq: e.tensor_reduce(ssq, scr[0][0:P, :].rearrange("p (h d) -> p h d", d=64), AX.X, ALU.add), r=["scr0"], w=["small"])
                rms_rstd(ssq, 64, RMS_EPS, rq, P)
                op("dve", lambda e, pq=pq, rq=rq: e.tensor_tensor(scr[1][0:P, :].rearrange("p (h d) -> p h d", d=64), pq.rearrange("p (h d) -> p h d", d=64),
                                                               bc_last(rq, 64), ALU.mult), r=[f"ps{pbi}", "small"], w=["scr1"])
                if is_k:
                    op("pool", lambda e, grep=grep: e.tensor_tensor(scr[2][0:P, :].rearrange("p (h d) -> p h d", d=64), scr[1][0:P, :].rearrange("p (h d) -> p h d", d=64),
                                                                  bc_mid(grep[0:P, :], 8), ALU.mult), r=["scr1", "gk_rep"], w=["scr2"])
                    dma("sp", lambda e: e.dma_start(out=k_dst, in_=scr[2][0:P, :]), r=["scr2"])
                    op("pool", lambda e, dstb=dstb: e.tensor_copy(dstb[0:P, :], scr[2][0:P, :]), r=["scr2"], w=["knb"])
                else:
                    op("pool", lambda e, grep=grep, dstb=dstb: e.tensor_tensor(dstb[0:P, :].rearrange("p (h d) -> p h d", d=64), scr[1][0:P, :].rearrange("p (h d) -> p h d", d=64),
                                                                             bc_mid(grep[0:P, :], 8), ALU.mult), r=["scr1", "gq_rep"], w=["qnb"])
                    if cond is not None:
                        op("pool", lambda e: e.tensor_tensor(qs32.rearrange("p (h d) -> p h d", d=64), scr[1][0:P, :].rearrange("p (h d) -> p h d", d=64),
                                                             bc_mid(gq_rep[0:P, :], 8), ALU.mult), r=["scr1", "gq_rep"], w=["qs32"])
            is_s = cond is not None
            op("act", lambda e: e.activation(v32[0:P, :], PS[4][0:P, :], AF.Copy), r=["ps4"], w=["v32"])
            dma("sp", lambda e: e.dma_start(out=v_dst, in_=v32[0:P, :]), r=["v32"])
            if not is_s:
                op("pool", lambda e: e.tensor_copy(Vaug[:, ti, :, 0:64], v32.rearrange("p (h d) -> p h d", d=64)), r=["v32"], w=["Vaug"])
            else:
                op("pool", lambda e: e.tensor_copy(vs32, v32[0:P, :]), r=["v32"], w=["vs32"])
                op("pool", lambda e: e.tensor_copy(ks32, scr[2][0:P, :]), r=["scr2"], w=["ks32"])
            sk = small[0:P, 2:3]
            rk = small[0:P, 3:4]
            op("act", lambda e: e.activation(scr[0][0:P, 0:64], PS[5][0:P, 0:64], AF.Square, accum_out=sk), r=["ps5"], w=["scr0", "small"])
            rms_rstd(sk, 64, RMS_EPS, rk, P)
            op("dve", lambda e: e.tensor_scalar(ki32[0:P, :], PS[5][0:P, 0:64], rk, None, ALU.mult), r=["ps5", "small"], w=["ki32"])
            dma("sp", lambda e: e.dma_start(out=ik_dst, in_=ki32[0:P, :]), r=["ki32"])
            wa = ws_abs if is_s else wabs[:, ti, :]
            wsg = ws_sgn if is_s else wsgn[:, ti, :]
            op("act", lambda e: e.activation(wa, PS[5][0:P, 64:80], AF.Abs, scale=IDX_W_SCALE), r=["ps5"], w=["wabs"])
            op("act", lambda e: e.activation(wsg, PS[5][0:P, 64:80], AF.Sign), r=["ps5"], w=["wsgn"])
            if is_s:
                op("pool", lambda e: e.tensor_copy(kis32, ki32[0:P, :]), r=["ki32"], w=["kis32"])
                return
            tl = ti % 4
            op("pool", lambda e: e.tensor_copy(kidup[:, 0:64], ki32), r=["ki32"], w=["kidup"])
            op("pool", lambda e: e.tensor_copy(kidup[:, 64:128], ki32), r=["ki32"], w=["kidup"])
            for hp in range(4):
                op("pe", lambda e, hp=hp: e.transpose(PSb[6][:, hp * 128:(hp + 1) * 128], qnb[:, hp * 128:(hp + 1) * 128], identb), r=["qnb", "identb"], w=["ps6"])
            op("pe", lambda e: e.transpose(PSb[6][:, 512:640], kidup, identb), r=["kidup", "identb"], w=["ps6"])
            op("act", lambda e: e.activation(qT[:, :, tl * 128:(tl + 1) * 128], PSb[6][:, 0:512].rearrange("p (a b) -> p a b", a=4), AF.Copy), r=["ps6"], w=["qT"])
            op("act", lambda e: e.activation(kiT[:, ti * 128:(ti + 1) * 128], PSb[6][:, 512:640], AF.Copy), r=["ps6"], w=["kiT"])
            for hp in range(4):
                op("pe", lambda e, hp=hp: e.transpose(PSb[7][:, hp * 128:(hp + 1) * 128], knb[:, hp * 128:(hp + 1) * 128], identb), r=["knb", "identb"], w=["ps7"])
            op("act", lambda e: e.activation(KT[:, :, ti * 128:(ti + 1) * 128], PSb[7][:, 0:512].rearrange("p (a b) -> p a b", a=4), AF.Copy), r=["ps7"], w=["KT"])

        def qi_group(hT_src, ncols, dst):
            for c in range(8):
                pb = PS[c % 2]
                for kc in range(8):
                    op("pe", lambda e, c=c, kc=kc, pb=pb: e.matmul(pb[:, 0:ncols], lhsT=win_att[:, kc, 1536 + c * 128:1536 + (c + 1) * 128],
                                                                  rhs=hT_src[:, kc, 0:ncols], start=(kc == 0), stop=(kc == 7)), r=["hT", "win"], w=[f"ps{c % 2}"])
                op("act", lambda e, c=c, pb=pb: e.activation(dst[:, c, 0:ncols], pb[:, 0:ncols], AF.Copy), r=[f"ps{c % 2}"], w=["qiT"])


        def attention_tile(ti):
            tl = ti % 4
            L = (ti + 1) * 128
            nsp = (L + 511) // 512
            tq = slice(tl * 128, (tl + 1) * 128)
            cnt = [0, 0]
            for h in range(16):
                hp, half = h // 2, h % 2
                pr_ = slice(half * 64, (half + 1) * 64)
                for sp in range(nsp):
                    s0 = sp * 512
                    n = min(512, L - s0)
                    bk = 2 * half + cnt[half] % 2
                    cnt[half] += 1
                    rb = (h * nsp + sp) % 2
                    op("pe", lambda e, bk=bk, n=n, pr_=pr_, hp=hp, s0=s0: e.matmul(PS[bk][:, 0:n], lhsT=qiT[pr_, hp, tq], rhs=kiT[pr_, s0:s0 + n], start=True, stop=True),
                       r=["qiT", "kiT"], w=[f"ps{bk}"])
                    op("act", lambda e, bk=bk, n=n, rb=rb, h=h: e.activation(Rbuf[rb][:, 0:n], PS[bk][:, 0:n], AF.Relu, scale=wabs[:, ti, h:h + 1]),
                       r=[f"ps{bk}", "wabs"], w=[f"R{rb}"])
                    if h == 0:
                        op("dve", lambda e, n=n, rb=rb, s0=s0: e.tensor_scalar(Ibuf[:, s0:s0 + n], Rbuf[rb][:, 0:n], wsgn[:, ti, 0:1], None, ALU.mult),
                           r=[f"R{rb}", "wsgn"], w=["I"])
                    else:
                        op("dve", lambda e, n=n, rb=rb, s0=s0, h=h: e.scalar_tensor_tensor(Ibuf[:, s0:s0 + n], Rbuf[rb][:, 0:n], wsgn[:, ti, h:h + 1], Ibuf[:, s0:s0 + n], ALU.mult, ALU.add),
                           r=[f"R{rb}", "wsgn", "I"], w=["I"])
            if ATT_CUT <= 1:
                return
            lo, hi, step, mid, cntv, tmp, tau = (bs[:, i:i + 1] for i in range(7))
            if ti >= 2:
                op("dve", lambda e: e.tensor_reduce(hi, Ibuf[:, 0:L], AX.X, ALU.max), r=["I"], w=["bs"])
                op("dve", lambda e: e.tensor_reduce(lo, Ibuf[:, 0:L], AX.X, ALU.min), r=["I"], w=["bs"])
            op("dve", lambda e: e.tensor_tensor(Ibuf[:, ti * 128:(ti + 1) * 128], Ibuf[:, ti * 128:(ti + 1) * 128], negmask, ALU.add), r=["I", "negmask"], w=["I"])
            if ti >= 2:
                op("dve", lambda e: e.tensor_tensor(step, hi, lo, ALU.subtract), r=["bs"], w=["bs"])
                for it in range(NIT):
                    op("dve", lambda e: e.tensor_scalar(step, step, 0.5, None, ALU.mult), r=["bs"], w=["bs"])
                    op("dve", lambda e: e.tensor_tensor(mid, lo, step, ALU.add), r=["bs"], w=["bs"])
                    op("dve", lambda e: e.tensor_scalar(maskb[:, 0:L], Ibuf[:, 0:L], mid, 0.0, ALU.is_ge, ALU.add, accum_out=cntv), r=["I", "bs"], w=["maskb", "bs"])
                    op("dve", lambda e: e.scalar_tensor_tensor(tmp, cntv, 255.5, step, ALU.is_ge, ALU.mult), r=["bs"], w=["bs"])
                    op("dve", lambda e: e.tensor_tensor(lo, lo, tmp, ALU.add), r=["bs"], w=["bs"])
                thr = lo
            else:
                op("dve", lambda e: e.memset(tau, -1.0e29), w=["bs"])
                thr = tau
            op("dve", lambda e: e.tensor_scalar(maskb[:, 0:L], Ibuf[:, 0:L], thr, None, ALU.is_ge), r=["I", "bs"], w=["maskb"])
            if ATT_CUT <= 2:
                return
            for j0 in range(0, ti + 1, 8):
                nj = min(8, ti + 1 - j0)
                for j in range(nj):
                    sj = j0 + j
                    op("pe", lambda e, j=j, sj=sj: e.transpose(PSb[6][:, j * 128:(j + 1) * 128], maskb[:, sj * 128:(sj + 1) * 128], identb), r=["maskb", "identb"], w=["ps6"])
                op("act", lambda e, j0=j0, nj=nj: e.activation(maskT[:, j0:j0 + nj, :], PSb[6][:, 0:nj * 128].rearrange("p (a b) -> p a b", b=128), AF.Copy), r=["ps6"], w=["maskT"])
            if ATT_CUT <= 3:
                return
            cnt = [0, 0]
            pcount = 0
            for h in range(8):
                hp, half = h // 2, h % 2
                pr_ = slice(half * 64, (half + 1) * 64)
                pvb = 4 + h // 4
                pvc = (h % 4) * 65
                for sp in range(nsp):
                    j0 = sp * 4
                    nj = min(4, ti + 1 - j0)
                    bk = 2 * half + cnt[half] % 2
                    cnt[half] += 1
                    pb = pcount % 2
                    pcount += 1
                    for j in range(nj):
                        sj = j0 + j
                        op("pe", lambda e, bk=bk, j=j, sj=sj, pr_=pr_, hp=hp: e.matmul(PS[bk][:, j * 128:(j + 1) * 128], lhsT=KT[pr_, hp, sj * 128:(sj + 1) * 128],
                                                                                      rhs=qT[pr_, hp, tq], start=True, stop=True), r=["KT", "qT"], w=[f"ps{bk}"])
                    op("act", lambda e, bk=bk, nj=nj, pb=pb: e.activation(PTb[pb][:, 0:nj * 128], PS[bk][:, 0:nj * 128], AF.Exp, scale=0.125), r=[f"ps{bk}"], w=[f"PT{pb}"])
                    op("pool", lambda e, nj=nj, pb=pb, j0=j0: e.tensor_tensor(PmT[pb][:, 0:nj, :], PTb[pb][:, 0:nj * 128].rearrange("p (a b) -> p a b", b=128),
                                                                             maskT[:, j0:j0 + nj, :], ALU.mult), r=[f"PT{pb}", "maskT"], w=[f"PmT{pb}"])
                    for j in range(nj):
                        sj = j0 + j
                        op("pe", lambda e, pvb=pvb, pvc=pvc, pb=pb, j=j, sj=sj, h=h: e.matmul(PS[pvb][:, pvc:pvc + 65], lhsT=PmT[pb][:, j, :], rhs=Vaug[:, sj, h, :],
                                                                                           start=(sj == 0), stop=(sj == ti)), r=[f"PmT{pb}", "Vaug"], w=[f"ps{pvb}"])
            if ATT_CUT <= 4:
                return
            rden = bs[:, 8:16]
            for hb in range(2):
                pv = PS[4 + hb][:, 0:260].rearrange("p (h d) -> p h d", d=65)
                op("dve", lambda e, pv=pv, hb=hb: e.reciprocal(rden[:, hb * 4:(hb + 1) * 4], pv[:, :, 64]), r=[f"ps{4 + hb}"], w=["bs"])
                op("dve", lambda e, pv=pv, hb=hb: e.tensor_tensor(attb[:, hb * 256:(hb + 1) * 256].rearrange("p (h d) -> p h d", d=64), pv[:, :, 0:64],
                                                                 bc_last(rden[:, hb * 4:(hb + 1) * 4], 64), ALU.mult), r=[f"ps{4 + hb}", "bs"], w=["attb"])
            for hp in range(4):
                op("pe", lambda e, hp=hp: e.transpose(PSb[7][:, hp * 128:(hp + 1) * 128], attb[:, hp * 128:(hp + 1) * 128], identb), r=["attb", "identb"], w=["ps7"])
            op("act", lambda e: e.activation(attT[:, :, ti * 128:(ti + 1) * 128], PSb[7][:, 0:512].rearrange("p (a b) -> p a b", a=4), AF.Copy), r=["ps7"], w=["attT"])

        for g in range((NTILES + 3) // 4):
            for tl in range(4):
                ti = g * 4 + tl
                if ti >= NTILES:
                    break
                r0 = ti * 128
                front_tile(ti, 128, x_p[r0:r0 + 128, :], None, hT, k_po[r0:r0 + 128, :], v_po[r0:r0 + 128, :], ik_po[r0:r0 + 128, :], tl * 128)
            qi_group(hT, 512, qiT)
            if STAGE >= 2:
                for tl in range(4):
                    ti = g * 4 + tl
                    if ti < NTILES:
                        attention_tile(ti)
        front_tile(16, NS, x_s, 1, hTs, k_so, v_so, ik_so, 0)
        qi_group(hTs, NS, qiTs)
        if not SAMPLE_ATT:
            op("pool", lambda e: e.memset(attT[:, :, T:T + NS], 0.0), w=["attT"])
        else:
            S.barrier()
            AR.release(m1)
            NP1 = NPAGE + 1
            ptb = AR.alloc([NPAGE], I32)
            physf = AR.alloc([NS, NPAGE], F32)
            idx32 = AR.alloc([NS, NPAGE], I32)
            kib = AR.alloc([NP1, 64], F32)
            graw = AR.alloc([8320], F32)
            kdup = graw[:, 0:4160].bitcast(BF16).rearrange("p (a b) -> p a b", b=128)
            kiTa = graw[:, 4160:8320].bitcast(BF16).rearrange("p (a b) -> p a b", b=128)
            Gpg = graw[0:NPAGE, 0:8192]
            idxp = AR.alloc([NS], I32)
            ikscr = nc.dram_tensor("ikscr", [NS, NPAGE, PAGE * 64], F32, kind="Internal").ap()
            cache_ik_pg = cache_ik.rearrange("(n p) d -> n (p d)", p=128)
            knew = AR.alloc([NS, 64], F32)
            selB = AR.alloc([NS, 128], F32, parts=NS)
            wsig = AR.alloc([16], F32, parts=NS)
            wbc = AR.alloc([16], F32)
            rl = AR.alloc([NP1, 16], F32)
            score4 = AR.alloc([NS, NP1], F32)
            mask4 = AR.alloc([NS, NP1], F32)
            maskp = AR.alloc([NS, NPAGE], F32)
            negcols = AR.alloc([NS], F32)
            sb = AR.alloc([64], F32)
            iop_i = AR.alloc([2], I32)
            iop = AR.alloc([2], F32)
            islot_i = AR.alloc([256], I32)
            islot = AR.alloc([256], F32)
            ustrict = AR.alloc([128], F32)
            rank = AR.alloc([NS, NPAGE], F32)
            offs = AR.alloc([NS, NPAGE], F32)
            OH = [AR.alloc([256], F32) for _ in range(2)]
            selidx_f = AR.alloc([8], F32)
            selidx = AR.alloc([8], I32)
            Ksel = AR.alloc([2, 512], F32)
            Vsel = AR.alloc([2, 512], F32)
            scb = AR.alloc([2, 8], F32)
            Pb = AR.alloc([2, 8], F32)
            validb = AR.alloc([2], F32)
            scn = AR.alloc([8], F32, parts=NS)
            Pn = AR.alloc([8], F32, parts=NS)
            Pnn = AR.alloc([8], F32, parts=NS)
            mnew = AR.alloc([4], F32, parts=NS)
            prodn = AR.alloc([512], F32, parts=NS)
            wvn = prodn
            rdenb = AR.alloc([8], F32)
            cache_pages = cache_ik.rearrange("(n p) d -> n p d", p=128)
            ptrow = ptrow_t[:, :]
            dma("sp", lambda e: e.dma_start(out=ptrow, in_=ptab.rearrange("b c -> (b c)").unsqueeze(0)), w=["ptrow"])

            op("pool", lambda e: e.iota(iop_i, [[128, 2]], base=0, channel_multiplier=1), w=["iop_i"])
            op("dve", lambda e: e.tensor_copy(iop, iop_i), r=["iop_i"], w=["iop"])
            op("pool", lambda e: e.iota(islot_i, [[1, 256]], base=0, channel_multiplier=0), w=["islot_i"])
            op("dve", lambda e: e.tensor_copy(islot, islot_i), r=["islot_i"], w=["islot"])
            op("dve", lambda e: e.tensor_single_scalar(ustrict, iot, 0, ALU.is_gt), r=["iot"], w=["ustrict"])
            op("dve", lambda e: e.tensor_copy(selB, bc_last(ident[0:NS, 0:NS], 128)), r=["ident"], w=["selB"])
            op("dve", lambda e: e.tensor_scalar(negcols, ident[:, 0:NS], -1.0, BIG, ALU.add, ALU.mult), r=["ident"], w=["negcols"])
            op("dve", lambda e: e.tensor_tensor(wsig, ws_abs, ws_sgn, ALU.mult), r=["wabs", "wsgn"], w=["wsig"])
            op("pool", lambda e: e.memset(knew, 0.0), w=["knew"])
            op("dve", lambda e: e.tensor_tensor(knew[0:NS, :, :], bc_mid(kis32, NS), bc_last(ident[0:NS, 0:NS], 64), ALU.mult), r=["kis32", "ident", "knew"], w=["knew"])
            op("dve", lambda e: e.tensor_tensor(prodn, qs32, ks32, ALU.mult), r=["qs32", "ks32"], w=["prodn"])
            op("dve", lambda e: e.tensor_reduce(scn, prodn.rearrange("p (h d) -> p h d", d=64), AX.X, ALU.add), r=["prodn"], w=["scn"])
            op("act", lambda e: e.activation(Pn, scn, AF.Exp, scale=0.125), r=["scn"], w=["Pn"])

            for b_ in range(NS):
                dma("sp", lambda e, b_=b_: e.dma_start(out=ptb, in_=ptab[b_].partition_broadcast(128)), w=["ptb"])
                op("dve", lambda e, b_=b_: e.tensor_scalar(physf[:, b_, :], ptb, 128.0, iop[:, 0:1], ALU.mult, ALU.add), r=["ptb", "iop"], w=["physf"])
                op("dve", lambda e, b_=b_: e.tensor_copy(idx32[:, b_, :], physf[:, b_, :]), r=["physf"], w=["idx32"])
                dma("sp", lambda e, b_=b_: e.dma_start(out=idxp[0:NPAGE, b_:b_ + 1], in_=ptab[b_].rearrange("(c o) -> c o", o=1)), w=["idxp"])
                S.pool_dma_once([lambda e, b_=b_: e.indirect_dma_start(out=Gpg, out_offset=None, in_=cache_ik_pg,
                                                                       in_offset=bass.IndirectOffsetOnAxis(ap=idxp[0:NPAGE, b_:b_ + 1], axis=0))],
                                r=["idxp"], w=["kdup", "kiTa"])
                dma("sp", lambda e, b_=b_: e.dma_start(out=ikscr[b_], in_=Gpg), r=["kdup", "kiTa"], w=["ikscr"])
                dma("sp", lambda e, b_=b_: e.dma_start(out=kib[:, 0:NPAGE, :], in_=ikscr[b_].rearrange("c (s d) -> s c d", d=64)), r=["ikscr"], w=["kib"])
                op("pool", lambda e, b_=b_: e.tensor_copy(kib[:, NPAGE, :], knew[:, b_, :]), r=["knew"], w=["kib"])
                op("pool", lambda e: e.tensor_copy(kdup[:, :, 0:64], kib), r=["kib"], w=["kdup"])
                op("act", lambda e: e.activation(kdup[:, :, 64:128], kib, AF.Copy), r=["kib"], w=["kdup"])
                for c0 in range(0, NP1, 8):
                    ncc = min(8, NP1 - c0)
                    for j in range(ncc):
                        op("pe", lambda e, c0=c0, j=j: e.transpose(PSb[6][:, j * 128:(j + 1) * 128], kdup[:, c0 + j, :], identb), r=["kdup", "identb"], w=["ps6"])
                    op("act", lambda e, c0=c0, ncc=ncc: e.activation(kiTa[:, c0:c0 + ncc, :], PSb[6][:, 0:ncc * 128].rearrange("p (a b) -> p a b", b=128), AF.Copy),
                       r=["ps6"], w=["kiTa"])
                for c in range(NP1):
                    for half in range(2):
                        pr_ = slice(half * 64, (half + 1) * 64)
                        bk = 4 * half + (0 if c < NPAGE else 1)
                        col = (c % NPAGE) * 8
                        op("pe", lambda e, bk=bk, col=col, pr_=pr_, c=c, b_=b_: e.matmul(PS[bk][:, col:col + 8], lhsT=kiTa[pr_, c, :],
                                                                                       rhs=qiTs[pr_, :, b_], start=True, stop=True),
                           r=["kiTa", "qiT"], w=[f"ps{bk}"])
                op("pe", lambda e, b_=b_: e.matmul(PS[3][:, 0:16], lhsT=selB[:, b_, :], rhs=wsig, start=True, stop=True), r=["selB", "wsig"], w=["ps3"])
                op("dve", lambda e: e.tensor_copy(wbc.rearrange("p (a b) -> p a b", a=2), PS[3][:, 0:16].rearrange("p (b a) -> p a b", a=2)), r=["ps3"], w=["wbc"])
                for half in range(2):
                    op("dve", lambda e, half=half: e.tensor_scalar(rl[:, 0:NPAGE, half * 8:half * 8 + 8], PS[4 * half].rearrange("p (c h) -> p c h", h=8), 0.0, None, ALU.max),
                       r=[f"ps{4 * half}"], w=["rl"])
                    op("dve", lambda e, half=half: e.tensor_scalar(rl[:, NPAGE, half * 8:half * 8 + 8], PS[4 * half + 1][:, 0:8], 0.0, None, ALU.max),
                       r=[f"ps{4 * half + 1}"], w=["rl"])
                op("dve", lambda e: e.tensor_tensor(rl, rl, bc_mid(wbc, NP1), ALU.mult), r=["rl", "wbc"], w=["rl"])
                op("dve", lambda e, b_=b_: e.tensor_reduce(score4[:, b_, :], rl, AX.X, ALU.add), r=["rl"], w=["score4"])
            bnd = sb[:, 0:4]
            lo4, st4, mid4, cnt4, tmp4 = (sb[:, 4 + 4 * i:8 + 4 * i] for i in range(5))
            op("dve", lambda e: e.tensor_reduce(bnd, score4, AX.X, ALU.max, apply_absolute_value=True), r=["score4"], w=["sb"])
            op("pe", lambda e: e.matmul(PS[3][:, 16:20], lhsT=ones_f, rhs=bnd, start=True, stop=True), r=["ones_f", "sb"], w=["ps3"])
            op("dve", lambda e: e.tensor_scalar(lo4, PS[3][:, 16:20], -1.0, None, ALU.mult), r=["ps3"], w=["sb"])
            op("dve", lambda e: e.tensor_scalar(st4, PS[3][:, 16:20], 2.0, None, ALU.mult), r=["ps3"], w=["sb"])
            op("dve", lambda e: e.tensor_tensor(score4[:, :, NPAGE], score4[:, :, NPAGE], negcols, ALU.add), r=["score4", "negcols"], w=["score4"])
            for it in range(NIT + 8):
                op("dve", lambda e: e.tensor_scalar(st4, st4, 0.5, None, ALU.mult), r=["sb"], w=["sb"])
                op("dve", lambda e: e.tensor_tensor(mid4, lo4, st4, ALU.add), r=["sb"], w=["sb"])
                op("dve", lambda e: e.tensor_tensor(mask4, score4, bc_last(mid4, NP1), ALU.is_ge), r=["score4", "sb"], w=["mask4"])
                op("dve", lambda e: e.tensor_reduce(cnt4, mask4, AX.X, ALU.add), r=["mask4"], w=["sb"])
                op("pe", lambda e: e.matmul(PS[3][:, 16:20], lhsT=ones_f, rhs=cnt4, start=True, stop=True), r=["ones_f", "sb"], w=["ps3"])
                op("dve", lambda e: e.scalar_tensor_tensor(tmp4, PS[3][:, 16:20], 255.5, st4, ALU.is_ge, ALU.mult), r=["ps3", "sb"], w=["sb"])
                op("dve", lambda e: e.tensor_tensor(lo4, lo4, tmp4, ALU.add), r=["sb"], w=["sb"])
            op("dve", lambda e: e.tensor_tensor(mask4, score4, bc_last(lo4, NP1), ALU.is_ge), r=["score4", "sb"], w=["mask4"])
            op("dve", lambda e: e.tensor_copy(maskp, mask4[:, :, 0:NPAGE]), r=["mask4"], w=["maskp"])
            op("dve", lambda e: e.tensor_tensor(mnew, mask4[0:NS, :, NPAGE], ident[0:NS, 0:NS], ALU.mult), r=["mask4", "ident"], w=["mnew"])
            op("dve", lambda e: e.tensor_reduce(mnew[:, 0:1], mnew, AX.X, ALU.add), r=["mnew"], w=["mnew"])
            op("dve", lambda e: e.tensor_scalar(Pn, Pn, mnew[:, 0:1], None, ALU.mult), r=["Pn", "mnew"], w=["Pn"])
            maskp2 = maskp.rearrange("p a b -> p (a b)")
            op("pe", lambda e: e.matmul(PS[0][:, 0:256], lhsT=ustrict, rhs=maskp2, start=True, stop=True), r=["ustrict", "maskp"], w=["ps0"])
            op("pe", lambda e: e.matmul(PS[1][:, 0:256], lhsT=ones_f, rhs=maskp2, start=True, stop=True), r=["ones_f", "maskp"], w=["ps1"])
            op("dve", lambda e: e.tensor_copy(rank.rearrange("p a b -> p (a b)"), PS[1][:, 0:256]), r=["ps1"], w=["rank"])
            for b_ in range(NS):
                op("dve", lambda e, b_=b_: e.tensor_tensor_scan(offs[:, b_, :], ones_f[:, 0:NPAGE], rank[:, b_, :], 0.0, ALU.mult, ALU.add), r=["rank", "ones_f"], w=["offs"])
            op("dve", lambda e: e.tensor_tensor(rank, offs, rank, ALU.subtract), r=["offs", "rank"], w=["rank"])
            op("dve", lambda e: e.tensor_tensor(rank.rearrange("p a b -> p (a b)"), rank.rearrange("p a b -> p (a b)"), PS[0][:, 0:256], ALU.add), r=["rank", "ps0"], w=["rank"])
            ohc = 0
            for b_ in range(NS):
                for c in range(NPAGE):
                    ob = ohc % 2
                    ohc += 1
                    op("dve", lambda e, b_=b_, c=c, ob=ob: e.tensor_scalar(OH[ob], islot, rank[:, b_, c:c + 1], maskp[:, b_, c:c + 1], ALU.is_equal, ALU.mult),
                       r=["islot", "rank", "maskp"], w=[f"OH{ob}"])
                    for half in range(2):
                        op("pe", lambda e, b_=b_, c=c, ob=ob, half=half: e.matmul(PS[2 + half][:, b_:b_ + 1], lhsT=OH[ob][:, half * 128:(half + 1) * 128],
                                                                               rhs=physf[:, b_, c:c + 1], start=(c == 0), stop=(c == NPAGE - 1)),
                           r=[f"OH{ob}", "physf"], w=[f"ps{2 + half}"])
            for half in range(2):
                op("dve", lambda e, half=half: e.tensor_scalar(selidx_f.rearrange("p (b a) -> p b a", a=2)[:, :, half], PS[2 + half][:, 0:NS], 0.25, None, ALU.add),
                   r=[f"ps{2 + half}"], w=["selidx_f"])
            op("dve", lambda e: e.tensor_copy(selidx, selidx_f), r=["selidx_f"], w=["selidx"])
            for b_ in range(NS):
                fl = []
                for half in range(2):
                    fl.append(lambda e, b_=b_, half=half: e.indirect_dma_start(out=Ksel[:, half, :], out_offset=None, in_=cache_k,
                                                                              in_offset=bass.IndirectOffsetOnAxis(ap=selidx[:, b_ * 2 + half:b_ * 2 + half + 1], axis=0)))
                    fl.append(lambda e, b_=b_, half=half: e.indirect_dma_start(out=Vsel[:, half, :], out_offset=None, in_=cache_v,
                                                                              in_offset=bass.IndirectOffsetOnAxis(ap=selidx[:, b_ * 2 + half:b_ * 2 + half + 1], axis=0)))
                S.pool_dma_once(fl, r=["selidx"], w=["Ksel", "Vsel"])
                op("dve", lambda e, b_=b_: e.tensor_scalar(validb, iop, offs[:, b_, NPAGE - 1:NPAGE], None, ALU.is_lt), r=["iop", "offs"], w=["validb"])
                op("pe", lambda e, b_=b_: e.matmul(PS[4], lhsT=selB[:, b_, :], rhs=qs32, start=True, stop=True), r=["selB", "qs32"], w=["ps4"])
                for half in range(2):
                    op("dve", lambda e, half=half: e.tensor_tensor(Ksel[:, half, :], Ksel[:, half, :], PS[4], ALU.mult), r=["Ksel", "ps4"], w=["Ksel"])
                op("dve", lambda e: e.tensor_reduce(scb.rearrange("p a h -> p (a h)"), Ksel.rearrange("p a (h d) -> p (a h) d", d=64), AX.X, ALU.add), r=["Ksel"], w=["scb"])
                op("act", lambda e: e.activation(Pb, scb, AF.Exp, scale=0.125), r=["scb"], w=["Pb"])
                op("dve", lambda e: e.tensor_tensor(Pb, Pb, bc_last(validb, 8), ALU.mult), r=["Pb", "validb"], w=["Pb"])
                op("pe", lambda e: e.matmul(PS[5][:, 0:8], lhsT=ones_f, rhs=Pb[:, 0, :], start=True, stop=False), r=["ones_f", "Pb"], w=["ps5"])
                op("pe", lambda e: e.matmul(PS[5][:, 0:8], lhsT=ones_f, rhs=Pb[:, 1, :], start=False, stop=False), r=["ones_f", "Pb"], w=["ps5"])
                op("pe", lambda e, b_=b_: e.matmul(PS[5][:, 0:8], lhsT=selB[:, b_, :], rhs=Pn, start=False, stop=True), r=["selB", "Pn"], w=["ps5"])
                op("dve", lambda e: e.reciprocal(rdenb, PS[5][:, 0:8]), r=["ps5"], w=["rdenb"])
                op("dve", lambda e: e.tensor_tensor(Pb, Pb, bc_mid(rdenb, 2), ALU.mult), r=["Pb", "rdenb"], w=["Pb"])
                op("dve", lambda e: e.tensor_tensor(Pnn, Pn, rdenb[0:NS, :], ALU.mult), r=["Pn", "rdenb"], w=["Pnn"])
                for half in range(2):
                    op("dve", lambda e, half=half: e.tensor_tensor(Vsel[:, half, :].rearrange("p (h d) -> p h d", d=64), Vsel[:, half, :].rearrange("p (h d) -> p h d", d=64),
                                                                 bc_last(Pb[:, half, :], 64), ALU.mult), r=["Vsel", "Pb"], w=["Vsel"])
                op("dve", lambda e: e.tensor_tensor(wvn.rearrange("p (h d) -> p h d", d=64), vs32.rearrange("p (h d) -> p h d", d=64), bc_last(Pnn, 64), ALU.mult),
                   r=["vs32", "Pnn"], w=["prodn"])
                for cch in range(4):
                    cs = slice(cch * 128, (cch + 1) * 128)
                    col = cch * NS + b_
                    op("pe", lambda e, cs=cs, col=col: e.matmul(PS[7][:, col:col + 1], lhsT=Vsel[:, 0, cs], rhs=ones_f[:, 0:1], start=True, stop=False), r=["Vsel", "ones_f"], w=["ps7"])
                    op("pe", lambda e, cs=cs, col=col: e.matmul(PS[7][:, col:col + 1], lhsT=Vsel[:, 1, cs], rhs=ones_f[:, 0:1], start=False, stop=False), r=["Vsel", "ones_f"], w=["ps7"])
                    op("pe", lambda e, cs=cs, col=col, b_=b_: e.matmul(PS[7][:, col:col + 1], lhsT=wvn[:, cs], rhs=ident[0:NS, b_:b_ + 1], start=False, stop=True), r=["prodn", "ident"], w=["ps7"])
            op("act", lambda e: e.activation(attT[:, :, T:T + NS], PS[7][:, 0:16].rearrange("p (c b) -> p c b", b=NS), AF.Copy), r=["ps7"], w=["attT"])
            if os.environ.get("MK_DBG_S", "0") == "1":
                d1 = nc.dram_tensor("dbg_score", [128, NS * NP1], F32, kind="ExternalOutput").ap()
                dma("sp", lambda e: e.dma_start(out=d1, in_=score4.rearrange("p a b -> p (a b)")), r=["score4"])
                d2 = nc.dram_tensor("dbg_mask", [128, NS * NP1], F32, kind="ExternalOutput").ap()
                dma("sp", lambda e: e.dma_start(out=d2, in_=mask4.rearrange("p a b -> p (a b)")), r=["mask4"])
                d3 = nc.dram_tensor("dbg_sel", [128, 8], F32, kind="ExternalOutput").ap()
                dma("sp", lambda e: e.dma_start(out=d3, in_=selidx_f), r=["selidx_f"])
                d4 = nc.dram_tensor("dbg_atts", [128, 4, NS], BF16, kind="ExternalOutput").ap()
                dma("sp", lambda e: e.dma_start(out=d4, in_=attT[:, :, T:T + NS]), r=["attT"])
                d6 = nc.dram_tensor("dbg_ksel", [128, 1024], F32, kind="ExternalOutput").ap()
                dma("sp", lambda e: e.dma_start(out=d6, in_=Ksel.rearrange("p a b -> p (a b)")), r=["Ksel"])
                d7 = nc.dram_tensor("dbg_pb", [128, 16], F32, kind="ExternalOutput").ap()
                dma("sp", lambda e: e.dma_start(out=d7, in_=Pb.rearrange("p a b -> p (a b)")), r=["Pb"])
                d8 = nc.dram_tensor("dbg_scb", [128, 16], F32, kind="ExternalOutput").ap()
                dma("sp", lambda e: e.dma_start(out=d8, in_=scb.rearrange("p a b -> p (a b)")), r=["scb"])
                d9 = nc.dram_tensor("dbg_selidx", [128, 8], I32, kind="ExternalOutput").ap()
                dma("sp", lambda e: e.dma_start(out=d9, in_=selidx), r=["selidx"])
                d5 = nc.dram_tensor("dbg_rank", [128, NS * NPAGE], F32, kind="ExternalOutput").ap()
                dma("sp", lambda e: e.dma_start(out=d5, in_=rank.rearrange("p a b -> p (a b)")), r=["rank"])


        if STAGE >= 4:
            S.barrier()
            AR.release(m_att)
            gsc = nc.dram_tensor("gsc", [2, 5, D], F32, kind="Internal").ap()
            win_rw = AR.alloc([8, RW_COLS], BF16)
            wo_att = AR.alloc([4, D], BF16)
            wo_rw = AR.alloc([8, D], BF16, parts=64)
            w2b = AR.alloc([512], BF16, parts=64)
            a2b = AR.alloc([512], BF16, parts=64)
            g2b = AR.alloc([2, 512], BF16, parts=64)
            muD = AR.alloc([28], F32, parts=64)
            Msel = AR.alloc([16], F32)
            Mh = AR.alloc([8], F32)
            maskLA = AR.alloc([16, 128], F32, parts=64)
            g1rep = AR.alloc([D], F32)
            g1s = AR.alloc([D], F32, parts=NS)
            shiftT = AR.alloc([28, NS], F32, parts=64)
            m2 = AR.mark()
            grow = AR.alloc([D], F32, parts=5)
            stg = [AR.alloc([4096], F32) for _ in range(2)]
            stcnt = [0]

            def load_cast(dst, src, parts, shape):
                b = stcnt[0] % 2
                stcnt[0] += 1
                ne = _prod(shape)
                v = stg[b][0:parts, 0:ne]
                if len(shape) == 2:
                    v = v.rearrange("p (a b) -> p a b", a=shape[0])
                dma("sp", lambda e: e.dma_start(out=v, in_=src), w=[f"stg{b}"])
                op("pool", lambda e: e.tensor_copy(dst, v), r=[f"stg{b}"], w=["wts2"])

            w_in_v2 = w_in.rearrange("(kc p) n -> p kc n", p=128)
            for c in range(0, RW_COLS, 512):
                n = min(512, RW_COLS - c)
                load_cast(win_rw[:, :, c:c + n], w_in_v2[:, :, ATT_COLS + c:ATT_COLS + c + n], 128, [8, n])
            load_cast(wo_att, w_out[0:512, :].rearrange("(c p) n -> p c n", p=128), 128, [4, D])
            wo_rw_v = w_out[512:1024, :].rearrange("(h p) n -> p h n", p=64)
            for hh in range(0, 8, 4):
                load_cast(wo_rw[:, hh:hh + 4, :], wo_rw_v[:, hh:hh + 4, :], 64, [4, D])
            load_cast(w2b, rw_w2, 64, [512])
            load_cast(a2b, rw_a2, 64, [512])
            load_cast(g2b, rw_g2.rearrange("(b p) n -> p b n", p=64), 64, [2, 512])
            stm = stg[0][0:28, 0:64]
            dma("sp", lambda e: e.dma_start(out=stm, in_=rw_mu.rearrange("(a b) -> a b", b=64)), w=["stg0"])
            op("pe", lambda e: e.transpose(PS[0][0:64, 0:28], stm, ident[0:28, 0:28]), r=["stg0", "ident"], w=["ps0"])
            op("dve", lambda e: e.tensor_copy(muD, PS[0][0:64, 0:28]), r=["ps0"], w=["muD"])
            op("dve", lambda e: e.tensor_reduce(Msel, ident.rearrange("p (h t) -> p t h", t=16), AX.X, ALU.add), r=["ident"], w=["Msel"])
            op("dve", lambda e: e.tensor_reduce(Mh, ident.rearrange("p (h t) -> p h t", t=16), AX.X, ALU.add), r=["ident"], w=["Mh"])
            op("pool", lambda e: e.memset(maskLA, 0.0), w=["maskLA"])
            mla = maskLA.rearrange("p a (h t) -> p a h t", t=16)
            for tp in range(16):
                op("pool", lambda e, tp=tp: e.memset(mla[:, tp, :, tp:tp + 1], 1.0), w=["maskLA"])
            for kc in range(8):
                op("pe", lambda e, kc=kc: e.transpose(PS[kc // 4][0:5, (kc % 4) * 128:(kc % 4 + 1) * 128], gate1[:, kc, :], ident), r=["modT", "ident"], w=[f"ps{kc // 4}"])
            op("dve", lambda e: e.tensor_copy(grow[:, 0:512], PS[0][0:5, :]), r=["ps0"], w=["grow"])
            op("dve", lambda e: e.tensor_copy(grow[:, 512:1024], PS[1][0:5, :]), r=["ps1"], w=["grow"])
            dma("sp", lambda e: e.dma_start(out=gsc[0], in_=grow), r=["grow"], w=["gsc0"])
            dma("sp", lambda e: e.dma_start(out=g1rep, in_=gsc[0, 0].partition_broadcast(128)), r=["gsc0"], w=["g1rep"])
            dma("sp", lambda e: e.dma_start(out=g1s, in_=gsc[0, 1:5, :]), r=["gsc0"], w=["g1s"])
            shs = stg[1][0:NS, 0:RW_COLS]
            dma("sp", lambda e: e.dma_start(out=shs, in_=st_shift), w=["stg1"])
            for blk in range(28):
                op("pe", lambda e, blk=blk: e.transpose(PS[2][0:64, blk * 4:blk * 4 + NS], shs[:, blk * 64:(blk + 1) * 64], ident[0:NS, 0:NS]), r=["stg1", "ident"], w=["ps2"])
            op("dve", lambda e: e.tensor_copy(shiftT, PS[2][0:64, 0:112].rearrange("p (a b) -> p a b", b=4)), r=["ps2"], w=["shiftT"])
            S.barrier()
            AR.release(m2)

            W = 128
            xt2 = AR.alloc([D], F32)
            xn2 = AR.alloc([D], F32)
            hT2 = AR.alloc([8, W], BF16)
            prd = AR.alloc([28, W + 1], F32, parts=64)
            xm = AR.alloc([28, W], F32, parts=64)
            dv = {k: AR.alloc([8, W], F32, parts=64) for k in ["wdec", "asig", "kkn", "na", "bb", "kmod", "gg", "bon", "t1", "t2"]}
            dv["Yd"] = dv["kkn"]
            rwd = AR.alloc([8, W], BF16, parts=64)
            tanh_wd = AR.alloc([W], BF16, parts=64)
            ad_bf = AR.alloc([W], BF16, parts=64)
            sg = AR.alloc([2, W], BF16, parts=64)
            LA_sel = LA_sel_r
            Xb16 = Xall[:, 0:64]
            Xb16_r = Xall_r[:, 0:64]
            STb = AR.alloc([8, 64], BF16, parts=64)
            rbf = AR.alloc([8, W], BF16, parts=64)
            ST = AR.alloc([8, 64], F32, parts=64)
            ST2 = AR.alloc([8, 64], F32, parts=64)
            Sio = AR.alloc([8, 64], F32, parts=64)
            sm2 = AR.alloc([8], F32)
            tstg = AR.alloc([3, 128], F32, parts=64)
            w0T, a0T, kkT, kaT, lnwT, lnbT, rkT = (vecB[:, 8 * i:8 * i + 8] for i in range(7))
            for k_ in dv:
                if k_ != "Yd":
                    op("pool", lambda e, k_=k_: e.memset(dv[k_], 0.0), w=[k_])
            op("pool", lambda e: e.memset(xm, 0.0), w=["xm"])
            op("pool", lambda e: e.memset(prd, 0.0), w=["prd"])
            op("pool", lambda e: e.memset(ST, 0.0), w=["ST"])
            op("act", lambda e: e.activation(STr, ST.rearrange("p h i -> p (h i)"), AF.Copy), r=["ST"], w=["STr"])
            ones64 = ones_f[0:64, 0:64]

            def v4(ap):
                return ap[0:64, :].rearrange("p (a b) -> p a b", b=128)

            def sum64(src, P):
                for hb in range(2):
                    if P == 128:
                        op("pe", lambda e, hb=hb: e.matmul(v4(PS[6 + hb])[:, :, 0:P], lhsT=ones64, rhs=src[:, hb * 4:(hb + 1) * 4, 0:P], start=True, stop=True),
                           r=["ones_f", "dvsrc"], w=[f"ps{6 + hb}"])
                    else:
                        for hl in range(4):
                            op("pe", lambda e, hb=hb, hl=hl: e.matmul(PS[6 + hb][0:64, hl * 128:hl * 128 + P], lhsT=ones64, rhs=src[:, hb * 4 + hl, 0:P], start=True, stop=True),
                               r=["ones_f", "dvsrc"], w=[f"ps{6 + hb}"])

            def rw_tile(ti, P, x_src, cond, tcol0, prev_view, is_sample):
                dma("sp", lambda e: e.dma_start(out=xt2[0:P, :], in_=x_src), w=["xt2"])
                ss = sm2[0:P, 0:1]
                rs = sm2[0:P, 1:2]
                op("act", lambda e: e.activation(xn2[0:P, :], xt2[0:P, :], AF.Square, accum_out=ss), r=["xt2"], w=["xn2", "sm2"])
                op("dve", lambda e: e.tensor_scalar(rs, ss, 1.0 / D, RMS_EPS, ALU.mult, ALU.add), r=["sm2"], w=["sm2"])
                op("act", lambda e: e.activation(rs, rs, AF.Sqrt), r=["sm2"], w=["sm2"])
                op("dve", lambda e: e.reciprocal(rs, rs), r=["sm2"], w=["sm2"])
                op("dve", lambda e: e.tensor_scalar(xn2[0:P, :], xt2[0:P, :], rs, None, ALU.mult), r=["xt2", "sm2"], w=["xn2"])
                for half in range(2):
                    pb = PS[half]
                    for j in range(4):
                        kc = half * 4 + j
                        op("pe", lambda e, kc=kc, j=j, pb=pb: e.transpose(pb[:, j * 128:j * 128 + P], xn2[0:P, kc * 128:(kc + 1) * 128], ident[0:P, 0:P]),
                           r=["xn2", "ident"], w=[f"ps{half}"])
                    for j in range(4):
                        kc = half * 4 + j
                        if not is_sample:
                            op("act", lambda e, kc=kc, j=j, pb=pb: e.activation(hT2[:, kc, 0:P], pb[:, j * 128:j * 128 + P], AF.Identity,
                                                                              bias=shift1[:, kc, 0:1], scale=A1[:, kc, 0:1]), r=[f"ps{half}", "A1", "modT"], w=["hT2"])
                        else:
                            op("dve", lambda e, kc=kc, j=j, pb=pb: e.tensor_tensor(scr2[:, 0:P], pb[:, j * 128:j * 128 + P], A1[:, kc, 1:1 + P], ALU.mult),
                               r=[f"ps{half}", "A1"], w=["scr2b"])
                            op("dve", lambda e, kc=kc: e.tensor_tensor(hT2[:, kc, 0:P], scr2[:, 0:P], shift1[:, kc, 1:1 + P], ALU.add), r=["scr2b", "modT"], w=["hT2"])
                for blk0 in range(0, 28, 4):
                    bk = 2 + (blk0 // 4) % 2
                    for j in range(4):
                        blk = blk0 + j
                        for kc in range(8):
                            op("pe", lambda e, bk=bk, j=j, blk=blk, kc=kc: e.matmul(PS[bk][0:64, j * 128:j * 128 + P], lhsT=win_rw[:, kc, blk * 64:(blk + 1) * 64],
                                                                                    rhs=hT2[:, kc, 0:P], start=(kc == 0), stop=(kc == 7)), r=["hT2", "wts2"], w=[f"ps{bk}"])
                    op("act", lambda e, bk=bk, blk0=blk0: e.activation(prd[:, blk0:blk0 + 4, 1:1 + P], v4(PS[bk])[:, :, 0:P], AF.Copy), r=[f"ps{bk}"], w=["prd"])
                if P2_CUT <= 2:
                    return
                cur = prd[:, :, 1:1 + P]
                prev = prev_view if prev_view is not None else prd[:, :, 0:P]
                xmv = xm[:, :, 0:P]
                op("dve", lambda e: e.tensor_tensor(xmv, prev, cur, ALU.subtract), r=["prd", "shiftT"], w=["xm"])
                op("dve", lambda e: e.tensor_tensor(xmv, xmv, bc_last(muD, P), ALU.mult), r=["xm", "muD"], w=["xm"])
                op("dve", lambda e: e.tensor_tensor(xmv, xmv, cur, ALU.add), r=["xm", "prd"], w=["xm"])
                if is_sample:
                    for b_ in range(P):
                        op("pe", lambda e, b_=b_: e.transpose(PS[0][0:28, b_ * 64:(b_ + 1) * 64], prd[:, :, 1 + b_], ident[0:64, 0:64]), r=["prd", "ident"], w=["ps0"])
                    op("dve", lambda e: e.tensor_copy(xn2[0:28, 0:P * 64], PS[0][0:28, 0:P * 64]), r=["ps0"], w=["xn2"])
                    for b_ in range(P):
                        dma("sp", lambda e, b_=b_: e.dma_start(out=sh_so[b_].rearrange("(a c) -> a c", c=64), in_=xn2[0:28, b_ * 64:(b_ + 1) * 64]), r=["xn2"])
                else:
                    if ti == NTILES - 1:
                        op("pe", lambda e: e.transpose(PS[0][0:28, 0:64], prd[:, :, P], ident[0:64, 0:64]), r=["prd", "ident"], w=["ps0"])
                        op("dve", lambda e: e.tensor_copy(xn2[0:28, 0:64], PS[0][0:28, 0:64]), r=["ps0"], w=["xn2"])
                        dma("sp", lambda e: e.dma_start(out=sh_po.rearrange("(a c) -> a c", c=64), in_=xn2[0:28, 0:64]), r=["xn2"])
                    op("pool", lambda e: e.tensor_copy(prd[:, :, 0:1], prd[:, :, P:P + 1]), r=["prd", "xm"], w=["prd"])
                if P2_CUT <= 3:
                    return
                r_ = xm[:, 0:8, :]
                k_ = xm[:, 8:16, :]
                v_ = xm[:, 16:24, :]
                D_ = {k2: dv[k2][:, :, 0:P] for k2 in dv}
                op("act", lambda e: e.activation(tanh_wd[:, 0:P], xm[:, 24, 0:P], AF.Tanh), r=["xm"], w=["tanh_wd"])
                op("act", lambda e: e.activation(ad_bf[:, 0:P], xm[:, 25, 0:P], AF.Copy), r=["xm"], w=["ad_bf"])
                op("act", lambda e: e.activation(sg[:, :, 0:P], xm[:, 26:28, 0:P], AF.Sigmoid), r=["xm"], w=["sg"])

                def lora(wb, rhs_ap, rname):
                    for h in range(8):
                        op("pe", lambda e, h=h: e.matmul(PS[4 + h // 4][0:64, (h % 4) * 128:(h % 4) * 128 + P], lhsT=wb[:, h * 64:(h + 1) * 64], rhs=rhs_ap,
                                                       start=True, stop=True), r=[rname, "wts2"], w=[f"ps{4 + h // 4}"])
                lora(w2b, tanh_wd[:, 0:P], "tanh_wd")
                for hb in range(2):
                    op("dve", lambda e, hb=hb: e.tensor_tensor(D_["t1"][:, hb * 4:(hb + 1) * 4, :], v4(PS[4 + hb])[:, :, 0:P], bc_last(w0T[:, hb * 4:(hb + 1) * 4], P), ALU.add),
                       r=[f"ps{4 + hb}", "vecB"], w=["t1"])
                op("act", lambda e: e.activation(D_["t1"], D_["t1"], AF.Sigmoid), r=["t1"], w=["t1"])
                op("act", lambda e: e.activation(D_["wdec"], D_["t1"], AF.Exp, scale=-0.6065306597126334), r=["t1"], w=["wdec"])
                lora(a2b, ad_bf[:, 0:P], "ad_bf")
                for hb in range(2):
                    op("dve", lambda e, hb=hb: e.tensor_tensor(D_["t1"][:, hb * 4:(hb + 1) * 4, :], v4(PS[4 + hb])[:, :, 0:P], bc_last(a0T[:, hb * 4:(hb + 1) * 4], P), ALU.add),
                       r=[f"ps{4 + hb}", "vecB"], w=["t1"])
                op("act", lambda e: e.activation(D_["asig"], D_["t1"], AF.Sigmoid), r=["t1"], w=["asig"])
                for h in range(8):
                    for blk in range(2):
                        op("pe", lambda e, h=h, blk=blk: e.matmul(PS[4 + h // 4][0:64, (h % 4) * 128:(h % 4) * 128 + P], lhsT=g2b[:, blk, h * 64:(h + 1) * 64], rhs=sg[:, blk, 0:P],
                                                                 start=(blk == 0), stop=(blk == 1)), r=["sg", "wts2"], w=[f"ps{4 + h // 4}"])
                for hb in range(2):
                    op("act", lambda e, hb=hb: e.activation(D_["gg"][:, hb * 4:(hb + 1) * 4, :], v4(PS[4 + hb])[:, :, 0:P], AF.Copy), r=[f"ps{4 + hb}"], w=["gg"])
                kP, rP, vP = k_[:, :, 0:P], r_[:, :, 0:P], v_[:, :, 0:P]
                op("dve", lambda e: e.tensor_tensor(D_["kkn"], kP, bc_last(kkT, P), ALU.mult), r=["xm", "vecB"], w=["kkn"])
                op("dve", lambda e: e.tensor_tensor(D_["t1"], D_["kkn"], D_["kkn"], ALU.mult), r=["kkn"], w=["t1", "dvsrc"])
                sum64(dv["t1"], P)
                for hb in range(2):
                    op("dve", lambda e, hb=hb: e.tensor_scalar(D_["t2"][:, hb * 4:(hb + 1) * 4, :], v4(PS[6 + hb])[:, :, 0:P], 1e-24, None, ALU.max), r=[f"ps{6 + hb}"], w=["t2"])
                op("act", lambda e: e.activation(D_["t2"], D_["t2"], AF.Sqrt), r=["t2"], w=["t2"])
                op("dve", lambda e: e.reciprocal(D_["t2"], D_["t2"]), r=["t2"], w=["t2"])
                op("dve", lambda e: e.tensor_tensor(D_["kkn"], D_["kkn"], D_["t2"], ALU.mult), r=["kkn", "t2"], w=["kkn"])
                op("dve", lambda e: e.tensor_scalar(D_["t1"], D_["asig"], -1.0, None, ALU.add), r=["asig"], w=["t1"])
                op("dve", lambda e: e.tensor_tensor(D_["t1"], D_["t1"], bc_last(kaT, P), ALU.mult), r=["t1", "vecB"], w=["t1"])
                op("dve", lambda e: e.tensor_scalar(D_["t1"], D_["t1"], 1.0, None, ALU.add), r=["t1"], w=["t1"])
                op("dve", lambda e: e.tensor_tensor(D_["kmod"], kP, D_["t1"], ALU.mult), r=["xm", "t1"], w=["kmod"])
                op("dve", lambda e: e.tensor_tensor(D_["bb"], D_["kkn"], D_["asig"], ALU.mult), r=["kkn", "asig"], w=["bb"])
                op("pool", lambda e: e.tensor_scalar(D_["na"], D_["kkn"], -1.0, None, ALU.mult), r=["kkn"], w=["na"])
                op("dve", lambda e: e.tensor_tensor(D_["t1"], rP, D_["kmod"], ALU.mult), r=["xm", "kmod"], w=["t1"])
                op("dve", lambda e: e.tensor_tensor(D_["t1"], D_["t1"], bc_last(rkT, P), ALU.mult), r=["t1", "vecB"], w=["t1", "dvsrc"])
                sum64(dv["t1"], P)
                for hb in range(2):
                    op("dve", lambda e, hb=hb: e.tensor_tensor(D_["bon"][:, hb * 4:(hb + 1) * 4, :], v4(PS[6 + hb])[:, :, 0:P], vP[:, hb * 4:(hb + 1) * 4, :], ALU.mult),
                       r=[f"ps{6 + hb}", "xm"], w=["bon"])
                if os.environ.get("MK_DBG_DV", "0") == "1" and ti == 0:
                    dbg_dv = nc.dram_tensor("dbg_dv", [64, 7, 8, 128], F32, kind="ExternalOutput").ap()
                    for i_, nm in enumerate(["wdec", "asig", "kkn", "kmod", "bb", "gg", "bon"]):
                        dma("sp", lambda e, i_=i_, nm=nm: e.dma_start(out=dbg_dv[:, i_], in_=dv[nm]), r=[nm])
                    dbg_xm = nc.dram_tensor("dbg_xm", [64, 28, 128], F32, kind="ExternalOutput").ap()
                    dma("sp", lambda e: e.dma_start(out=dbg_xm, in_=xm), r=["xm"])
                if P2_CUT <= 4:
                    return
                acnt = 0
                pend_y = []
                op("act", lambda e: e.activation(rbf[:, :, 0:P], r_[:, :, 0:P], AF.Copy), r=["xm"], w=["rbf"])
                PSC = min(P, int(os.environ.get("MK_NSTEPS", "100000")))
                for t0 in range(0, PSC, 16):
                    nst = min(16, PSC - t0)
                    for (ci, srcv, rn) in [(0, dv["bb"], "bb"), (1, dv["kmod"], "kmod"), (2, v_, "xm")]:
                        op("pool", lambda e, ci=ci, srcv=srcv, t0=t0: e.tensor_copy(tstg[:, ci, :].rearrange("p (h t) -> p h t", t=16), srcv[:, :, t0:t0 + 16]), r=[rn], w=["tstg"])
                        op("pe", lambda e, ci=ci: e.transpose(PS[6][:, ci * 64:ci * 64 + 64], tstg[:, ci, :], ident[0:64, 0:64]), r=["tstg", "ident"], w=["ps6"])
                    op("act", lambda e: e.activation(Xall_r, PS[6][:, 0:192], AF.Copy), r=["ps6"], w=["Xb16"])
                    if os.environ.get("MK_T2", "0") != "1":
                        op("dve", lambda e: e.tensor_tensor(Xk_sel_r, bc_mid(Xall[:, 64:128], 16), bc_last(Msel, 64), ALU.mult), r=["Xb16", "Msel"], w=["Xk_sel"])
                        op("dve", lambda e: e.tensor_tensor(Vbd16_r, bc_mid(Xall[:, 128:192], 8), bc_last(Mh, 64), ALU.mult), r=["Xb16", "Mh"], w=["Vbd16"])
                    if os.environ.get("MK_T1", "0") != "1":
                        op("dve", lambda e, t0=t0: e.tensor_tensor(LA_sel.rearrange("p a (h t) -> p a h t", t=16), dv["na"][:, :, t0:t0 + 16].unsqueeze(1).to_broadcast([64, 16, 8, 16]),
                                                             maskLA.rearrange("p a (h t) -> p a h t", t=16), ALU.mult), r=["na", "maskLA"], w=["LA_sel"])
                    for tp in range(nst if SCAN_CUT > 1 else 0):
                        t = t0 + tp
                        if is_sample:
                            load_state(t)
                        pa = acnt % 2
                        pu = 2 + acnt % 2
                        rb = acnt % 2
                        acnt += 1
                        op("pe", lambda e, pa=pa, tp=tp: e.matmul(PS[pa], lhsT=LA_sel_r[:, tp, :], rhs=STr, start=True, stop=True),
                           r=["LA_sel", "STr"], w=[f"ps{pa}"])
                        op("dve", lambda e, pa=pa, rb=rb: e.tensor_tensor(R128_r[rb], PS[pa].rearrange("p (h i) -> p h i", i=64), bc_last(Mh, 64), ALU.mult),
                           r=[f"ps{pa}", "Mh"], w=[f"R{rb}"])
                        if SCAN_CUT <= 2:
                            continue
                        op("pool", lambda e, t=t: e.tensor_tensor(ST2, ST, bc_last(dv["wdec"][:, :, t], 64), ALU.mult), r=["ST", "wdec"], w=["ST2"])
                        op("pe", lambda e, pu=pu, tp=tp: e.matmul(PS[pu][0:64, :], lhsT=Xk_sel_r[:, tp, :], rhs=Vbd16_r.rearrange("p h i -> p (h i)"), start=True, stop=False),
                           r=["Xk_sel", "Vbd16"], w=[f"ps{pu}"])
                        op("pe", lambda e, pu=pu, rb=rb: e.matmul(PS[pu][0:64, :], lhsT=Xb16_r, rhs=R128_r[rb].rearrange("p h i -> p (h i)"), start=False, stop=True),
                           r=["Xb16", f"R{rb}"], w=[f"ps{pu}"])
                        while pend_y:
                            pend_y.pop(0)()
                        op("dve", lambda e, pu=pu: e.tensor_tensor(ST.rearrange("p h i -> p (h i)"), ST2.rearrange("p h i -> p (h i)"), PS[pu][0:64, :], ALU.add),
                           r=["ST2", f"ps{pu}"], w=["ST"])
                        op("act", lambda e: e.activation(STr, ST.rearrange("p h i -> p (h i)"), AF.Copy), r=["ST"], w=["STr"])
                        op("act", lambda e: e.activation(STb, ST, AF.Copy), r=["ST"], w=["STb"])

                        def emit_y(t=t):
                            for h in range(8):
                                op("pe", lambda e, h=h, t=t: e.matmul(PS[4 + h // 4][0:64, (h % 4) * 128 + t:(h % 4) * 128 + t + 1], lhsT=STb[:, h, :], rhs=rbf[:, h, t:t + 1],
                                                                   start=True, stop=True), r=["STb", "rbf"], w=[f"ps{4 + h // 4}"])
                        if is_sample:
                            emit_y()
                            store_state(wkv_so[t])
                        else:
                            pend_y.append(emit_y)
                while pend_y:
                    pend_y.pop(0)()
                if P2_CUT <= 5:
                    return
                for hb in range(2):
                    op("act", lambda e, hb=hb: e.activation(D_["Yd"][:, hb * 4:(hb + 1) * 4, :], v4(PS[4 + hb])[:, :, 0:P], AF.Copy), r=[f"ps{4 + hb}"], w=["kkn", "dvsrc"])
                sum64(dv["Yd"], P)
                for hb in range(2):
                    op("dve", lambda e, hb=hb: e.scalar_tensor_tensor(D_["t1"][:, hb * 4:(hb + 1) * 4, :], v4(PS[6 + hb])[:, :, 0:P], -1.0 / 64, D_["Yd"][:, hb * 4:(hb + 1) * 4, :],
                                                                      ALU.mult, ALU.add), r=[f"ps{6 + hb}", "kkn"], w=["t1"])
                op("dve", lambda e: e.tensor_tensor(D_["t2"], D_["t1"], D_["t1"], ALU.mult), r=["t1"], w=["t2", "dvsrc"])
                sum64(dv["t2"], P)
                for hb in range(2):
                    op("dve", lambda e, hb=hb: e.tensor_scalar(D_["t2"][:, hb * 4:(hb + 1) * 4, :], v4(PS[6 + hb])[:, :, 0:P], 1.0 / 64, GN_EPS, ALU.mult, ALU.add),
                       r=[f"ps{6 + hb}"], w=["t2"])
                op("act", lambda e: e.activation(D_["t2"], D_["t2"], AF.Sqrt), r=["t2"], w=["t2"])
                op("dve", lambda e: e.reciprocal(D_["t2"], D_["t2"]), r=["t2"], w=["t2"])
                op("dve", lambda e: e.tensor_tensor(D_["t1"], D_["t1"], D_["t2"], ALU.mult), r=["t1", "t2"], w=["t1"])
                op("dve", lambda e: e.tensor_tensor(D_["t1"], D_["t1"], bc_last(lnwT, P), ALU.mult), r=["t1", "vecB"], w=["t1"])
                op("dve", lambda e: e.tensor_tensor(D_["t1"], D_["t1"], bc_last(lnbT, P), ALU.add), r=["t1", "vecB"], w=["t1"])
                op("dve", lambda e: e.tensor_tensor(D_["t1"], D_["t1"], D_["bon"], ALU.add), r=["t1", "bon"], w=["t1"])
                op("dve", lambda e: e.tensor_tensor(rwd[:, :, 0:P], D_["t1"], D_["gg"], ALU.mult), r=["t1", "gg"], w=["rwd"])
                grep_ = g1s if is_sample else g1rep
                for half in range(2):
                    cs = slice(half * 512, (half + 1) * 512)
                    pb = 2 + half
                    for c in range(4):
                        op("pe", lambda e, c=c, cs=cs, pb=pb: e.matmul(PS[pb][0:P, :], lhsT=attT[:, c, tcol0:tcol0 + P], rhs=wo_att[:, c, cs], start=(c == 0), stop=False),
                           r=["attT", "wts2"], w=[f"ps{pb}"])
                    for h in range(8):
                        op("pe", lambda e, h=h, cs=cs, pb=pb: e.matmul(PS[pb][0:P, :], lhsT=rwd[:, h, 0:P], rhs=wo_rw[:, h, cs], start=False, stop=(h == 7)),
                           r=["rwd", "wts2"], w=[f"ps{pb}"])
                    op("dve", lambda e, cs=cs, pb=pb: e.tensor_tensor(xn2[0:P, cs], PS[pb][0:P, :], grep_[0:P, cs], ALU.mult), r=[f"ps{pb}", "g1rep", "g1s"], w=["xn2"])
                    op("dve", lambda e, cs=cs: e.tensor_tensor(xn2[0:P, cs], xn2[0:P, cs], xt2[0:P, cs], ALU.add), r=["xn2", "xt2"], w=["xn2"])

            scr2 = AR.alloc([W], F32)

            def load_state(b_):
                dma("sp", lambda e: e.dma_start(out=Sio, in_=st_wkv[b_].rearrange("h i j -> i h j")), w=["Sio"])
                for h in range(8):
                    op("pe", lambda e, h=h: e.transpose(PS[7][0:64, h * 64:(h + 1) * 64], Sio[:, h, :], ident[0:64, 0:64]), r=["Sio", "ident"], w=["ps7"])
                op("dve", lambda e: e.tensor_copy(ST.rearrange("p h i -> p (h i)"), PS[7][0:64, :]), r=["ps7"], w=["ST"])
                op("act", lambda e: e.activation(STr, ST.rearrange("p h i -> p (h i)"), AF.Copy), r=["ST"], w=["STr"])

            def store_state(dst):
                for h in range(8):
                    op("pe", lambda e, h=h: e.transpose(PS[7][0:64, h * 64:(h + 1) * 64], ST[:, h, :], ident[0:64, 0:64]), r=["ST", "ident"], w=["ps7"])
                op("dve", lambda e: e.tensor_copy(Sio.rearrange("p h i -> p (h i)"), PS[7][0:64, :]), r=["ps7"], w=["Sio"])
                dma("sp", lambda e: e.dma_start(out=dst.rearrange("h i j -> i h j"), in_=Sio), r=["Sio"])

            for ti in range(NTILES if P2_CUT > 1 else 0):
                r0 = ti * 128
                rw_tile(ti, 128, x_p[r0:r0 + 128, :], None, r0, None, False)
                dma("sp", lambda e, r0=r0: e.dma_start(out=y_p[r0:r0 + 128, :], in_=xn2), r=["xn2"], w=["y_p"])
            if P2_CUT > 5:
                store_state(wkv_po)
            if P2_CUT > 1 and os.environ.get("MK_NOSAMP", "0") != "1":
                rw_tile(16, NS, x_s, 1, T, shiftT, True)
                dma("sp", lambda e: e.dma_start(out=y_s, in_=xn2[0:NS, :]), r=["xn2"], w=["y_s"])


        if STAGE >= 5:
            S.barrier()
            AR.release(m_attT)
            wg = AR.alloc([8, FFN], BF16)
            wu = AR.alloc([8, FFN], BF16)
            wd_ = AR.alloc([NFC, D], BF16)
            g2rep = AR.alloc([D], F32)
            g2s = AR.alloc([D], F32, parts=NS)
            m3 = AR.mark()
            grow2 = AR.alloc([D], F32, parts=5)
            stg3 = [AR.alloc([4096], F32) for _ in range(2)]
            st3 = [0]

            def load_cast3(dst, src, shape):
                b = st3[0] % 2
                st3[0] += 1
                ne = _prod(shape)
                v = stg3[b][:, 0:ne].rearrange("p (a b) -> p a b", a=shape[0])
                dma("sp", lambda e: e.dma_start(out=v, in_=src), w=[f"stg3{b}"])
                op("pool", lambda e: e.tensor_copy(dst, v), r=[f"stg3{b}"], w=["wts3"])

            wg_v = w_gate.rearrange("(kc p) n -> p kc n", p=128)
            wu_v = w_up.rearrange("(kc p) n -> p kc n", p=128)
            wd_v = w_down.rearrange("(fc p) n -> p fc n", p=128)
            for c in range(0, FFN, 512):
                n = min(512, FFN - c)
                load_cast3(wg[:, :, c:c + n], wg_v[:, :, c:c + n], [8, n])
                load_cast3(wu[:, :, c:c + n], wu_v[:, :, c:c + n], [8, n])
            for fc0 in range(0, NFC, 4):
                nf = min(4, NFC - fc0)
                load_cast3(wd_[:, fc0:fc0 + nf, :], wd_v[:, fc0:fc0 + nf, :], [nf, D])
            for kc in range(8):
                op("pe", lambda e, kc=kc: e.transpose(PS[kc // 4][0:5, (kc % 4) * 128:(kc % 4 + 1) * 128], gate2[:, kc, :], ident), r=["modT", "ident"], w=[f"ps{kc // 4}"])
            op("dve", lambda e: e.tensor_copy(grow2[:, 0:512], PS[0][0:5, :]), r=["ps0"], w=["grow2"])
            op("dve", lambda e: e.tensor_copy(grow2[:, 512:1024], PS[1][0:5, :]), r=["ps1"], w=["grow2"])
            dma("sp", lambda e: e.dma_start(out=gsc[1], in_=grow2), r=["grow2"], w=["gsc1"])
            dma("sp", lambda e: e.dma_start(out=g2rep, in_=gsc[1, 0].partition_broadcast(128)), r=["gsc1"], w=["g2rep"])
            dma("sp", lambda e: e.dma_start(out=g2s, in_=gsc[1, 1:5, :]), r=["gsc1"], w=["g2s"])
            S.barrier()
            AR.release(m3)
            x1t = [AR.alloc([D], F32) for _ in range(1)]
            xn3 = AR.alloc([D], F32)
            h2T = AR.alloc([8, 128], BF16)
            hidT = AR.alloc([NFC, 128], BF16)
            sil = [AR.alloc([128], F32) for _ in range(2)]
            yt = [AR.alloc([D], F32)] * 2
            sm3 = AR.alloc([8], F32)
            scr3 = AR.alloc([NS], F32)

            def ffn_group(tiles, is_sample):
                ncol = 0
                for i_, (P, src, dst) in enumerate(tiles):
                    xb = x1t[i_]
                    dma("sp", lambda e, xb=xb, P=P, src=src: e.dma_start(out=xb[0:P, :], in_=src), r=["y_p", "y_s"], w=[f"x1t{i_}"])
                    ss = sm3[0:P, 0:1]
                    rs = sm3[0:P, 1:2]
                    op("act", lambda e, xb=xb, P=P, ss=ss: e.activation(xn3[0:P, :], xb[0:P, :], AF.Square, accum_out=ss), r=[f"x1t{i_}"], w=["xn3", "sm3"])
                    op("dve", lambda e, ss=ss, rs=rs: e.tensor_scalar(rs, ss, 1.0 / D, RMS_EPS, ALU.mult, ALU.add), r=["sm3"], w=["sm3"])
                    op("act", lambda e, rs=rs: e.activation(rs, rs, AF.Sqrt), r=["sm3"], w=["sm3"])
                    op("dve", lambda e, rs=rs: e.reciprocal(rs, rs), r=["sm3"], w=["sm3"])
                    op("dve", lambda e, xb=xb, P=P, rs=rs: e.tensor_scalar(xn3[0:P, :], xb[0:P, :], rs, None, ALU.mult), r=[f"x1t{i_}", "sm3"], w=["xn3"])
                    for half in range(2):
                        pb = PS[half]
                        for j in range(4):
                            kc = half * 4 + j
                            op("pe", lambda e, kc=kc, j=j, pb=pb, P=P: e.transpose(pb[:, j * 128:j * 128 + P], xn3[0:P, kc * 128:(kc + 1) * 128], ident[0:P, 0:P]),
                               r=["xn3", "ident"], w=[f"ps{half}"])
                        for j in range(4):
                            kc = half * 4 + j
                            if not is_sample:
                                op("act", lambda e, kc=kc, j=j, pb=pb, P=P, ncol=ncol: e.activation(h2T[:, kc, ncol:ncol + P], pb[:, j * 128:j * 128 + P], AF.Identity,
                                                                                                  bias=shift2[:, kc, 0:1], scale=A2[:, kc, 0:1]), r=[f"ps{half}", "A2", "modT"], w=["h2T"])
                            else:
                                op("dve", lambda e, kc=kc, j=j, pb=pb, P=P: e.tensor_tensor(scr3[:, 0:P], pb[:, j * 128:j * 128 + P], A2[:, kc, 1:1 + P], ALU.mult),
                                   r=[f"ps{half}", "A2"], w=["scr3"])
                                op("dve", lambda e, kc=kc, P=P, ncol=ncol: e.tensor_tensor(h2T[:, kc, ncol:ncol + P], scr3[:, 0:P], shift2[:, kc, 1:1 + P], ALU.add),
                                   r=["scr3", "modT"], w=["h2T"])
                    ncol += P
                N = ncol
                for fc in range(NFC):
                    pg = PS[2 + fc % 2]
                    pu_ = PS[4 + fc % 2]
                    sb = sil[fc % 2]
                    for kc in range(8):
                        op("pe", lambda e, fc=fc, kc=kc, pg=pg: e.matmul(pg[:, 0:N], lhsT=wg[:, kc, fc * 128:(fc + 1) * 128], rhs=h2T[:, kc, 0:N], start=(kc == 0), stop=(kc == 7)),
                           r=["h2T", "wts3"], w=[f"ps{2 + fc % 2}"])
                    for kc in range(8):
                        op("pe", lambda e, fc=fc, kc=kc, pu_=pu_: e.matmul(pu_[:, 0:N], lhsT=wu[:, kc, fc * 128:(fc + 1) * 128], rhs=h2T[:, kc, 0:N], start=(kc == 0), stop=(kc == 7)),
                           r=["h2T", "wts3"], w=[f"ps{4 + fc % 2}"])
                    op("act", lambda e, pg=pg, sb=sb: e.activation(sb[:, 0:N], pg[:, 0:N], AF.Silu), r=[f"ps{2 + fc % 2}"], w=[f"sil{fc % 2}"])
                    op("dve", lambda e, fc=fc, pu_=pu_, sb=sb: e.tensor_tensor(hidT[:, fc, 0:N], sb[:, 0:N], pu_[:, 0:N], ALU.mult), r=[f"sil{fc % 2}", f"ps{4 + fc % 2}"], w=["hidT"])
                ncol = 0
                for i_, (P, src, dst) in enumerate(tiles):
                    xb = x1t[i_]
                    yb = yt[i_ % 2]
                    grep_ = g2s if is_sample else g2rep
                    for half in range(2):
                        cs = slice(half * 512, (half + 1) * 512)
                        pb = 6 + half
                        for fc in range(NFC):
                            op("pe", lambda e, fc=fc, cs=cs, pb=pb, P=P, ncol=ncol: e.matmul(PS[pb][0:P, :], lhsT=hidT[:, fc, ncol:ncol + P], rhs=wd_[:, fc, cs],
                                                                                         start=(fc == 0), stop=(fc == NFC - 1)), r=["hidT", "wts3"], w=[f"ps{pb}"])
                        op("dve", lambda e, cs=cs, pb=pb, P=P, yb=yb, grep_=grep_: e.tensor_tensor(yb[0:P, cs], PS[pb][0:P, :], grep_[0:P, cs], ALU.mult),
                           r=[f"ps{pb}", "g2rep", "g2s"], w=["yt0"])
                        op("dve", lambda e, cs=cs, P=P, yb=yb, xb=xb: e.tensor_tensor(yb[0:P, cs], yb[0:P, cs], xb[0:P, cs], ALU.add), r=["yt0", f"x1t{i_}"], w=["yt0"])
                    dma("sp", lambda e, yb=yb, P=P, dst=dst: e.dma_start(out=dst, in_=yb[0:P, :]), r=["yt0"], w=["y_out"])
                    ncol += P

            for g in range(NTILES):
                tl_ = []
                for tl in range(1):
                    ti = g + tl
                    if ti < NTILES:
                        tl_.append((128, y_p[ti * 128:(ti + 1) * 128, :], y_p[ti * 128:(ti + 1) * 128, :]))
                ffn_group(tl_, False)
            ffn_group([(NS, y_s, y_s)], True)

        if os.environ.get("MK_DBG_ATT", "0") == "1":
            dbg_att = nc.dram_tensor("dbg_att", [128, 4, NTILES * 128], BF16, kind="ExternalOutput").ap()
            dma("sp", lambda e: e.dma_start(out=dbg_att, in_=attT[:, :, 0:NTILES * 128]), r=["attT"])
        S.finish()
        with nc.Block() as block:
            S.emit(block)
        print("arena peak words", AR.peak, "of", AR.n)
    return nc


_NC_CACHE = {}


def kernel(x_prompt, x_sample, cache_k, cache_v, cache_idx_k, state_wkv, state_shift, page_table,
           c_prompt, c_sample, norm1_g, norm2_g, w_ada, b_ada, w_in, q_norm_g, k_norm_g,
           rw_mu, rw_w0, rw_w2, rw_a0, rw_a2, rw_g2, rw_k_k, rw_k_a, rw_r_k, rw_ln_w, rw_ln_b,
           w_out, w_ffn_gate, w_ffn_up, w_ffn_down):
    f = lambda a: np.ascontiguousarray(np.asarray(a, dtype=np.float32))
    if "nc" not in _NC_CACHE:
        _NC_CACHE["nc"] = build_program()
    nc = _NC_CACHE["nc"]
    if NOCACHE:
        ck = np.zeros((128, 512), np.float32)
        cv = ck
        cik = np.zeros((128, 64), np.float32)
    else:
        ck = f(cache_k).reshape(NPOOL * PAGE, 512)
        cv = f(cache_v).reshape(NPOOL * PAGE, 512)
        cik = f(cache_idx_k).reshape(NPOOL * PAGE, 64)
    shared = {
        "cache_k": ck, "cache_v": cv, "cache_ik": cik,
        "norm1_g": f(norm1_g).reshape(D), "norm2_g": f(norm2_g).reshape(D),
        "w_ada": f(w_ada).reshape(D, 6 * D), "b_ada": f(b_ada).reshape(6 * D),
        "w_in": f(w_in).reshape(D, IN_COLS), "q_norm_g": f(q_norm_g).reshape(64), "k_norm_g": f(k_norm_g).reshape(64),
        "rw_mu": f(rw_mu).reshape(RW_COLS), "rw_w0": f(rw_w0).reshape(512), "rw_w2": f(rw_w2).reshape(64, 512),
        "rw_a0": f(rw_a0).reshape(512), "rw_a2": f(rw_a2).reshape(64, 512), "rw_g2": f(rw_g2).reshape(128, 512),
        "rw_k_k": f(rw_k_k).reshape(512), "rw_k_a": f(rw_k_a).reshape(512), "rw_r_k": f(rw_r_k).reshape(512),
        "rw_ln_w": f(rw_ln_w).reshape(512), "rw_ln_b": f(rw_ln_b).reshape(512),
        "w_out": f(w_out).reshape(D, D), "w_gate": f(w_ffn_gate).reshape(D, FFN), "w_up": f(w_ffn_up).reshape(D, FFN),
        "w_down": f(w_ffn_down).reshape(FFN, D),
    }
    xp = f(x_prompt)
    xs = f(x_sample).reshape(32, D)
    cp = f(c_prompt)
    cs = f(c_sample)
    sw = f(state_wkv).reshape(32, 8, 64, 64)
    ss = f(state_shift).reshape(32, RW_COLS)
    pt = np.ascontiguousarray(np.asarray(page_table, dtype=np.int32))
    in_maps = []
    for i in range(8):
        m = dict(shared)
        m["x_p"] = xp[i]
        m["x_s"] = np.ascontiguousarray(xs[4 * i:4 * i + 4])
        m["c5"] = np.ascontiguousarray(np.concatenate([cp[i:i + 1], cs[4 * i:4 * i + 4]], axis=0))
        m["st_wkv"] = np.ascontiguousarray(sw[4 * i:4 * i + 4])
        m["st_shift"] = np.ascontiguousarray(ss[4 * i:4 * i + 4])
        m["ptab"] = np.ascontiguousarray(pt[4 * i:4 * i + 4])
        in_maps.append(m)
    res = run_bass_kernel_spmd(nc, in_maps, core_ids=list(range(8)))
    R = res.results
    _NC_CACHE["last"] = R
    g = lambda name: np.stack([np.asarray(R[i][name]) for i in range(8)], axis=0)
    y_p = g("y_p")
    y_s = g("y_s").reshape(32, 1, D)
    k_p = g("k_po").reshape(1, 8, T, 8, 64)
    v_p = g("v_po").reshape(1, 8, T, 8, 64)
    ik_p = g("ik_po").reshape(1, 8, T, 64)
    wkv_p = g("wkv_po").reshape(1, 8, 8, 64, 64)
    sh_p = g("sh_po").reshape(1, 8, RW_COLS)
    k_s = g("k_so").reshape(1, 32, 1, 8, 64)
    v_s = g("v_so").reshape(1, 32, 1, 8, 64)
    ik_s = g("ik_so").reshape(1, 32, 1, 64)
    wkv_s = g("wkv_so").reshape(1, 32, 8, 64, 64)
    sh_s = g("sh_so").reshape(1, 32, RW_COLS)
    return (y_p, y_s, k_p, v_p, ik_p, wkv_p, sh_p, k_s, v_s, ik_s, wkv_s, sh_s)
```
